# Optimizing a Trainium2 kernel written in Bass

```python
import math
import jax, jax.numpy as jnp
from jax import lax
import numpy as np

D_MODEL = 1024
BATCH = 8
SEQ = 4096
DEPTH = 1

GRID_W = 64
CTX_LEN = 256
N_FOURIER_GROUPS = 4
FOURIER_GROUP_DIM = 128
FOURIER_WIDTH = N_FOURIER_GROUPS * FOURIER_GROUP_DIM
DN_HEADS = 8
DN_HEAD_DIM = 128
DN_WIDTH = DN_HEADS * DN_HEAD_DIM
DN_CONV = 3
DN_CHUNK = 64
N_DIR = 2
N_BRANCH = 2
D_FF = 2816
FFN_CONV = 3
NORM_EPS = 1e-6
L2_EPS = 1e-6

OFF_F = 0
OFF_Q = OFF_F + FOURIER_WIDTH
OFF_K = OFF_Q + DN_WIDTH
OFF_V = OFF_K + DN_WIDTH
OFF_Z = OFF_V + DN_WIDTH
OFF_B = OFF_Z + DN_WIDTH
OFF_A = OFF_B + N_DIR * DN_HEADS
OFF_G = OFF_A + N_DIR * DN_HEADS
IN_WIDTH = OFF_G + N_BRANCH * D_MODEL

kernel_name = "hybrid_fourier_deltanet_convffn_dit"


def rms_norm(x, w):
    xf = x.astype(jnp.float32)
    y = xf * lax.rsqrt(jnp.mean(xf * xf, axis=-1, keepdims=True) + NORM_EPS)
    return (y * w.astype(jnp.float32)).astype(x.dtype)


def l2norm(x):
    return x * lax.rsqrt(jnp.sum(x * x, axis=-1, keepdims=True) + L2_EPS)


def modulate(h, shift, scale):
    return h * (1 + scale) + shift


def dwconv1d(x, w):
    k = w.shape[0]
    pad = (k - 1) // 2
    return lax.conv_general_dilated(
        x, w[:, None, :].astype(x.dtype), window_strides=(1,), padding=[(pad, pad)],
        dimension_numbers=('NWC', 'WIO', 'NWC'), feature_group_count=x.shape[-1])


def dwconv2d(x, w, rows, cols):
    b, t, ch = x.shape
    kh, kw = w.shape[0], w.shape[1]
    y = lax.conv_general_dilated(
        x.reshape(b, rows, cols, ch), w[:, :, None, :].astype(x.dtype), window_strides=(1, 1),
        padding=[((kh - 1) // 2, (kh - 1) // 2), ((kw - 1) // 2, (kw - 1) // 2)],
        dimension_numbers=('NHWC', 'HWIO', 'NHWC'), feature_group_count=ch)
    return y.reshape(b, t, ch)


def fourier_mix(f):
    b, t, _ = f.shape
    fg = f.astype(jnp.float32).reshape(b, t, N_FOURIER_GROUPS, FOURIER_GROUP_DIM)
    out = jnp.fft.fft2(fg, axes=(1, 3), norm="ortho").real
    return out.reshape(b, t, FOURIER_WIDTH).astype(f.dtype)


def chunk_gated_delta(q, k, v, g, beta, state):
    b, t, h, dk = q.shape
    dv = v.shape[-1]
    n = t // DN_CHUNK

    def chunks(a):
        a = a.reshape((b, n, DN_CHUNK, h) + a.shape[3:])
        return jnp.moveaxis(a, (1, 3), (0, 2))

    qc, kc, vc = chunks(q), chunks(k), chunks(v)
    gc = jnp.cumsum(chunks(g), axis=-1)
    bc = chunks(beta)
    idx = jnp.arange(DN_CHUNK)
    lower_strict = idx[:, None] > idx[None, :]
    lower_incl = idx[:, None] >= idx[None, :]
    decay = jnp.exp(jnp.where(lower_incl, gc[..., :, None] - gc[..., None, :], -jnp.inf))
    kk = jnp.einsum('nbhcd,nbhsd->nbhcs', kc, kc)
    l_mat = jnp.where(lower_strict, bc[..., :, None] * kk * decay, 0.0)
    rhs = jnp.concatenate([vc * bc[..., None], kc * (bc * jnp.exp(gc))[..., None]], axis=-1)
    sol = lax.linalg.triangular_solve(l_mat, rhs, left_side=True, lower=True, unit_diagonal=True)
    u, w = sol[..., :dv], sol[..., dv:]
    qk = jnp.einsum('nbhcd,nbhsd->nbhcs', qc, kc) * decay
    g_last = gc[..., -1]
    k_tail = kc * jnp.exp(g_last[..., None] - gc)[..., None]
    q_head = qc * jnp.exp(gc)[..., None]

    def step(s, xs):
        q_h, qk_i, u_i, w_i, k_t, gl = xs
        v_new = u_i - jnp.einsum('bhcd,bhde->bhce', w_i, s)
        o = jnp.einsum('bhcd,bhde->bhce', q_h, s) + jnp.einsum('bhcs,bhse->bhce', qk_i, v_new)
        s = s * jnp.exp(gl)[..., None, None] + jnp.einsum('bhcd,bhce->bhde', k_t, v_new)
        return s, o

    state, o = lax.scan(step, state, (q_head, qk, u, w, k_tail, g_last))
    o = jnp.moveaxis(o, (0, 2), (1, 3)).reshape(b, t, h, dv)
    return o, state


def mixer_core(h, init_states, w_in, conv_qkv, a_log, dt_bias):
    b, t, _ = h.shape
    p = h @ w_in
    f_in = p[..., OFF_F:OFF_Q]
    qkv = jax.nn.silu(dwconv1d(p[..., OFF_Q:OFF_Z], conv_qkv)).astype(jnp.float32)
    q, k, v = [a.reshape(b, t, DN_HEADS, DN_HEAD_DIM) for a in jnp.split(qkv, 3, axis=-1)]
    q = l2norm(q) * (DN_HEAD_DIM ** -0.5)
    k = l2norm(k)
    z = p[..., OFF_Z:OFF_B]
    beta = jax.nn.sigmoid(p[..., OFF_B:OFF_A].astype(jnp.float32)).reshape(b, t, N_DIR, DN_HEADS)
    a = p[..., OFF_A:OFF_G].astype(jnp.float32).reshape(b, t, N_DIR, DN_HEADS)
    g = -jnp.exp(a_log.astype(jnp.float32)) * jax.nn.softplus(a + dt_bias.astype(jnp.float32))
    gates = jax.nn.sigmoid(p[..., OFF_G:]).reshape(b, t, N_BRANCH, D_MODEL)
    flip = lambda arr: jnp.flip(arr, axis=1)
    o_f, s_f = chunk_gated_delta(q, k, v, g[:, :, 0], beta[:, :, 0], init_states[0])
    o_b, s_b = chunk_gated_delta(flip(q), flip(k), flip(v), flip(g[:, :, 1]), flip(beta[:, :, 1]),
                                 init_states[1])
    return f_in, o_f + flip(o_b), z, gates, (s_f, s_b)


def merge_branches(f_in, o_dn, z, gates, dn_norm, w_fourier, w_dn, w_out):
    b, t, _ = z.shape
    zh = z.reshape(b, t, DN_HEADS, DN_HEAD_DIM).astype(jnp.float32)
    o = (rms_norm(o_dn, dn_norm) * jax.nn.silu(zh)).astype(z.dtype).reshape(b, t, DN_WIDTH)
    y_f = fourier_mix(f_in) @ w_fourier
    y_d = o @ w_dn
    return (gates[:, :, 0] * y_f + gates[:, :, 1] * y_d) @ w_out


def ffn_sublayer(stream, shift, scale, gate, g_pre, g_post, w_up, conv_w, w_down, rows, cols):
    h = modulate(rms_norm(stream, g_pre), shift, scale)
    a, u = jnp.split(h @ w_up, 2, axis=-1)
    a = dwconv2d(a, conv_w, rows, cols)
    y = (jax.nn.silu(a) * u) @ w_down
    return stream + gate * rms_norm(y, g_post)


def setup_inputs(seed: int = 0) -> dict:
    key = jax.random.key(seed)
    ks = jax.random.split(key, 24)
    nrm = lambda k, shape, scale: jax.random.normal(k, shape, jnp.float32) * scale
    gain = lambda k, n: 1.0 + nrm(k, (DEPTH, n), 0.02)
    x = nrm(ks[0], (BATCH, SEQ, D_MODEL), 1.0)
    c = nrm(ks[1], (BATCH, D_MODEL), 1.0)
    ctx = nrm(ks[2], (BATCH, CTX_LEN, D_MODEL), 1.0)
    c_ctx = nrm(ks[3], (D_MODEL,), 1.0)
    w_ada = nrm(ks[4], (DEPTH, D_MODEL, 6 * D_MODEL), 0.5 * D_MODEL ** -0.5)
    b_ada = nrm(ks[5], (DEPTH, 6 * D_MODEL), 0.01)
    norm_pre_mix = gain(ks[6], D_MODEL)
    norm_post_mix = gain(ks[7], D_MODEL)
    norm_pre_ffn = gain(ks[8], D_MODEL)
    norm_post_ffn = gain(ks[9], D_MODEL)
    w_in = nrm(ks[10], (DEPTH, D_MODEL, IN_WIDTH), D_MODEL ** -0.5)
    conv_qkv = nrm(ks[11], (DEPTH, DN_CONV, 3 * DN_WIDTH), DN_CONV ** -0.5)
    a_log = jnp.log(jax.random.uniform(ks[12], (DEPTH, N_DIR, DN_HEADS), jnp.float32, 1.0, 16.0))
    dt = jnp.exp(jax.random.uniform(ks[13], (DEPTH, N_DIR, DN_HEADS), jnp.float32,
                                    math.log(1e-3), math.log(1e-1)))
    dt_bias = dt + jnp.log(-jnp.expm1(-dt))
    dn_norm = gain(ks[14], DN_HEAD_DIM)
    w_fourier = nrm(ks[15], (DEPTH, FOURIER_WIDTH, D_MODEL), FOURIER_WIDTH ** -0.5)
    w_dn = nrm(ks[16], (DEPTH, DN_WIDTH, D_MODEL), DN_WIDTH ** -0.5)
    w_out = nrm(ks[17], (DEPTH, D_MODEL, D_MODEL), D_MODEL ** -0.5)
    w_up = nrm(ks[18], (DEPTH, D_MODEL, 2 * D_FF), D_MODEL ** -0.5)
    conv_ffn = nrm(ks[19], (DEPTH, FFN_CONV, FFN_CONV, D_FF), 1.0 / FFN_CONV)
    w_down = nrm(ks[20], (DEPTH, D_FF, D_MODEL), D_FF ** -0.5)
    return {"x": x, "c": c, "ctx": ctx, "c_ctx": c_ctx, "w_ada": w_ada, "b_ada": b_ada,
            "norm_pre_mix": norm_pre_mix, "norm_post_mix": norm_post_mix,
            "norm_pre_ffn": norm_pre_ffn, "norm_post_ffn": norm_post_ffn,
            "w_in": w_in, "conv_qkv": conv_qkv, "a_log": a_log, "dt_bias": dt_bias,
            "dn_norm": dn_norm, "w_fourier": w_fourier, "w_dn": w_dn, "w_out": w_out,
            "w_up": w_up, "conv_ffn": conv_ffn, "w_down": w_down}


def reference(x, c, ctx, c_ctx, w_ada, b_ada, norm_pre_mix, norm_post_mix, norm_pre_ffn,
              norm_post_ffn, w_in, conv_qkv, a_log, dt_bias, dn_norm, w_fourier, w_dn, w_out,
              w_up, conv_ffn, w_down):
    bsz, seq, _ = x.shape
    rows = seq // GRID_W
    ctx_len = ctx.shape[1]
    for l in range(DEPTH):
        mod_x = (jax.nn.silu(c) @ w_ada[l] + b_ada[l])[:, None, :]
        mod_c = (jax.nn.silu(c_ctx) @ w_ada[l] + b_ada[l])[None, None, :]
        sh1, sc1, gt1, sh2, sc2, gt2 = jnp.split(mod_x, 6, axis=-1)
        csh1, csc1, cgt1, csh2, csc2, cgt2 = jnp.split(mod_c, 6, axis=-1)
        zero = jnp.zeros((bsz, DN_HEADS, DN_HEAD_DIM, DN_HEAD_DIM), jnp.float32)

        hc = modulate(rms_norm(ctx, norm_pre_mix[l]), csh1, csc1)
        fc, oc, zc, gtc, ctx_states = mixer_core(hc, (zero, zero), w_in[l], conv_qkv[l],
                                                 a_log[l], dt_bias[l])

        hx = modulate(rms_norm(x, norm_pre_mix[l]), sh1, sc1)
        fx, ox, zx, gtx, _ = mixer_core(hx, ctx_states, w_in[l], conv_qkv[l], a_log[l], dt_bias[l])
        yx = merge_branches(fx, ox, zx, gtx, dn_norm[l], w_fourier[l], w_dn[l], w_out[l])
        x = x + gt1 * rms_norm(yx, norm_post_mix[l])
        x = ffn_sublayer(x, sh2, sc2, gt2, norm_pre_ffn[l], norm_post_ffn[l], w_up[l],
                         conv_ffn[l], w_down[l], rows, GRID_W)

        if l < DEPTH - 1:
            yc = merge_branches(fc, oc, zc, gtc, dn_norm[l], w_fourier[l], w_dn[l], w_out[l])
            ctx = ctx + cgt1 * rms_norm(yc, norm_post_mix[l])
            ctx = ffn_sublayer(ctx, csh2, csc2, cgt2, norm_pre_ffn[l], norm_post_ffn[l], w_up[l],
                               conv_ffn[l], w_down[l], 1, ctx_len)
    return x
```

```python
import os
from contextlib import ExitStack
import numpy as np
import ml_dtypes
import concourse.bass as bass
import concourse.mybir as mybir
from concourse.bass_utils import run_bass_kernel_spmd

F32 = mybir.dt.float32
BF16 = mybir.dt.bfloat16
AF = mybir.ActivationFunctionType
ALU = mybir.AluOpType

D = 1024
T = 4096
TC = 256
TA = TC + T
NT = TA // 128
H = 8
OFF_F, OFF_Q, OFF_K, OFF_V, OFF_Z, OFF_B, OFF_A, OFF_G = 0, 512, 1536, 2560, 3584, 4608, 4624, 4640
INW = 6688
DFF = 2816
NFF = DFF // 128
EPS = 1e-6


class Buf:
    __slots__ = ("name", "last_w", "readers", "dsem", "dcount", "excl")

    def __init__(self, name, excl=False):
        self.name = name
        self.excl = excl
        self.last_w = None
        self.readers = []
        self.dsem = None
        self.dcount = 0


class V:
    __slots__ = ("ap", "buf")

    def __init__(self, ap, buf):
        self.ap = ap
        self.buf = buf

    def __getitem__(self, idx):
        return V(self.ap[idx], self.buf)

    def sub(self, idx, buf):
        return V(self.ap[idx], buf)


def _bufs(*vs):
    out = []
    for v in vs:
        if isinstance(v, V) and v.buf is not None and v.buf not in out:
            out.append(v.buf)
    return out


def _ap(v):
    return v.ap if isinstance(v, V) else v


class K:
    def __init__(self, nc):
        self.nc = nc
        self.engs = {"pe": nc.tensor, "act": nc.scalar, "dve": nc.vector, "pool": nc.gpsimd, "sp": nc.sync}
        self.sem = {n: nc.alloc_semaphore(f"s_{n}") for n in self.engs}
        self.cnt = {n: 0 for n in self.engs}
        self.known = {n: {} for n in self.engs}
        self.dsems = []
        self.nins = 0
        self.nwaits = 0
        self.stack = ExitStack()
        self.limit = None
        self.log = []

    def _uid(self):
        self.uid = getattr(self, 'uid', 0) + 1
        return self.uid

    def sb(self, name, shape, dt, nbuf=None):
        t = self.stack.enter_context(self.nc.sbuf_tensor(f"sb{self._uid()}_" + name, list(shape), dt))
        return V(t[:] if hasattr(t, "__getitem__") else t.ap(), Buf(name) if nbuf is None else nbuf)

    def init_banks(self):
        self.banks = []
        for i in range(8):
            t = self.nc.psum_tensor(f"ps_bank{i}", [128, 512], F32).__enter__()
            self.banks.append(V(t[:], Buf(f"bank{i}", excl=True)))

    def pv(self, bank, lo, hi, dt=F32, inner=None):
        b = self.banks[bank]
        ap = b.ap[:, lo:hi]
        if dt != F32:
            ap = ap.bitcast(dt)
        if inner is not None:
            ap = ap.rearrange("p (a b) -> p a b", b=inner)
        return V(ap, b.buf)

    def _wait(self, e, ev):
        sem, val, src = ev
        if src == "pe" and e == "pe":
            return
        kn = self.known[e]
        if kn.get(sem.num, 0) >= val:
            return
        kn[sem.num] = val
        self.engs[e].wait_ge(sem, val)
        self.nwaits += 1

    def _deps(self, e, reads, writes):
        best = {}
        def add(ev):
            s = ev[0].num
            if s not in best or best[s][1] < ev[1]:
                best[s] = ev
        for b in reads:
            if b.last_w is not None:
                add(b.last_w)
        for b in writes:
            if b.last_w is not None:
                add(b.last_w)
            for ev in b.readers:
                add(ev)
        for ev in best.values():
            self._wait(e, ev)

    def _record(self, ev, reads, writes):
        for b in reads:
            if b in writes:
                continue
            b.readers.append(ev)
            if len(b.readers) > 10:
                best = {}
                for x in b.readers:
                    s = x[0].num
                    if s not in best or best[s][1] < x[1]:
                        best[s] = x
                b.readers = list(best.values())
        for b in writes:
            b.last_w = ev
            b.readers = []

    def op(self, e, fn, reads, writes):
        if self.limit is not None and self.nins >= self.limit:
            return
        ex = [b for b in reads if b.excl and b not in writes]
        if ex:
            writes = list(writes) + ex
        self._deps(e, reads, writes)
        ins = fn(self.engs[e])
        if os.environ.get('PRINS') and self.nins in range(int(os.environ.get('PRINS','0')), int(os.environ.get('PRINS','0')) + 4):
            print('INS', self.nins, ins.concise())
        self.cnt[e] += 1
        ins.then_inc(self.sem[e], 1)
        self._record((self.sem[e], self.cnt[e], e), reads, writes)
        self.nins += 1

    def dma(self, q, out, in_, key=None, **kw):
        if self.limit is not None and self.nins >= self.limit:
            return
        reads, writes = _bufs(in_), _bufs(out)
        self._deps(q, reads, writes)
        kb = key.buf if key is not None else (out.buf if not isinstance(out.buf, DBuf) else in_.buf)
        if kb.dsem is None:
            kb.dsem = self.nc.alloc_semaphore(f"d{self._uid()}_{kb.name}")
            self.dsems.append(kb)
        ins = self.engs[q].dma_start(out=_ap(out), in_=_ap(in_), **kw)
        kb.dcount += 1
        ins.then_inc(kb.dsem, 16)
        self._record((kb.dsem, 16 * kb.dcount, "dma"), reads, writes)
        self.nins += 1

    def barrier(self):
        for e in self.engs:
            for f in self.engs:
                if f != e and self.cnt[f] > 0:
                    self._wait(e, (self.sem[f], self.cnt[f], f))
            for kb in self.dsems:
                if kb.dcount > 0:
                    self._wait(e, (kb.dsem, 16 * kb.dcount, "dma"))

    def mm(self, out, lhsT, rhs, start=True, stop=True):
        self.op("pe", lambda e: e.matmul(_ap(out), lhsT=_ap(lhsT), rhs=_ap(rhs), start=start, stop=stop),
                _bufs(lhsT, rhs) + ([] if start else _bufs(out)), _bufs(out))

    def tr(self, out, in_, ident):
        self.op("pe", lambda e: e.transpose(out=_ap(out), in_=_ap(in_), identity=_ap(ident)), _bufs(in_, ident), _bufs(out))

    def act(self, out, in_, func, bias=0.0, scale=1.0, accum=None, eng="act"):
        kw = {}
        if accum is not None:
            kw["accum_out"] = _ap(accum)
        self.op("act", lambda e: e.activation(out=_ap(out), in_=_ap(in_), func=func, bias=_ap(bias), scale=_ap(scale), **kw),
                _bufs(in_, bias, scale), _bufs(out, accum))

    def ts(self, e, out, in0, s1, s2=None, op0=ALU.mult, op1=None):
        if op1 is None:
            f = lambda g: g.tensor_scalar(out=_ap(out), in0=_ap(in0), scalar1=_ap(s1), scalar2=None, op0=op0)
        else:
            f = lambda g: g.tensor_scalar(out=_ap(out), in0=_ap(in0), scalar1=_ap(s1), scalar2=_ap(s2), op0=op0, op1=op1)
        self.op(e, f, _bufs(in0, s1, s2), _bufs(out))

    def tt(self, e, out, a, b, op):
        self.op(e, lambda g: g.tensor_tensor(out=_ap(out), in0=_ap(a), in1=_ap(b), op=op), _bufs(a, b), _bufs(out))

    def stt(self, e, out, in0, scalar, in1, op0, op1):
        self.op(e, lambda g: g.scalar_tensor_tensor(out=_ap(out), in0=_ap(in0), scalar=_ap(scalar), in1=_ap(in1), op0=op0, op1=op1),
                _bufs(in0, scalar, in1), _bufs(out))

    def copy(self, e, out, in_):
        if e == "act":
            self.act(out, in_, AF.Copy)
        else:
            self.op(e, lambda g: g.tensor_copy(out=_ap(out), in_=_ap(in_)), _bufs(in_), _bufs(out))

    def recip(self, out, in_):
        self.op("dve", lambda g: g.reciprocal(out=_ap(out), in_=_ap(in_)), _bufs(in_), _bufs(out))

    def memset(self, e, out, val):
        self.op(e, lambda g: g.memset(_ap(out), val), [], _bufs(out))

    def asel(self, out, in_, pattern, cmp, fill, base, cm):
        self.op("pool", lambda g: g.affine_select(out=_ap(out), in_=_ap(in_), pattern=pattern, compare_op=cmp, fill=fill,
                                                  base=base, channel_multiplier=cm), _bufs(in_), _bufs(out))


class DBuf(Buf):
    __slots__ = ("is_dram",)

    def __init__(self, name):
        super().__init__(name)
        self.is_dram = True


def dramv(nc, name, shape, dt, kind):
    t = nc.dram_tensor(name, list(shape), dt, kind=kind)
    return V(t.ap(), DBuf(name))


def build(stage=99, dbg=None):
    nc = bass.Bass("TRN2", target_bir_lowering=False)
    k = K(nc)
    k.init_banks()
    if dbg and 'limit' in dbg:
        k.limit = dbg['limit']
    IN = lambda name, shape, dt=F32: dramv(nc, name, shape, dt, "ExternalInput")
    x_d = IN("x", [T, D])
    ctx_d = IN("ctx", [TC, D])
    cc_d = IN("cc", [128, 8, 2])
    wada_d = IN("w_ada", [D, 6 * D])
    bada_d = IN("b_ada", [128, 48])
    nrm_d = IN("norms", [128, 4, 8])
    win_d = IN("w_in", [D, INW])
    cqkv_d = IN("conv_qkv", [128, 24, 3])
    gpar_d = IN("gpar", [128, 2, 16])
    dnn_d = IN("dn_norm", [128, 1])
    wf_d = IN("w_fourier", [512, D])
    wdn_d = IN("w_dn", [D, D])
    wout_d = IN("w_out", [D, D])
    wup_d = IN("w_up", [D, 2 * DFF])
    cffn_d = IN("conv_ffn", [128, NFF, 9])
    wdown_d = IN("w_down", [DFF, D])
    cos_d = IN("dft_cos", [T, T], BF16)
    sin_d = IN("dft_sin", [T, T], BF16)
    c128_d = IN("dft128", [128, 256], BF16)
    hmask_d = IN("hmask", [128, 7, 128], BF16)
    out_d = dramv(nc, "out", [T, D], F32, "ExternalOutput")
    dbg_out = {}
    if dbg:
        for nm, shp in dbg.items():
            if nm in ("heads", "nsteps", "limit"):
                continue
            dbg_out[nm] = dramv(nc, "dbg_" + nm, shp, F32, "ExternalOutput")
    x1_d = dramv(nc, "x1_scr", [T, D], F32, "Internal")
    oT_d = dramv(nc, "oT_scr", [H, 128, T], BF16, "Internal")
    gT_d = dramv(nc, "gT_scr", [NFF, 128, T], BF16, "Internal")
    yT_d = dramv(nc, "yT_scr", [4, 128, T], BF16, "Internal")

    identf = k.sb("identf", [128, 128], F32)
    ident = k.sb("ident", [128, 128], BF16)
    onesf = k.sb("onesf", [128, 128], F32)
    onesb = k.sb("onesb", [128, 128], BF16)
    negm = [k.sb(f"negm{d}", [128, 128], F32) for d in range(2)]
    smask = [k.sb(f"smask{d}", [128, 128], F32) for d in range(2)]
    ut = [k.sb(f"ut{d}", [128, 128], F32) for d in range(2)]
    zerof = k.sb("zerof", [128, 128], F32)
    scal_t = k.sb("scal_t", [128, 8], F32)
    k.memset("pool", zerof, 0.0)
    k.memset("pool", onesf, 1.0)
    k.copy("dve", onesb, onesf)
    k.asel(identf, zerof, [[-1, 128]], ALU.not_equal, 1.0, 0, 1)
    k.copy("dve", ident, identf)
    k.asel(negm[0], zerof, [[1, 128]], ALU.is_ge, -1e5, 0, -1)
    k.asel(smask[0], onesf, [[1, 128]], ALU.is_gt, 0.0, 0, -1)
    k.asel(ut[0], onesf, [[1, 128]], ALU.is_ge, 0.0, 0, -1)
    k.asel(negm[1], zerof, [[-1, 128]], ALU.is_ge, -1e5, 0, 1)
    k.asel(smask[1], onesf, [[-1, 128]], ALU.is_gt, 0.0, 0, 1)
    k.asel(ut[1], onesf, [[-1, 128]], ALU.is_ge, 0.0, 0, 1)

    nrm = k.sb("nrm", [128, 4, 8], F32)
    k.dma("sp", nrm, nrm_d)
    cqkv = k.sb("cqkv", [128, 24, 3], F32)
    k.dma("sp", cqkv, cqkv_d)
    gpar = k.sb("gpar", [128, 2, 16], F32)
    k.dma("sp", gpar, gpar_d)
    dnn = k.sb("dnn", [128, 1], F32)
    k.dma("sp", dnn, dnn_d)
    cffn = k.sb("cffn", [128, NFF, 9], F32)
    k.dma("sp", cffn, cffn_d)
    c128 = k.sb("c128", [128, 256], BF16)
    k.dma("sp", c128, c128_d)
    hmask = k.sb("hmask", [128, 7, 128], BF16)
    k.dma("sp", hmask, hmask_d)
    bada = k.sb("bada", [128, 48], F32)
    k.dma("sp", bada, bada_d)
    cc = k.sb("cc", [128, 8, 2], F32)
    k.dma("sp", cc, cc_d)

    mod = k.sb("mod", [128, 48, 2], F32)
    scc = k.sb("scc", [128, 8, 2], F32)
    k.act(scc, cc, AF.Silu)
    with ExitStack() as st:
        k.stack = st
        wa = [k.sb(f"wa{i}", [128, 8, 512], F32) for i in range(2)]
        pm = k.pv(0, 0, 96, F32, 2)
        wv = wada_d.ap.rearrange("(k p) c -> p k c", p=128)
        for blk in range(12):
            w = wa[blk % 2]
            k.dma("sp" if blk % 2 == 0 else "pool", w, V(wv[:, :, blk * 512:(blk + 1) * 512], wada_d.buf))
            for oc in range(4):
                for kk in range(8):
                    k.mm(pm[:, blk * 4 + oc, :], w[:, kk, oc * 128:(oc + 1) * 128], scc[:, kk, :], start=(kk == 0), stop=(kk == 7))
        for j in range(2):
            k.tt("dve", mod[:, :, j], pm[:, :, j], bada, ALU.add)
        k.barrier()
    k.stack = ExitStack()
    coef = k.sb("coef", [128, 8, 8], F32)
    def modc(i, j):
        return mod[:, i * 8:(i + 1) * 8, j]
    k.stt("dve", coef[:, 0, :], modc(1, 0), 1.0, nrm[:, 0, :], ALU.add, ALU.mult)
    k.copy("dve", coef[:, 1, :], modc(0, 0))
    k.stt("dve", coef[:, 2, :], modc(1, 1), 1.0, nrm[:, 0, :], ALU.add, ALU.mult)
    k.copy("dve", coef[:, 3, :], modc(0, 1))
    k.tt("dve", coef[:, 4, :], modc(2, 0), nrm[:, 1, :], ALU.mult)
    k.stt("dve", coef[:, 5, :], modc(4, 0), 1.0, nrm[:, 2, :], ALU.add, ALU.mult)
    k.copy("dve", coef[:, 6, :], modc(3, 0))
    k.tt("dve", coef[:, 7, :], modc(5, 0), nrm[:, 3, :], ALU.mult)
    if "coef" in dbg_out:
        k.dma("sp", dbg_out["coef"], coef)
    if stage <= 0:
        return finish(nc, k, out_d)

    s_h = ExitStack()
    k.stack = s_h
    hT = k.sb("hT", [128, 8, TA], BF16)
    hbuf = [Buf(f"hT{t}") for t in range(NT)]

    def norm_tile(src_tile_v, tile_idx, ca, cb, dst, dstbufs, tm):
        nb = len(tm["sq"])
        sq = tm["sq"][tile_idx % nb]
        ss = tm["ss"][tile_idx % nb]
        xn = tm["xn"][tile_idx % nb]
        pt = tm["pt"][tile_idx % len(tm["pt"])]
        k.act(sq, src_tile_v, AF.Square, accum=ss)
        k.act(ss, ss, AF.Sqrt, bias=EPS, scale=1.0 / D)
        k.recip(ss, ss)
        k.ts("dve", xn, src_tile_v, ss[:, 0:1])
        for c in range(8):
            k.tr(pt[:, c, :], xn[:, c * 128:(c + 1) * 128], ident)
        for c in range(8):
            dv = V(dst.ap[:, c, tile_idx * 128:(tile_idx + 1) * 128], dstbufs[tile_idx] if isinstance(dstbufs, list) else dstbufs)
            if c % 2 == 0:
                k.ts("dve", dv, pt[:, c, :], coef[:, ca, c:c + 1], coef[:, cb, c:c + 1], ALU.mult, ALU.add)
            else:
                k.act(dv, pt[:, c, :], AF.Identity, bias=coef[:, cb, c:c + 1], scale=coef[:, ca, c:c + 1])

    p1 = ExitStack()
    k.stack = p1
    nt_sq = [k.sb(f"nt_sq{i}", [128, D], F32) for i in range(2)]
    nt_ss = [k.sb(f"nt_ss{i}", [128, 1], F32) for i in range(2)]
    nt_xn = [k.sb(f"nt_xn{i}", [128, D], BF16) for i in range(2)]
    xin = [k.sb(f"xin{i}", [128, D], F32) for i in range(3)]
    nt_pt = [k.pv(i, 0, 512, BF16, 128) for i in range(2)]
    tm1 = {"sq": nt_sq, "ss": nt_ss, "xn": nt_xn, "pt": nt_pt}
    for t in range(NT):
        xi = xin[t % 3]
        if t < 2:
            src = V(ctx_d.ap[t * 128:(t + 1) * 128, :], ctx_d.buf)
        else:
            src = V(x_d.ap[(t - 2) * 128:(t - 1) * 128, :], x_d.buf)
        k.dma("sp" if t % 2 == 0 else "pool", xi, src)
        norm_tile(xi, t, 2 if t < 2 else 0, 3 if t < 2 else 1, hT, hbuf, tm1)
    k.barrier()
    p1.close()
    k.stack = ExitStack()
    if "hT" in dbg_out:
        with ExitStack() as st:
            k.stack = st
            tmpf = k.sb("dbg_hT", [128, 8, 1024], F32)
            k.copy("dve", tmpf, V(hT.ap[:, :, 0:1024], None))
            k.dma("sp", dbg_out["hT"], tmpf)
            k.barrier()
        k.stack = ExitStack()
    hTv = V(hT.ap, Buf("hT_all"))
    if stage <= 1:
        return finish(nc, k, out_d)

    winv = win_d.ap.rearrange("(k p) c -> p k c", p=128)

    def bcast_t(v, n):
        return V(v.ap.unsqueeze(1).to_broadcast([128, n, 16]), v.buf)

    s_g = ExitStack()
    k.stack = s_g
    beta = k.sb("beta", [128, NT, 16], F32)
    nbeta = k.sb("nbeta", [128, NT, 16], F32)
    gc = k.sb("gc", [128, NT, 16], F32)
    egc = k.sb("egc", [128, NT, 16], F32)
    ekt = k.sb("ekt", [128, NT, 16], F32)
    egl = k.sb("egl", [128, NT, 16], F32)
    with ExitStack() as st:
        k.stack = st
        wbaf = k.sb("wbaf", [128, 8, 32], F32)
        wba = k.sb("wba", [128, 8, 32], BF16)
        graw = k.sb("graw", [128, NT, 32], F32)
        gg = k.sb("gg", [128, NT, 16], F32)
        gtmp = k.sb("gtmp", [128, NT, 16], F32)
        negA = k.sb("negA", [128, 16], F32)
        pg = k.pv(0, 0, 512, F32, 32)
        pc0, pc1, pt0, pt1 = k.banks[1], k.banks[2], k.banks[3], k.banks[4]
        k.dma("sp", wbaf, V(winv[:, :, OFF_B:OFF_B + 32], win_d.buf))
        k.copy("dve", wba, wbaf)
        for g0 in range(0, NT, 16):
            n = min(16, NT - g0)
            for j in range(n):
                t = g0 + j
                for kk in range(8):
                    k.mm(pg[:, j, :], hTv[:, kk, t * 128:(t + 1) * 128], wba[:, kk, :], start=(kk == 0), stop=(kk == 7))
            k.copy("act", graw[:, g0:g0 + n, :], pg[:, 0:n, :])
        k.act(beta, graw[:, :, 0:16], AF.Sigmoid)
        k.ts("dve", nbeta, beta, -1.0)
        k.act(negA, gpar[:, 0, :], AF.Exp)
        k.ts("dve", negA, negA, -1.0)
        k.tt("dve", gg, graw[:, :, 16:32], bcast_t(gpar[:, 1, :], NT), ALU.add)
        k.act(gg, gg, AF.Exp)
        k.act(gg, gg, AF.Ln, bias=1.0)
        k.tt("dve", gg, gg, bcast_t(negA, NT), ALU.mult)
        if "g" in dbg_out:
            k.dma("sp", dbg_out["g"], gg)
            k.dma("sp", dbg_out["beta"], beta)
        pcs = [pc0, pc1]
        pts = [pt0, pt1]
        for d in range(2):
            pcv = V(pcs[d].ap[:, 0:NT * 8].rearrange("p (t c) -> p t c", c=8), pcs[d].buf)
            ptv = V(pts[d].ap[:, 0:NT * 8].rearrange("p (t c) -> p t c", c=8), pts[d].buf)
            k.mm(pcv, ut[d], gg[:, :, d * 8:(d + 1) * 8])
            k.mm(ptv, onesf, gg[:, :, d * 8:(d + 1) * 8])
            sl = slice(d * 8, (d + 1) * 8)
            k.copy("act", gc[:, :, sl], pcv)
            k.act(egc[:, :, sl], pcv, AF.Exp)
            k.tt("dve", gtmp[:, :, sl], ptv, gc[:, :, sl], ALU.subtract)
            k.act(ekt[:, :, sl], gtmp[:, :, sl], AF.Exp)
            k.act(egl[:, :, sl], ptv, AF.Exp)
        k.barrier()
    k.stack = ExitStack()
    if stage <= 2:
        return finish(nc, k, out_d)

    blocks = [(0, TC)] + [(TC + 512 * i, 512) for i in range(8)]
    xblocks = blocks[1:]
    poff = lambda tok: 1 + tok if tok < TC else 3 + tok
    heads = list(range(H)) if dbg is None or "heads" not in dbg else dbg["heads"]
    p4 = ExitStack()
    k.stack = p4
    praw = k.sb("praw", [128, TA + 4], BF16)
    qT = k.sb("qT", [128, TA], BF16)
    kT = k.sb("kT", [128, TA], BF16)
    vT = k.sb("vT", [128, TA], BF16)
    zs = k.sb("zs", [128, T], BF16)
    osum = k.sb("osum", [128, T], F32)
    wst_f = [k.sb(f"wstf{i}", [128, 8, 128], F32) for i in range(2)]
    wst_b = [k.sb(f"wstb{i}", [128, 8, 128], BF16) for i in range(2)]
    dgt = k.sb("dgt", [128, 3, 128], BF16)
    rn = [k.sb(f"rn{i}", [128, 512], F32) for i in range(2)]
    ofin_f = [k.sb(f"ofinf{i}", [128, 512], F32) for i in range(2)]
    ofin = [k.sb(f"ofin{i}", [128, 512], BF16) for i in range(2)]
    osb = [Buf(f"osum{t}") for t in range(32)]
    pbig = [k.banks[0], k.banks[1]]
    ptrK = [k.pv(2 + d, 0, 64, BF16) for d in range(2)]
    ptrV = [k.pv(2 + d, 64, 128, BF16) for d in range(2)]
    ptrQ = [k.pv(2 + d, 128, 192, BF16) for d in range(2)]
    pGB = [k.pv(2 + d, 192, 320) for d in range(2)]
    pkk = [k.pv(2 + d, 320, 448) for d in range(2)]
    pZ = [k.pv(4 + d, 0, 128) for d in range(2)]
    pD = [k.pv(4 + d, 128, 256) for d in range(2)]
    pG = [k.pv(4 + d, 256, 384) for d in range(2)]
    pqk = [k.pv(4 + d, 384, 512) for d in range(2)]
    pvn = [k.pv(6 + d, 0, 128) for d in range(2)]
    poT = [k.pv(6 + d, 128, 256) for d in range(2)]
    pS = [k.pv(6 + d, 256, 384) for d in range(2)]
    pwT = [k.pv(6 + d, 384, 512) for d in range(2)]
    def tmp(name, dt):
        return [[k.sb(f"{name}{d}{p}", [128, 128], dt) for p in range(1)] for d in range(2)]
    t_dgc, t_arg, t_E, t_Es = tmp("dgc", F32), tmp("arg", F32), tmp("E", F32), tmp("Es", F32)
    t_eg, t_qhT, t_Vt, t_Rk0, t_ktail = tmp("eg", BF16), tmp("qhT", BF16), tmp("Vt", BF16), tmp("Rk0", BF16), tmp("ktail", BF16)
    t_nwT, t_vn, t_qkm = tmp("nwT", BF16), tmp("vn", BF16), tmp("qkm", BF16)
    t_Q0, t_Q0T, t_Z, t_tm = tmp("Q0", BF16), tmp("Q0T", BF16), tmp("Z", BF16), tmp("tm", BF16)
    t_LT = [[k.sb(f"LT{d}{p}", [128, 6, 128], BF16) for p in range(1)] for d in range(2)]
    t_D = [[[k.sb(f"D{d}{p}{i}", [128, 128], BF16) for i in range(2)] for p in range(1)] for d in range(2)]
    t_G = [[[k.sb(f"G{d}{p}{i}", [128, 128], BF16) for i in range(2)] for p in range(1)] for d in range(2)]
    Sf = [k.sb(f"Sf{d}", [128, 128], F32) for d in range(2)]
    Sb = [k.sb(f"Sb{d}", [128, 128], BF16) for d in range(2)]
    k.memset("pool", praw, 0.0)
    nbig = [0]
    def nextbig():
        nbig[0] += 1
        return pbig[nbig[0] % 2]
    wcnt = [0]

    def load_w(col0):
        i = wcnt[0] % 2
        wcnt[0] += 1
        k.dma("sp" if i == 0 else "pool", wst_f[i], V(winv[:, :, col0:col0 + 128], win_d.buf))
        k.copy("pool", wst_b[i], wst_f[i])
        return wst_b[i]

    order_f = list(range(NT))
    order_b = [1, 0] + list(range(NT - 1, 1, -1))

    for h in heads:
        for ty in range(3):
            ci = ty * 8 + h
            wb = load_w(OFF_Q + ci * 128)
            for (s, n) in blocks:
                pp = nextbig()
                for kk in range(8):
                    k.mm(pp[:, 0:n], wb[:, kk, :], hTv[:, kk, s:s + n], start=(kk == 0), stop=(kk == 7))
                k.copy("act", praw[:, poff(s):poff(s) + n], pp[:, 0:n])
            for tap in range(3):
                k.ts("pool", dgt[:, tap, :], identf, cqkv[:, ci, tap:tap + 1])
            dst = (qT, kT, vT)[ty]
            for (s, n) in blocks:
                pp = nextbig()
                base = poff(s) - 1
                for tap in range(3):
                    k.mm(pp[:, 0:n], dgt[:, tap, :], praw[:, base + tap:base + tap + n], start=(tap == 0), stop=(tap == 2))
                k.act(dst[:, s:s + n], pp[:, 0:n], AF.Silu)
            if ty < 2:
                dstn = qT if ty == 0 else kT
                for bi, (s, n) in enumerate(blocks):
                    sqv = praw[:, 4:4 + n] if False else None
                for bi, (s, n) in enumerate(blocks):
                    pp = nextbig()
                    r = rn[bi % 2]
                    sqv = ofin[bi % 2]
                    k.tt("pool", sqv[:, 0:n], dstn[:, s:s + n], dstn[:, s:s + n], ALU.mult)
                    k.mm(pp[:, 0:n], onesb, sqv[:, 0:n])
                    k.act(r[:, 0:n], pp[:, 0:n], AF.Sqrt, bias=EPS)
                    k.recip(r[:, 0:n], r[:, 0:n])
                    k.stt("dve", dstn[:, s:s + n], dstn[:, s:s + n], (128.0 ** -0.5) if ty == 0 else 1.0, r[:, 0:n], ALU.mult, ALU.mult)
        wb = load_w(OFF_Z + h * 128)
        for (s, n) in xblocks:
            pp = nextbig()
            for kk in range(8):
                k.mm(pp[:, 0:n], wb[:, kk, :], hTv[:, kk, s:s + n], start=(kk == 0), stop=(kk == 7))
            k.act(zs[:, s - TC:s - TC + n], pp[:, 0:n], AF.Silu)
        if "qkv" in dbg_out and h == heads[0]:
            with ExitStack() as st2:
                old = k.stack
                k.stack = st2
                tf_ = k.sb("dbgqkv", [128, 3, 512], F32)
                k.copy("dve", tf_[:, 0, :], qT[:, 0:512])
                k.copy("dve", tf_[:, 1, :], kT[:, 0:512])
                k.copy("dve", tf_[:, 2, :], vT[:, 0:512])
                k.dma("sp", dbg_out["qkv"], tf_)
                k.barrier()
                k.stack = old
        for d in range(2):
            k.memset("pool", Sf[d], 0.0)
            k.memset("pool", Sb[d], 0.0)
        visited = set()
        for s_ in range(NT if dbg is None or 'nsteps' not in dbg else dbg['nsteps']):
            par = 0
            for d in range(2):
                t = order_f[s_] if d == 0 else order_b[s_]
                isx = t >= 2
                col = d * 8 + h
                tk = slice(t * 128, (t + 1) * 128)
                sc = lambda arr: arr[:, t, col:col + 1]
                dgc, arg, E, Es = t_dgc[d][par], t_arg[d][par], t_E[d][par], t_Es[d][par]
                eg, qhT, Vt, Rk0, ktail = t_eg[d][par], t_qhT[d][par], t_Vt[d][par], t_Rk0[d][par], t_ktail[d][par]
                nwT, vn, qkm = t_nwT[d][par], t_vn[d][par], t_qkm[d][par]
                Q0, Q0T, Z, tm_, LT, Dm, Gm = t_Q0[d][par], t_Q0T[d][par], t_Z[d][par], t_tm[d][par], t_LT[d][par], t_D[d][par], t_G[d][par]
                k.tr(ptrK[d], kT[:, tk], ident)
                k.tr(ptrV[d], vT[:, tk], ident)
                k.ts("dve", Rk0, ptrK[d], sc(egc))
                k.copy("act", Vt, ptrV[d])
                k.act(ktail, ptrK[d], AF.Identity, scale=sc(ekt))
                k.mm(pkk[d], kT[:, tk], kT[:, tk])
                if isx:
                    k.mm(pqk[d], kT[:, tk], qT[:, tk])
                k.ts("pool", dgc, identf, sc(gc))
                k.mm(pGB[d], onesf, dgc)
                k.stt("dve", arg, pGB[d], sc(gc), negm[d], ALU.subtract, ALU.add)
                k.act(E, arg, AF.Exp)
                if isx:
                    k.act(eg, pGB[d], AF.Exp)
                    k.tt("pool", qhT, qT[:, tk], eg, ALU.mult)
                k.tt("dve", Es, E, smask[d], ALU.mult)
                k.stt("dve", Q0, pkk[d], sc(nbeta), Es, ALU.mult, ALU.mult)
                if isx:
                    k.tt("dve", qkm, pqk[d], E, ALU.mult)
                k.tr(ptrQ[d], Q0, ident)
                k.act(Q0T, ptrQ[d], AF.Copy)
                k.tt("pool", LT, V(Q0T.ap.unsqueeze(1).to_broadcast([128, 6, 128]), Q0T.buf), hmask[:, 1:7, :], ALU.mult)
                k.tt("pool", tm_, Q0, hmask[:, 0, :], ALU.mult)
                k.tt("pool", Dm[0], ident, tm_, ALU.subtract)
                k.tt("pool", tm_, Q0T, hmask[:, 0, :], ALU.mult)
                k.tt("pool", Gm[0], ident, tm_, ALU.subtract)
                for m_ in range(1, 7):
                    a_, b_ = (m_ - 1) % 2, m_ % 2
                    k.mm(pZ[d], LT[:, m_ - 1, :], Dm[a_])
                    k.act(Z, pZ[d], AF.Copy)
                    k.mm(pD[d], Gm[a_], Z)
                    if m_ < 6:
                        k.mm(pG[d], Z, Gm[a_])
                    k.tt("dve", Dm[b_], Dm[a_], pD[d], ALU.subtract)
                    if m_ < 6:
                        k.tt("dve", Gm[b_], Gm[a_], pG[d], ALU.subtract)
                Wf = Dm[0]
                k.mm(pwT[d], Rk0, Wf)
                k.act(nwT, pwT[d], AF.Copy, scale=-1.0)
                k.mm(pvn[d], Wf, Vt, start=True, stop=False)
                k.mm(pvn[d], nwT, Sb[d], start=False, stop=True)
                k.act(vn, pvn[d], AF.Identity, scale=sc(beta))
                if isx:
                    xt = t - 2
                    ov = V(osum.ap[:, xt * 128:(xt + 1) * 128], osb[xt])
                    k.mm(poT[d], Sb[d], qhT, start=True, stop=False)
                    k.mm(poT[d], vn, qkm, start=False, stop=True)
                    if xt not in visited:
                        visited.add(xt)
                        k.copy("act", ov, poT[d])
                    else:
                        k.tt("dve", ov, poT[d], ov, ALU.add)
                k.mm(pS[d], ktail, vn)
                k.stt("dve", Sf[d], Sf[d], sc(egl), pS[d], ALU.mult, ALU.add)
                k.copy("pool", Sb[d], Sf[d])
        if "S" in dbg_out and h == heads[0]:
            k.dma("sp", dbg_out["S"][0], Sf[0])
            k.dma("sp", dbg_out["S"][1], Sf[1])
        for bi in range(8):
            s = bi * 512
            ovs = [V(osum.ap[:, s:s + 512], osb[bi * 4 + j]) for j in range(4)]
            class _M:
                pass
            ovall = V(osum.ap[:, s:s + 512], osb[bi * 4])
            extra = [osb[bi * 4 + j] for j in range(1, 4)]
            pp = nextbig()
            sqv = ofin[bi % 2]
            r = rn[bi % 2]
            of_ = ofin_f[bi % 2]
            k.op("pool", lambda g, sqv=sqv, s=s: g.tensor_tensor(out=sqv.ap, in0=osum.ap[:, s:s + 512], in1=osum.ap[:, s:s + 512], op=ALU.mult),
                 [osb[bi * 4 + j] for j in range(4)], [sqv.buf])
            k.mm(pp, onesb, sqv)
            k.act(r, pp, AF.Sqrt, bias=EPS, scale=1.0 / 128)
            k.recip(r, r)
            k.op("dve", lambda g, of_=of_, r=r, s=s: g.scalar_tensor_tensor(out=of_.ap, in0=osum.ap[:, s:s + 512], scalar=dnn.ap[:, 0:1], in1=r.ap,
                                                                           op0=ALU.mult, op1=ALU.mult),
                 [osb[bi * 4 + j] for j in range(4)] + [dnn.buf, r.buf], [of_.buf])
            if "o0" in dbg_out and h == heads[0]:
                k.tt("pool", of_, of_, zs[:, s:s + 512], ALU.mult)
                k.dma("sp", V(dbg_out["o0"].ap[:, s:s + 512], dbg_out["o0"].buf), of_)
                k.copy("pool", sqv, of_)
            else:
                k.tt("pool", sqv, of_, zs[:, s:s + 512], ALU.mult)
            k.dma("sp", V(oT_d.ap[h, :, s:s + 512], oT_d.buf), sqv)
    k.barrier()
    p4.close()
    s_g.close()
    k.stack = ExitStack()
    if stage <= 4:
        return finish(nc, k, out_d)

    def wchunk_loader(stf, stb):
        cnt = [0]
        def load(src_v, K):
            i = cnt[0] % len(stf)
            cnt[0] += 1
            k.dma("sp" if i == 0 else "pool", stf[i][:, 0:K, :], src_v)
            k.copy("pool", stb[i][:, 0:K, :], stf[i][:, 0:K, :])
            return stb[i]
        return load

    def load_resident(dst_bf, src_ap, src_buf, K, ncols, stg):
        i = 0
        for k0 in range(0, K, 8):
            kn = min(8, K - k0)
            for c0 in range(0, ncols, 512):
                cn = min(512, ncols - c0)
                st_ = stg[i % len(stg)]
                k.dma("sp" if i % 2 == 0 else "pool", st_[:, 0:kn, 0:cn], V(src_ap[:, k0:k0 + kn, c0:c0 + cn], src_buf))
                k.copy("pool" if i % 2 == 0 else "dve", dst_bf[:, k0:k0 + kn, c0:c0 + cn], st_[:, 0:kn, 0:cn])
                i += 1

    with ExitStack() as st:
        k.stack = st
        FCS = k.sb("FCS", [128, 32, 4, 256], BF16)
        fT = [k.sb(f"fT{i}", [128, 512], BF16) for i in range(2)]
        stf = [k.sb(f"p3stf{i}", [128, 8, 128], F32) for i in range(2)]
        stb = [k.sb(f"p3stb{i}", [128, 8, 128], BF16) for i in range(2)]
        ctab = [k.sb(f"ctab{i}", [128, 4, 512], BF16) for i in range(2)]
        stab = [k.sb(f"stab{i}", [128, 4, 512], BF16) for i in range(2)]
        yblk = [k.sb(f"yblk{i}", [128, 4, 512], BF16) for i in range(2)]
        lw = wchunk_loader(stf, stb)
        nb_ = 0
        for g in range(4):
            wb = lw(V(winv[:, :, OFF_F + g * 128:OFF_F + (g + 1) * 128], win_d.buf), 8)
            for bi, (s, n) in enumerate(xblocks):
                pp = k.banks[nb_ % 2]
                ft = fT[nb_ % 2]
                nb_ += 1
                for kk in range(8):
                    k.mm(pp, wb[:, kk, :], hTv[:, kk, s:s + n], start=(kk == 0), stop=(kk == 7))
                k.act(ft, pp, AF.Copy)
                pf = k.banks[2 + (nb_ % 2)]
                pfv = V(pf.ap.rearrange("p (a b) -> p a b", b=256), pf.buf)
                for j2 in range(2):
                    for jj in range(2):
                        j = j2 * 2 + jj
                        k.mm(pfv[:, jj, :], ft[:, j * 128:(j + 1) * 128], c128)
                    t0 = bi * 4 + j2 * 2
                    if j2 == 0:
                        k.act(FCS[:, t0:t0 + 2, g, :], pfv, AF.Copy)
                    else:
                        k.copy("dve", FCS[:, t0:t0 + 2, g, :], pfv)
        k.barrier()
        cosv = cos_d.ap.rearrange("(tt p) f -> p tt f", p=128)
        sinv = sin_d.ap.rearrange("(tt p) f -> p tt f", p=128)
        ld = 0
        for kb in range(8):
            for t4 in range(8):
                ct, st_ = ctab[ld % 2], stab[ld % 2]
                ld += 1
                k.dma("sp", ct, V(cosv[:, t4 * 4:(t4 + 1) * 4, kb * 512:(kb + 1) * 512], cos_d.buf))
                k.dma("pool", st_, V(sinv[:, t4 * 4:(t4 + 1) * 4, kb * 512:(kb + 1) * 512], sin_d.buf))
                for ti in range(4):
                    tt_ = t4 * 4 + ti
                    for g in range(4):
                        k.mm(k.banks[4 + g], FCS[:, tt_, g, 0:128], ct[:, ti, :], start=(tt_ == 0), stop=False)
                        k.mm(k.banks[4 + g], FCS[:, tt_, g, 128:256], st_[:, ti, :], start=False, stop=(tt_ == 31))
            yb = yblk[kb % 2]
            for g in range(4):
                if g % 2 == 0:
                    k.act(yb[:, g, :], k.banks[4 + g], AF.Copy)
                else:
                    k.copy("dve", yb[:, g, :], k.banks[4 + g])
            k.dma("sp", V(yT_d.ap[:, :, kb * 512:(kb + 1) * 512].rearrange("g p f -> p g f"), yT_d.buf), yb)
        if "fm" in dbg_out:
            pass
        k.barrier()
    k.stack = ExitStack()
    if stage <= 5:
        return finish(nc, k, out_d)

    with ExitStack() as st:
        k.stack = st
        wg = k.sb("wg", [128, 8, 2048], BF16)
        wf4 = k.sb("wf4", [128, 4, 1024], BF16)
        wdn = k.sb("wdn", [128, 8, 1024], BF16)
        stg = [k.sb(f"p5stg{i}", [128, 8, 512], F32) for i in range(1)]
        ytb = [k.sb(f"ytb{i}", [128, 4, 512], BF16) for i in range(2)]
        otb = [k.sb(f"otb{i}", [128, 8, 512], BF16) for i in range(2)]
        g0 = [k.sb(f"g0_{i}", [128, 512], BF16) for i in range(2)]
        g1 = [k.sb(f"g1_{i}", [128, 512], BF16) for i in range(2)]
        m0 = [k.sb(f"m0_{i}", [128, 512], BF16) for i in range(2)]
        m1 = [k.sb(f"m1_{i}", [128, 512], BF16) for i in range(2)]
        mixs = k.sb("mixs", [128, 2, 8, 512], BF16)
        mixs_buf = [Buf("mixs0"), Buf("mixs1")]
        load_resident(wg, winv[:, :, OFF_G:OFF_G + 2048], win_d.buf, 8, 2048, stg)
        load_resident(wf4, wf_d.ap.rearrange("(g p) d -> p g d", p=128), wf_d.buf, 4, 1024, stg)
        load_resident(wdn, wdn_d.ap.rearrange("(h p) d -> p h d", p=128), wdn_d.buf, 8, 1024, stg)
        it = 0
        for mt in range(8):
            s = TC + mt * 512
            yt, ot = ytb[mt % 2], otb[mt % 2]
            k.dma("sp", yt, V(yT_d.ap[:, :, mt * 512:(mt + 1) * 512].rearrange("g p f -> p g f"), yT_d.buf))
            k.dma("pool", ot, V(oT_d.ap[:, :, mt * 512:(mt + 1) * 512].rearrange("h p f -> p h f"), oT_d.buf))
            for dc in range(8):
                dsl = slice(dc * 128, (dc + 1) * 128)
                i2 = it % 2
                it += 1
                pb = [k.banks[4 * i2 + j] for j in range(4)]
                for g in range(4):
                    k.mm(pb[0], wf4[:, g, dsl], yt[:, g, :], start=(g == 0), stop=(g == 3))
                for kk in range(8):
                    k.mm(pb[1], wg[:, kk, dc * 128:(dc + 1) * 128], hTv[:, kk, s:s + 512], start=(kk == 0), stop=(kk == 7))
                for hh in range(8):
                    k.mm(pb[2], wdn[:, hh, dsl], ot[:, hh, :], start=(hh == 0), stop=(hh == 7))
                for kk in range(8):
                    k.mm(pb[3], wg[:, kk, 1024 + dc * 128:1024 + (dc + 1) * 128], hTv[:, kk, s:s + 512], start=(kk == 0), stop=(kk == 7))
                k.act(g0[i2], pb[1], AF.Sigmoid)
                k.act(g1[i2], pb[3], AF.Sigmoid)
                k.tt("dve", m0[i2], pb[0], g0[i2], ALU.mult)
                k.tt("dve", m1[i2], pb[2], g1[i2], ALU.mult)
                k.tt("pool", V(mixs.ap[:, mt % 2, dc, :], mixs_buf[mt % 2]), m0[i2], m1[i2], ALU.add)
            k.copy("pool", hTv[:, :, s:s + 512], V(mixs.ap[:, mt % 2, :, :], mixs_buf[mt % 2]))
        k.barrier()
    k.stack = ExitStack()
    if stage <= 6:
        return finish(nc, k, out_d)

    def branch_tail(mt, producer, cidx, resid_d, final, tb):
        yx, sq, rst, xin_, x1t = tb["yx"], tb["sq"], tb["rst"], tb["xin"], tb["x1t"]
        for dc in range(8):
            pb = k.banks[dc % 2]
            producer(dc, pb)
            k.act(yx[:, dc, :], pb, AF.Copy)
            k.act(sq[:, dc, :], pb, AF.Square)
        pss = k.banks[2]
        for dc in range(8):
            k.mm(pss, onesb, sq[:, dc, :], start=(dc == 0), stop=(dc == 7))
        k.act(rst, pss, AF.Sqrt, bias=EPS, scale=1.0 / D)
        k.recip(rst, rst)
        for dc in range(8):
            k.stt("dve", yx[:, dc, :], yx[:, dc, :], coef[:, cidx, dc:dc + 1], rst, ALU.mult, ALU.mult)
        for j in range(4):
            tok0 = mt * 512 + j * 128
            xi = xin_[j % len(xin_)]
            xo = x1t[j % len(x1t)]
            k.dma("sp" if j % 2 == 0 else "pool", xi, V(resid_d.ap[tok0:tok0 + 128, :], resid_d.buf))
            ba, bb = k.banks[3 + 2 * (j % 2)], k.banks[4 + 2 * (j % 2)]
            for dc in range(8):
                bk = ba if dc < 4 else bb
                k.tr(bk[:, (dc % 4) * 128:(dc % 4 + 1) * 128], yx[:, dc, j * 128:(j + 1) * 128], identf)
            k.tt("dve", xo[:, 0:512], ba, xi[:, 0:512], ALU.add)
            k.tt("dve", xo[:, 512:1024], bb, xi[:, 512:1024], ALU.add)
            if final:
                k.dma("sp", V(out_d.ap[tok0:tok0 + 128, :], out_d.buf), xo)
            else:
                k.dma("sp", V(x1_d.ap[tok0:tok0 + 128, :], x1_d.buf), xo)
                norm_tile(xo, 2 + mt * 4 + j, 5, 6, hT, hTv.buf, tb["tm"])

    def tail_bufs(nbuf):
        return {"yx": k.sb("yx", [128, 8, 512], F32), "sq": k.sb("sqb", [128, 8, 512], BF16), "rst": k.sb("rst", [128, 512], F32),
                "xin": [k.sb(f"rxin{i}", [128, D], F32) for i in range(nbuf)], "x1t": [k.sb(f"x1t{i}", [128, D], F32) for i in range(nbuf)],
                "tm": {"sq": [k.sb("t_sq", [128, D], F32)], "ss": [k.sb("t_ss", [128, 1], F32)], "xn": [k.sb("t_xn", [128, D], BF16)],
                       "pt": [k.pv(7, 0, 512, BF16, 128)]}}

    with ExitStack() as st:
        k.stack = st
        wout = k.sb("wout", [128, 8, 1024], BF16)
        stg = [k.sb(f"p5bstg{i}", [128, 8, 512], F32) for i in range(1)]
        load_resident(wout, wout_d.ap.rearrange("(c p) d -> p c d", p=128), wout_d.buf, 8, 1024, stg)
        tb = tail_bufs(2)
        mixl = k.sb("mixl", [128, 8, 512], BF16)
        for mt in range(8):
            s = TC + mt * 512
            k.copy("pool", mixl, hTv[:, :, s:s + 512])
            def prod(dc, pb):
                for c in range(8):
                    k.mm(pb, wout[:, c, dc * 128:(dc + 1) * 128], mixl[:, c, :], start=(c == 0), stop=(c == 7))
            branch_tail(mt, prod, 4, x_d, False, tb)
        k.barrier()
    k.stack = ExitStack()
    if stage <= 7:
        return finish(nc, k, out_d)

    wupv = wup_d.ap.rearrange("(k p) c -> p k c", p=128)
    with ExitStack() as st:
        k.stack = st
        stf = [k.sb(f"p6stf{i}", [128, 8, 128], F32) for i in range(2)]
        stb = [k.sb(f"p6stb{i}", [128, 8, 128], BF16) for i in range(2)]
        lw = wchunk_loader(stf, stb)
        apad = k.sb("apad", [128, 66, 66], BF16)
        dg9 = k.sb("dg9", [128, 9, 128], BF16)
        sa = [k.sb(f"sa{i}", [128, 512], BF16) for i in range(2)]
        gtc = [k.sb(f"gtc{i}", [128, T], BF16) for i in range(2)]
        k.memset("pool", apad, 0.0)
        nb_ = 0
        for c in range(NFF):
            wa = lw(V(wupv[:, :, c * 128:(c + 1) * 128], wup_d.buf), 8)
            wu = lw(V(wupv[:, :, DFF + c * 128:DFF + (c + 1) * 128], wup_d.buf), 8)
            for tap in range(9):
                k.ts("pool", dg9[:, tap, :], identf, cffn[:, c, tap:tap + 1])
            for bi in range(8):
                s = TC + bi * 512
                pp = k.banks[nb_ % 2]
                nb_ += 1
                for kk in range(8):
                    k.mm(pp, wa[:, kk, :], hTv[:, kk, s:s + 512], start=(kk == 0), stop=(kk == 7))
                k.act(apad[:, 1 + bi * 8:1 + bi * 8 + 8, 1:65], V(pp.ap.rearrange("p (r c) -> p r c", c=64), pp.buf), AF.Copy)
            gt = gtc[c % 2]
            for bi in range(8):
                s = TC + bi * 512
                pc = k.banks[2 + (bi % 2)]
                pu = k.banks[4 + (bi % 2)]
                pcv = V(pc.ap.rearrange("p (r c) -> p r c", c=64), pc.buf)
                for tap in range(9):
                    dr, dcc = tap // 3, tap % 3
                    k.mm(pcv, dg9[:, tap, :], apad[:, bi * 8 + dr:bi * 8 + dr + 8, dcc:dcc + 64], start=(tap == 0), stop=(tap == 8))
                k.act(sa[bi % 2], pc, AF.Silu)
                for kk in range(8):
                    k.mm(pu, wu[:, kk, :], hTv[:, kk, s:s + 512], start=(kk == 0), stop=(kk == 7))
                k.tt("dve", gt[:, bi * 512:(bi + 1) * 512], pu, sa[bi % 2], ALU.mult)
            k.dma("sp" if c % 2 == 0 else "pool", V(gT_d.ap[c], gT_d.buf), gt)
        k.barrier()
    k.stack = ExitStack()
    s_h.close()
    k.stack = ExitStack()
    if stage <= 8:
        return finish(nc, k, out_d)

    with ExitStack() as st:
        k.stack = st
        wdown = k.sb("wdown", [128, NFF, 1024], BF16)
        stg = [k.sb(f"p7stg{i}", [128, 8, 512], F32) for i in range(2)]
        load_resident(wdown, wdown_d.ap.rearrange("(c p) d -> p c d", p=128), wdown_d.buf, NFF, 1024, stg)
        gbl = [k.sb(f"gbl{i}", [128, NFF, 512], BF16) for i in range(2)]
        tb = tail_bufs(2)
        for mt in range(8):
            gb = gbl[mt % 2]
            k.dma("sp", gb[:, 0:11, :], V(gT_d.ap[0:11, :, mt * 512:(mt + 1) * 512].rearrange("c p f -> p c f"), gT_d.buf))
            k.dma("pool", gb[:, 11:NFF, :], V(gT_d.ap[11:NFF, :, mt * 512:(mt + 1) * 512].rearrange("c p f -> p c f"), gT_d.buf))
            def prod(dc, pb, gb=gb):
                for c in range(NFF):
                    k.mm(pb, wdown[:, c, dc * 128:(dc + 1) * 128], gb[:, c, :], start=(c == 0), stop=(c == NFF - 1))
            branch_tail(mt, prod, 7, x1_d, True, tb)
        k.barrier()
    k.stack = ExitStack()
    return finish(nc, k, out_d)


def finish(nc, k, out_d):
    k.barrier()
    return nc


def prep_inputs(inp, b):
    f = lambda a: np.ascontiguousarray(a, dtype=np.float32)
    colmajor = lambda v: f(np.asarray(v).reshape(-1, 128).T)
    m = {}
    m["x"] = f(inp["x"][b])
    m["ctx"] = f(inp["ctx"][b])
    m["cc"] = f(np.stack([colmajor(inp["c"][b]), colmajor(inp["c_ctx"])], axis=-1))
    m["w_ada"] = f(inp["w_ada"][0])
    m["b_ada"] = colmajor(inp["b_ada"][0])
    m["norms"] = f(np.stack([colmajor(inp[n][0]) for n in ("norm_pre_mix", "norm_post_mix", "norm_pre_ffn", "norm_post_ffn")], axis=1))
    m["w_in"] = f(inp["w_in"][0])
    cq = np.asarray(inp["conv_qkv"][0])
    m["conv_qkv"] = f(cq.T.reshape(24, 128, 3).transpose(1, 0, 2))
    gp = np.stack([np.asarray(inp["a_log"][0]).reshape(16), np.asarray(inp["dt_bias"][0]).reshape(16)], 0)
    m["gpar"] = f(np.broadcast_to(gp[None], (128, 2, 16)))
    m["dn_norm"] = f(np.asarray(inp["dn_norm"][0]).reshape(128, 1))
    m["w_fourier"] = f(inp["w_fourier"][0])
    m["w_dn"] = f(inp["w_dn"][0])
    m["w_out"] = f(inp["w_out"][0])
    m["w_up"] = f(inp["w_up"][0])
    cf = np.asarray(inp["conv_ffn"][0]).reshape(9, DFF)
    m["conv_ffn"] = f(cf.T.reshape(NFF, 128, 9).transpose(1, 0, 2))
    m["w_down"] = f(inp["w_down"][0])
    return m


_CONST = {}


def consts():
    if not _CONST:
        idx = np.arange(T, dtype=np.int64)
        ang = (2.0 * np.pi / T) * ((idx[:, None] * idx[None, :]) % T).astype(np.float64)
        s = 1.0 / np.sqrt(float(T) * 128.0)
        _CONST["dft_cos"] = (np.cos(ang) * s).astype(ml_dtypes.bfloat16)
        _CONST["dft_sin"] = (np.sin(ang) * s).astype(ml_dtypes.bfloat16)
        i8 = np.arange(128, dtype=np.int64)
        a8 = (2.0 * np.pi / 128) * ((i8[:, None] * i8[None, :]) % 128).astype(np.float64)
        j8 = np.arange(128)
        hm = []
        for m_ in range(7):
            s_ = 2 ** m_
            blk2 = (j8[:, None] // (2 * s_)) == (j8[None, :] // (2 * s_))
            half = (j8[:, None] // s_) != (j8[None, :] // s_)
            hm.append(-(blk2 & half).astype(np.float32))
        _CONST["hmask"] = np.stack(hm, axis=1).astype(ml_dtypes.bfloat16)
        _CONST["dft128"] = np.concatenate([np.cos(a8), -np.sin(a8)], axis=1).astype(ml_dtypes.bfloat16)
    return _CONST


def kernel(**inputs):
    inp = {k_: np.asarray(v) for k_, v in inputs.items()}
    nc = build()
    cst = consts()
    in_maps = []
    for b in range(8):
        m = prep_inputs(inp, b)
        m.update(cst)
        in_maps.append(m)
    res = run_bass_kernel_spmd(nc, in_maps, core_ids=list(range(8)))
    return np.stack([np.asarray(r["out"], dtype=np.float32) for r in res.results], axis=0)
```

```python
import os
from contextlib import ExitStack
import numpy as np
import ml_dtypes
import concourse.bass as bass
import concourse.mybir as mybir
from concourse.bass_utils import run_bass_kernel_spmd

F32 = mybir.dt.float32
BF16 = mybir.dt.bfloat16
AF = mybir.ActivationFunctionType
ALU = mybir.AluOpType

D = 1024
T = 4096
TC = 256
TA = TC + T
NT = TA // 128
H = 8
OFF_F, OFF_Q, OFF_K, OFF_V, OFF_Z, OFF_B, OFF_A, OFF_G = 0, 512, 1536, 2560, 3584, 4608, 4624, 4640
INW = 6688
DFF = 2816
NFF = DFF // 128
EPS = 1e-6


class Buf:
    __slots__ = ("name", "last_w", "readers", "dsem", "dcount", "excl")

    def __init__(self, name, excl=False):
        self.name = name
        self.excl = excl
        self.last_w = None
        self.readers = []
        self.dsem = None
        self.dcount = 0


class V:
    __slots__ = ("ap", "buf")

    def __init__(self, ap, buf):
        self.ap = ap
        self.buf = buf

    def __getitem__(self, idx):
        return V(self.ap[idx], self.buf)

    def sub(self, idx, buf):
        return V(self.ap[idx], buf)


def _bufs(*vs):
    out = []
    for v in vs:
        if isinstance(v, V) and v.buf is not None and v.buf not in out:
            out.append(v.buf)
    return out


def _ap(v):
    return v.ap if isinstance(v, V) else v


class K:
    def __init__(self, nc):
        self.nc = nc
        self.engs = {"pe": nc.tensor, "act": nc.scalar, "dve": nc.vector, "pool": nc.gpsimd, "sp": nc.sync}
        self.sem = {n: nc.alloc_semaphore(f"s_{n}") for n in self.engs}
        self.cnt = {n: 0 for n in self.engs}
        self.known = {n: {} for n in self.engs}
        self.dsems = []
        self.nins = 0
        self.nwaits = 0
        self.stack = ExitStack()
        self.limit = None
        self.log = []

    def _uid(self):
        self.uid = getattr(self, 'uid', 0) + 1
        return self.uid

    def sb(self, name, shape, dt, nbuf=None):
        t = self.stack.enter_context(self.nc.sbuf_tensor(f"sb{self._uid()}_" + name, list(shape), dt))
        return V(t[:] if hasattr(t, "__getitem__") else t.ap(), Buf(name) if nbuf is None else nbuf)

    def init_banks(self):
        self.banks = []
        for i in range(8):
            t = self.nc.psum_tensor(f"ps_bank{i}", [128, 512], F32).__enter__()
            self.banks.append(V(t[:], Buf(f"bank{i}", excl=True)))

    def pv(self, bank, lo, hi, dt=F32, inner=None):
        b = self.banks[bank]
        ap = b.ap[:, lo:hi]
        if dt != F32:
            ap = ap.bitcast(dt)
        if inner is not None:
            ap = ap.rearrange("p (a b) -> p a b", b=inner)
        return V(ap, b.buf)

    def _wait(self, e, ev):
        sem, val, src = ev
        if src == "pe" and e == "pe":
            return
        kn = self.known[e]
        if kn.get(sem.num, 0) >= val:
            return
        kn[sem.num] = val
        self.engs[e].wait_ge(sem, val)
        self.nwaits += 1

    def _deps(self, e, reads, writes):
        best = {}
        def add(ev):
            s = ev[0].num
            if s not in best or best[s][1] < ev[1]:
                best[s] = ev
        for b in reads:
            if b.last_w is not None:
                add(b.last_w)
        for b in writes:
            if b.last_w is not None:
                add(b.last_w)
            for ev in b.readers:
                add(ev)
        for ev in best.values():
            self._wait(e, ev)

    def _record(self, ev, reads, writes):
        for b in reads:
            if b in writes:
                continue
            b.readers.append(ev)
            if len(b.readers) > 10:
                best = {}
                for x in b.readers:
                    s = x[0].num
                    if s not in best or best[s][1] < x[1]:
                        best[s] = x
                b.readers = list(best.values())
        for b in writes:
            b.last_w = ev
            b.readers = []

    def op(self, e, fn, reads, writes):
        if self.limit is not None and self.nins >= self.limit:
            return
        ex = [b for b in reads if b.excl and b not in writes]
        if ex:
            writes = list(writes) + ex
        self._deps(e, reads, writes)
        ins = fn(self.engs[e])
        if os.environ.get('PRINS') and self.nins in range(int(os.environ.get('PRINS','0')), int(os.environ.get('PRINS','0')) + 4):
            print('INS', self.nins, ins.concise())
        self.cnt[e] += 1
        ins.then_inc(self.sem[e], 1)
        self._record((self.sem[e], self.cnt[e], e), reads, writes)
        self.nins += 1

    def dma(self, q, out, in_, key=None, **kw):
        if self.limit is not None and self.nins >= self.limit:
            return
        reads, writes = _bufs(in_), _bufs(out)
        self._deps(q, reads, writes)
        kb = key.buf if key is not None else (out.buf if not isinstance(out.buf, DBuf) else in_.buf)
        if kb.dsem is None:
            kb.dsem = self.nc.alloc_semaphore(f"d{self._uid()}_{kb.name}")
            self.dsems.append(kb)
        ins = self.engs[q].dma_start(out=_ap(out), in_=_ap(in_), **kw)
        kb.dcount += 1
        ins.then_inc(kb.dsem, 16)
        self._record((kb.dsem, 16 * kb.dcount, "dma"), reads, writes)
        self.nins += 1

    def barrier(self):
        for e in self.engs:
            for f in self.engs:
                if f != e and self.cnt[f] > 0:
                    self._wait(e, (self.sem[f], self.cnt[f], f))
            for kb in self.dsems:
                if kb.dcount > 0:
                    self._wait(e, (kb.dsem, 16 * kb.dcount, "dma"))

    def mm(self, out, lhsT, rhs, start=True, stop=True):
        self.op("pe", lambda e: e.matmul(_ap(out), lhsT=_ap(lhsT), rhs=_ap(rhs), start=start, stop=stop),
                _bufs(lhsT, rhs) + ([] if start else _bufs(out)), _bufs(out))

    def tr(self, out, in_, ident):
        self.op("pe", lambda e: e.transpose(out=_ap(out), in_=_ap(in_), identity=_ap(ident)), _bufs(in_, ident), _bufs(out))

    def act(self, out, in_, func, bias=0.0, scale=1.0, accum=None, eng="act"):
        kw = {}
        if accum is not None:
            kw["accum_out"] = _ap(accum)
        self.op("act", lambda e: e.activation(out=_ap(out), in_=_ap(in_), func=func, bias=_ap(bias), scale=_ap(scale), **kw),
                _bufs(in_, bias, scale), _bufs(out, accum))

    def ts(self, e, out, in0, s1, s2=None, op0=ALU.mult, op1=None):
        if op1 is None:
            f = lambda g: g.tensor_scalar(out=_ap(out), in0=_ap(in0), scalar1=_ap(s1), scalar2=None, op0=op0)
        else:
            f = lambda g: g.tensor_scalar(out=_ap(out), in0=_ap(in0), scalar1=_ap(s1), scalar2=_ap(s2), op0=op0, op1=op1)
        self.op(e, f, _bufs(in0, s1, s2), _bufs(out))

    def tt(self, e, out, a, b, op):
        self.op(e, lambda g: g.tensor_tensor(out=_ap(out), in0=_ap(a), in1=_ap(b), op=op), _bufs(a, b), _bufs(out))

    def stt(self, e, out, in0, scalar, in1, op0, op1):
        self.op(e, lambda g: g.scalar_tensor_tensor(out=_ap(out), in0=_ap(in0), scalar=_ap(scalar), in1=_ap(in1), op0=op0, op1=op1),
                _bufs(in0, scalar, in1), _bufs(out))

    def copy(self, e, out, in_):
        if e == "act":
            self.act(out, in_, AF.Copy)
        else:
            self.op(e, lambda g: g.tensor_copy(out=_ap(out), in_=_ap(in_)), _bufs(in_), _bufs(out))

    def recip(self, out, in_):
        self.op("dve", lambda g: g.reciprocal(out=_ap(out), in_=_ap(in_)), _bufs(in_), _bufs(out))

    def memset(self, e, out, val):
        self.op(e, lambda g: g.memset(_ap(out), val), [], _bufs(out))

    def asel(self, out, in_, pattern, cmp, fill, base, cm):
        self.op("pool", lambda g: g.affine_select(out=_ap(out), in_=_ap(in_), pattern=pattern, compare_op=cmp, fill=fill,
                                                  base=base, channel_multiplier=cm), _bufs(in_), _bufs(out))


class DBuf(Buf):
    __slots__ = ("is_dram",)

    def __init__(self, name):
        super().__init__(name)
        self.is_dram = True


def dramv(nc, name, shape, dt, kind):
    t = nc.dram_tensor(name, list(shape), dt, kind=kind)
    return V(t.ap(), DBuf(name))


def build(stage=99, dbg=None):
    nc = bass.Bass("TRN2", target_bir_lowering=False)
    k = K(nc)
    k.init_banks()
    if dbg and 'limit' in dbg:
        k.limit = dbg['limit']
    IN = lambda name, shape, dt=F32: dramv(nc, name, shape, dt, "ExternalInput")
    x_d = IN("x", [T, D])
    ctx_d = IN("ctx", [TC, D])
    cc_d = IN("cc", [128, 8, 2])
    wada_d = IN("w_ada", [D, 6 * D])
    bada_d = IN("b_ada", [128, 48])
    nrm_d = IN("norms", [128, 4, 8])
    win_d = IN("w_in", [D, INW])
    cqkv_d = IN("conv_qkv", [128, 24, 3])
    gpar_d = IN("gpar", [128, 2, 16])
    dnn_d = IN("dn_norm", [128, 1])
    wf_d = IN("w_fourier", [512, D])
    wdn_d = IN("w_dn", [D, D])
    wout_d = IN("w_out", [D, D])
    wup_d = IN("w_up", [D, 2 * DFF])
    cffn_d = IN("conv_ffn", [128, NFF, 9])
    wdown_d = IN("w_down", [DFF, D])
    cos_d = IN("dft_cos", [T, T], BF16)
    sin_d = IN("dft_sin", [T, T], BF16)
    c128_d = IN("dft128", [128, 256], BF16)
    hmask_d = IN("hmask", [128, 7, 128], BF16)
    out_d = dramv(nc, "out", [T, D], F32, "ExternalOutput")
    dbg_out = {}
    if dbg:
        for nm, shp in dbg.items():
            if nm in ("heads", "nsteps", "limit"):
                continue
            dbg_out[nm] = dramv(nc, "dbg_" + nm, shp, F32, "ExternalOutput")
    x1_d = dramv(nc, "x1_scr", [T, D], F32, "Internal")
    oT_d = dramv(nc, "oT_scr", [H, 128, T], BF16, "Internal")
    gT_d = dramv(nc, "gT_scr", [NFF, 128, T], BF16, "Internal")
    yT_d = dramv(nc, "yT_scr", [4, 128, T], BF16, "Internal")

    identf = k.sb("identf", [128, 128], F32)
    ident = k.sb("ident", [128, 128], BF16)
    onesf = k.sb("onesf", [128, 128], F32)
    onesb = k.sb("onesb", [128, 128], BF16)
    negm = [k.sb(f"negm{d}", [128, 128], F32) for d in range(2)]
    smask = [k.sb(f"smask{d}", [128, 128], F32) for d in range(2)]
    ut = [k.sb(f"ut{d}", [128, 128], F32) for d in range(2)]
    zerof = k.sb("zerof", [128, 128], F32)
    scal_t = k.sb("scal_t", [128, 8], F32)
    k.memset("pool", zerof, 0.0)
    k.memset("pool", onesf, 1.0)
    k.copy("dve", onesb, onesf)
    k.asel(identf, zerof, [[-1, 128]], ALU.not_equal, 1.0, 0, 1)
    k.copy("dve", ident, identf)
    k.asel(negm[0], zerof, [[1, 128]], ALU.is_ge, -1e5, 0, -1)
    k.asel(smask[0], onesf, [[1, 128]], ALU.is_gt, 0.0, 0, -1)
    k.asel(ut[0], onesf, [[1, 128]], ALU.is_ge, 0.0, 0, -1)
    k.asel(negm[1], zerof, [[-1, 128]], ALU.is_ge, -1e5, 0, 1)
    k.asel(smask[1], onesf, [[-1, 128]], ALU.is_gt, 0.0, 0, 1)
    k.asel(ut[1], onesf, [[-1, 128]], ALU.is_ge, 0.0, 0, 1)

    nrm = k.sb("nrm", [128, 4, 8], F32)
    k.dma("sp", nrm, nrm_d)
    cqkv = k.sb("cqkv", [128, 24, 3], F32)
    k.dma("sp", cqkv, cqkv_d)
    gpar = k.sb("gpar", [128, 2, 16], F32)
    k.dma("sp", gpar, gpar_d)
    dnn = k.sb("dnn", [128, 1], F32)
    k.dma("sp", dnn, dnn_d)
    cffn = k.sb("cffn", [128, NFF, 9], F32)
    k.dma("sp", cffn, cffn_d)
    c128 = k.sb("c128", [128, 256], BF16)
    k.dma("sp", c128, c128_d)
    hmask = k.sb("hmask", [128, 7, 128], BF16)
    k.dma("sp", hmask, hmask_d)
    bada = k.sb("bada", [128, 48], F32)
    k.dma("sp", bada, bada_d)
    cc = k.sb("cc", [128, 8, 2], F32)
    k.dma("sp", cc, cc_d)

    mod = k.sb("mod", [128, 48, 2], F32)
    scc = k.sb("scc", [128, 8, 2], F32)
    k.act(scc, cc, AF.Silu)
    with ExitStack() as st:
        k.stack = st
        wa = [k.sb(f"wa{i}", [128, 8, 512], F32) for i in range(2)]
        pm = k.pv(0, 0, 96, F32, 2)
        wv = wada_d.ap.rearrange("(k p) c -> p k c", p=128)
        for blk in range(12):
            w = wa[blk % 2]
            k.dma("sp" if blk % 2 == 0 else "pool", w, V(wv[:, :, blk * 512:(blk + 1) * 512], wada_d.buf))
            for oc in range(4):
                for kk in range(8):
                    k.mm(pm[:, blk * 4 + oc, :], w[:, kk, oc * 128:(oc + 1) * 128], scc[:, kk, :], start=(kk == 0), stop=(kk == 7))
        for j in range(2):
            k.tt("dve", mod[:, :, j], pm[:, :, j], bada, ALU.add)
        k.barrier()
    k.stack = ExitStack()
    coef = k.sb("coef", [128, 8, 8], F32)
    def modc(i, j):
        return mod[:, i * 8:(i + 1) * 8, j]
    k.stt("dve", coef[:, 0, :], modc(1, 0), 1.0, nrm[:, 0, :], ALU.add, ALU.mult)
    k.copy("dve", coef[:, 1, :], modc(0, 0))
    k.stt("dve", coef[:, 2, :], modc(1, 1), 1.0, nrm[:, 0, :], ALU.add, ALU.mult)
    k.copy("dve", coef[:, 3, :], modc(0, 1))
    k.tt("dve", coef[:, 4, :], modc(2, 0), nrm[:, 1, :], ALU.mult)
    k.stt("dve", coef[:, 5, :], modc(4, 0), 1.0, nrm[:, 2, :], ALU.add, ALU.mult)
    k.copy("dve", coef[:, 6, :], modc(3, 0))
    k.tt("dve", coef[:, 7, :], modc(5, 0), nrm[:, 3, :], ALU.mult)
    if "coef" in dbg_out:
        k.dma("sp", dbg_out["coef"], coef)
    if stage <= 0:
        return finish(nc, k, out_d)

    s_h = ExitStack()
    k.stack = s_h
    hT = k.sb("hT", [128, 8, TA], BF16)
    hbuf = [Buf(f"hT{t}") for t in range(NT)]

    def norm_tile(src_tile_v, tile_idx, ca, cb, dst, dstbufs, tm):
        nb = len(tm["sq"])
        sq = tm["sq"][tile_idx % nb]
        ss = tm["ss"][tile_idx % nb]
        xn = tm["xn"][tile_idx % nb]
        pt = tm["pt"][tile_idx % len(tm["pt"])]
        k.act(sq, src_tile_v, AF.Square, accum=ss)
        k.act(ss, ss, AF.Sqrt, bias=EPS, scale=1.0 / D)
        k.recip(ss, ss)
        k.ts("dve", xn, src_tile_v, ss[:, 0:1])
        for c in range(8):
            k.tr(pt[:, c, :], xn[:, c * 128:(c + 1) * 128], ident)
        for c in range(8):
            dv = V(dst.ap[:, c, tile_idx * 128:(tile_idx + 1) * 128], dstbufs[tile_idx] if isinstance(dstbufs, list) else dstbufs)
            if c % 2 == 0:
                k.ts("dve", dv, pt[:, c, :], coef[:, ca, c:c + 1], coef[:, cb, c:c + 1], ALU.mult, ALU.add)
            else:
                k.act(dv, pt[:, c, :], AF.Identity, bias=coef[:, cb, c:c + 1], scale=coef[:, ca, c:c + 1])

    p1 = ExitStack()
    k.stack = p1
    nt_sq = [k.sb(f"nt_sq{i}", [128, D], F32) for i in range(2)]
    nt_ss = [k.sb(f"nt_ss{i}", [128, 1], F32) for i in range(2)]
    nt_xn = [k.sb(f"nt_xn{i}", [128, D], BF16) for i in range(2)]
    xin = [k.sb(f"xin{i}", [128, D], F32) for i in range(3)]
    nt_pt = [k.pv(i, 0, 512, BF16, 128) for i in range(2)]
    tm1 = {"sq": nt_sq, "ss": nt_ss, "xn": nt_xn, "pt": nt_pt}
    for t in range(NT):
        xi = xin[t % 3]
        if t < 2:
            src = V(ctx_d.ap[t * 128:(t + 1) * 128, :], ctx_d.buf)
        else:
            src = V(x_d.ap[(t - 2) * 128:(t - 1) * 128, :], x_d.buf)
        k.dma("sp" if t % 2 == 0 else "pool", xi, src)
        norm_tile(xi, t, 2 if t < 2 else 0, 3 if t < 2 else 1, hT, hbuf, tm1)
    k.barrier()
    p1.close()
    k.stack = ExitStack()
    if "hT" in dbg_out:
        with ExitStack() as st:
            k.stack = st
            tmpf = k.sb("dbg_hT", [128, 8, 1024], F32)
            k.copy("dve", tmpf, V(hT.ap[:, :, 0:1024], None))
            k.dma("sp", dbg_out["hT"], tmpf)
            k.barrier()
        k.stack = ExitStack()
    hTv = V(hT.ap, Buf("hT_all"))
    if stage <= 1:
        return finish(nc, k, out_d)

    winv = win_d.ap.rearrange("(k p) c -> p k c", p=128)

    def bcast_t(v, n):
        return V(v.ap.unsqueeze(1).to_broadcast([128, n, 16]), v.buf)

    s_g = ExitStack()
    k.stack = s_g
    beta = k.sb("beta", [128, NT, 16], F32)
    nbeta = k.sb("nbeta", [128, NT, 16], F32)
    gc = k.sb("gc", [128, NT, 16], F32)
    egc = k.sb("egc", [128, NT, 16], F32)
    ekt = k.sb("ekt", [128, NT, 16], F32)
    egl = k.sb("egl", [128, NT, 16], F32)
    with ExitStack() as st:
        k.stack = st
        wbaf = k.sb("wbaf", [128, 8, 32], F32)
        wba = k.sb("wba", [128, 8, 32], BF16)
        graw = k.sb("graw", [128, NT, 32], F32)
        gg = k.sb("gg", [128, NT, 16], F32)
        gtmp = k.sb("gtmp", [128, NT, 16], F32)
        negA = k.sb("negA", [128, 16], F32)
        pg = k.pv(0, 0, 512, F32, 32)
        pc0, pc1, pt0, pt1 = k.banks[1], k.banks[2], k.banks[3], k.banks[4]
        k.dma("sp", wbaf, V(winv[:, :, OFF_B:OFF_B + 32], win_d.buf))
        k.copy("dve", wba, wbaf)
        for g0 in range(0, NT, 16):
            n = min(16, NT - g0)
            for j in range(n):
                t = g0 + j
                for kk in range(8):
                    k.mm(pg[:, j, :], hTv[:, kk, t * 128:(t + 1) * 128], wba[:, kk, :], start=(kk == 0), stop=(kk == 7))
            k.copy("act", graw[:, g0:g0 + n, :], pg[:, 0:n, :])
        k.act(beta, graw[:, :, 0:16], AF.Sigmoid)
        k.ts("dve", nbeta, beta, -1.0)
        k.act(negA, gpar[:, 0, :], AF.Exp)
        k.ts("dve", negA, negA, -1.0)
        k.tt("dve", gg, graw[:, :, 16:32], bcast_t(gpar[:, 1, :], NT), ALU.add)
        k.act(gg, gg, AF.Exp)
        k.act(gg, gg, AF.Ln, bias=1.0)
        k.tt("dve", gg, gg, bcast_t(negA, NT), ALU.mult)
        if "g" in dbg_out:
            k.dma("sp", dbg_out["g"], gg)
            k.dma("sp", dbg_out["beta"], beta)
        pcs = [pc0, pc1]
        pts = [pt0, pt1]
        for d in range(2):
            pcv = V(pcs[d].ap[:, 0:NT * 8].rearrange("p (t c) -> p t c", c=8), pcs[d].buf)
            ptv = V(pts[d].ap[:, 0:NT * 8].rearrange("p (t c) -> p t c", c=8), pts[d].buf)
            k.mm(pcv, ut[d], gg[:, :, d * 8:(d + 1) * 8])
            k.mm(ptv, onesf, gg[:, :, d * 8:(d + 1) * 8])
            sl = slice(d * 8, (d + 1) * 8)
            k.copy("act", gc[:, :, sl], pcv)
            k.act(egc[:, :, sl], pcv, AF.Exp)
            k.tt("dve", gtmp[:, :, sl], ptv, gc[:, :, sl], ALU.subtract)
            k.act(ekt[:, :, sl], gtmp[:, :, sl], AF.Exp)
            k.act(egl[:, :, sl], ptv, AF.Exp)
        k.barrier()
    k.stack = ExitStack()
    if stage <= 2:
        return finish(nc, k, out_d)

    blocks = [(0, TC)] + [(TC + 512 * i, 512) for i in range(8)]
    xblocks = blocks[1:]
    poff = lambda tok: 1 + tok if tok < TC else 3 + tok
    heads = list(range(H)) if dbg is None or "heads" not in dbg else dbg["heads"]
    p4 = ExitStack()
    k.stack = p4
    praw = k.sb("praw", [128, TA + 4], BF16)
    qT = k.sb("qT", [128, TA], BF16)
    kT = k.sb("kT", [128, TA], BF16)
    vT = k.sb("vT", [128, TA], BF16)
    zs_b = [k.sb(f"zsb{i}", [128, 512], BF16) for i in range(2)]
    osum = k.sb("osum", [128, T], F32)
    wst_f = [k.sb(f"wstf{i}", [128, 8, 128], F32) for i in range(1)]
    wst_b = [k.sb(f"wstb{i}", [128, 8, 128], BF16) for i in range(2)]
    dgt = k.sb("dgt", [128, 3, 128], BF16)
    rn = [k.sb(f"rn{i}", [128, 512], F32) for i in range(2)]
    ofin_f = [k.sb(f"ofinf{i}", [128, 512], F32) for i in range(1)]
    ofin = [k.sb(f"ofin{i}", [128, 512], BF16) for i in range(2)]
    osb = [Buf(f"osum{t}") for t in range(32)]
    pbig = [k.banks[0], k.banks[1]]
    GS = 3
    rot = [[k.banks[3 * d + i] for i in range(3)] for d in range(2)]
    rcnt = [0, 0]
    def nextb(d):
        rcnt[d] += 1
        return rot[d][rcnt[d] % 3]
    def b3(bank, n, dt=F32, off=0):
        if dt == F32:
            ap = bank.ap[:, off * 128:(off + n) * 128].rearrange("p (a b) -> p a b", b=128)
        else:
            ap = bank.ap[:, off * 64:(off + n) * 64].bitcast(BF16).rearrange("p (a b) -> p a b", b=128)
        return V(ap, bank.buf)
    pvn = [k.pv(6 + d, 0, 128) for d in range(2)]
    poT = [k.pv(6 + d, 128, 256) for d in range(2)]
    pS = [k.pv(6 + d, 256, 384) for d in range(2)]
    def gtmp(name, dt, nb=1):
        return [[k.sb(f"{name}{d}_{i}", [128, GS, 128], dt) for i in range(nb)] for d in range(2)]
    g_dgc, g_E = gtmp("dgc", F32), gtmp("E", F32)
    g_eg, g_Rk0, g_Q0, g_Q0T, g_tm, g_Z = (gtmp(nm, BF16) for nm in ("eg", "Rk0", "Q0", "Q0T", "tm", "Z"))
    g_LT, g_D, g_G = gtmp("LT", BF16, 2), gtmp("D", BF16, 2), gtmp("G", BF16, 2)
    g_W, g_nwT, g_Vt, g_ktail, g_qhT, g_qkm = (gtmp(nm, BF16, 2) for nm in ("W", "nwT", "Vt", "ktail", "qhT", "qkm"))
    t_vn = [k.sb(f"vn{d}", [128, 128], BF16) for d in range(2)]
    Sf = [k.sb(f"Sf{d}", [128, 128], F32) for d in range(2)]
    Sb = [k.sb(f"Sb{d}", [128, 128], BF16) for d in range(2)]
    def bcn(v, n):
        return V(v.ap.unsqueeze(1).to_broadcast([128, n, 128]), v.buf)
    k.memset("pool", praw, 0.0)
    nbig = [0]
    def nextbig():
        nbig[0] += 1
        return pbig[nbig[0] % 2]
    wcnt = [0]

    def load_w(col0):
        i = wcnt[0] % 2
        wcnt[0] += 1
        k.dma("sp", wst_f[0], V(winv[:, :, col0:col0 + 128], win_d.buf))
        k.copy("pool", wst_b[i], wst_f[0])
        return wst_b[i]

    order_f = list(range(NT))
    order_b = [1, 0] + list(range(NT - 1, 1, -1))

    for h in heads:
        for ty in range(3):
            ci = ty * 8 + h
            wb = load_w(OFF_Q + ci * 128)
            for (s, n) in blocks:
                pp = nextbig()
                for kk in range(8):
                    k.mm(pp[:, 0:n], wb[:, kk, :], hTv[:, kk, s:s + n], start=(kk == 0), stop=(kk == 7))
                k.copy("act", praw[:, poff(s):poff(s) + n], pp[:, 0:n])
            for tap in range(3):
                k.ts("pool", dgt[:, tap, :], identf, cqkv[:, ci, tap:tap + 1])
            dst = (qT, kT, vT)[ty]
            for (s, n) in blocks:
                pp = nextbig()
                base = poff(s) - 1
                for tap in range(3):
                    k.mm(pp[:, 0:n], dgt[:, tap, :], praw[:, base + tap:base + tap + n], start=(tap == 0), stop=(tap == 2))
                k.act(dst[:, s:s + n], pp[:, 0:n], AF.Silu)
            if ty < 2:
                dstn = qT if ty == 0 else kT
                for bi, (s, n) in enumerate(blocks):
                    sqv = praw[:, 4:4 + n] if False else None
                for bi, (s, n) in enumerate(blocks):
                    pp = nextbig()
                    r = rn[bi % 2]
                    sqv = ofin[bi % 2]
                    k.tt("pool", sqv[:, 0:n], dstn[:, s:s + n], dstn[:, s:s + n], ALU.mult)
                    k.mm(pp[:, 0:n], onesb, sqv[:, 0:n])
                    k.act(r[:, 0:n], pp[:, 0:n], AF.Sqrt, bias=EPS)
                    k.recip(r[:, 0:n], r[:, 0:n])
                    k.stt("dve", dstn[:, s:s + n], dstn[:, s:s + n], (128.0 ** -0.5) if ty == 0 else 1.0, r[:, 0:n], ALU.mult, ALU.mult)
        if "qkv" in dbg_out and h == heads[0]:
            with ExitStack() as st2:
                old = k.stack
                k.stack = st2
                tf_ = k.sb("dbgqkv", [128, 3, 512], F32)
                k.copy("dve", tf_[:, 0, :], qT[:, 0:512])
                k.copy("dve", tf_[:, 1, :], kT[:, 0:512])
                k.copy("dve", tf_[:, 2, :], vT[:, 0:512])
                k.dma("sp", dbg_out["qkv"], tf_)
                k.barrier()
                k.stack = old
        for d in range(2):
            k.memset("pool", Sf[d], 0.0)
            k.memset("pool", Sb[d], 0.0)
        visited = set()
        groups = [(0, 2)] + [(2 + 3 * i, 3) for i in range(10)] + [(32, 2)]
        if dbg is not None and "nsteps" in dbg:
            groups = groups[:dbg["nsteps"]]
        gorder = [groups, [groups[0]] + groups[:0:-1]]

        def pre_gen(d, a, n, gp):
            col = d * 8 + h
            isx = a >= 2
            def bc(arr):
                return V(arr.ap[:, a:a + n, col].unsqueeze(2).to_broadcast([128, n, 128]), arr.buf)
            tl = lambda v: V(v.ap[:, a * 128:(a + n) * 128].rearrange("p (a b) -> p a b", b=128), v.buf)
            tsl = lambda i: slice((a + i) * 128, (a + i + 1) * 128)
            dgc, E, eg, Rk0, Q0, Q0T, tm_, Z = (x[d][0][:, 0:n, :] for x in (g_dgc, g_E, g_eg, g_Rk0, g_Q0, g_Q0T, g_tm, g_Z))
            LT, Dm, Gm = g_LT[d], g_D[d], g_G[d]
            W, nwT, Vt, ktail, qhT, qkm = (x[d][gp][:, 0:n, :] for x in (g_W, g_nwT, g_Vt, g_ktail, g_qhT, g_qkm))
            pA = nextb(d)
            pAk, pAv = b3(pA, n, BF16, 0), b3(pA, n, BF16, GS)
            for i in range(n):
                k.tr(pAk[:, i, :], kT[:, tsl(i)], ident)
                k.tr(pAv[:, i, :], vT[:, tsl(i)], ident)
            yield
            k.act(Vt, pAv, AF.Copy)
            k.tt("dve", Rk0, pAk, bc(egc), ALU.mult)
            k.tt("dve", ktail, pAk, bc(ekt), ALU.mult)
            yield
            k.tt("pool", dgc, bcn(identf, n), bc(gc), ALU.mult)
            pB = nextb(d)
            pBv = b3(pB, n)
            for i in range(n):
                k.mm(pBv[:, i, :], onesf, dgc[:, i, :])
            yield
            k.tt("dve", E, pBv, bc(gc), ALU.subtract)
            if isx:
                k.act(eg, pBv, AF.Exp)
            k.tt("pool", E, E, bcn(negm[d], n), ALU.add)
            k.act(E, E, AF.Exp)
            if isx:
                k.tt("pool", qhT, tl(qT), eg, ALU.mult)
            yield
            pC = nextb(d)
            pCv = b3(pC, n)
            for i in range(n):
                k.mm(pCv[:, i, :], kT[:, tsl(i)], kT[:, tsl(i)])
            k.tt("dve", Q0, pCv, bc(nbeta), ALU.mult)
            if isx:
                pQ = nextb(d)
                pQv = b3(pQ, n)
                for i in range(n):
                    k.mm(pQv[:, i, :], kT[:, tsl(i)], qT[:, tsl(i)])
                k.tt("dve", qkm, pQv, E, ALU.mult)
            yield
            k.tt("pool", E, E, bcn(smask[d], n), ALU.mult)
            k.tt("pool", Q0, Q0, E, ALU.mult)
            pT = nextb(d)
            pTv = b3(pT, n, BF16, 0)
            for i in range(n):
                k.tr(pTv[:, i, :], Q0[:, i, :], ident)
            k.act(Q0T, pTv, AF.Copy)
            yield
            k.tt("pool", tm_, Q0, bcn(hmask[:, 0, :], n), ALU.mult)
            k.tt("pool", Dm[0][:, 0:n, :], bcn(ident, n), tm_, ALU.subtract)
            k.tt("pool", tm_, Q0T, bcn(hmask[:, 0, :], n), ALU.mult)
            k.tt("pool", Gm[0][:, 0:n, :], bcn(ident, n), tm_, ALU.subtract)
            k.tt("pool", LT[1][:, 0:n, :], Q0T, bcn(hmask[:, 1, :], n), ALU.mult)
            yield
            for m_ in range(1, 7):
                a_, b_ = (m_ - 1) % 2, m_ % 2
                Da, Ga, Lm = Dm[a_][:, 0:n, :], Gm[a_][:, 0:n, :], LT[m_ % 2][:, 0:n, :]
                pz = nextb(d)
                pzv = b3(pz, n)
                for i in range(n):
                    k.mm(pzv[:, i, :], Lm[:, i, :], Da[:, i, :])
                if m_ < 6:
                    k.tt("pool", LT[(m_ + 1) % 2][:, 0:n, :], Q0T, bcn(hmask[:, m_ + 1, :], n), ALU.mult)
                k.act(Z, pzv, AF.Copy)
                yield
                pd_ = nextb(d)
                pdv = b3(pd_, n)
                for i in range(n):
                    k.mm(pdv[:, i, :], Ga[:, i, :], Z[:, i, :])
                if m_ < 6:
                    pg_ = nextb(d)
                    pgv = b3(pg_, n)
                    for i in range(n):
                        k.mm(pgv[:, i, :], Z[:, i, :], Ga[:, i, :])
                k.tt("dve", W if m_ == 6 else Dm[b_][:, 0:n, :], Da, pdv, ALU.subtract)
                if m_ < 6:
                    k.tt("dve", Gm[b_][:, 0:n, :], Ga, pgv, ALU.subtract)
                yield
            pw = nextb(d)
            pwv = b3(pw, n)
            for i in range(n):
                k.mm(pwv[:, i, :], Rk0[:, i, :], W[:, i, :])
            k.act(nwT, pwv, AF.Copy, scale=-1.0)
            yield

        def state_gen(d, a, n, gp):
            col = d * 8 + h
            isx = a >= 2
            W, nwT, Vt, ktail, qhT, qkm = (x[d][gp] for x in (g_W, g_nwT, g_Vt, g_ktail, g_qhT, g_qkm))
            vn = t_vn[d]
            for i in (range(n) if d == 0 else range(n - 1, -1, -1)):
                t = a + i
                sc = lambda arr: arr[:, t, col:col + 1]
                k.mm(pvn[d], W[:, i, :], Vt[:, i, :], start=True, stop=False)
                k.mm(pvn[d], nwT[:, i, :], Sb[d], start=False, stop=True)
                k.act(vn, pvn[d], AF.Identity, scale=sc(beta))
                yield
                if isx:
                    xt = t - 2
                    ov = V(osum.ap[:, xt * 128:(xt + 1) * 128], osb[xt])
                    k.mm(poT[d], Sb[d], qhT[:, i, :], start=True, stop=False)
                    k.mm(poT[d], vn, qkm[:, i, :], start=False, stop=True)
                k.mm(pS[d], ktail[:, i, :], vn)
                if isx:
                    if xt not in visited:
                        visited.add(xt)
                        k.act(ov, poT[d], AF.Copy)
                    else:
                        k.tt("dve", ov, poT[d], ov, ALU.add)
                k.stt("dve", Sf[d], Sf[d], sc(egl), pS[d], ALU.mult, ALU.add)
                k.act(Sb[d], Sf[d], AF.Copy)
                yield

        def run_all(gens):
            gens = list(gens)
            while gens:
                for g_ in list(gens):
                    try:
                        next(g_)
                    except StopIteration:
                        gens.remove(g_)

        ng = len(groups)
        run_all([pre_gen(0, *gorder[0][0], 0), pre_gen(1, *gorder[1][0], 0)])
        for gi in range(ng):
            gens = [state_gen(0, *gorder[0][gi], gi % 2), state_gen(1, *gorder[1][gi], gi % 2)]
            if gi + 1 < ng:
                gens += [pre_gen(0, *gorder[0][gi + 1], (gi + 1) % 2), pre_gen(1, *gorder[1][gi + 1], (gi + 1) % 2)]
            run_all(gens)
        if "S" in dbg_out and h == heads[0]:
            k.dma("sp", dbg_out["S"][0], Sf[0])
            k.dma("sp", dbg_out["S"][1], Sf[1])
        for bi in range(8):
            s = bi * 512
            ovs = [V(osum.ap[:, s:s + 512], osb[bi * 4 + j]) for j in range(4)]
            class _M:
                pass
            ovall = V(osum.ap[:, s:s + 512], osb[bi * 4])
            extra = [osb[bi * 4 + j] for j in range(1, 4)]
            if bi == 0:
                wbz = load_w(OFF_Z + h * 128)
            pz_ = nextbig()
            zsb = zs_b[bi % 2]
            for kk in range(8):
                k.mm(pz_, wbz[:, kk, :], hTv[:, kk, TC + s:TC + s + 512], start=(kk == 0), stop=(kk == 7))
            k.act(zsb, pz_, AF.Silu)
            pp = nextbig()
            sqv = ofin[bi % 2]
            r = rn[bi % 2]
            of_ = ofin_f[0]
            k.op("pool", lambda g, sqv=sqv, s=s: g.tensor_tensor(out=sqv.ap, in0=osum.ap[:, s:s + 512], in1=osum.ap[:, s:s + 512], op=ALU.mult),
                 [osb[bi * 4 + j] for j in range(4)], [sqv.buf])
            k.mm(pp, onesb, sqv)
            k.act(r, pp, AF.Sqrt, bias=EPS, scale=1.0 / 128)
            k.recip(r, r)
            k.op("dve", lambda g, of_=of_, r=r, s=s: g.scalar_tensor_tensor(out=of_.ap, in0=osum.ap[:, s:s + 512], scalar=dnn.ap[:, 0:1], in1=r.ap,
                                                                           op0=ALU.mult, op1=ALU.mult),
                 [osb[bi * 4 + j] for j in range(4)] + [dnn.buf, r.buf], [of_.buf])
            if "o0" in dbg_out and h == heads[0]:
                k.tt("pool", of_, of_, zsb, ALU.mult)
                k.dma("sp", V(dbg_out["o0"].ap[:, s:s + 512], dbg_out["o0"].buf), of_)
                k.copy("pool", sqv, of_)
            else:
                k.tt("pool", sqv, of_, zsb, ALU.mult)
            k.dma("sp", V(oT_d.ap[h, :, s:s + 512], oT_d.buf), sqv)
    k.barrier()
    p4.close()
    s_g.close()
    k.stack = ExitStack()
    if stage <= 4:
        return finish(nc, k, out_d)

    def wchunk_loader(stf, stb):
        cnt = [0]
        def load(src_v, K):
            i = cnt[0] % len(stf)
            cnt[0] += 1
            k.dma("sp" if i == 0 else "pool", stf[i][:, 0:K, :], src_v)
            k.copy("pool", stb[i][:, 0:K, :], stf[i][:, 0:K, :])
            return stb[i]
        return load

    def load_resident(dst_bf, src_ap, src_buf, K, ncols, stg):
        i = 0
        for k0 in range(0, K, 8):
            kn = min(8, K - k0)
            for c0 in range(0, ncols, 512):
                cn = min(512, ncols - c0)
                st_ = stg[i % len(stg)]
                k.dma("sp" if i % 2 == 0 else "pool", st_[:, 0:kn, 0:cn], V(src_ap[:, k0:k0 + kn, c0:c0 + cn], src_buf))
                k.copy("pool" if i % 2 == 0 else "dve", dst_bf[:, k0:k0 + kn, c0:c0 + cn], st_[:, 0:kn, 0:cn])
                i += 1

    with ExitStack() as st:
        k.stack = st
        FCS = k.sb("FCS", [128, 32, 4, 256], BF16)
        fT = [k.sb(f"fT{i}", [128, 512], BF16) for i in range(2)]
        stf = [k.sb(f"p3stf{i}", [128, 8, 128], F32) for i in range(2)]
        stb = [k.sb(f"p3stb{i}", [128, 8, 128], BF16) for i in range(2)]
        ctab = [k.sb(f"ctab{i}", [128, 4, 512], BF16) for i in range(2)]
        stab = [k.sb(f"stab{i}", [128, 4, 512], BF16) for i in range(2)]
        yblk = [k.sb(f"yblk{i}", [128, 4, 512], BF16) for i in range(2)]
        lw = wchunk_loader(stf, stb)
        nb_ = 0
        for g in range(4):
            wb = lw(V(winv[:, :, OFF_F + g * 128:OFF_F + (g + 1) * 128], win_d.buf), 8)
            for bi, (s, n) in enumerate(xblocks):
                pp = k.banks[nb_ % 2]
                ft = fT[nb_ % 2]
                nb_ += 1
                for kk in range(8):
                    k.mm(pp, wb[:, kk, :], hTv[:, kk, s:s + n], start=(kk == 0), stop=(kk == 7))
                k.act(ft, pp, AF.Copy)
                pf = k.banks[2 + (nb_ % 2)]
                pfv = V(pf.ap.rearrange("p (a b) -> p a b", b=256), pf.buf)
                for j2 in range(2):
                    for jj in range(2):
                        j = j2 * 2 + jj
                        k.mm(pfv[:, jj, :], ft[:, j * 128:(j + 1) * 128], c128)
                    t0 = bi * 4 + j2 * 2
                    if j2 == 0:
                        k.act(FCS[:, t0:t0 + 2, g, :], pfv, AF.Copy)
                    else:
                        k.copy("dve", FCS[:, t0:t0 + 2, g, :], pfv)
        k.barrier()
        cosv = cos_d.ap.rearrange("(tt p) f -> p tt f", p=128)
        sinv = sin_d.ap.rearrange("(tt p) f -> p tt f", p=128)
        ld = 0
        for kb in range(8):
            for t4 in range(8):
                ct, st_ = ctab[ld % 2], stab[ld % 2]
                ld += 1
                k.dma("sp", ct, V(cosv[:, t4 * 4:(t4 + 1) * 4, kb * 512:(kb + 1) * 512], cos_d.buf))
                k.dma("pool", st_, V(sinv[:, t4 * 4:(t4 + 1) * 4, kb * 512:(kb + 1) * 512], sin_d.buf))
                for ti in range(4):
                    tt_ = t4 * 4 + ti
                    for g in range(4):
                        k.mm(k.banks[4 + g], FCS[:, tt_, g, 0:128], ct[:, ti, :], start=(tt_ == 0), stop=False)
                        k.mm(k.banks[4 + g], FCS[:, tt_, g, 128:256], st_[:, ti, :], start=False, stop=(tt_ == 31))
            yb = yblk[kb % 2]
            for g in range(4):
                if g % 2 == 0:
                    k.act(yb[:, g, :], k.banks[4 + g], AF.Copy)
                else:
                    k.copy("dve", yb[:, g, :], k.banks[4 + g])
            k.dma("sp", V(yT_d.ap[:, :, kb * 512:(kb + 1) * 512].rearrange("g p f -> p g f"), yT_d.buf), yb)
        if "fm" in dbg_out:
            pass
        k.barrier()
    k.stack = ExitStack()
    if stage <= 5:
        return finish(nc, k, out_d)

    with ExitStack() as st:
        k.stack = st
        wg = k.sb("wg", [128, 8, 2048], BF16)
        wf4 = k.sb("wf4", [128, 4, 1024], BF16)
        wdn = k.sb("wdn", [128, 8, 1024], BF16)
        stg = [k.sb(f"p5stg{i}", [128, 8, 512], F32) for i in range(1)]
        ytb = [k.sb(f"ytb{i}", [128, 4, 512], BF16) for i in range(2)]
        otb = [k.sb(f"otb{i}", [128, 8, 512], BF16) for i in range(2)]
        g0 = [k.sb(f"g0_{i}", [128, 512], BF16) for i in range(2)]
        g1 = [k.sb(f"g1_{i}", [128, 512], BF16) for i in range(2)]
        m0 = [k.sb(f"m0_{i}", [128, 512], BF16) for i in range(2)]
        m1 = [k.sb(f"m1_{i}", [128, 512], BF16) for i in range(2)]
        mixs = k.sb("mixs", [128, 2, 8, 512], BF16)
        mixs_buf = [Buf("mixs0"), Buf("mixs1")]
        load_resident(wg, winv[:, :, OFF_G:OFF_G + 2048], win_d.buf, 8, 2048, stg)
        load_resident(wf4, wf_d.ap.rearrange("(g p) d -> p g d", p=128), wf_d.buf, 4, 1024, stg)
        load_resident(wdn, wdn_d.ap.rearrange("(h p) d -> p h d", p=128), wdn_d.buf, 8, 1024, stg)
        it = 0
        for mt in range(8):
            s = TC + mt * 512
            yt, ot = ytb[mt % 2], otb[mt % 2]
            k.dma("sp", yt, V(yT_d.ap[:, :, mt * 512:(mt + 1) * 512].rearrange("g p f -> p g f"), yT_d.buf))
            k.dma("pool", ot, V(oT_d.ap[:, :, mt * 512:(mt + 1) * 512].rearrange("h p f -> p h f"), oT_d.buf))
            for dc in range(8):
                dsl = slice(dc * 128, (dc + 1) * 128)
                i2 = it % 2
                it += 1
                pb = [k.banks[4 * i2 + j] for j in range(4)]
                for g in range(4):
                    k.mm(pb[0], wf4[:, g, dsl], yt[:, g, :], start=(g == 0), stop=(g == 3))
                for kk in range(8):
                    k.mm(pb[1], wg[:, kk, dc * 128:(dc + 1) * 128], hTv[:, kk, s:s + 512], start=(kk == 0), stop=(kk == 7))
                for hh in range(8):
                    k.mm(pb[2], wdn[:, hh, dsl], ot[:, hh, :], start=(hh == 0), stop=(hh == 7))
                for kk in range(8):
                    k.mm(pb[3], wg[:, kk, 1024 + dc * 128:1024 + (dc + 1) * 128], hTv[:, kk, s:s + 512], start=(kk == 0), stop=(kk == 7))
                k.act(g0[i2], pb[1], AF.Sigmoid)
                k.act(g1[i2], pb[3], AF.Sigmoid)
                k.tt("dve", m0[i2], pb[0], g0[i2], ALU.mult)
                k.tt("dve", m1[i2], pb[2], g1[i2], ALU.mult)
                k.tt("pool", V(mixs.ap[:, mt % 2, dc, :], mixs_buf[mt % 2]), m0[i2], m1[i2], ALU.add)
            k.copy("pool", hTv[:, :, s:s + 512], V(mixs.ap[:, mt % 2, :, :], mixs_buf[mt % 2]))
        k.barrier()
    k.stack = ExitStack()
    if stage <= 6:
        return finish(nc, k, out_d)

    def branch_tail(mt, producer, cidx, resid_d, final, tb):
        yx, sq, rst, xin_, x1t = tb["yx"], tb["sq"], tb["rst"], tb["xin"], tb["x1t"]
        for dc in range(8):
            pb = k.banks[dc % 2]
            producer(dc, pb)
            k.act(yx[:, dc, :], pb, AF.Copy)
            k.act(sq[:, dc, :], pb, AF.Square)
        pss = k.banks[2]
        for dc in range(8):
            k.mm(pss, onesb, sq[:, dc, :], start=(dc == 0), stop=(dc == 7))
        k.act(rst, pss, AF.Sqrt, bias=EPS, scale=1.0 / D)
        k.recip(rst, rst)
        for dc in range(8):
            k.stt("dve", yx[:, dc, :], yx[:, dc, :], coef[:, cidx, dc:dc + 1], rst, ALU.mult, ALU.mult)
        for j in range(4):
            tok0 = mt * 512 + j * 128
            xi = xin_[j % len(xin_)]
            xo = x1t[j % len(x1t)]
            k.dma("sp" if j % 2 == 0 else "pool", xi, V(resid_d.ap[tok0:tok0 + 128, :], resid_d.buf))
            ba, bb = k.banks[3 + 2 * (j % 2)], k.banks[4 + 2 * (j % 2)]
            for dc in range(8):
                bk = ba if dc < 4 else bb
                k.tr(bk[:, (dc % 4) * 128:(dc % 4 + 1) * 128], yx[:, dc, j * 128:(j + 1) * 128], identf)
            k.tt("dve", xo[:, 0:512], ba, xi[:, 0:512], ALU.add)
            k.tt("dve", xo[:, 512:1024], bb, xi[:, 512:1024], ALU.add)
            if final:
                k.dma("sp", V(out_d.ap[tok0:tok0 + 128, :], out_d.buf), xo)
            else:
                k.dma("sp", V(x1_d.ap[tok0:tok0 + 128, :], x1_d.buf), xo)
                norm_tile(xo, 2 + mt * 4 + j, 5, 6, hT, hTv.buf, tb["tm"])

    def tail_bufs(nbuf):
        return {"yx": k.sb("yx", [128, 8, 512], F32), "sq": k.sb("sqb", [128, 8, 512], BF16), "rst": k.sb("rst", [128, 512], F32),
                "xin": [k.sb(f"rxin{i}", [128, D], F32) for i in range(nbuf)], "x1t": [k.sb(f"x1t{i}", [128, D], F32) for i in range(nbuf)],
                "tm": {"sq": [k.sb("t_sq", [128, D], F32)], "ss": [k.sb("t_ss", [128, 1], F32)], "xn": [k.sb("t_xn", [128, D], BF16)],
                       "pt": [k.pv(7, 0, 512, BF16, 128)]}}

    with ExitStack() as st:
        k.stack = st
        wout = k.sb("wout", [128, 8, 1024], BF16)
        stg = [k.sb(f"p5bstg{i}", [128, 8, 512], F32) for i in range(1)]
        load_resident(wout, wout_d.ap.rearrange("(c p) d -> p c d", p=128), wout_d.buf, 8, 1024, stg)
        tb = tail_bufs(2)
        mixl = k.sb("mixl", [128, 8, 512], BF16)
        for mt in range(8):
            s = TC + mt * 512
            k.copy("pool", mixl, hTv[:, :, s:s + 512])
            def prod(dc, pb):
                for c in range(8):
                    k.mm(pb, wout[:, c, dc * 128:(dc + 1) * 128], mixl[:, c, :], start=(c == 0), stop=(c == 7))
            branch_tail(mt, prod, 4, x_d, False, tb)
        k.barrier()
    k.stack = ExitStack()
    if stage <= 7:
        return finish(nc, k, out_d)

    wupv = wup_d.ap.rearrange("(k p) c -> p k c", p=128)
    with ExitStack() as st:
        k.stack = st
        stf = [k.sb(f"p6stf{i}", [128, 8, 128], F32) for i in range(2)]
        stb = [k.sb(f"p6stb{i}", [128, 8, 128], BF16) for i in range(2)]
        lw = wchunk_loader(stf, stb)
        apad = k.sb("apad", [128, 66, 66], BF16)
        dg9 = k.sb("dg9", [128, 9, 128], BF16)
        sa = [k.sb(f"sa{i}", [128, 512], BF16) for i in range(2)]
        gtc = [k.sb(f"gtc{i}", [128, T], BF16) for i in range(2)]
        k.memset("pool", apad, 0.0)
        nb_ = 0
        for c in range(NFF):
            wa = lw(V(wupv[:, :, c * 128:(c + 1) * 128], wup_d.buf), 8)
            wu = lw(V(wupv[:, :, DFF + c * 128:DFF + (c + 1) * 128], wup_d.buf), 8)
            for tap in range(9):
                k.ts("pool", dg9[:, tap, :], identf, cffn[:, c, tap:tap + 1])
            for bi in range(8):
                s = TC + bi * 512
                pp = k.banks[nb_ % 2]
                nb_ += 1
                for kk in range(8):
                    k.mm(pp, wa[:, kk, :], hTv[:, kk, s:s + 512], start=(kk == 0), stop=(kk == 7))
                k.act(apad[:, 1 + bi * 8:1 + bi * 8 + 8, 1:65], V(pp.ap.rearrange("p (r c) -> p r c", c=64), pp.buf), AF.Copy)
            gt = gtc[c % 2]
            for bi in range(8):
                s = TC + bi * 512
                pc = k.banks[2 + (bi % 2)]
                pu = k.banks[4 + (bi % 2)]
                pcv = V(pc.ap.rearrange("p (r c) -> p r c", c=64), pc.buf)
                for tap in range(9):
                    dr, dcc = tap // 3, tap % 3
                    k.mm(pcv, dg9[:, tap, :], apad[:, bi * 8 + dr:bi * 8 + dr + 8, dcc:dcc + 64], start=(tap == 0), stop=(tap == 8))
                k.act(sa[bi % 2], pc, AF.Silu)
                for kk in range(8):
                    k.mm(pu, wu[:, kk, :], hTv[:, kk, s:s + 512], start=(kk == 0), stop=(kk == 7))
                k.tt("dve", gt[:, bi * 512:(bi + 1) * 512], pu, sa[bi % 2], ALU.mult)
            k.dma("sp" if c % 2 == 0 else "pool", V(gT_d.ap[c], gT_d.buf), gt)
        k.barrier()
    k.stack = ExitStack()
    s_h.close()
    k.stack = ExitStack()
    if stage <= 8:
        return finish(nc, k, out_d)

    with ExitStack() as st:
        k.stack = st
        wdown = k.sb("wdown", [128, NFF, 1024], BF16)
        stg = [k.sb(f"p7stg{i}", [128, 8, 512], F32) for i in range(2)]
        load_resident(wdown, wdown_d.ap.rearrange("(c p) d -> p c d", p=128), wdown_d.buf, NFF, 1024, stg)
        gbl = [k.sb(f"gbl{i}", [128, NFF, 512], BF16) for i in range(2)]
        tb = tail_bufs(2)
        for mt in range(8):
            gb = gbl[mt % 2]
            k.dma("sp", gb[:, 0:11, :], V(gT_d.ap[0:11, :, mt * 512:(mt + 1) * 512].rearrange("c p f -> p c f"), gT_d.buf))
            k.dma("pool", gb[:, 11:NFF, :], V(gT_d.ap[11:NFF, :, mt * 512:(mt + 1) * 512].rearrange("c p f -> p c f"), gT_d.buf))
            def prod(dc, pb, gb=gb):
                for c in range(NFF):
                    k.mm(pb, wdown[:, c, dc * 128:(dc + 1) * 128], gb[:, c, :], start=(c == 0), stop=(c == NFF - 1))
            branch_tail(mt, prod, 7, x1_d, True, tb)
        k.barrier()
    k.stack = ExitStack()
    return finish(nc, k, out_d)


def finish(nc, k, out_d):
    k.barrier()
    return nc


def prep_inputs(inp, b):
    f = lambda a: np.ascontiguousarray(a, dtype=np.float32)
    colmajor = lambda v: f(np.asarray(v).reshape(-1, 128).T)
    m = {}
    m["x"] = f(inp["x"][b])
    m["ctx"] = f(inp["ctx"][b])
    m["cc"] = f(np.stack([colmajor(inp["c"][b]), colmajor(inp["c_ctx"])], axis=-1))
    m["w_ada"] = f(inp["w_ada"][0])
    m["b_ada"] = colmajor(inp["b_ada"][0])
    m["norms"] = f(np.stack([colmajor(inp[n][0]) for n in ("norm_pre_mix", "norm_post_mix", "norm_pre_ffn", "norm_post_ffn")], axis=1))
    m["w_in"] = f(inp["w_in"][0])
    cq = np.asarray(inp["conv_qkv"][0])
    m["conv_qkv"] = f(cq.T.reshape(24, 128, 3).transpose(1, 0, 2))
    gp = np.stack([np.asarray(inp["a_log"][0]).reshape(16), np.asarray(inp["dt_bias"][0]).reshape(16)], 0)
    m["gpar"] = f(np.broadcast_to(gp[None], (128, 2, 16)))
    m["dn_norm"] = f(np.asarray(inp["dn_norm"][0]).reshape(128, 1))
    m["w_fourier"] = f(inp["w_fourier"][0])
    m["w_dn"] = f(inp["w_dn"][0])
    m["w_out"] = f(inp["w_out"][0])
    m["w_up"] = f(inp["w_up"][0])
    cf = np.asarray(inp["conv_ffn"][0]).reshape(9, DFF)
    m["conv_ffn"] = f(cf.T.reshape(NFF, 128, 9).transpose(1, 0, 2))
    m["w_down"] = f(inp["w_down"][0])
    return m


_CONST = {}


def consts():
    if not _CONST:
        idx = np.arange(T, dtype=np.int64)
        ang = (2.0 * np.pi / T) * ((idx[:, None] * idx[None, :]) % T).astype(np.float64)
        s = 1.0 / np.sqrt(float(T) * 128.0)
        _CONST["dft_cos"] = (np.cos(ang) * s).astype(ml_dtypes.bfloat16)
        _CONST["dft_sin"] = (np.sin(ang) * s).astype(ml_dtypes.bfloat16)
        i8 = np.arange(128, dtype=np.int64)
        a8 = (2.0 * np.pi / 128) * ((i8[:, None] * i8[None, :]) % 128).astype(np.float64)
        j8 = np.arange(128)
        hm = []
        for m_ in range(7):
            s_ = 2 ** m_
            blk2 = (j8[:, None] // (2 * s_)) == (j8[None, :] // (2 * s_))
            half = (j8[:, None] // s_) != (j8[None, :] // s_)
            hm.append(-(blk2 & half).astype(np.float32))
        _CONST["hmask"] = np.stack(hm, axis=1).astype(ml_dtypes.bfloat16)
        _CONST["dft128"] = np.concatenate([np.cos(a8), -np.sin(a8)], axis=1).astype(ml_dtypes.bfloat16)
    return _CONST


def kernel(**inputs):
    inp = {k_: np.asarray(v) for k_, v in inputs.items()}
    nc = build()
    cst = consts()
    in_maps = []
    for b in range(8):
        m = prep_inputs(inp, b)
        m.update(cst)
        in_maps.append(m)
    res = run_bass_kernel_spmd(nc, in_maps, core_ids=list(range(8)))
    return np.stack([np.asarray(r["out"], dtype=np.float32) for r in res.results], axis=0)
```

```python
import os
from contextlib import ExitStack
import numpy as np
import ml_dtypes
import concourse.bass as bass
import concourse.mybir as mybir
from concourse.bass_utils import run_bass_kernel_spmd

F32 = mybir.dt.float32
BF16 = mybir.dt.bfloat16
AF = mybir.ActivationFunctionType
ALU = mybir.AluOpType

D = 1024
T = 4096
TC = 256
TA = TC + T
NT = TA // 128
H = 8
OFF_F, OFF_Q, OFF_K, OFF_V, OFF_Z, OFF_B, OFF_A, OFF_G = 0, 512, 1536, 2560, 3584, 4608, 4624, 4640
INW = 6688
DFF = 2816
NFF = DFF // 128
EPS = 1e-6


class Buf:
    __slots__ = ("name", "last_w", "readers", "dsem", "dcount", "excl")

    def __init__(self, name, excl=False):
        self.name = name
        self.excl = excl
        self.last_w = None
        self.readers = []
        self.dsem = None
        self.dcount = 0


class V:
    __slots__ = ("ap", "buf")

    def __init__(self, ap, buf):
        self.ap = ap
        self.buf = buf

    def __getitem__(self, idx):
        return V(self.ap[idx], self.buf)

    def sub(self, idx, buf):
        return V(self.ap[idx], buf)


def _bufs(*vs):
    out = []
    for v in vs:
        if isinstance(v, V) and v.buf is not None and v.buf not in out:
            out.append(v.buf)
    return out


def _ap(v):
    return v.ap if isinstance(v, V) else v


class K:
    def __init__(self, nc):
        self.nc = nc
        self.engs = {"pe": nc.tensor, "act": nc.scalar, "dve": nc.vector, "pool": nc.gpsimd, "sp": nc.sync}
        self.sem = {n: nc.alloc_semaphore(f"s_{n}") for n in self.engs}
        self.cnt = {n: 0 for n in self.engs}
        self.known = {n: {} for n in self.engs}
        self.dsems = []
        self.nins = 0
        self.nwaits = 0
        self.stack = ExitStack()
        self.limit = None
        self.log = []
        self.sched = os.environ.get('KSCHED', '1') == '1'
        self.pending = []

    def _uid(self):
        self.uid = getattr(self, 'uid', 0) + 1
        return self.uid

    def sb(self, name, shape, dt, nbuf=None):
        t = self.stack.enter_context(self.nc.sbuf_tensor(f"sb{self._uid()}_" + name, list(shape), dt))
        return V(t[:] if hasattr(t, "__getitem__") else t.ap(), Buf(name) if nbuf is None else nbuf)

    def init_banks(self):
        self.banks = []
        for i in range(8):
            t = self.nc.psum_tensor(f"ps_bank{i}", [128, 512], F32).__enter__()
            self.banks.append(V(t[:], Buf(f"bank{i}", excl=True)))

    def pv(self, bank, lo, hi, dt=F32, inner=None):
        b = self.banks[bank]
        ap = b.ap[:, lo:hi]
        if dt != F32:
            ap = ap.bitcast(dt)
        if inner is not None:
            ap = ap.rearrange("p (a b) -> p a b", b=inner)
        return V(ap, b.buf)

    def _wait(self, e, ev):
        sem, val, src = ev
        if src == "pe" and e == "pe":
            return
        kn = self.known[e]
        if kn.get(sem.num, 0) >= val:
            return
        kn[sem.num] = val
        self.engs[e].wait_ge(sem, val)
        self.nwaits += 1

    def _deps(self, e, reads, writes):
        best = {}
        def add(ev):
            s = ev[0].num
            if s not in best or best[s][1] < ev[1]:
                best[s] = ev
        for b in reads:
            if b.last_w is not None:
                add(b.last_w)
        for b in writes:
            if b.last_w is not None:
                add(b.last_w)
            for ev in b.readers:
                add(ev)
        for ev in best.values():
            self._wait(e, ev)

    def _record(self, ev, reads, writes):
        for b in reads:
            if b in writes:
                continue
            b.readers.append(ev)
            if len(b.readers) > 10:
                best = {}
                for x in b.readers:
                    s = x[0].num
                    if s not in best or best[s][1] < x[1]:
                        best[s] = x
                b.readers = list(best.values())
        for b in writes:
            b.last_w = ev
            b.readers = []

    def op(self, e, fn, reads, writes, cost=300.0):
        if self.sched:
            self.pending.append(("op", e, fn, list(reads), list(writes), float(cost)))
            return
        self._emit_op(e, fn, reads, writes)

    def _emit_op(self, e, fn, reads, writes):
        if self.limit is not None and self.nins >= self.limit:
            return
        ex = [b for b in reads if b.excl and b not in writes]
        if ex:
            writes = list(writes) + ex
        self._deps(e, reads, writes)
        ins = fn(self.engs[e])
        if os.environ.get('PRINS') and self.nins in range(int(os.environ.get('PRINS','0')), int(os.environ.get('PRINS','0')) + 4):
            print('INS', self.nins, ins.concise())
        self.cnt[e] += 1
        ins.then_inc(self.sem[e], 1)
        self._record((self.sem[e], self.cnt[e], e), reads, writes)
        self.nins += 1

    def dma(self, q, out, in_, key=None, nbytes=None, **kw):
        if self.sched:
            if nbytes is None:
                shp = _ap(out).shape
                nbytes = 4
                for d_ in shp:
                    nbytes *= d_
            self.pending.append(("dma", q, (out, in_, key, kw), _bufs(in_), _bufs(out), 2000.0 + nbytes / 100.0))
            return
        self._emit_dma(q, out, in_, key, **kw)

    def _emit_dma(self, q, out, in_, key=None, **kw):
        if self.limit is not None and self.nins >= self.limit:
            return
        reads, writes = _bufs(in_), _bufs(out)
        self._deps(q, reads, writes)
        kb = key.buf if key is not None else (out.buf if not isinstance(out.buf, DBuf) else in_.buf)
        if kb.dsem is None:
            kb.dsem = self.nc.alloc_semaphore(f"d{self._uid()}_{kb.name}")
            self.dsems.append(kb)
        ins = self.engs[q].dma_start(out=_ap(out), in_=_ap(in_), **kw)
        kb.dcount += 1
        ins.then_inc(kb.dsem, 16)
        self._record((kb.dsem, 16 * kb.dcount, "dma"), reads, writes)
        self.nins += 1

    def flush(self):
        ops = self.pending
        self.pending = []
        n = len(ops)
        if n == 0:
            return
        SYNC = 200.0
        preds = [[] for _ in range(n)]
        lastw = {}
        rdrs = {}
        for i, (kind, e, fn, reads, writes, cost) in enumerate(ops):
            wr = list(writes) + [b for b in reads if b.excl and b not in writes]
            ps = set()
            for b in reads:
                if b in lastw:
                    ps.add(lastw[b])
            for b in wr:
                if b in lastw:
                    ps.add(lastw[b])
                for r_ in rdrs.get(b, ()):
                    ps.add(r_)
            ps.discard(i)
            preds[i] = list(ps)
            for b in reads:
                if b not in wr:
                    rdrs.setdefault(b, []).append(i)
            for b in wr:
                lastw[b] = i
                rdrs[b] = []
        succs = [[] for _ in range(n)]
        for i in range(n):
            for p in preds[i]:
                succs[p].append(i)
        occ = [0.0] * n
        lat = [0.0] * n
        for i, (kind, e, fn, reads, writes, cost) in enumerate(ops):
            if kind == "dma":
                occ[i] = 60.0
                lat[i] = cost
            else:
                occ[i] = cost
                lat[i] = cost
        blevel = [0.0] * n
        for i in range(n - 1, -1, -1):
            m_ = 0.0
            for s_ in succs[i]:
                if blevel[s_] > m_:
                    m_ = blevel[s_]
            blevel[i] = lat[i] + m_
        import heapq
        npred = [len(p) for p in preds]
        ready_t = [0.0] * n
        eng_free = {}
        readyq = {}
        for i in range(n):
            if npred[i] == 0:
                heapq.heappush(readyq.setdefault(ops[i][1], []), (-blevel[i], i))
        order = []
        done = 0
        while done < n:
            best = None
            for e, hq in readyq.items():
                if not hq:
                    continue
                tfree = eng_free.get(e, 0.0)
                cand = None
                top = heapq.nsmallest(6, hq)
                for pr, i in top:
                    st_ = max(tfree, ready_t[i])
                    key = (st_, pr)
                    if cand is None or key < cand[0]:
                        cand = (key, i)
                if best is None or cand[0] < best[0]:
                    best = (cand[0], cand[1], e)
            (st_, pr), i, e = best
            hq = readyq[e]
            hq.remove((-blevel[i], i))
            heapq.heapify(hq)
            eng_free[e] = st_ + occ[i]
            fin = st_ + lat[i]
            order.append(i)
            done += 1
            for s_ in succs[i]:
                rt = fin + (0.0 if ops[s_][1] == e else SYNC)
                if rt > ready_t[s_]:
                    ready_t[s_] = rt
                npred[s_] -= 1
                if npred[s_] == 0:
                    heapq.heappush(readyq.setdefault(ops[s_][1], []), (-blevel[s_], s_))
        for i in order:
            kind, e, fn, reads, writes, cost = ops[i]
            if kind == "dma":
                out, in_, key, kw = fn
                self._emit_dma(e, out, in_, key, **kw)
            else:
                self._emit_op(e, fn, reads, writes)

    def barrier(self):
        self.flush()
        for e in self.engs:
            for f in self.engs:
                if f != e and self.cnt[f] > 0:
                    self._wait(e, (self.sem[f], self.cnt[f], f))
            for kb in self.dsems:
                if kb.dcount > 0:
                    self._wait(e, (kb.dsem, 16 * kb.dcount, "dma"))

    @staticmethod
    def _fsz(v):
        shp = _ap(v).shape
        n = 1
        for d_ in shp[1:]:
            n *= d_
        return n

    def _ecost(self, e, out, in_):
        n = self._fsz(out)
        if e == "pool":
            return 150.0 + 2.0 * n
        if e == "act":
            return 220.0 + 0.72 * n
        return 100.0 + (1.05 * n if (_ap(in_).dtype == F32 or _ap(out).dtype == F32) else 0.6 * n)

    def mm(self, out, lhsT, rhs, start=True, stop=True):
        n = self._fsz(out)
        c = 40.0 + n * (1.9 if _ap(lhsT).dtype == F32 else 0.45)
        self.op("pe", lambda e: e.matmul(_ap(out), lhsT=_ap(lhsT), rhs=_ap(rhs), start=start, stop=stop),
                _bufs(lhsT, rhs) + ([] if start else _bufs(out)), _bufs(out), cost=c)

    def tr(self, out, in_, ident):
        n = self._fsz(out)
        c = 40.0 + n * (1.9 if _ap(in_).dtype == F32 else 0.45)
        self.op("pe", lambda e: e.transpose(out=_ap(out), in_=_ap(in_), identity=_ap(ident)), _bufs(in_, ident), _bufs(out), cost=c)

    def act(self, out, in_, func, bias=0.0, scale=1.0, accum=None, eng="act"):
        kw = {}
        if accum is not None:
            kw["accum_out"] = _ap(accum)
        self.op("act", lambda e: e.activation(out=_ap(out), in_=_ap(in_), func=func, bias=_ap(bias), scale=_ap(scale), **kw),
                _bufs(in_, bias, scale), _bufs(out, accum), cost=self._ecost("act", out, in_))

    def ts(self, e, out, in0, s1, s2=None, op0=ALU.mult, op1=None):
        if op1 is None:
            f = lambda g: g.tensor_scalar(out=_ap(out), in0=_ap(in0), scalar1=_ap(s1), scalar2=None, op0=op0)
        else:
            f = lambda g: g.tensor_scalar(out=_ap(out), in0=_ap(in0), scalar1=_ap(s1), scalar2=_ap(s2), op0=op0, op1=op1)
        self.op(e, f, _bufs(in0, s1, s2), _bufs(out), cost=self._ecost(e, out, in0))

    def tt(self, e, out, a, b, op):
        self.op(e, lambda g: g.tensor_tensor(out=_ap(out), in0=_ap(a), in1=_ap(b), op=op), _bufs(a, b), _bufs(out), cost=self._ecost(e, out, a))

    def stt(self, e, out, in0, scalar, in1, op0, op1):
        self.op(e, lambda g: g.scalar_tensor_tensor(out=_ap(out), in0=_ap(in0), scalar=_ap(scalar), in1=_ap(in1), op0=op0, op1=op1),
                _bufs(in0, scalar, in1), _bufs(out), cost=self._ecost(e, out, in0))

    def copy(self, e, out, in_):
        if e == "act":
            self.act(out, in_, AF.Copy)
        else:
            self.op(e, lambda g: g.tensor_copy(out=_ap(out), in_=_ap(in_)), _bufs(in_), _bufs(out), cost=self._ecost(e, out, in_))

    def recip(self, out, in_):
        self.op("dve", lambda g: g.reciprocal(out=_ap(out), in_=_ap(in_)), _bufs(in_), _bufs(out), cost=100.0 + 6.3 * self._fsz(out))

    def memset(self, e, out, val):
        self.op(e, lambda g: g.memset(_ap(out), val), [], _bufs(out))

    def asel(self, out, in_, pattern, cmp, fill, base, cm):
        self.op("pool", lambda g: g.affine_select(out=_ap(out), in_=_ap(in_), pattern=pattern, compare_op=cmp, fill=fill,
                                                  base=base, channel_multiplier=cm), _bufs(in_), _bufs(out))


class DBuf(Buf):
    __slots__ = ("is_dram",)

    def __init__(self, name):
        super().__init__(name)
        self.is_dram = True


def dramv(nc, name, shape, dt, kind):
    t = nc.dram_tensor(name, list(shape), dt, kind=kind)
    return V(t.ap(), DBuf(name))


def build(stage=99, dbg=None):
    nc = bass.Bass("TRN2", target_bir_lowering=False)
    k = K(nc)
    k.init_banks()
    if dbg and 'limit' in dbg:
        k.limit = dbg['limit']
    IN = lambda name, shape, dt=F32: dramv(nc, name, shape, dt, "ExternalInput")
    x_d = IN("x", [T, D])
    ctx_d = IN("ctx", [TC, D])
    cc_d = IN("cc", [128, 8, 2])
    wada_d = IN("w_ada", [D, 6 * D])
    bada_d = IN("b_ada", [128, 48])
    nrm_d = IN("norms", [128, 4, 8])
    win_d = IN("w_in", [D, INW])
    cqkv_d = IN("conv_qkv", [128, 24, 3])
    gpar_d = IN("gpar", [128, 2, 16])
    dnn_d = IN("dn_norm", [128, 1])
    wf_d = IN("w_fourier", [512, D])
    wdn_d = IN("w_dn", [D, D])
    wout_d = IN("w_out", [D, D])
    wup_d = IN("w_up", [D, 2 * DFF])
    cffn_d = IN("conv_ffn", [128, NFF, 9])
    wdown_d = IN("w_down", [DFF, D])
    cos_d = IN("dft_cos", [T, T], BF16)
    sin_d = IN("dft_sin", [T, T], BF16)
    c128_d = IN("dft128", [128, 256], BF16)
    hmask_d = IN("hmask", [128, 7, 128], BF16)
    out_d = dramv(nc, "out", [T, D], F32, "ExternalOutput")
    dbg_out = {}
    if dbg:
        for nm, shp in dbg.items():
            if nm in ("heads", "nsteps", "limit"):
                continue
            dbg_out[nm] = dramv(nc, "dbg_" + nm, shp, F32, "ExternalOutput")
    x1_d = dramv(nc, "x1_scr", [T, D], F32, "Internal")
    oT_d = dramv(nc, "oT_scr", [H, 128, T], BF16, "Internal")
    gT_d = dramv(nc, "gT_scr", [NFF, 128, T], BF16, "Internal")
    yT_d = dramv(nc, "yT_scr", [4, 128, T], BF16, "Internal")

    identf = k.sb("identf", [128, 128], F32)
    ident = k.sb("ident", [128, 128], BF16)
    onesf = k.sb("onesf", [128, 128], F32)
    onesb = k.sb("onesb", [128, 128], BF16)
    negm = [k.sb(f"negm{d}", [128, 128], F32) for d in range(2)]
    smask = [k.sb(f"smask{d}", [128, 128], F32) for d in range(2)]
    ut = [k.sb(f"ut{d}", [128, 128], F32) for d in range(2)]
    zerof = k.sb("zerof", [128, 128], F32)
    scal_t = k.sb("scal_t", [128, 8], F32)
    k.memset("pool", zerof, 0.0)
    k.memset("pool", onesf, 1.0)
    k.copy("dve", onesb, onesf)
    k.asel(identf, zerof, [[-1, 128]], ALU.not_equal, 1.0, 0, 1)
    k.copy("dve", ident, identf)
    k.asel(negm[0], zerof, [[1, 128]], ALU.is_ge, -1e5, 0, -1)
    k.asel(smask[0], onesf, [[1, 128]], ALU.is_gt, 0.0, 0, -1)
    k.asel(ut[0], onesf, [[1, 128]], ALU.is_ge, 0.0, 0, -1)
    k.asel(negm[1], zerof, [[-1, 128]], ALU.is_ge, -1e5, 0, 1)
    k.asel(smask[1], onesf, [[-1, 128]], ALU.is_gt, 0.0, 0, 1)
    k.asel(ut[1], onesf, [[-1, 128]], ALU.is_ge, 0.0, 0, 1)

    nrm = k.sb("nrm", [128, 4, 8], F32)
    k.dma("sp", nrm, nrm_d)
    cqkv = k.sb("cqkv", [128, 24, 3], F32)
    k.dma("sp", cqkv, cqkv_d)
    gpar = k.sb("gpar", [128, 2, 16], F32)
    k.dma("sp", gpar, gpar_d)
    dnn = k.sb("dnn", [128, 1], F32)
    k.dma("sp", dnn, dnn_d)
    cffn = k.sb("cffn", [128, NFF, 9], F32)
    k.dma("sp", cffn, cffn_d)
    c128 = k.sb("c128", [128, 256], BF16)
    k.dma("sp", c128, c128_d)
    hmask = k.sb("hmask", [128, 7, 128], BF16)
    k.dma("sp", hmask, hmask_d)
    bada = k.sb("bada", [128, 48], F32)
    k.dma("sp", bada, bada_d)
    cc = k.sb("cc", [128, 8, 2], F32)
    k.dma("sp", cc, cc_d)

    mod = k.sb("mod", [128, 48, 2], F32)
    scc = k.sb("scc", [128, 8, 2], F32)
    k.act(scc, cc, AF.Silu)
    with ExitStack() as st:
        k.stack = st
        wa = [k.sb(f"wa{i}", [128, 8, 512], F32) for i in range(2)]
        pm = k.pv(0, 0, 96, F32, 2)
        wv = wada_d.ap.rearrange("(k p) c -> p k c", p=128)
        for blk in range(12):
            w = wa[blk % 2]
            k.dma("sp" if blk % 2 == 0 else "pool", w, V(wv[:, :, blk * 512:(blk + 1) * 512], wada_d.buf))
            for oc in range(4):
                for kk in range(8):
                    k.mm(pm[:, blk * 4 + oc, :], w[:, kk, oc * 128:(oc + 1) * 128], scc[:, kk, :], start=(kk == 0), stop=(kk == 7))
        for j in range(2):
            k.tt("dve", mod[:, :, j], pm[:, :, j], bada, ALU.add)
        k.barrier()
    k.stack = ExitStack()
    coef = k.sb("coef", [128, 8, 8], F32)
    def modc(i, j):
        return mod[:, i * 8:(i + 1) * 8, j]
    k.stt("dve", coef[:, 0, :], modc(1, 0), 1.0, nrm[:, 0, :], ALU.add, ALU.mult)
    k.copy("dve", coef[:, 1, :], modc(0, 0))
    k.stt("dve", coef[:, 2, :], modc(1, 1), 1.0, nrm[:, 0, :], ALU.add, ALU.mult)
    k.copy("dve", coef[:, 3, :], modc(0, 1))
    k.tt("dve", coef[:, 4, :], modc(2, 0), nrm[:, 1, :], ALU.mult)
    k.stt("dve", coef[:, 5, :], modc(4, 0), 1.0, nrm[:, 2, :], ALU.add, ALU.mult)
    k.copy("dve", coef[:, 6, :], modc(3, 0))
    k.tt("dve", coef[:, 7, :], modc(5, 0), nrm[:, 3, :], ALU.mult)
    if "coef" in dbg_out:
        k.dma("sp", dbg_out["coef"], coef)
    if stage <= 0:
        return finish(nc, k, out_d)

    s_h = ExitStack()
    k.stack = s_h
    hT = k.sb("hT", [128, 8, TA], BF16)
    hbuf = [Buf(f"hT{t}") for t in range(NT)]

    def norm_tile(src_tile_v, tile_idx, ca, cb, dst, dstbufs, tm):
        nb = len(tm["sq"])
        sq = tm["sq"][tile_idx % nb]
        ss = tm["ss"][tile_idx % nb]
        xn = tm["xn"][tile_idx % nb]
        pt = tm["pt"][tile_idx % len(tm["pt"])]
        k.act(sq, src_tile_v, AF.Square, accum=ss)
        k.act(ss, ss, AF.Sqrt, bias=EPS, scale=1.0 / D)
        k.recip(ss, ss)
        k.ts("dve", xn, src_tile_v, ss[:, 0:1])
        for c in range(8):
            k.tr(pt[:, c, :], xn[:, c * 128:(c + 1) * 128], ident)
        for c in range(8):
            dv = V(dst.ap[:, c, tile_idx * 128:(tile_idx + 1) * 128], dstbufs[tile_idx] if isinstance(dstbufs, list) else dstbufs)
            if c % 2 == 0:
                k.ts("dve", dv, pt[:, c, :], coef[:, ca, c:c + 1], coef[:, cb, c:c + 1], ALU.mult, ALU.add)
            else:
                k.act(dv, pt[:, c, :], AF.Identity, bias=coef[:, cb, c:c + 1], scale=coef[:, ca, c:c + 1])

    p1 = ExitStack()
    k.stack = p1
    nt_sq = [k.sb(f"nt_sq{i}", [128, D], F32) for i in range(2)]
    nt_ss = [k.sb(f"nt_ss{i}", [128, 1], F32) for i in range(2)]
    nt_xn = [k.sb(f"nt_xn{i}", [128, D], BF16) for i in range(2)]
    xin = [k.sb(f"xin{i}", [128, D], F32) for i in range(3)]
    nt_pt = [k.pv(i, 0, 512, BF16, 128) for i in range(2)]
    tm1 = {"sq": nt_sq, "ss": nt_ss, "xn": nt_xn, "pt": nt_pt}
    for t in range(NT):
        xi = xin[t % 3]
        if t < 2:
            src = V(ctx_d.ap[t * 128:(t + 1) * 128, :], ctx_d.buf)
        else:
            src = V(x_d.ap[(t - 2) * 128:(t - 1) * 128, :], x_d.buf)
        k.dma("sp" if t % 2 == 0 else "pool", xi, src)
        norm_tile(xi, t, 2 if t < 2 else 0, 3 if t < 2 else 1, hT, hbuf, tm1)
    k.barrier()
    p1.close()
    k.stack = ExitStack()
    if "hT" in dbg_out:
        with ExitStack() as st:
            k.stack = st
            tmpf = k.sb("dbg_hT", [128, 8, 1024], F32)
            k.copy("dve", tmpf, V(hT.ap[:, :, 0:1024], None))
            k.dma("sp", dbg_out["hT"], tmpf)
            k.barrier()
        k.stack = ExitStack()
    hTv = V(hT.ap, Buf("hT_all"))
    if stage <= 1:
        return finish(nc, k, out_d)

    winv = win_d.ap.rearrange("(k p) c -> p k c", p=128)

    def bcast_t(v, n):
        return V(v.ap.unsqueeze(1).to_broadcast([128, n, 16]), v.buf)

    s_g = ExitStack()
    k.stack = s_g
    beta = k.sb("beta", [128, NT, 16], F32)
    nbeta = k.sb("nbeta", [128, NT, 16], F32)
    gc = k.sb("gc", [128, NT, 16], F32)
    egc = k.sb("egc", [128, NT, 16], F32)
    ekt = k.sb("ekt", [128, NT, 16], F32)
    egl = k.sb("egl", [128, NT, 16], F32)
    with ExitStack() as st:
        k.stack = st
        wbaf = k.sb("wbaf", [128, 8, 32], F32)
        wba = k.sb("wba", [128, 8, 32], BF16)
        graw = k.sb("graw", [128, NT, 32], F32)
        gg = k.sb("gg", [128, NT, 16], F32)
        gtmp = k.sb("gtmp", [128, NT, 16], F32)
        negA = k.sb("negA", [128, 16], F32)
        pg = k.pv(0, 0, 512, F32, 32)
        pc0, pc1, pt0, pt1 = k.banks[1], k.banks[2], k.banks[3], k.banks[4]
        k.dma("sp", wbaf, V(winv[:, :, OFF_B:OFF_B + 32], win_d.buf))
        k.copy("dve", wba, wbaf)
        for g0 in range(0, NT, 16):
            n = min(16, NT - g0)
            for j in range(n):
                t = g0 + j
                for kk in range(8):
                    k.mm(pg[:, j, :], hTv[:, kk, t * 128:(t + 1) * 128], wba[:, kk, :], start=(kk == 0), stop=(kk == 7))
            k.copy("act", graw[:, g0:g0 + n, :], pg[:, 0:n, :])
        k.act(beta, graw[:, :, 0:16], AF.Sigmoid)
        k.ts("dve", nbeta, beta, -1.0)
        k.act(negA, gpar[:, 0, :], AF.Exp)
        k.ts("dve", negA, negA, -1.0)
        k.tt("dve", gg, graw[:, :, 16:32], bcast_t(gpar[:, 1, :], NT), ALU.add)
        k.act(gg, gg, AF.Exp)
        k.act(gg, gg, AF.Ln, bias=1.0)
        k.tt("dve", gg, gg, bcast_t(negA, NT), ALU.mult)
        if "g" in dbg_out:
            k.dma("sp", dbg_out["g"], gg)
            k.dma("sp", dbg_out["beta"], beta)
        pcs = [pc0, pc1]
        pts = [pt0, pt1]
        for d in range(2):
            pcv = V(pcs[d].ap[:, 0:NT * 8].rearrange("p (t c) -> p t c", c=8), pcs[d].buf)
            ptv = V(pts[d].ap[:, 0:NT * 8].rearrange("p (t c) -> p t c", c=8), pts[d].buf)
            k.mm(pcv, ut[d], gg[:, :, d * 8:(d + 1) * 8])
            k.mm(ptv, onesf, gg[:, :, d * 8:(d + 1) * 8])
            sl = slice(d * 8, (d + 1) * 8)
            k.copy("act", gc[:, :, sl], pcv)
            k.act(egc[:, :, sl], pcv, AF.Exp)
            k.tt("dve", gtmp[:, :, sl], ptv, gc[:, :, sl], ALU.subtract)
            k.act(ekt[:, :, sl], gtmp[:, :, sl], AF.Exp)
            k.act(egl[:, :, sl], ptv, AF.Exp)
        k.barrier()
    k.stack = ExitStack()
    if stage <= 2:
        return finish(nc, k, out_d)

    blocks = [(0, TC)] + [(TC + 512 * i, 512) for i in range(8)]
    xblocks = blocks[1:]
    poff = lambda tok: 1 + tok if tok < TC else 3 + tok
    heads = list(range(H)) if dbg is None or "heads" not in dbg else dbg["heads"]
    p4 = ExitStack()
    k.stack = p4
    praw = k.sb("praw", [128, TA + 4], BF16)
    qT = k.sb("qT", [128, TA], BF16)
    kT = k.sb("kT", [128, TA], BF16)
    vT = k.sb("vT", [128, TA], BF16)
    zs_b = [k.sb(f"zsb{i}", [128, 512], BF16) for i in range(2)]
    osum = k.sb("osum", [128, T], F32)
    wst_f = [k.sb(f"wstf{i}", [128, 8, 128], F32) for i in range(1)]
    wst_b = [k.sb(f"wstb{i}", [128, 8, 128], BF16) for i in range(2)]
    dgt = k.sb("dgt", [128, 3, 128], BF16)
    rn = [k.sb(f"rn{i}", [128, 512], F32) for i in range(2)]
    ofin_f = [k.sb(f"ofinf{i}", [128, 512], F32) for i in range(1)]
    ofin = [k.sb(f"ofin{i}", [128, 512], BF16) for i in range(2)]
    osb = [Buf(f"osum{t}") for t in range(32)]
    pbig = [k.banks[0], k.banks[1]]
    GS = 3
    rot = [[k.banks[3 * d + i] for i in range(3)] for d in range(2)]
    rcnt = [0, 0]
    def nextb(d):
        rcnt[d] += 1
        return rot[d][rcnt[d] % 3]
    def b3(bank, n, dt=F32, off=0):
        if dt == F32:
            ap = bank.ap[:, off * 128:(off + n) * 128].rearrange("p (a b) -> p a b", b=128)
        else:
            ap = bank.ap[:, off * 64:(off + n) * 64].bitcast(BF16).rearrange("p (a b) -> p a b", b=128)
        return V(ap, bank.buf)
    pvn = [k.pv(6 + d, 0, 128) for d in range(2)]
    poT = [k.pv(6 + d, 128, 256) for d in range(2)]
    pS = [k.pv(6 + d, 256, 384) for d in range(2)]
    def gtmp(name, dt, nb=1):
        return [[k.sb(f"{name}{d}_{i}", [128, GS, 128], dt) for i in range(nb)] for d in range(2)]
    g_dgc, g_E = gtmp("dgc", F32), gtmp("E", F32)
    g_eg, g_Rk0, g_Q0, g_Q0T, g_tm, g_Z = (gtmp(nm, BF16) for nm in ("eg", "Rk0", "Q0", "Q0T", "tm", "Z"))
    g_LT, g_D, g_G = gtmp("LT", BF16, 2), gtmp("D", BF16, 2), gtmp("G", BF16, 2)
    g_W, g_nwT, g_Vt, g_ktail, g_qhT, g_qkm = (gtmp(nm, BF16, 2) for nm in ("W", "nwT", "Vt", "ktail", "qhT", "qkm"))
    t_vn = [k.sb(f"vn{d}", [128, 128], BF16) for d in range(2)]
    Sf = [k.sb(f"Sf{d}", [128, 128], F32) for d in range(2)]
    Sb = [k.sb(f"Sb{d}", [128, 128], BF16) for d in range(2)]
    def bcn(v, n):
        return V(v.ap.unsqueeze(1).to_broadcast([128, n, 128]), v.buf)
    k.memset("pool", praw, 0.0)
    nbig = [0]
    def nextbig():
        nbig[0] += 1
        return pbig[nbig[0] % 2]
    wcnt = [0]

    def load_w(col0):
        i = wcnt[0] % 2
        wcnt[0] += 1
        k.dma("sp", wst_f[0], V(winv[:, :, col0:col0 + 128], win_d.buf))
        k.copy("pool", wst_b[i], wst_f[0])
        return wst_b[i]

    order_f = list(range(NT))
    order_b = [1, 0] + list(range(NT - 1, 1, -1))

    for h in heads:
        for ty in range(3):
            ci = ty * 8 + h
            wb = load_w(OFF_Q + ci * 128)
            for (s, n) in blocks:
                pp = nextbig()
                for kk in range(8):
                    k.mm(pp[:, 0:n], wb[:, kk, :], hTv[:, kk, s:s + n], start=(kk == 0), stop=(kk == 7))
                k.copy("act", praw[:, poff(s):poff(s) + n], pp[:, 0:n])
            for tap in range(3):
                k.ts("pool", dgt[:, tap, :], identf, cqkv[:, ci, tap:tap + 1])
            dst = (qT, kT, vT)[ty]
            for (s, n) in blocks:
                pp = nextbig()
                base = poff(s) - 1
                for tap in range(3):
                    k.mm(pp[:, 0:n], dgt[:, tap, :], praw[:, base + tap:base + tap + n], start=(tap == 0), stop=(tap == 2))
                k.act(dst[:, s:s + n], pp[:, 0:n], AF.Silu)
            if ty < 2:
                dstn = qT if ty == 0 else kT
                for bi, (s, n) in enumerate(blocks):
                    sqv = praw[:, 4:4 + n] if False else None
                for bi, (s, n) in enumerate(blocks):
                    pp = nextbig()
                    r = rn[bi % 2]
                    sqv = ofin[bi % 2]
                    k.tt("pool", sqv[:, 0:n], dstn[:, s:s + n], dstn[:, s:s + n], ALU.mult)
                    k.mm(pp[:, 0:n], onesb, sqv[:, 0:n])
                    k.act(r[:, 0:n], pp[:, 0:n], AF.Sqrt, bias=EPS)
                    k.recip(r[:, 0:n], r[:, 0:n])
                    k.stt("dve", dstn[:, s:s + n], dstn[:, s:s + n], (128.0 ** -0.5) if ty == 0 else 1.0, r[:, 0:n], ALU.mult, ALU.mult)
        if "qkv" in dbg_out and h == heads[0]:
            with ExitStack() as st2:
                old = k.stack
                k.stack = st2
                tf_ = k.sb("dbgqkv", [128, 3, 512], F32)
                k.copy("dve", tf_[:, 0, :], qT[:, 0:512])
                k.copy("dve", tf_[:, 1, :], kT[:, 0:512])
                k.copy("dve", tf_[:, 2, :], vT[:, 0:512])
                k.dma("sp", dbg_out["qkv"], tf_)
                k.barrier()
                k.stack = old
        for d in range(2):
            k.memset("pool", Sf[d], 0.0)
            k.memset("pool", Sb[d], 0.0)
        visited = set()
        groups = [(0, 2)] + [(2 + 3 * i, 3) for i in range(10)] + [(32, 2)]
        if dbg is not None and "nsteps" in dbg:
            groups = groups[:dbg["nsteps"]]
        gorder = [groups, [groups[0]] + groups[:0:-1]]

        def pre_gen(d, a, n, gp):
            col = d * 8 + h
            isx = a >= 2
            def bc(arr):
                return V(arr.ap[:, a:a + n, col].unsqueeze(2).to_broadcast([128, n, 128]), arr.buf)
            tl = lambda v: V(v.ap[:, a * 128:(a + n) * 128].rearrange("p (a b) -> p a b", b=128), v.buf)
            tsl = lambda i: slice((a + i) * 128, (a + i + 1) * 128)
            dgc, E, eg, Rk0, Q0, Q0T, tm_, Z = (x[d][0][:, 0:n, :] for x in (g_dgc, g_E, g_eg, g_Rk0, g_Q0, g_Q0T, g_tm, g_Z))
            LT, Dm, Gm = g_LT[d], g_D[d], g_G[d]
            W, nwT, Vt, ktail, qhT, qkm = (x[d][gp][:, 0:n, :] for x in (g_W, g_nwT, g_Vt, g_ktail, g_qhT, g_qkm))
            pA = nextb(d)
            pAk, pAv = b3(pA, n, BF16, 0), b3(pA, n, BF16, GS)
            for i in range(n):
                k.tr(pAk[:, i, :], kT[:, tsl(i)], ident)
                k.tr(pAv[:, i, :], vT[:, tsl(i)], ident)
            yield
            k.act(Vt, pAv, AF.Copy)
            k.tt("dve", Rk0, pAk, bc(egc), ALU.mult)
            k.tt("dve", ktail, pAk, bc(ekt), ALU.mult)
            yield
            k.tt("pool", dgc, bcn(identf, n), bc(gc), ALU.mult)
            pB = nextb(d)
            pBv = b3(pB, n)
            for i in range(n):
                k.mm(pBv[:, i, :], onesf, dgc[:, i, :])
            yield
            k.tt("dve", E, pBv, bc(gc), ALU.subtract)
            if isx:
                k.act(eg, pBv, AF.Exp)
            k.tt("pool", E, E, bcn(negm[d], n), ALU.add)
            k.act(E, E, AF.Exp)
            if isx:
                k.tt("pool", qhT, tl(qT), eg, ALU.mult)
            yield
            pC = nextb(d)
            pCv = b3(pC, n)
            for i in range(n):
                k.mm(pCv[:, i, :], kT[:, tsl(i)], kT[:, tsl(i)])
            k.tt("dve", Q0, pCv, bc(nbeta), ALU.mult)
            if isx:
                pQ = nextb(d)
                pQv = b3(pQ, n)
                for i in range(n):
                    k.mm(pQv[:, i, :], kT[:, tsl(i)], qT[:, tsl(i)])
                k.tt("dve", qkm, pQv, E, ALU.mult)
            yield
            k.tt("pool", E, E, bcn(smask[d], n), ALU.mult)
            k.tt("pool", Q0, Q0, E, ALU.mult)
            pT = nextb(d)
            pTv = b3(pT, n, BF16, 0)
            for i in range(n):
                k.tr(pTv[:, i, :], Q0[:, i, :], ident)
            k.act(Q0T, pTv, AF.Copy)
            yield
            k.tt("pool", tm_, Q0, bcn(hmask[:, 0, :], n), ALU.mult)
            k.tt("pool", Dm[0][:, 0:n, :], bcn(ident, n), tm_, ALU.subtract)
            k.tt("pool", tm_, Q0T, bcn(hmask[:, 0, :], n), ALU.mult)
            k.tt("pool", Gm[0][:, 0:n, :], bcn(ident, n), tm_, ALU.subtract)
            k.tt("pool", LT[1][:, 0:n, :], Q0T, bcn(hmask[:, 1, :], n), ALU.mult)
            yield
            for m_ in range(1, 7):
                a_, b_ = (m_ - 1) % 2, m_ % 2
                Da, Ga, Lm = Dm[a_][:, 0:n, :], Gm[a_][:, 0:n, :], LT[m_ % 2][:, 0:n, :]
                pz = nextb(d)
                pzv = b3(pz, n)
                for i in range(n):
                    k.mm(pzv[:, i, :], Lm[:, i, :], Da[:, i, :])
                if m_ < 6:
                    k.tt("pool", LT[(m_ + 1) % 2][:, 0:n, :], Q0T, bcn(hmask[:, m_ + 1, :], n), ALU.mult)
                k.act(Z, pzv, AF.Copy)
                yield
                pd_ = nextb(d)
                pdv = b3(pd_, n)
                for i in range(n):
                    k.mm(pdv[:, i, :], Ga[:, i, :], Z[:, i, :])
                if m_ < 6:
                    pg_ = nextb(d)
                    pgv = b3(pg_, n)
                    for i in range(n):
                        k.mm(pgv[:, i, :], Z[:, i, :], Ga[:, i, :])
                k.tt("dve", W if m_ == 6 else Dm[b_][:, 0:n, :], Da, pdv, ALU.subtract)
                if m_ < 6:
                    k.tt("dve", Gm[b_][:, 0:n, :], Ga, pgv, ALU.subtract)
                yield
            pw = nextb(d)
            pwv = b3(pw, n)
            for i in range(n):
                k.mm(pwv[:, i, :], Rk0[:, i, :], W[:, i, :])
            k.act(nwT, pwv, AF.Copy, scale=-1.0)
            yield

        def state_gen(d, a, n, gp):
            col = d * 8 + h
            isx = a >= 2
            W, nwT, Vt, ktail, qhT, qkm = (x[d][gp] for x in (g_W, g_nwT, g_Vt, g_ktail, g_qhT, g_qkm))
            vn = t_vn[d]
            for i in (range(n) if d == 0 else range(n - 1, -1, -1)):
                t = a + i
                sc = lambda arr: arr[:, t, col:col + 1]
                k.mm(pvn[d], W[:, i, :], Vt[:, i, :], start=True, stop=False)
                k.mm(pvn[d], nwT[:, i, :], Sb[d], start=False, stop=True)
                k.act(vn, pvn[d], AF.Identity, scale=sc(beta))
                yield
                if isx:
                    xt = t - 2
                    ov = V(osum.ap[:, xt * 128:(xt + 1) * 128], osb[xt])
                    k.mm(poT[d], Sb[d], qhT[:, i, :], start=True, stop=False)
                    k.mm(poT[d], vn, qkm[:, i, :], start=False, stop=True)
                k.mm(pS[d], ktail[:, i, :], vn)
                if isx:
                    if xt not in visited:
                        visited.add(xt)
                        k.act(ov, poT[d], AF.Copy)
                    else:
                        k.tt("dve", ov, poT[d], ov, ALU.add)
                k.stt("dve", Sf[d], Sf[d], sc(egl), pS[d], ALU.mult, ALU.add)
                k.act(Sb[d], Sf[d], AF.Copy)
                yield

        def run_all(gens):
            gens = list(gens)
            while gens:
                for g_ in list(gens):
                    try:
                        next(g_)
                    except StopIteration:
                        gens.remove(g_)

        ng = len(groups)
        run_all([pre_gen(0, *gorder[0][0], 0), pre_gen(1, *gorder[1][0], 0)])
        for gi in range(ng):
            gens = [state_gen(0, *gorder[0][gi], gi % 2), state_gen(1, *gorder[1][gi], gi % 2)]
            if gi + 1 < ng:
                gens += [pre_gen(0, *gorder[0][gi + 1], (gi + 1) % 2), pre_gen(1, *gorder[1][gi + 1], (gi + 1) % 2)]
            run_all(gens)
        if "S" in dbg_out and h == heads[0]:
            k.dma("sp", dbg_out["S"][0], Sf[0])
            k.dma("sp", dbg_out["S"][1], Sf[1])
        for bi in range(8):
            s = bi * 512
            ovs = [V(osum.ap[:, s:s + 512], osb[bi * 4 + j]) for j in range(4)]
            class _M:
                pass
            ovall = V(osum.ap[:, s:s + 512], osb[bi * 4])
            extra = [osb[bi * 4 + j] for j in range(1, 4)]
            if bi == 0:
                wbz = load_w(OFF_Z + h * 128)
            pz_ = nextbig()
            zsb = zs_b[bi % 2]
            for kk in range(8):
                k.mm(pz_, wbz[:, kk, :], hTv[:, kk, TC + s:TC + s + 512], start=(kk == 0), stop=(kk == 7))
            k.act(zsb, pz_, AF.Silu)
            pp = nextbig()
            sqv = ofin[bi % 2]
            r = rn[bi % 2]
            of_ = ofin_f[0]
            k.op("pool", lambda g, sqv=sqv, s=s: g.tensor_tensor(out=sqv.ap, in0=osum.ap[:, s:s + 512], in1=osum.ap[:, s:s + 512], op=ALU.mult),
                 [osb[bi * 4 + j] for j in range(4)], [sqv.buf])
            k.mm(pp, onesb, sqv)
            k.act(r, pp, AF.Sqrt, bias=EPS, scale=1.0 / 128)
            k.recip(r, r)
            k.op("dve", lambda g, of_=of_, r=r, s=s: g.scalar_tensor_tensor(out=of_.ap, in0=osum.ap[:, s:s + 512], scalar=dnn.ap[:, 0:1], in1=r.ap,
                                                                           op0=ALU.mult, op1=ALU.mult),
                 [osb[bi * 4 + j] for j in range(4)] + [dnn.buf, r.buf], [of_.buf])
            if "o0" in dbg_out and h == heads[0]:
                k.tt("pool", of_, of_, zsb, ALU.mult)
                k.dma("sp", V(dbg_out["o0"].ap[:, s:s + 512], dbg_out["o0"].buf), of_)
                k.copy("pool", sqv, of_)
            else:
                k.tt("pool", sqv, of_, zsb, ALU.mult)
            k.dma("sp", V(oT_d.ap[h, :, s:s + 512], oT_d.buf), sqv)
    k.barrier()
    p4.close()
    s_g.close()
    k.stack = ExitStack()
    if stage <= 4:
        return finish(nc, k, out_d)

    def wchunk_loader(stf, stb):
        cnt = [0]
        def load(src_v, K):
            i = cnt[0] % len(stf)
            cnt[0] += 1
            k.dma("sp" if i == 0 else "pool", stf[i][:, 0:K, :], src_v)
            k.copy("pool", stb[i][:, 0:K, :], stf[i][:, 0:K, :])
            return stb[i]
        return load

    def load_resident(dst_bf, src_ap, src_buf, K, ncols, stg):
        i = 0
        for k0 in range(0, K, 8):
            kn = min(8, K - k0)
            for c0 in range(0, ncols, 512):
                cn = min(512, ncols - c0)
                st_ = stg[i % len(stg)]
                k.dma("sp" if i % 2 == 0 else "pool", st_[:, 0:kn, 0:cn], V(src_ap[:, k0:k0 + kn, c0:c0 + cn], src_buf))
                k.copy("pool" if i % 2 == 0 else "dve", dst_bf[:, k0:k0 + kn, c0:c0 + cn], st_[:, 0:kn, 0:cn])
                i += 1

    with ExitStack() as st:
        k.stack = st
        FCS = k.sb("FCS", [128, 32, 4, 256], BF16)
        fT = [k.sb(f"fT{i}", [128, 512], BF16) for i in range(2)]
        stf = [k.sb(f"p3stf{i}", [128, 8, 128], F32) for i in range(2)]
        stb = [k.sb(f"p3stb{i}", [128, 8, 128], BF16) for i in range(2)]
        ctab = [k.sb(f"ctab{i}", [128, 4, 512], BF16) for i in range(2)]
        stab = [k.sb(f"stab{i}", [128, 4, 512], BF16) for i in range(2)]
        yblk = [k.sb(f"yblk{i}", [128, 4, 512], BF16) for i in range(2)]
        lw = wchunk_loader(stf, stb)
        nb_ = 0
        for g in range(4):
            wb = lw(V(winv[:, :, OFF_F + g * 128:OFF_F + (g + 1) * 128], win_d.buf), 8)
            for bi, (s, n) in enumerate(xblocks):
                pp = k.banks[nb_ % 2]
                ft = fT[nb_ % 2]
                nb_ += 1
                for kk in range(8):
                    k.mm(pp, wb[:, kk, :], hTv[:, kk, s:s + n], start=(kk == 0), stop=(kk == 7))
                k.act(ft, pp, AF.Copy)
                pf = k.banks[2 + (nb_ % 2)]
                pfv = V(pf.ap.rearrange("p (a b) -> p a b", b=256), pf.buf)
                for j2 in range(2):
                    for jj in range(2):
                        j = j2 * 2 + jj
                        k.mm(pfv[:, jj, :], ft[:, j * 128:(j + 1) * 128], c128)
                    t0 = bi * 4 + j2 * 2
                    if j2 == 0:
                        k.act(FCS[:, t0:t0 + 2, g, :], pfv, AF.Copy)
                    else:
                        k.copy("dve", FCS[:, t0:t0 + 2, g, :], pfv)
        k.barrier()
        cosv = cos_d.ap.rearrange("(tt p) f -> p tt f", p=128)
        sinv = sin_d.ap.rearrange("(tt p) f -> p tt f", p=128)
        ld = 0
        for kb in range(8):
            for t4 in range(8):
                ct, st_ = ctab[ld % 2], stab[ld % 2]
                ld += 1
                k.dma("sp", ct, V(cosv[:, t4 * 4:(t4 + 1) * 4, kb * 512:(kb + 1) * 512], cos_d.buf))
                k.dma("pool", st_, V(sinv[:, t4 * 4:(t4 + 1) * 4, kb * 512:(kb + 1) * 512], sin_d.buf))
                for ti in range(4):
                    tt_ = t4 * 4 + ti
                    for g in range(4):
                        k.mm(k.banks[4 + g], FCS[:, tt_, g, 0:128], ct[:, ti, :], start=(tt_ == 0), stop=False)
                        k.mm(k.banks[4 + g], FCS[:, tt_, g, 128:256], st_[:, ti, :], start=False, stop=(tt_ == 31))
            yb = yblk[kb % 2]
            for g in range(4):
                if g % 2 == 0:
                    k.act(yb[:, g, :], k.banks[4 + g], AF.Copy)
                else:
                    k.copy("dve", yb[:, g, :], k.banks[4 + g])
            k.dma("sp", V(yT_d.ap[:, :, kb * 512:(kb + 1) * 512].rearrange("g p f -> p g f"), yT_d.buf), yb)
        if "fm" in dbg_out:
            pass
        k.barrier()
    k.stack = ExitStack()
    if stage <= 5:
        return finish(nc, k, out_d)

    with ExitStack() as st:
        k.stack = st
        wg = k.sb("wg", [128, 8, 2048], BF16)
        wf4 = k.sb("wf4", [128, 4, 1024], BF16)
        wdn = k.sb("wdn", [128, 8, 1024], BF16)
        stg = [k.sb(f"p5stg{i}", [128, 8, 512], F32) for i in range(1)]
        ytb = [k.sb(f"ytb{i}", [128, 4, 512], BF16) for i in range(2)]
        otb = [k.sb(f"otb{i}", [128, 8, 512], BF16) for i in range(2)]
        g0 = [k.sb(f"g0_{i}", [128, 512], BF16) for i in range(2)]
        g1 = [k.sb(f"g1_{i}", [128, 512], BF16) for i in range(2)]
        m0 = [k.sb(f"m0_{i}", [128, 512], BF16) for i in range(2)]
        m1 = [k.sb(f"m1_{i}", [128, 512], BF16) for i in range(2)]
        mixs = k.sb("mixs", [128, 2, 8, 512], BF16)
        mixs_buf = [Buf("mixs0"), Buf("mixs1")]
        load_resident(wg, winv[:, :, OFF_G:OFF_G + 2048], win_d.buf, 8, 2048, stg)
        load_resident(wf4, wf_d.ap.rearrange("(g p) d -> p g d", p=128), wf_d.buf, 4, 1024, stg)
        load_resident(wdn, wdn_d.ap.rearrange("(h p) d -> p h d", p=128), wdn_d.buf, 8, 1024, stg)
        it = 0
        for mt in range(8):
            s = TC + mt * 512
            yt, ot = ytb[mt % 2], otb[mt % 2]
            k.dma("sp", yt, V(yT_d.ap[:, :, mt * 512:(mt + 1) * 512].rearrange("g p f -> p g f"), yT_d.buf))
            k.dma("pool", ot, V(oT_d.ap[:, :, mt * 512:(mt + 1) * 512].rearrange("h p f -> p h f"), oT_d.buf))
            for dc in range(8):
                dsl = slice(dc * 128, (dc + 1) * 128)
                i2 = it % 2
                it += 1
                pb = [k.banks[4 * i2 + j] for j in range(4)]
                for g in range(4):
                    k.mm(pb[0], wf4[:, g, dsl], yt[:, g, :], start=(g == 0), stop=(g == 3))
                for kk in range(8):
                    k.mm(pb[1], wg[:, kk, dc * 128:(dc + 1) * 128], hTv[:, kk, s:s + 512], start=(kk == 0), stop=(kk == 7))
                for hh in range(8):
                    k.mm(pb[2], wdn[:, hh, dsl], ot[:, hh, :], start=(hh == 0), stop=(hh == 7))
                for kk in range(8):
                    k.mm(pb[3], wg[:, kk, 1024 + dc * 128:1024 + (dc + 1) * 128], hTv[:, kk, s:s + 512], start=(kk == 0), stop=(kk == 7))
                k.act(g0[i2], pb[1], AF.Sigmoid)
                k.act(g1[i2], pb[3], AF.Sigmoid)
                k.tt("dve", m0[i2], pb[0], g0[i2], ALU.mult)
                k.tt("dve", m1[i2], pb[2], g1[i2], ALU.mult)
                k.tt("pool", V(mixs.ap[:, mt % 2, dc, :], mixs_buf[mt % 2]), m0[i2], m1[i2], ALU.add)
            k.copy("pool", hTv[:, :, s:s + 512], V(mixs.ap[:, mt % 2, :, :], mixs_buf[mt % 2]))
        k.barrier()
    k.stack = ExitStack()
    if stage <= 6:
        return finish(nc, k, out_d)

    def branch_tail(mt, producer, cidx, resid_d, final, tb):
        yx, sq, rst, xin_, x1t = tb["yx"], tb["sq"], tb["rst"], tb["xin"], tb["x1t"]
        for dc in range(8):
            pb = k.banks[dc % 2]
            producer(dc, pb)
            k.act(yx[:, dc, :], pb, AF.Copy)
            k.act(sq[:, dc, :], pb, AF.Square)
        pss = k.banks[2]
        for dc in range(8):
            k.mm(pss, onesb, sq[:, dc, :], start=(dc == 0), stop=(dc == 7))
        k.act(rst, pss, AF.Sqrt, bias=EPS, scale=1.0 / D)
        k.recip(rst, rst)
        for dc in range(8):
            k.stt("dve", yx[:, dc, :], yx[:, dc, :], coef[:, cidx, dc:dc + 1], rst, ALU.mult, ALU.mult)
        for j in range(4):
            tok0 = mt * 512 + j * 128
            xi = xin_[j % len(xin_)]
            xo = x1t[j % len(x1t)]
            k.dma("sp" if j % 2 == 0 else "pool", xi, V(resid_d.ap[tok0:tok0 + 128, :], resid_d.buf))
            ba, bb = k.banks[3 + 2 * (j % 2)], k.banks[4 + 2 * (j % 2)]
            for dc in range(8):
                bk = ba if dc < 4 else bb
                k.tr(bk[:, (dc % 4) * 128:(dc % 4 + 1) * 128], yx[:, dc, j * 128:(j + 1) * 128], identf)
            k.tt("dve", xo[:, 0:512], ba, xi[:, 0:512], ALU.add)
            k.tt("dve", xo[:, 512:1024], bb, xi[:, 512:1024], ALU.add)
            if final:
                k.dma("sp", V(out_d.ap[tok0:tok0 + 128, :], out_d.buf), xo)
            else:
                k.dma("sp", V(x1_d.ap[tok0:tok0 + 128, :], x1_d.buf), xo)
                norm_tile(xo, 2 + mt * 4 + j, 5, 6, hT, hTv.buf, tb["tm"])

    def tail_bufs(nbuf):
        return {"yx": k.sb("yx", [128, 8, 512], F32), "sq": k.sb("sqb", [128, 8, 512], BF16), "rst": k.sb("rst", [128, 512], F32),
                "xin": [k.sb(f"rxin{i}", [128, D], F32) for i in range(nbuf)], "x1t": [k.sb(f"x1t{i}", [128, D], F32) for i in range(nbuf)],
                "tm": {"sq": [k.sb("t_sq", [128, D], F32)], "ss": [k.sb("t_ss", [128, 1], F32)], "xn": [k.sb("t_xn", [128, D], BF16)],
                       "pt": [k.pv(7, 0, 512, BF16, 128)]}}

    with ExitStack() as st:
        k.stack = st
        wout = k.sb("wout", [128, 8, 1024], BF16)
        stg = [k.sb(f"p5bstg{i}", [128, 8, 512], F32) for i in range(1)]
        load_resident(wout, wout_d.ap.rearrange("(c p) d -> p c d", p=128), wout_d.buf, 8, 1024, stg)
        tb = tail_bufs(2)
        mixl = k.sb("mixl", [128, 8, 512], BF16)
        for mt in range(8):
            s = TC + mt * 512
            k.copy("pool", mixl, hTv[:, :, s:s + 512])
            def prod(dc, pb):
                for c in range(8):
                    k.mm(pb, wout[:, c, dc * 128:(dc + 1) * 128], mixl[:, c, :], start=(c == 0), stop=(c == 7))
            branch_tail(mt, prod, 4, x_d, False, tb)
        k.barrier()
    k.stack = ExitStack()
    if stage <= 7:
        return finish(nc, k, out_d)

    wupv = wup_d.ap.rearrange("(k p) c -> p k c", p=128)
    with ExitStack() as st:
        k.stack = st
        stf = [k.sb(f"p6stf{i}", [128, 8, 128], F32) for i in range(2)]
        stb = [k.sb(f"p6stb{i}", [128, 8, 128], BF16) for i in range(2)]
        lw = wchunk_loader(stf, stb)
        apad = k.sb("apad", [128, 66, 66], BF16)
        dg9 = k.sb("dg9", [128, 9, 128], BF16)
        sa = [k.sb(f"sa{i}", [128, 512], BF16) for i in range(2)]
        gtc = [k.sb(f"gtc{i}", [128, T], BF16) for i in range(2)]
        k.memset("pool", apad, 0.0)
        nb_ = 0
        for c in range(NFF):
            wa = lw(V(wupv[:, :, c * 128:(c + 1) * 128], wup_d.buf), 8)
            wu = lw(V(wupv[:, :, DFF + c * 128:DFF + (c + 1) * 128], wup_d.buf), 8)
            for tap in range(9):
                k.ts("pool", dg9[:, tap, :], identf, cffn[:, c, tap:tap + 1])
            for bi in range(8):
                s = TC + bi * 512
                pp = k.banks[nb_ % 2]
                nb_ += 1
                for kk in range(8):
                    k.mm(pp, wa[:, kk, :], hTv[:, kk, s:s + 512], start=(kk == 0), stop=(kk == 7))
                k.act(apad[:, 1 + bi * 8:1 + bi * 8 + 8, 1:65], V(pp.ap.rearrange("p (r c) -> p r c", c=64), pp.buf), AF.Copy)
            gt = gtc[c % 2]
            for bi in range(8):
                s = TC + bi * 512
                pc = k.banks[2 + (bi % 2)]
                pu = k.banks[4 + (bi % 2)]
                pcv = V(pc.ap.rearrange("p (r c) -> p r c", c=64), pc.buf)
                for tap in range(9):
                    dr, dcc = tap // 3, tap % 3
                    k.mm(pcv, dg9[:, tap, :], apad[:, bi * 8 + dr:bi * 8 + dr + 8, dcc:dcc + 64], start=(tap == 0), stop=(tap == 8))
                k.act(sa[bi % 2], pc, AF.Silu)
                for kk in range(8):
                    k.mm(pu, wu[:, kk, :], hTv[:, kk, s:s + 512], start=(kk == 0), stop=(kk == 7))
                k.tt("dve", gt[:, bi * 512:(bi + 1) * 512], pu, sa[bi % 2], ALU.mult)
            k.dma("sp" if c % 2 == 0 else "pool", V(gT_d.ap[c], gT_d.buf), gt)
        k.barrier()
    k.stack = ExitStack()
    s_h.close()
    k.stack = ExitStack()
    if stage <= 8:
        return finish(nc, k, out_d)

    with ExitStack() as st:
        k.stack = st
        wdown = k.sb("wdown", [128, NFF, 1024], BF16)
        stg = [k.sb(f"p7stg{i}", [128, 8, 512], F32) for i in range(2)]
        load_resident(wdown, wdown_d.ap.rearrange("(c p) d -> p c d", p=128), wdown_d.buf, NFF, 1024, stg)
        gbl = [k.sb(f"gbl{i}", [128, NFF, 512], BF16) for i in range(2)]
        tb = tail_bufs(2)
        for mt in range(8):
            gb = gbl[mt % 2]
            k.dma("sp", gb[:, 0:11, :], V(gT_d.ap[0:11, :, mt * 512:(mt + 1) * 512].rearrange("c p f -> p c f"), gT_d.buf))
            k.dma("pool", gb[:, 11:NFF, :], V(gT_d.ap[11:NFF, :, mt * 512:(mt + 1) * 512].rearrange("c p f -> p c f"), gT_d.buf))
            def prod(dc, pb, gb=gb):
                for c in range(NFF):
                    k.mm(pb, wdown[:, c, dc * 128:(dc + 1) * 128], gb[:, c, :], start=(c == 0), stop=(c == NFF - 1))
            branch_tail(mt, prod, 7, x1_d, True, tb)
        k.barrier()
    k.stack = ExitStack()
    return finish(nc, k, out_d)


def finish(nc, k, out_d):
    k.barrier()
    return nc


def prep_inputs(inp, b):
    f = lambda a: np.ascontiguousarray(a, dtype=np.float32)
    colmajor = lambda v: f(np.asarray(v).reshape(-1, 128).T)
    m = {}
    m["x"] = f(inp["x"][b])
    m["ctx"] = f(inp["ctx"][b])
    m["cc"] = f(np.stack([colmajor(inp["c"][b]), colmajor(inp["c_ctx"])], axis=-1))
    m["w_ada"] = f(inp["w_ada"][0])
    m["b_ada"] = colmajor(inp["b_ada"][0])
    m["norms"] = f(np.stack([colmajor(inp[n][0]) for n in ("norm_pre_mix", "norm_post_mix", "norm_pre_ffn", "norm_post_ffn")], axis=1))
    m["w_in"] = f(inp["w_in"][0])
    cq = np.asarray(inp["conv_qkv"][0])
    m["conv_qkv"] = f(cq.T.reshape(24, 128, 3).transpose(1, 0, 2))
    gp = np.stack([np.asarray(inp["a_log"][0]).reshape(16), np.asarray(inp["dt_bias"][0]).reshape(16)], 0)
    m["gpar"] = f(np.broadcast_to(gp[None], (128, 2, 16)))
    m["dn_norm"] = f(np.asarray(inp["dn_norm"][0]).reshape(128, 1))
    m["w_fourier"] = f(inp["w_fourier"][0])
    m["w_dn"] = f(inp["w_dn"][0])
    m["w_out"] = f(inp["w_out"][0])
    m["w_up"] = f(inp["w_up"][0])
    cf = np.asarray(inp["conv_ffn"][0]).reshape(9, DFF)
    m["conv_ffn"] = f(cf.T.reshape(NFF, 128, 9).transpose(1, 0, 2))
    m["w_down"] = f(inp["w_down"][0])
    return m


_CONST = {}


def consts():
    if not _CONST:
        idx = np.arange(T, dtype=np.int64)
        ang = (2.0 * np.pi / T) * ((idx[:, None] * idx[None, :]) % T).astype(np.float64)
        s = 1.0 / np.sqrt(float(T) * 128.0)
        _CONST["dft_cos"] = (np.cos(ang) * s).astype(ml_dtypes.bfloat16)
        _CONST["dft_sin"] = (np.sin(ang) * s).astype(ml_dtypes.bfloat16)
        i8 = np.arange(128, dtype=np.int64)
        a8 = (2.0 * np.pi / 128) * ((i8[:, None] * i8[None, :]) % 128).astype(np.float64)
        j8 = np.arange(128)
        hm = []
        for m_ in range(7):
            s_ = 2 ** m_
            blk2 = (j8[:, None] // (2 * s_)) == (j8[None, :] // (2 * s_))
            half = (j8[:, None] // s_) != (j8[None, :] // s_)
            hm.append(-(blk2 & half).astype(np.float32))
        _CONST["hmask"] = np.stack(hm, axis=1).astype(ml_dtypes.bfloat16)
        _CONST["dft128"] = np.concatenate([np.cos(a8), -np.sin(a8)], axis=1).astype(ml_dtypes.bfloat16)
    return _CONST


def kernel(**inputs):
    inp = {k_: np.asarray(v) for k_, v in inputs.items()}
    nc = build()
    cst = consts()
    in_maps = []
    for b in range(8):
        m = prep_inputs(inp, b)
        m.update(cst)
        in_maps.append(m)
    res = run_bass_kernel_spmd(nc, in_maps, core_ids=list(range(8)))
    return np.stack([np.asarray(r["out"], dtype=np.float32) for r in res.results], axis=0)
```

```python
import os
from contextlib import ExitStack
import numpy as np
import ml_dtypes
import concourse.bass as bass
import concourse.mybir as mybir
from concourse.bass_utils import run_bass_kernel_spmd

F32 = mybir.dt.float32
BF16 = mybir.dt.bfloat16
AF = mybir.ActivationFunctionType
ALU = mybir.AluOpType

D = 1024
T = 4096
TC = 256
TA = TC + T
NT = TA // 128
H = 8
OFF_F, OFF_Q, OFF_K, OFF_V, OFF_Z, OFF_B, OFF_A, OFF_G = 0, 512, 1536, 2560, 3584, 4608, 4624, 4640
INW = 6688
DFF = 2816
NFF = DFF // 128
EPS = 1e-6


class Buf:
    __slots__ = ("name", "last_w", "readers", "dsem", "dcount", "excl")

    def __init__(self, name, excl=False):
        self.name = name
        self.excl = excl
        self.last_w = None
        self.readers = []
        self.dsem = None
        self.dcount = 0


class V:
    __slots__ = ("ap", "buf")

    def __init__(self, ap, buf):
        self.ap = ap
        self.buf = buf

    def __getitem__(self, idx):
        return V(self.ap[idx], self.buf)

    def sub(self, idx, buf):
        return V(self.ap[idx], buf)


def _bufs(*vs):
    out = []
    for v in vs:
        if isinstance(v, V) and v.buf is not None and v.buf not in out:
            out.append(v.buf)
    return out


def _ap(v):
    return v.ap if isinstance(v, V) else v


class K:
    def __init__(self, nc):
        self.nc = nc
        self.engs = {"pe": nc.tensor, "act": nc.scalar, "dve": nc.vector, "pool": nc.gpsimd, "sp": nc.sync}
        self.sem = {n: nc.alloc_semaphore(f"s_{n}") for n in self.engs}
        self.cnt = {n: 0 for n in self.engs}
        self.known = {n: {} for n in self.engs}
        self.dsems = []
        self.nins = 0
        self.nwaits = 0
        self.stack = ExitStack()
        self.limit = None
        self.log = []
        self.sched = os.environ.get('KSCHED', '1') == '1'
        self.pending = []

    def _uid(self):
        self.uid = getattr(self, 'uid', 0) + 1
        return self.uid

    def sb(self, name, shape, dt, nbuf=None):
        t = self.stack.enter_context(self.nc.sbuf_tensor(f"sb{self._uid()}_" + name, list(shape), dt))
        return V(t[:] if hasattr(t, "__getitem__") else t.ap(), Buf(name) if nbuf is None else nbuf)

    def init_banks(self):
        self.banks = []
        for i in range(8):
            t = self.nc.psum_tensor(f"ps_bank{i}", [128, 512], F32).__enter__()
            self.banks.append(V(t[:], Buf(f"bank{i}", excl=True)))

    def pv(self, bank, lo, hi, dt=F32, inner=None):
        b = self.banks[bank]
        ap = b.ap[:, lo:hi]
        if dt != F32:
            ap = ap.bitcast(dt)
        if inner is not None:
            ap = ap.rearrange("p (a b) -> p a b", b=inner)
        return V(ap, b.buf)

    def _wait(self, e, ev):
        sem, val, src = ev
        if src == "pe" and e == "pe":
            return
        kn = self.known[e]
        if kn.get(sem.num, 0) >= val:
            return
        kn[sem.num] = val
        self.engs[e].wait_ge(sem, val)
        self.nwaits += 1

    def _deps(self, e, reads, writes):
        best = {}
        def add(ev):
            s = ev[0].num
            if s not in best or best[s][1] < ev[1]:
                best[s] = ev
        for b in reads:
            if b.last_w is not None:
                add(b.last_w)
        for b in writes:
            if b.last_w is not None:
                add(b.last_w)
            for ev in b.readers:
                add(ev)
        for ev in best.values():
            self._wait(e, ev)

    def _record(self, ev, reads, writes):
        for b in reads:
            if b in writes:
                continue
            b.readers.append(ev)
            if len(b.readers) > 10:
                best = {}
                for x in b.readers:
                    s = x[0].num
                    if s not in best or best[s][1] < x[1]:
                        best[s] = x
                b.readers = list(best.values())
        for b in writes:
            b.last_w = ev
            b.readers = []

    def op(self, e, fn, reads, writes, cost=300.0):
        if self.sched:
            self.pending.append(("op", e, fn, list(reads), list(writes), float(cost)))
            return
        self._emit_op(e, fn, reads, writes)

    def _emit_op(self, e, fn, reads, writes):
        if self.limit is not None and self.nins >= self.limit:
            return
        ex = [b for b in reads if b.excl and b not in writes]
        if ex:
            writes = list(writes) + ex
        self._deps(e, reads, writes)
        ins = fn(self.engs[e])
        if os.environ.get('PRINS') and self.nins in range(int(os.environ.get('PRINS','0')), int(os.environ.get('PRINS','0')) + 4):
            print('INS', self.nins, ins.concise())
        self.cnt[e] += 1
        ins.then_inc(self.sem[e], 1)
        self._record((self.sem[e], self.cnt[e], e), reads, writes)
        self.nins += 1

    def dma(self, q, out, in_, key=None, nbytes=None, **kw):
        if self.sched:
            if nbytes is None:
                shp = _ap(out).shape
                nbytes = 4
                for d_ in shp:
                    nbytes *= d_
            self.pending.append(("dma", q, (out, in_, key, kw), _bufs(in_), _bufs(out), 2000.0 + nbytes / 100.0))
            return
        self._emit_dma(q, out, in_, key, **kw)

    def _emit_dma(self, q, out, in_, key=None, **kw):
        if self.limit is not None and self.nins >= self.limit:
            return
        reads, writes = _bufs(in_), _bufs(out)
        self._deps(q, reads, writes)
        kb = key.buf if key is not None else (out.buf if not isinstance(out.buf, DBuf) else in_.buf)
        if kb.dsem is None:
            kb.dsem = self.nc.alloc_semaphore(f"d{self._uid()}_{kb.name}")
            self.dsems.append(kb)
        ins = self.engs[q].dma_start(out=_ap(out), in_=_ap(in_), **kw)
        kb.dcount += 1
        ins.then_inc(kb.dsem, 16)
        self._record((kb.dsem, 16 * kb.dcount, "dma"), reads, writes)
        self.nins += 1

    def flush(self):
        ops = self.pending
        self.pending = []
        n = len(ops)
        if n == 0:
            return
        SYNC = 200.0
        preds = [[] for _ in range(n)]
        lastw = {}
        rdrs = {}
        for i, (kind, e, fn, reads, writes, cost) in enumerate(ops):
            wr = list(writes) + [b for b in reads if b.excl and b not in writes]
            ps = set()
            for b in reads:
                if b in lastw:
                    ps.add(lastw[b])
            for b in wr:
                if b in lastw:
                    ps.add(lastw[b])
                for r_ in rdrs.get(b, ()):
                    ps.add(r_)
            ps.discard(i)
            preds[i] = list(ps)
            for b in reads:
                if b not in wr:
                    rdrs.setdefault(b, []).append(i)
            for b in wr:
                lastw[b] = i
                rdrs[b] = []
        succs = [[] for _ in range(n)]
        for i in range(n):
            for p in preds[i]:
                succs[p].append(i)
        occ = [0.0] * n
        lat = [0.0] * n
        for i, (kind, e, fn, reads, writes, cost) in enumerate(ops):
            if kind == "dma":
                occ[i] = 60.0
                lat[i] = cost
            else:
                occ[i] = cost
                lat[i] = cost
        blevel = [0.0] * n
        for i in range(n - 1, -1, -1):
            m_ = 0.0
            for s_ in succs[i]:
                if blevel[s_] > m_:
                    m_ = blevel[s_]
            blevel[i] = lat[i] + m_
        import heapq
        npred = [len(p) for p in preds]
        ready_t = [0.0] * n
        eng_free = {}
        readyq = {}
        for i in range(n):
            if npred[i] == 0:
                heapq.heappush(readyq.setdefault(ops[i][1], []), (-blevel[i], i))
        order = []
        done = 0
        while done < n:
            best = None
            for e, hq in readyq.items():
                if not hq:
                    continue
                tfree = eng_free.get(e, 0.0)
                cand = None
                top = heapq.nsmallest(6, hq)
                for pr, i in top:
                    st_ = max(tfree, ready_t[i])
                    key = (st_, pr)
                    if cand is None or key < cand[0]:
                        cand = (key, i)
                if best is None or cand[0] < best[0]:
                    best = (cand[0], cand[1], e)
            (st_, pr), i, e = best
            hq = readyq[e]
            hq.remove((-blevel[i], i))
            heapq.heapify(hq)
            eng_free[e] = st_ + occ[i]
            fin = st_ + lat[i]
            order.append(i)
            done += 1
            for s_ in succs[i]:
                rt = fin + (0.0 if ops[s_][1] == e else SYNC)
                if rt > ready_t[s_]:
                    ready_t[s_] = rt
                npred[s_] -= 1
                if npred[s_] == 0:
                    heapq.heappush(readyq.setdefault(ops[s_][1], []), (-blevel[s_], s_))
        for i in order:
            kind, e, fn, reads, writes, cost = ops[i]
            if kind == "dma":
                out, in_, key, kw = fn
                self._emit_dma(e, out, in_, key, **kw)
            else:
                self._emit_op(e, fn, reads, writes)

    def barrier(self):
        self.flush()
        for e in self.engs:
            for f in self.engs:
                if f != e and self.cnt[f] > 0:
                    self._wait(e, (self.sem[f], self.cnt[f], f))
            for kb in self.dsems:
                if kb.dcount > 0:
                    self._wait(e, (kb.dsem, 16 * kb.dcount, "dma"))

    @staticmethod
    def _fsz(v):
        shp = _ap(v).shape
        n = 1
        for d_ in shp[1:]:
            n *= d_
        return n

    def _ecost(self, e, out, in_):
        n = self._fsz(out)
        if e == "pool":
            return 150.0 + 2.0 * n
        if e == "act":
            return 220.0 + 0.72 * n
        return 100.0 + (1.05 * n if (_ap(in_).dtype == F32 or _ap(out).dtype == F32) else 0.6 * n)

    def mm(self, out, lhsT, rhs, start=True, stop=True):
        n = self._fsz(out)
        c = 40.0 + n * (1.9 if _ap(lhsT).dtype == F32 else 0.45)
        self.op("pe", lambda e: e.matmul(_ap(out), lhsT=_ap(lhsT), rhs=_ap(rhs), start=start, stop=stop),
                _bufs(lhsT, rhs) + ([] if start else _bufs(out)), _bufs(out), cost=c)

    def tr(self, out, in_, ident):
        n = self._fsz(out)
        c = 40.0 + n * (1.9 if _ap(in_).dtype == F32 else 0.45)
        self.op("pe", lambda e: e.transpose(out=_ap(out), in_=_ap(in_), identity=_ap(ident)), _bufs(in_, ident), _bufs(out), cost=c)

    def act(self, out, in_, func, bias=0.0, scale=1.0, accum=None, eng="act"):
        kw = {}
        if accum is not None:
            kw["accum_out"] = _ap(accum)
        self.op("act", lambda e: e.activation(out=_ap(out), in_=_ap(in_), func=func, bias=_ap(bias), scale=_ap(scale), **kw),
                _bufs(in_, bias, scale), _bufs(out, accum), cost=self._ecost("act", out, in_))

    def ts(self, e, out, in0, s1, s2=None, op0=ALU.mult, op1=None):
        if op1 is None:
            f = lambda g: g.tensor_scalar(out=_ap(out), in0=_ap(in0), scalar1=_ap(s1), scalar2=None, op0=op0)
        else:
            f = lambda g: g.tensor_scalar(out=_ap(out), in0=_ap(in0), scalar1=_ap(s1), scalar2=_ap(s2), op0=op0, op1=op1)
        self.op(e, f, _bufs(in0, s1, s2), _bufs(out), cost=self._ecost(e, out, in0))

    def tt(self, e, out, a, b, op):
        self.op(e, lambda g: g.tensor_tensor(out=_ap(out), in0=_ap(a), in1=_ap(b), op=op), _bufs(a, b), _bufs(out), cost=self._ecost(e, out, a))

    def stt(self, e, out, in0, scalar, in1, op0, op1):
        self.op(e, lambda g: g.scalar_tensor_tensor(out=_ap(out), in0=_ap(in0), scalar=_ap(scalar), in1=_ap(in1), op0=op0, op1=op1),
                _bufs(in0, scalar, in1), _bufs(out), cost=self._ecost(e, out, in0))

    def copy(self, e, out, in_):
        if e == "act":
            self.act(out, in_, AF.Copy)
        else:
            self.op(e, lambda g: g.tensor_copy(out=_ap(out), in_=_ap(in_)), _bufs(in_), _bufs(out), cost=self._ecost(e, out, in_))

    def recip(self, out, in_):
        self.op("dve", lambda g: g.reciprocal(out=_ap(out), in_=_ap(in_)), _bufs(in_), _bufs(out), cost=100.0 + 6.3 * self._fsz(out))

    def memset(self, e, out, val):
        self.op(e, lambda g: g.memset(_ap(out), val), [], _bufs(out))

    def asel(self, out, in_, pattern, cmp, fill, base, cm):
        self.op("pool", lambda g: g.affine_select(out=_ap(out), in_=_ap(in_), pattern=pattern, compare_op=cmp, fill=fill,
                                                  base=base, channel_multiplier=cm), _bufs(in_), _bufs(out))


class DBuf(Buf):
    __slots__ = ("is_dram",)

    def __init__(self, name):
        super().__init__(name)
        self.is_dram = True


def dramv(nc, name, shape, dt, kind):
    t = nc.dram_tensor(name, list(shape), dt, kind=kind)
    return V(t.ap(), DBuf(name))


def build(stage=99, dbg=None):
    nc = bass.Bass("TRN2", target_bir_lowering=False)
    k = K(nc)
    k.init_banks()
    if dbg and 'limit' in dbg:
        k.limit = dbg['limit']
    IN = lambda name, shape, dt=F32: dramv(nc, name, shape, dt, "ExternalInput")
    x_d = IN("x", [T, D])
    ctx_d = IN("ctx", [TC, D])
    cc_d = IN("cc", [128, 8, 2])
    wada_d = IN("w_ada", [D, 6 * D])
    bada_d = IN("b_ada", [128, 48])
    nrm_d = IN("norms", [128, 4, 8])
    win_d = IN("w_in", [D, INW])
    cqkv_d = IN("conv_qkv", [128, 24, 3])
    gpar_d = IN("gpar", [128, 2, 16])
    dnn_d = IN("dn_norm", [128, 1])
    wf_d = IN("w_fourier", [512, D])
    wdn_d = IN("w_dn", [D, D])
    wout_d = IN("w_out", [D, D])
    wup_d = IN("w_up", [D, 2 * DFF])
    cffn_d = IN("conv_ffn", [128, NFF, 9])
    wdown_d = IN("w_down", [DFF, D])
    cos_d = IN("dft_cos", [T, T], BF16)
    sin_d = IN("dft_sin", [T, T], BF16)
    c128_d = IN("dft128", [128, 256], BF16)
    hmask_d = IN("hmask", [128, 7, 128], BF16)
    out_d = dramv(nc, "out", [T, D], F32, "ExternalOutput")
    dbg_out = {}
    if dbg:
        for nm, shp in dbg.items():
            if nm in ("heads", "nsteps", "limit"):
                continue
            dbg_out[nm] = dramv(nc, "dbg_" + nm, shp, F32, "ExternalOutput")
    x1_d = dramv(nc, "x1_scr", [T, D], F32, "Internal")
    oT_d = dramv(nc, "oT_scr", [H, 128, T], BF16, "Internal")
    gT_d = dramv(nc, "gT_scr", [NFF, 128, T], BF16, "Internal")
    yT_d = dramv(nc, "yT_scr", [4, 128, T], BF16, "Internal")

    identf = k.sb("identf", [128, 128], F32)
    ident = k.sb("ident", [128, 128], BF16)
    onesf = k.sb("onesf", [128, 128], F32)
    onesb = k.sb("onesb", [128, 128], BF16)
    negm = [k.sb(f"negm{d}", [128, 128], F32) for d in range(2)]
    smask = [k.sb(f"smask{d}", [128, 128], F32) for d in range(2)]
    ut = [k.sb(f"ut{d}", [128, 128], F32) for d in range(2)]
    zerof = k.sb("zerof", [128, 128], F32)
    scal_t = k.sb("scal_t", [128, 8], F32)
    k.memset("pool", zerof, 0.0)
    k.memset("pool", onesf, 1.0)
    k.copy("dve", onesb, onesf)
    k.asel(identf, zerof, [[-1, 128]], ALU.not_equal, 1.0, 0, 1)
    k.copy("dve", ident, identf)
    k.asel(negm[0], zerof, [[1, 128]], ALU.is_ge, -1e5, 0, -1)
    k.asel(smask[0], onesf, [[1, 128]], ALU.is_gt, 0.0, 0, -1)
    k.asel(ut[0], onesf, [[1, 128]], ALU.is_ge, 0.0, 0, -1)
    k.asel(negm[1], zerof, [[-1, 128]], ALU.is_ge, -1e5, 0, 1)
    k.asel(smask[1], onesf, [[-1, 128]], ALU.is_gt, 0.0, 0, 1)
    k.asel(ut[1], onesf, [[-1, 128]], ALU.is_ge, 0.0, 0, 1)

    nrm = k.sb("nrm", [128, 4, 8], F32)
    k.dma("sp", nrm, nrm_d)
    cqkv = k.sb("cqkv", [128, 24, 3], F32)
    k.dma("sp", cqkv, cqkv_d)
    gpar = k.sb("gpar", [128, 2, 16], F32)
    k.dma("sp", gpar, gpar_d)
    dnn = k.sb("dnn", [128, 1], F32)
    k.dma("sp", dnn, dnn_d)
    cffn = k.sb("cffn", [128, NFF, 9], F32)
    k.dma("sp", cffn, cffn_d)
    c128 = k.sb("c128", [128, 256], BF16)
    k.dma("sp", c128, c128_d)
    hmask = k.sb("hmask", [128, 7, 128], BF16)
    k.dma("sp", hmask, hmask_d)
    bada = k.sb("bada", [128, 48], F32)
    k.dma("sp", bada, bada_d)
    cc = k.sb("cc", [128, 8, 2], F32)
    k.dma("sp", cc, cc_d)

    mod = k.sb("mod", [128, 48, 2], F32)
    scc = k.sb("scc", [128, 8, 2], F32)
    k.act(scc, cc, AF.Silu)
    with ExitStack() as st:
        k.stack = st
        wa = [k.sb(f"wa{i}", [128, 8, 512], F32) for i in range(2)]
        pm = k.pv(0, 0, 96, F32, 2)
        wv = wada_d.ap.rearrange("(k p) c -> p k c", p=128)
        for blk in range(12):
            w = wa[blk % 2]
            k.dma("sp" if blk % 2 == 0 else "pool", w, V(wv[:, :, blk * 512:(blk + 1) * 512], wada_d.buf))
            for oc in range(4):
                for kk in range(8):
                    k.mm(pm[:, blk * 4 + oc, :], w[:, kk, oc * 128:(oc + 1) * 128], scc[:, kk, :], start=(kk == 0), stop=(kk == 7))
        for j in range(2):
            k.tt("dve", mod[:, :, j], pm[:, :, j], bada, ALU.add)
        k.barrier()
    k.stack = ExitStack()
    coef = k.sb("coef", [128, 8, 8], F32)
    def modc(i, j):
        return mod[:, i * 8:(i + 1) * 8, j]
    k.stt("dve", coef[:, 0, :], modc(1, 0), 1.0, nrm[:, 0, :], ALU.add, ALU.mult)
    k.copy("dve", coef[:, 1, :], modc(0, 0))
    k.stt("dve", coef[:, 2, :], modc(1, 1), 1.0, nrm[:, 0, :], ALU.add, ALU.mult)
    k.copy("dve", coef[:, 3, :], modc(0, 1))
    k.tt("dve", coef[:, 4, :], modc(2, 0), nrm[:, 1, :], ALU.mult)
    k.stt("dve", coef[:, 5, :], modc(4, 0), 1.0, nrm[:, 2, :], ALU.add, ALU.mult)
    k.copy("dve", coef[:, 6, :], modc(3, 0))
    k.tt("dve", coef[:, 7, :], modc(5, 0), nrm[:, 3, :], ALU.mult)
    if "coef" in dbg_out:
        k.dma("sp", dbg_out["coef"], coef)
    if stage <= 0:
        return finish(nc, k, out_d)

    s_h = ExitStack()
    k.stack = s_h
    hT = k.sb("hT", [128, 8, TA], BF16)
    hbuf = [Buf(f"hT{t}") for t in range(NT)]

    def norm_tile(src_tile_v, tile_idx, ca, cb, dst, dstbufs, tm):
        nb = len(tm["sq"])
        sq = tm["sq"][tile_idx % nb]
        ss = tm["ss"][tile_idx % nb]
        xn = tm["xn"][tile_idx % nb]
        pt = tm["pt"][tile_idx % len(tm["pt"])]
        k.act(sq, src_tile_v, AF.Square, accum=ss)
        k.act(ss, ss, AF.Sqrt, bias=EPS, scale=1.0 / D)
        k.recip(ss, ss)
        k.ts("dve", xn, src_tile_v, ss[:, 0:1])
        for c in range(8):
            k.tr(pt[:, c, :], xn[:, c * 128:(c + 1) * 128], ident)
        for c in range(8):
            dv = V(dst.ap[:, c, tile_idx * 128:(tile_idx + 1) * 128], dstbufs[tile_idx] if isinstance(dstbufs, list) else dstbufs)
            if c % 2 == 0:
                k.ts("dve", dv, pt[:, c, :], coef[:, ca, c:c + 1], coef[:, cb, c:c + 1], ALU.mult, ALU.add)
            else:
                k.act(dv, pt[:, c, :], AF.Identity, bias=coef[:, cb, c:c + 1], scale=coef[:, ca, c:c + 1])

    p1 = ExitStack()
    k.stack = p1
    nt_sq = [k.sb(f"nt_sq{i}", [128, D], F32) for i in range(2)]
    nt_ss = [k.sb(f"nt_ss{i}", [128, 1], F32) for i in range(2)]
    nt_xn = [k.sb(f"nt_xn{i}", [128, D], BF16) for i in range(2)]
    xin = [k.sb(f"xin{i}", [128, D], F32) for i in range(3)]
    nt_pt = [k.pv(i, 0, 512, BF16, 128) for i in range(2)]
    tm1 = {"sq": nt_sq, "ss": nt_ss, "xn": nt_xn, "pt": nt_pt}
    for t in range(NT):
        xi = xin[t % 3]
        if t < 2:
            src = V(ctx_d.ap[t * 128:(t + 1) * 128, :], ctx_d.buf)
        else:
            src = V(x_d.ap[(t - 2) * 128:(t - 1) * 128, :], x_d.buf)
        k.dma("sp" if t % 2 == 0 else "pool", xi, src)
        norm_tile(xi, t, 2 if t < 2 else 0, 3 if t < 2 else 1, hT, hbuf, tm1)
    k.barrier()
    p1.close()
    k.stack = ExitStack()
    if "hT" in dbg_out:
        with ExitStack() as st:
            k.stack = st
            tmpf = k.sb("dbg_hT", [128, 8, 1024], F32)
            k.copy("dve", tmpf, V(hT.ap[:, :, 0:1024], None))
            k.dma("sp", dbg_out["hT"], tmpf)
            k.barrier()
        k.stack = ExitStack()
    hTv = V(hT.ap, Buf("hT_all"))
    if stage <= 1:
        return finish(nc, k, out_d)

    winv = win_d.ap.rearrange("(k p) c -> p k c", p=128)

    def bcast_t(v, n):
        return V(v.ap.unsqueeze(1).to_broadcast([128, n, 16]), v.buf)

    s_g = ExitStack()
    k.stack = s_g
    beta = k.sb("beta", [128, NT, 16], F32)
    ngc = k.sb("ngc", [128, NT, 16], F32)
    ngcb = k.sb("ngcb", [128, NT, 16], F32)
    gc = k.sb("gc", [128, NT, 16], F32)
    egc = k.sb("egc", [128, NT, 16], F32)
    ekt = k.sb("ekt", [128, NT, 16], F32)
    egl = k.sb("egl", [128, NT, 16], F32)
    with ExitStack() as st:
        k.stack = st
        wbaf = k.sb("wbaf", [128, 8, 32], F32)
        wba = k.sb("wba", [128, 8, 32], BF16)
        graw = k.sb("graw", [128, NT, 32], F32)
        gg = k.sb("gg", [128, NT, 16], F32)
        gtmp = k.sb("gtmp", [128, NT, 16], F32)
        lnb = k.sb("lnb", [128, NT, 16], F32)
        negA = k.sb("negA", [128, 16], F32)
        pg = k.pv(0, 0, 512, F32, 32)
        pc0, pc1, pt0, pt1 = k.banks[1], k.banks[2], k.banks[3], k.banks[4]
        k.dma("sp", wbaf, V(winv[:, :, OFF_B:OFF_B + 32], win_d.buf))
        k.copy("dve", wba, wbaf)
        for g0 in range(0, NT, 16):
            n = min(16, NT - g0)
            for j in range(n):
                t = g0 + j
                for kk in range(8):
                    k.mm(pg[:, j, :], hTv[:, kk, t * 128:(t + 1) * 128], wba[:, kk, :], start=(kk == 0), stop=(kk == 7))
            k.copy("act", graw[:, g0:g0 + n, :], pg[:, 0:n, :])
        k.act(beta, graw[:, :, 0:16], AF.Sigmoid)
        k.act(lnb, graw[:, :, 0:16], AF.Exp, scale=-1.0)
        k.act(lnb, lnb, AF.Ln, bias=1.0)
        k.act(negA, gpar[:, 0, :], AF.Exp)
        k.ts("dve", negA, negA, -1.0)
        k.tt("dve", gg, graw[:, :, 16:32], bcast_t(gpar[:, 1, :], NT), ALU.add)
        k.act(gg, gg, AF.Exp)
        k.act(gg, gg, AF.Ln, bias=1.0)
        k.tt("dve", gg, gg, bcast_t(negA, NT), ALU.mult)
        if "g" in dbg_out:
            k.dma("sp", dbg_out["g"], gg)
            k.dma("sp", dbg_out["beta"], beta)
        pcs = [pc0, pc1]
        pts = [pt0, pt1]
        for d in range(2):
            pcv = V(pcs[d].ap[:, 0:NT * 8].rearrange("p (t c) -> p t c", c=8), pcs[d].buf)
            ptv = V(pts[d].ap[:, 0:NT * 8].rearrange("p (t c) -> p t c", c=8), pts[d].buf)
            k.mm(pcv, ut[d], gg[:, :, d * 8:(d + 1) * 8])
            k.mm(ptv, onesf, gg[:, :, d * 8:(d + 1) * 8])
            sl = slice(d * 8, (d + 1) * 8)
            k.copy("act", gc[:, :, sl], pcv)
            k.act(ngc[:, :, sl], pcv, AF.Copy, scale=-1.0)
            k.stt("dve", ngcb[:, :, sl], pcv, -1.0, lnb[:, :, sl], ALU.mult, ALU.subtract)
            k.act(egc[:, :, sl], pcv, AF.Exp)
            k.tt("dve", gtmp[:, :, sl], ptv, gc[:, :, sl], ALU.subtract)
            k.act(ekt[:, :, sl], gtmp[:, :, sl], AF.Exp)
            k.act(egl[:, :, sl], ptv, AF.Exp)
        k.barrier()
    k.stack = ExitStack()
    if stage <= 2:
        return finish(nc, k, out_d)

    blocks = [(0, TC)] + [(TC + 512 * i, 512) for i in range(8)]
    xblocks = blocks[1:]
    poff = lambda tok: 1 + tok if tok < TC else 3 + tok
    heads = list(range(H)) if dbg is None or "heads" not in dbg else dbg["heads"]
    p4 = ExitStack()
    k.stack = p4
    praw = k.sb("praw", [128, TA + 4], BF16)
    qT = k.sb("qT", [128, TA], BF16)
    kT = k.sb("kT", [128, TA], BF16)
    vT = k.sb("vT", [128, TA], BF16)
    zs_b = [k.sb(f"zsb{i}", [128, 512], BF16) for i in range(1)]
    osum = k.sb("osum", [128, T], F32)
    wst_f = [k.sb(f"wstf{i}", [128, 8, 128], F32) for i in range(1)]
    wst_b = [k.sb(f"wstb{i}", [128, 8, 128], BF16) for i in range(1)]
    dgt = k.sb("dgt", [128, 3, 128], BF16)
    rn = [k.sb(f"rn{i}", [128, 512], F32) for i in range(1)]
    ofin_f = [k.sb(f"ofinf{i}", [128, 512], F32) for i in range(1)]
    ofin = [k.sb(f"ofin{i}", [128, 512], BF16) for i in range(2)]
    osb = [Buf(f"osum{t}") for t in range(32)]
    pbig = [k.banks[0], k.banks[1]]
    GS = 3
    rot = [[k.banks[3 * d + i] for i in range(3)] for d in range(2)]
    rcnt = [0, 0]
    def nextb(d):
        rcnt[d] += 1
        return rot[d][rcnt[d] % 3]
    def b3(bank, n, dt=F32, off=0):
        if dt == F32:
            ap = bank.ap[:, off * 128:(off + n) * 128].rearrange("p (a b) -> p a b", b=128)
        else:
            ap = bank.ap[:, off * 64:(off + n) * 64].bitcast(BF16).rearrange("p (a b) -> p a b", b=128)
        return V(ap, bank.buf)
    pvn = [k.pv(6 + d, 0, 128) for d in range(2)]
    poT = [k.pv(6 + d, 128, 256) for d in range(2)]
    pS = [k.pv(6 + d, 256, 384) for d in range(2)]
    def gtmp(name, dt, nb=1):
        return [[k.sb(f"{name}{d}_{i}", [128, GS, 128], dt) for i in range(nb)] for d in range(2)]
    g_dgc, g_E = gtmp("dgc", F32), gtmp("E", F32)
    g_eg, g_Rk0, g_Q0, g_Q0T, g_tm, g_Z = (gtmp(nm, BF16) for nm in ("eg", "Rk0", "Q0", "Q0T", "tm", "Z"))
    g_E1, g_E2, g_tm2 = gtmp("E1", BF16), gtmp("E2", BF16), gtmp("tm2", BF16)
    g_LT, g_D, g_G = gtmp("LT", BF16, 2), gtmp("D", BF16, 2), gtmp("G", BF16, 2)
    g_W, g_nwT, g_Vt, g_ktail, g_qhT, g_qkm = (gtmp(nm, BF16, 2) for nm in ("W", "nwT", "Vt", "ktail", "qhT", "qkm"))
    t_vn = [k.sb(f"vn{d}", [128, 128], BF16) for d in range(2)]
    Sf = [k.sb(f"Sf{d}", [128, 128], F32) for d in range(2)]
    Sb = [k.sb(f"Sb{d}", [128, 128], BF16) for d in range(2)]
    def bcn(v, n):
        return V(v.ap.unsqueeze(1).to_broadcast([128, n, 128]), v.buf)
    k.memset("pool", praw, 0.0)
    nbig = [0]
    def nextbig():
        nbig[0] += 1
        return pbig[nbig[0] % 2]
    wcnt = [0]

    def load_w(col0):
        i = 0
        wcnt[0] += 1
        k.dma("sp", wst_f[0], V(winv[:, :, col0:col0 + 128], win_d.buf))
        k.copy("pool", wst_b[i], wst_f[0])
        return wst_b[i]

    order_f = list(range(NT))
    order_b = [1, 0] + list(range(NT - 1, 1, -1))

    for h in heads:
        for ty in range(3):
            ci = ty * 8 + h
            wb = load_w(OFF_Q + ci * 128)
            for (s, n) in blocks:
                pp = nextbig()
                for kk in range(8):
                    k.mm(pp[:, 0:n], wb[:, kk, :], hTv[:, kk, s:s + n], start=(kk == 0), stop=(kk == 7))
                k.copy("act", praw[:, poff(s):poff(s) + n], pp[:, 0:n])
            for tap in range(3):
                k.ts("pool", dgt[:, tap, :], identf, cqkv[:, ci, tap:tap + 1])
            dst = (qT, kT, vT)[ty]
            for (s, n) in blocks:
                pp = nextbig()
                base = poff(s) - 1
                for tap in range(3):
                    k.mm(pp[:, 0:n], dgt[:, tap, :], praw[:, base + tap:base + tap + n], start=(tap == 0), stop=(tap == 2))
                k.act(dst[:, s:s + n], pp[:, 0:n], AF.Silu)
            if ty < 2:
                dstn = qT if ty == 0 else kT
                for bi, (s, n) in enumerate(blocks):
                    sqv = praw[:, 4:4 + n] if False else None
                for bi, (s, n) in enumerate(blocks):
                    pp = nextbig()
                    r = rn[0]
                    sqv = ofin[bi % 2]
                    k.tt("pool", sqv[:, 0:n], dstn[:, s:s + n], dstn[:, s:s + n], ALU.mult)
                    k.mm(pp[:, 0:n], onesb, sqv[:, 0:n])
                    k.act(r[:, 0:n], pp[:, 0:n], AF.Ln, bias=EPS)
                    k.act(r[:, 0:n], r[:, 0:n], AF.Exp, scale=-0.5)
                    k.stt("dve", dstn[:, s:s + n], dstn[:, s:s + n], (128.0 ** -0.5) if ty == 0 else 1.0, r[:, 0:n], ALU.mult, ALU.mult)
        if "qkv" in dbg_out and h == heads[0]:
            with ExitStack() as st2:
                old = k.stack
                k.stack = st2
                tf_ = k.sb("dbgqkv", [128, 3, 512], F32)
                k.copy("dve", tf_[:, 0, :], qT[:, 0:512])
                k.copy("dve", tf_[:, 1, :], kT[:, 0:512])
                k.copy("dve", tf_[:, 2, :], vT[:, 0:512])
                k.dma("sp", dbg_out["qkv"], tf_)
                k.barrier()
                k.stack = old
        for d in range(2):
            k.memset("pool", Sf[d], 0.0)
            k.memset("pool", Sb[d], 0.0)
        visited = set()
        groups = [(0, 2)] + [(2 + 3 * i, 3) for i in range(10)] + [(32, 2)]
        if dbg is not None and "nsteps" in dbg:
            groups = groups[:dbg["nsteps"]]
        gorder = [groups, [groups[0]] + groups[:0:-1]]

        def pre_gen(d, a, n, gp):
            col = d * 8 + h
            isx = a >= 2
            def bc(arr):
                return V(arr.ap[:, a:a + n, col].unsqueeze(2).to_broadcast([128, n, 128]), arr.buf)
            tl = lambda v: V(v.ap[:, a * 128:(a + n) * 128].rearrange("p (a b) -> p a b", b=128), v.buf)
            tsl = lambda i: slice((a + i) * 128, (a + i + 1) * 128)
            dgc, E, eg, Rk0, Q0, Q0T, tm_, Z = (x[d][0][:, 0:n, :] for x in (g_dgc, g_E, g_eg, g_Rk0, g_Q0, g_Q0T, g_tm, g_Z))
            LT, Dm, Gm = g_LT[d], g_D[d], g_G[d]
            W, nwT, Vt, ktail, qhT, qkm = (x[d][gp][:, 0:n, :] for x in (g_W, g_nwT, g_Vt, g_ktail, g_qhT, g_qkm))
            pA = nextb(d)
            pAk, pAv = b3(pA, n, BF16, 0), b3(pA, n, BF16, GS)
            for i in range(n):
                k.tr(pAk[:, i, :], kT[:, tsl(i)], ident)
                k.tr(pAv[:, i, :], vT[:, tsl(i)], ident)
            yield
            k.act(Vt, pAv, AF.Copy)
            k.tt("dve", Rk0, pAk, bc(egc), ALU.mult)
            k.tt("dve", ktail, pAk, bc(ekt), ALU.mult)
            yield
            E1, E2, tm2_ = g_E1[d][0][:, 0:n, :], g_E2[d][0][:, 0:n, :], g_tm2[d][0][:, 0:n, :]
            k.tt("pool", dgc, bcn(identf, n), bc(gc), ALU.mult)
            pB = nextb(d)
            pBv = b3(pB, n)
            for i in range(n):
                k.mm(pBv[:, i, :], onesf, dgc[:, i, :])
            yield
            k.tt("dve", E, pBv, bcn(negm[d], n), ALU.add)
            for i in range(n):
                k.act(E2[:, i, :], E[:, i, :], AF.Exp, bias=ngcb[:, a + i, col:col + 1])
            if isx:
                for i in range(n):
                    k.act(E1[:, i, :], E[:, i, :], AF.Exp, bias=ngc[:, a + i, col:col + 1])
                k.act(eg, pBv, AF.Exp)
                k.tt("pool", qhT, tl(qT), eg, ALU.mult)
            yield
            pC = nextb(d)
            pCv = b3(pC, n)
            for i in range(n):
                k.mm(pCv[:, i, :], kT[:, tsl(i)], kT[:, tsl(i)])
            k.tt("dve", Q0, pCv, E2, ALU.mult)
            if isx:
                pQ = nextb(d)
                pQv = b3(pQ, n)
                for i in range(n):
                    k.mm(pQv[:, i, :], kT[:, tsl(i)], qT[:, tsl(i)])
                k.tt("dve", qkm, pQv, E1, ALU.mult)
            yield
            pT = nextb(d)
            pTv = b3(pT, n, BF16, 0)
            for i in range(n):
                k.tr(pTv[:, i, :], Q0[:, i, :], ident)
            k.act(Q0T, pTv, AF.Copy)
            yield
            k.tt("dve", tm_, Q0, bcn(hmask[:, 0, :], n), ALU.mult)
            k.tt("dve", Dm[0][:, 0:n, :], bcn(ident, n), tm_, ALU.subtract)
            k.tt("pool", tm2_, Q0T, bcn(hmask[:, 0, :], n), ALU.mult)
            k.tt("pool", Gm[0][:, 0:n, :], bcn(ident, n), tm2_, ALU.subtract)
            k.tt("pool", LT[1][:, 0:n, :], Q0T, bcn(hmask[:, 1, :], n), ALU.mult)
            yield
            for m_ in range(1, 7):
                a_, b_ = (m_ - 1) % 2, m_ % 2
                Da, Ga, Lm = Dm[a_][:, 0:n, :], Gm[a_][:, 0:n, :], LT[m_ % 2][:, 0:n, :]
                pz = nextb(d)
                pzv = b3(pz, n)
                for i in range(n):
                    k.mm(pzv[:, i, :], Lm[:, i, :], Da[:, i, :])
                if m_ < 6:
                    k.tt("pool", LT[(m_ + 1) % 2][:, 0:n, :], Q0T, bcn(hmask[:, m_ + 1, :], n), ALU.mult)
                k.act(Z, pzv, AF.Copy)
                yield
                pd_ = nextb(d)
                pdv = b3(pd_, n)
                for i in range(n):
                    k.mm(pdv[:, i, :], Ga[:, i, :], Z[:, i, :])
                if m_ < 6:
                    pg_ = nextb(d)
                    pgv = b3(pg_, n)
                    for i in range(n):
                        k.mm(pgv[:, i, :], Z[:, i, :], Ga[:, i, :])
                k.tt("dve", W if m_ == 6 else Dm[b_][:, 0:n, :], Da, pdv, ALU.subtract)
                if m_ < 6:
                    k.tt("dve", Gm[b_][:, 0:n, :], Ga, pgv, ALU.subtract)
                yield
            pw = nextb(d)
            pwv = b3(pw, n)
            for i in range(n):
                k.mm(pwv[:, i, :], Rk0[:, i, :], W[:, i, :])
            k.act(nwT, pwv, AF.Copy, scale=-1.0)
            yield

        def state_gen(d, a, n, gp):
            col = d * 8 + h
            isx = a >= 2
            W, nwT, Vt, ktail, qhT, qkm = (x[d][gp] for x in (g_W, g_nwT, g_Vt, g_ktail, g_qhT, g_qkm))
            vn = t_vn[d]
            for i in (range(n) if d == 0 else range(n - 1, -1, -1)):
                t = a + i
                sc = lambda arr: arr[:, t, col:col + 1]
                k.mm(pvn[d], W[:, i, :], Vt[:, i, :], start=True, stop=False)
                k.mm(pvn[d], nwT[:, i, :], Sb[d], start=False, stop=True)
                k.act(vn, pvn[d], AF.Identity, scale=sc(beta))
                yield
                if isx:
                    xt = t - 2
                    ov = V(osum.ap[:, xt * 128:(xt + 1) * 128], osb[xt])
                    k.mm(poT[d], Sb[d], qhT[:, i, :], start=True, stop=False)
                    k.mm(poT[d], vn, qkm[:, i, :], start=False, stop=True)
                k.mm(pS[d], ktail[:, i, :], vn)
                if isx:
                    if xt not in visited:
                        visited.add(xt)
                        k.act(ov, poT[d], AF.Copy)
                    else:
                        k.tt("dve", ov, poT[d], ov, ALU.add)
                k.stt("dve", Sf[d], Sf[d], sc(egl), pS[d], ALU.mult, ALU.add)
                k.act(Sb[d], Sf[d], AF.Copy)
                yield

        def run_all(gens):
            gens = list(gens)
            while gens:
                for g_ in list(gens):
                    try:
                        next(g_)
                    except StopIteration:
                        gens.remove(g_)

        ng = len(groups)
        run_all([pre_gen(0, *gorder[0][0], 0), pre_gen(1, *gorder[1][0], 0)])
        for gi in range(ng):
            gens = [state_gen(0, *gorder[0][gi], gi % 2), state_gen(1, *gorder[1][gi], gi % 2)]
            if gi + 1 < ng:
                gens += [pre_gen(0, *gorder[0][gi + 1], (gi + 1) % 2), pre_gen(1, *gorder[1][gi + 1], (gi + 1) % 2)]
            run_all(gens)
        if "S" in dbg_out and h == heads[0]:
            k.dma("sp", dbg_out["S"][0], Sf[0])
            k.dma("sp", dbg_out["S"][1], Sf[1])
        for bi in range(8):
            s = bi * 512
            ovs = [V(osum.ap[:, s:s + 512], osb[bi * 4 + j]) for j in range(4)]
            class _M:
                pass
            ovall = V(osum.ap[:, s:s + 512], osb[bi * 4])
            extra = [osb[bi * 4 + j] for j in range(1, 4)]
            if bi == 0:
                wbz = load_w(OFF_Z + h * 128)
            pz_ = nextbig()
            zsb = zs_b[0]
            for kk in range(8):
                k.mm(pz_, wbz[:, kk, :], hTv[:, kk, TC + s:TC + s + 512], start=(kk == 0), stop=(kk == 7))
            k.act(zsb, pz_, AF.Silu)
            pp = nextbig()
            sqv = ofin[bi % 2]
            r = rn[0]
            of_ = ofin_f[0]
            k.op("pool", lambda g, sqv=sqv, s=s: g.tensor_tensor(out=sqv.ap, in0=osum.ap[:, s:s + 512], in1=osum.ap[:, s:s + 512], op=ALU.mult),
                 [osb[bi * 4 + j] for j in range(4)], [sqv.buf])
            k.mm(pp, onesb, sqv)
            k.act(r, pp, AF.Ln, bias=EPS, scale=1.0 / 128)
            k.act(r, r, AF.Exp, scale=-0.5)
            k.op("dve", lambda g, of_=of_, r=r, s=s: g.scalar_tensor_tensor(out=of_.ap, in0=osum.ap[:, s:s + 512], scalar=dnn.ap[:, 0:1], in1=r.ap,
                                                                           op0=ALU.mult, op1=ALU.mult),
                 [osb[bi * 4 + j] for j in range(4)] + [dnn.buf, r.buf], [of_.buf])
            if "o0" in dbg_out and h == heads[0]:
                k.tt("pool", of_, of_, zsb, ALU.mult)
                k.dma("sp", V(dbg_out["o0"].ap[:, s:s + 512], dbg_out["o0"].buf), of_)
                k.copy("pool", sqv, of_)
            else:
                k.tt("pool", sqv, of_, zsb, ALU.mult)
            k.dma("sp", V(oT_d.ap[h, :, s:s + 512], oT_d.buf), sqv)
    k.barrier()
    p4.close()
    s_g.close()
    k.stack = ExitStack()
    if stage <= 4:
        return finish(nc, k, out_d)

    def wchunk_loader(stf, stb):
        cnt = [0]
        def load(src_v, K):
            i = cnt[0] % len(stf)
            cnt[0] += 1
            k.dma("sp" if i == 0 else "pool", stf[i][:, 0:K, :], src_v)
            k.copy("pool", stb[i][:, 0:K, :], stf[i][:, 0:K, :])
            return stb[i]
        return load

    def load_resident(dst_bf, src_ap, src_buf, K, ncols, stg):
        i = 0
        for k0 in range(0, K, 8):
            kn = min(8, K - k0)
            for c0 in range(0, ncols, 512):
                cn = min(512, ncols - c0)
                st_ = stg[i % len(stg)]
                k.dma("sp" if i % 2 == 0 else "pool", st_[:, 0:kn, 0:cn], V(src_ap[:, k0:k0 + kn, c0:c0 + cn], src_buf))
                k.copy("pool" if i % 2 == 0 else "dve", dst_bf[:, k0:k0 + kn, c0:c0 + cn], st_[:, 0:kn, 0:cn])
                i += 1

    with ExitStack() as st:
        k.stack = st
        FCS = k.sb("FCS", [128, 32, 4, 256], BF16)
        fT = [k.sb(f"fT{i}", [128, 512], BF16) for i in range(2)]
        stf = [k.sb(f"p3stf{i}", [128, 8, 128], F32) for i in range(2)]
        stb = [k.sb(f"p3stb{i}", [128, 8, 128], BF16) for i in range(2)]
        ctab = [k.sb(f"ctab{i}", [128, 4, 512], BF16) for i in range(2)]
        stab = [k.sb(f"stab{i}", [128, 4, 512], BF16) for i in range(2)]
        yblk = [k.sb(f"yblk{i}", [128, 4, 512], BF16) for i in range(2)]
        lw = wchunk_loader(stf, stb)
        nb_ = 0
        for g in range(4):
            wb = lw(V(winv[:, :, OFF_F + g * 128:OFF_F + (g + 1) * 128], win_d.buf), 8)
            for bi, (s, n) in enumerate(xblocks):
                pp = k.banks[nb_ % 2]
                ft = fT[nb_ % 2]
                nb_ += 1
                for kk in range(8):
                    k.mm(pp, wb[:, kk, :], hTv[:, kk, s:s + n], start=(kk == 0), stop=(kk == 7))
                k.act(ft, pp, AF.Copy)
                pf = k.banks[2 + (nb_ % 2)]
                pfv = V(pf.ap.rearrange("p (a b) -> p a b", b=256), pf.buf)
                for j2 in range(2):
                    for jj in range(2):
                        j = j2 * 2 + jj
                        k.mm(pfv[:, jj, :], ft[:, j * 128:(j + 1) * 128], c128)
                    t0 = bi * 4 + j2 * 2
                    if j2 == 0:
                        k.act(FCS[:, t0:t0 + 2, g, :], pfv, AF.Copy)
                    else:
                        k.copy("dve", FCS[:, t0:t0 + 2, g, :], pfv)
        k.barrier()
        cosv = cos_d.ap.rearrange("(tt p) f -> p tt f", p=128)
        sinv = sin_d.ap.rearrange("(tt p) f -> p tt f", p=128)
        ld = 0
        for kb in range(8):
            for t4 in range(8):
                ct, st_ = ctab[ld % 2], stab[ld % 2]
                ld += 1
                k.dma("sp", ct, V(cosv[:, t4 * 4:(t4 + 1) * 4, kb * 512:(kb + 1) * 512], cos_d.buf))
                k.dma("pool", st_, V(sinv[:, t4 * 4:(t4 + 1) * 4, kb * 512:(kb + 1) * 512], sin_d.buf))
                for ti in range(4):
                    tt_ = t4 * 4 + ti
                    for g in range(4):
                        k.mm(k.banks[4 + g], FCS[:, tt_, g, 0:128], ct[:, ti, :], start=(tt_ == 0), stop=False)
                        k.mm(k.banks[4 + g], FCS[:, tt_, g, 128:256], st_[:, ti, :], start=False, stop=(tt_ == 31))
            yb = yblk[kb % 2]
            for g in range(4):
                if g % 2 == 0:
                    k.act(yb[:, g, :], k.banks[4 + g], AF.Copy)
                else:
                    k.copy("dve", yb[:, g, :], k.banks[4 + g])
            k.dma("sp", V(yT_d.ap[:, :, kb * 512:(kb + 1) * 512].rearrange("g p f -> p g f"), yT_d.buf), yb)
        if "fm" in dbg_out:
            pass
        k.barrier()
    k.stack = ExitStack()
    if stage <= 5:
        return finish(nc, k, out_d)

    with ExitStack() as st:
        k.stack = st
        wg = k.sb("wg", [128, 8, 2048], BF16)
        wf4 = k.sb("wf4", [128, 4, 1024], BF16)
        wdn = k.sb("wdn", [128, 8, 1024], BF16)
        stg = [k.sb(f"p5stg{i}", [128, 8, 512], F32) for i in range(1)]
        ytb = [k.sb(f"ytb{i}", [128, 4, 512], BF16) for i in range(2)]
        otb = [k.sb(f"otb{i}", [128, 8, 512], BF16) for i in range(2)]
        g0 = [k.sb(f"g0_{i}", [128, 512], BF16) for i in range(2)]
        g1 = [k.sb(f"g1_{i}", [128, 512], BF16) for i in range(2)]
        m0 = [k.sb(f"m0_{i}", [128, 512], BF16) for i in range(2)]
        m1 = [k.sb(f"m1_{i}", [128, 512], BF16) for i in range(2)]
        mixs = k.sb("mixs", [128, 2, 8, 512], BF16)
        mixs_buf = [Buf("mixs0"), Buf("mixs1")]
        load_resident(wg, winv[:, :, OFF_G:OFF_G + 2048], win_d.buf, 8, 2048, stg)
        load_resident(wf4, wf_d.ap.rearrange("(g p) d -> p g d", p=128), wf_d.buf, 4, 1024, stg)
        load_resident(wdn, wdn_d.ap.rearrange("(h p) d -> p h d", p=128), wdn_d.buf, 8, 1024, stg)
        it = 0
        for mt in range(8):
            s = TC + mt * 512
            yt, ot = ytb[mt % 2], otb[mt % 2]
            k.dma("sp", yt, V(yT_d.ap[:, :, mt * 512:(mt + 1) * 512].rearrange("g p f -> p g f"), yT_d.buf))
            k.dma("pool", ot, V(oT_d.ap[:, :, mt * 512:(mt + 1) * 512].rearrange("h p f -> p h f"), oT_d.buf))
            for dc in range(8):
                dsl = slice(dc * 128, (dc + 1) * 128)
                i2 = it % 2
                it += 1
                pb = [k.banks[4 * i2 + j] for j in range(4)]
                for g in range(4):
                    k.mm(pb[0], wf4[:, g, dsl], yt[:, g, :], start=(g == 0), stop=(g == 3))
                for kk in range(8):
                    k.mm(pb[1], wg[:, kk, dc * 128:(dc + 1) * 128], hTv[:, kk, s:s + 512], start=(kk == 0), stop=(kk == 7))
                for hh in range(8):
                    k.mm(pb[2], wdn[:, hh, dsl], ot[:, hh, :], start=(hh == 0), stop=(hh == 7))
                for kk in range(8):
                    k.mm(pb[3], wg[:, kk, 1024 + dc * 128:1024 + (dc + 1) * 128], hTv[:, kk, s:s + 512], start=(kk == 0), stop=(kk == 7))
                k.act(g0[i2], pb[1], AF.Sigmoid)
                k.act(g1[i2], pb[3], AF.Sigmoid)
                k.tt("dve", m0[i2], pb[0], g0[i2], ALU.mult)
                k.tt("dve", m1[i2], pb[2], g1[i2], ALU.mult)
                k.tt("pool", V(mixs.ap[:, mt % 2, dc, :], mixs_buf[mt % 2]), m0[i2], m1[i2], ALU.add)
            k.copy("pool", hTv[:, :, s:s + 512], V(mixs.ap[:, mt % 2, :, :], mixs_buf[mt % 2]))
        k.barrier()
    k.stack = ExitStack()
    if stage <= 6:
        return finish(nc, k, out_d)

    def branch_tail(mt, producer, cidx, resid_d, final, tb):
        yx, sq, rst, xin_, x1t = tb["yx"], tb["sq"], tb["rst"], tb["xin"], tb["x1t"]
        for dc in range(8):
            pb = k.banks[dc % 2]
            producer(dc, pb)
            k.act(yx[:, dc, :], pb, AF.Copy)
            k.act(sq[:, dc, :], pb, AF.Square)
        pss = k.banks[2]
        for dc in range(8):
            k.mm(pss, onesb, sq[:, dc, :], start=(dc == 0), stop=(dc == 7))
        k.act(rst, pss, AF.Ln, bias=EPS, scale=1.0 / D)
        k.act(rst, rst, AF.Exp, scale=-0.5)
        for dc in range(8):
            k.stt("dve", yx[:, dc, :], yx[:, dc, :], coef[:, cidx, dc:dc + 1], rst, ALU.mult, ALU.mult)
        for j in range(4):
            tok0 = mt * 512 + j * 128
            xi = xin_[j % len(xin_)]
            xo = x1t[j % len(x1t)]
            k.dma("sp" if j % 2 == 0 else "pool", xi, V(resid_d.ap[tok0:tok0 + 128, :], resid_d.buf))
            ba, bb = k.banks[3 + 2 * (j % 2)], k.banks[4 + 2 * (j % 2)]
            for dc in range(8):
                bk = ba if dc < 4 else bb
                k.tr(bk[:, (dc % 4) * 128:(dc % 4 + 1) * 128], yx[:, dc, j * 128:(j + 1) * 128], identf)
            k.tt("dve", xo[:, 0:512], ba, xi[:, 0:512], ALU.add)
            k.tt("dve", xo[:, 512:1024], bb, xi[:, 512:1024], ALU.add)
            if final:
                k.dma("sp", V(out_d.ap[tok0:tok0 + 128, :], out_d.buf), xo)
            else:
                k.dma("sp", V(x1_d.ap[tok0:tok0 + 128, :], x1_d.buf), xo)
                norm_tile(xo, 2 + mt * 4 + j, 5, 6, hT, hTv.buf, tb["tm"])

    def tail_bufs(nbuf):
        return {"yx": k.sb("yx", [128, 8, 512], F32), "sq": k.sb("sqb", [128, 8, 512], BF16), "rst": k.sb("rst", [128, 512], F32),
                "xin": [k.sb(f"rxin{i}", [128, D], F32) for i in range(nbuf)], "x1t": [k.sb(f"x1t{i}", [128, D], F32) for i in range(nbuf)],
                "tm": {"sq": [k.sb("t_sq", [128, D], F32)], "ss": [k.sb("t_ss", [128, 1], F32)], "xn": [k.sb("t_xn", [128, D], BF16)],
                       "pt": [k.pv(7, 0, 512, BF16, 128)]}}

    with ExitStack() as st:
        k.stack = st
        wout = k.sb("wout", [128, 8, 1024], BF16)
        stg = [k.sb(f"p5bstg{i}", [128, 8, 512], F32) for i in range(1)]
        load_resident(wout, wout_d.ap.rearrange("(c p) d -> p c d", p=128), wout_d.buf, 8, 1024, stg)
        tb = tail_bufs(2)
        mixl = k.sb("mixl", [128, 8, 512], BF16)
        for mt in range(8):
            s = TC + mt * 512
            k.copy("pool", mixl, hTv[:, :, s:s + 512])
            def prod(dc, pb):
                for c in range(8):
                    k.mm(pb, wout[:, c, dc * 128:(dc + 1) * 128], mixl[:, c, :], start=(c == 0), stop=(c == 7))
            branch_tail(mt, prod, 4, x_d, False, tb)
        k.barrier()
    k.stack = ExitStack()
    if stage <= 7:
        return finish(nc, k, out_d)

    wupv = wup_d.ap.rearrange("(k p) c -> p k c", p=128)
    with ExitStack() as st:
        k.stack = st
        stf = [k.sb(f"p6stf{i}", [128, 8, 128], F32) for i in range(2)]
        stb = [k.sb(f"p6stb{i}", [128, 8, 128], BF16) for i in range(2)]
        lw = wchunk_loader(stf, stb)
        apad = k.sb("apad", [128, 66, 66], BF16)
        dg9 = k.sb("dg9", [128, 9, 128], BF16)
        sa = [k.sb(f"sa{i}", [128, 512], BF16) for i in range(2)]
        gtc = [k.sb(f"gtc{i}", [128, T], BF16) for i in range(2)]
        k.memset("pool", apad, 0.0)
        nb_ = 0
        for c in range(NFF):
            wa = lw(V(wupv[:, :, c * 128:(c + 1) * 128], wup_d.buf), 8)
            wu = lw(V(wupv[:, :, DFF + c * 128:DFF + (c + 1) * 128], wup_d.buf), 8)
            for tap in range(9):
                k.ts("pool", dg9[:, tap, :], identf, cffn[:, c, tap:tap + 1])
            for bi in range(8):
                s = TC + bi * 512
                pp = k.banks[nb_ % 2]
                nb_ += 1
                for kk in range(8):
                    k.mm(pp, wa[:, kk, :], hTv[:, kk, s:s + 512], start=(kk == 0), stop=(kk == 7))
                k.act(apad[:, 1 + bi * 8:1 + bi * 8 + 8, 1:65], V(pp.ap.rearrange("p (r c) -> p r c", c=64), pp.buf), AF.Copy)
            gt = gtc[c % 2]
            for bi in range(8):
                s = TC + bi * 512
                pc = k.banks[2 + (bi % 2)]
                pu = k.banks[4 + (bi % 2)]
                pcv = V(pc.ap.rearrange("p (r c) -> p r c", c=64), pc.buf)
                for tap in range(9):
                    dr, dcc = tap // 3, tap % 3
                    k.mm(pcv, dg9[:, tap, :], apad[:, bi * 8 + dr:bi * 8 + dr + 8, dcc:dcc + 64], start=(tap == 0), stop=(tap == 8))
                k.act(sa[bi % 2], pc, AF.Silu)
                for kk in range(8):
                    k.mm(pu, wu[:, kk, :], hTv[:, kk, s:s + 512], start=(kk == 0), stop=(kk == 7))
                k.tt("dve", gt[:, bi * 512:(bi + 1) * 512], pu, sa[bi % 2], ALU.mult)
            k.dma("sp" if c % 2 == 0 else "pool", V(gT_d.ap[c], gT_d.buf), gt)
        k.barrier()
    k.stack = ExitStack()
    s_h.close()
    k.stack = ExitStack()
    if stage <= 8:
        return finish(nc, k, out_d)

    with ExitStack() as st:
        k.stack = st
        wdown = k.sb("wdown", [128, NFF, 1024], BF16)
        stg = [k.sb(f"p7stg{i}", [128, 8, 512], F32) for i in range(2)]
        load_resident(wdown, wdown_d.ap.rearrange("(c p) d -> p c d", p=128), wdown_d.buf, NFF, 1024, stg)
        gbl = [k.sb(f"gbl{i}", [128, NFF, 512], BF16) for i in range(2)]
        tb = tail_bufs(2)
        for mt in range(8):
            gb = gbl[mt % 2]
            k.dma("sp", gb[:, 0:11, :], V(gT_d.ap[0:11, :, mt * 512:(mt + 1) * 512].rearrange("c p f -> p c f"), gT_d.buf))
            k.dma("pool", gb[:, 11:NFF, :], V(gT_d.ap[11:NFF, :, mt * 512:(mt + 1) * 512].rearrange("c p f -> p c f"), gT_d.buf))
            def prod(dc, pb, gb=gb):
                for c in range(NFF):
                    k.mm(pb, wdown[:, c, dc * 128:(dc + 1) * 128], gb[:, c, :], start=(c == 0), stop=(c == NFF - 1))
            branch_tail(mt, prod, 7, x1_d, True, tb)
        k.barrier()
    k.stack = ExitStack()
    return finish(nc, k, out_d)


def finish(nc, k, out_d):
    k.barrier()
    return nc


def prep_inputs(inp, b):
    f = lambda a: np.ascontiguousarray(a, dtype=np.float32)
    colmajor = lambda v: f(np.asarray(v).reshape(-1, 128).T)
    m = {}
    m["x"] = f(inp["x"][b])
    m["ctx"] = f(inp["ctx"][b])
    m["cc"] = f(np.stack([colmajor(inp["c"][b]), colmajor(inp["c_ctx"])], axis=-1))
    m["w_ada"] = f(inp["w_ada"][0])
    m["b_ada"] = colmajor(inp["b_ada"][0])
    m["norms"] = f(np.stack([colmajor(inp[n][0]) for n in ("norm_pre_mix", "norm_post_mix", "norm_pre_ffn", "norm_post_ffn")], axis=1))
    m["w_in"] = f(inp["w_in"][0])
    cq = np.asarray(inp["conv_qkv"][0])
    m["conv_qkv"] = f(cq.T.reshape(24, 128, 3).transpose(1, 0, 2))
    gp = np.stack([np.asarray(inp["a_log"][0]).reshape(16), np.asarray(inp["dt_bias"][0]).reshape(16)], 0)
    m["gpar"] = f(np.broadcast_to(gp[None], (128, 2, 16)))
    m["dn_norm"] = f(np.asarray(inp["dn_norm"][0]).reshape(128, 1))
    m["w_fourier"] = f(inp["w_fourier"][0])
    m["w_dn"] = f(inp["w_dn"][0])
    m["w_out"] = f(inp["w_out"][0])
    m["w_up"] = f(inp["w_up"][0])
    cf = np.asarray(inp["conv_ffn"][0]).reshape(9, DFF)
    m["conv_ffn"] = f(cf.T.reshape(NFF, 128, 9).transpose(1, 0, 2))
    m["w_down"] = f(inp["w_down"][0])
    return m


_CONST = {}


def consts():
    if not _CONST:
        idx = np.arange(T, dtype=np.int64)
        ang = (2.0 * np.pi / T) * ((idx[:, None] * idx[None, :]) % T).astype(np.float64)
        s = 1.0 / np.sqrt(float(T) * 128.0)
        _CONST["dft_cos"] = (np.cos(ang) * s).astype(ml_dtypes.bfloat16)
        _CONST["dft_sin"] = (np.sin(ang) * s).astype(ml_dtypes.bfloat16)
        i8 = np.arange(128, dtype=np.int64)
        a8 = (2.0 * np.pi / 128) * ((i8[:, None] * i8[None, :]) % 128).astype(np.float64)
        j8 = np.arange(128)
        hm = []
        for m_ in range(7):
            s_ = 2 ** m_
            blk2 = (j8[:, None] // (2 * s_)) == (j8[None, :] // (2 * s_))
            half = (j8[:, None] // s_) != (j8[None, :] // s_)
            hm.append((blk2 & half).astype(np.float32))
        _CONST["hmask"] = np.stack(hm, axis=1).astype(ml_dtypes.bfloat16)
        _CONST["dft128"] = np.concatenate([np.cos(a8), -np.sin(a8)], axis=1).astype(ml_dtypes.bfloat16)
    return _CONST


def kernel(**inputs):
    inp = {k_: np.asarray(v) for k_, v in inputs.items()}
    nc = build()
    cst = consts()
    in_maps = []
    for b in range(8):
        m = prep_inputs(inp, b)
        m.update(cst)
        in_maps.append(m)
    res = run_bass_kernel_spmd(nc, in_maps, core_ids=list(range(8)))
    return np.stack([np.asarray(r["out"], dtype=np.float32) for r in res.results], axis=0)
```

```python
import os
from contextlib import ExitStack
import numpy as np
import ml_dtypes
import concourse.bass as bass
import concourse.mybir as mybir
from concourse.bass_utils import run_bass_kernel_spmd

F32 = mybir.dt.float32
BF16 = mybir.dt.bfloat16
AF = mybir.ActivationFunctionType
ALU = mybir.AluOpType

D = 1024
T = 4096
TC = 256
TA = TC + T
NT = TA // 128
H = 8
OFF_F, OFF_Q, OFF_K, OFF_V, OFF_Z, OFF_B, OFF_A, OFF_G = 0, 512, 1536, 2560, 3584, 4608, 4624, 4640
INW = 6688
DFF = 2816
NFF = DFF // 128
EPS = 1e-6


class Buf:
    __slots__ = ("name", "last_w", "readers", "dsem", "dcount", "excl")

    def __init__(self, name, excl=False):
        self.name = name
        self.excl = excl
        self.last_w = None
        self.readers = []
        self.dsem = None
        self.dcount = 0


class V:
    __slots__ = ("ap", "buf")

    def __init__(self, ap, buf):
        self.ap = ap
        self.buf = buf

    def __getitem__(self, idx):
        return V(self.ap[idx], self.buf)

    def sub(self, idx, buf):
        return V(self.ap[idx], buf)


def _bufs(*vs):
    out = []
    for v in vs:
        if isinstance(v, V) and v.buf is not None and v.buf not in out:
            out.append(v.buf)
    return out


def _ap(v):
    return v.ap if isinstance(v, V) else v


class K:
    def __init__(self, nc):
        self.nc = nc
        self.engs = {"pe": nc.tensor, "act": nc.scalar, "dve": nc.vector, "pool": nc.gpsimd, "sp": nc.sync}
        self.sem = {n: nc.alloc_semaphore(f"s_{n}") for n in self.engs}
        self.cnt = {n: 0 for n in self.engs}
        self.known = {n: {} for n in self.engs}
        self.dsems = []
        self.nins = 0
        self.nwaits = 0
        self.stack = ExitStack()
        self.limit = None
        self.log = []
        self.sched = os.environ.get('KSCHED', '1') == '1'
        self.pending = []

    def _uid(self):
        self.uid = getattr(self, 'uid', 0) + 1
        return self.uid

    def sb(self, name, shape, dt, nbuf=None):
        t = self.stack.enter_context(self.nc.sbuf_tensor(f"sb{self._uid()}_" + name, list(shape), dt))
        return V(t[:] if hasattr(t, "__getitem__") else t.ap(), Buf(name) if nbuf is None else nbuf)

    def init_banks(self):
        self.banks = []
        for i in range(8):
            t = self.nc.psum_tensor(f"ps_bank{i}", [128, 512], F32).__enter__()
            self.banks.append(V(t[:], Buf(f"bank{i}", excl=True)))

    def pv(self, bank, lo, hi, dt=F32, inner=None):
        b = self.banks[bank]
        ap = b.ap[:, lo:hi]
        if dt != F32:
            ap = ap.bitcast(dt)
        if inner is not None:
            ap = ap.rearrange("p (a b) -> p a b", b=inner)
        return V(ap, b.buf)

    def _wait(self, e, ev):
        sem, val, src = ev
        if src == "pe" and e == "pe":
            return
        kn = self.known[e]
        if kn.get(sem.num, 0) >= val:
            return
        kn[sem.num] = val
        self.engs[e].wait_ge(sem, val)
        self.nwaits += 1

    def _deps(self, e, reads, writes):
        best = {}
        def add(ev):
            s = ev[0].num
            if s not in best or best[s][1] < ev[1]:
                best[s] = ev
        for b in reads:
            if b.last_w is not None:
                add(b.last_w)
        for b in writes:
            if b.last_w is not None:
                add(b.last_w)
            for ev in b.readers:
                add(ev)
        for ev in best.values():
            self._wait(e, ev)

    def _record(self, ev, reads, writes):
        for b in reads:
            if b in writes:
                continue
            b.readers.append(ev)
            if len(b.readers) > 10:
                best = {}
                for x in b.readers:
                    s = x[0].num
                    if s not in best or best[s][1] < x[1]:
                        best[s] = x
                b.readers = list(best.values())
        for b in writes:
            b.last_w = ev
            b.readers = []

    def op(self, e, fn, reads, writes, cost=300.0):
        if self.sched:
            self.pending.append(("op", e, fn, list(reads), list(writes), float(cost)))
            return
        self._emit_op(e, fn, reads, writes)

    def _emit_op(self, e, fn, reads, writes):
        if self.limit is not None and self.nins >= self.limit:
            return
        ex = [b for b in reads if b.excl and b not in writes]
        if ex:
            writes = list(writes) + ex
        self._deps(e, reads, writes)
        ins = fn(self.engs[e])
        if os.environ.get('PRINS') and self.nins in range(int(os.environ.get('PRINS','0')), int(os.environ.get('PRINS','0')) + 4):
            print('INS', self.nins, ins.concise())
        self.cnt[e] += 1
        ins.then_inc(self.sem[e], 1)
        self._record((self.sem[e], self.cnt[e], e), reads, writes)
        self.nins += 1

    def dma(self, q, out, in_, key=None, nbytes=None, **kw):
        if self.sched:
            if nbytes is None:
                shp = _ap(out).shape
                nbytes = 4
                for d_ in shp:
                    nbytes *= d_
            self.pending.append(("dma", q, (out, in_, key, kw), _bufs(in_), _bufs(out), 2000.0 + nbytes / 100.0))
            return
        self._emit_dma(q, out, in_, key, **kw)

    def _emit_dma(self, q, out, in_, key=None, **kw):
        if self.limit is not None and self.nins >= self.limit:
            return
        reads, writes = _bufs(in_), _bufs(out)
        self._deps(q, reads, writes)
        kb = key.buf if key is not None else (out.buf if not isinstance(out.buf, DBuf) else in_.buf)
        if kb.dsem is None:
            kb.dsem = self.nc.alloc_semaphore(f"d{self._uid()}_{kb.name}")
            self.dsems.append(kb)
        ins = self.engs[q].dma_start(out=_ap(out), in_=_ap(in_), **kw)
        kb.dcount += 1
        ins.then_inc(kb.dsem, 16)
        self._record((kb.dsem, 16 * kb.dcount, "dma"), reads, writes)
        self.nins += 1

    def flush(self):
        ops = self.pending
        self.pending = []
        n = len(ops)
        if n == 0:
            return
        SYNC = 200.0
        preds = [[] for _ in range(n)]
        lastw = {}
        rdrs = {}
        for i, (kind, e, fn, reads, writes, cost) in enumerate(ops):
            wr = list(writes) + [b for b in reads if b.excl and b not in writes]
            ps = set()
            for b in reads:
                if b in lastw:
                    ps.add(lastw[b])
            for b in wr:
                if b in lastw:
                    ps.add(lastw[b])
                for r_ in rdrs.get(b, ()):
                    ps.add(r_)
            ps.discard(i)
            preds[i] = list(ps)
            for b in reads:
                if b not in wr:
                    rdrs.setdefault(b, []).append(i)
            for b in wr:
                lastw[b] = i
                rdrs[b] = []
        succs = [[] for _ in range(n)]
        for i in range(n):
            for p in preds[i]:
                succs[p].append(i)
        occ = [0.0] * n
        lat = [0.0] * n
        for i, (kind, e, fn, reads, writes, cost) in enumerate(ops):
            if kind == "dma":
                occ[i] = 60.0
                lat[i] = cost
            else:
                occ[i] = cost
                lat[i] = cost
        blevel = [0.0] * n
        for i in range(n - 1, -1, -1):
            m_ = 0.0
            for s_ in succs[i]:
                if blevel[s_] > m_:
                    m_ = blevel[s_]
            blevel[i] = lat[i] + m_
        import heapq
        npred = [len(p) for p in preds]
        ready_t = [0.0] * n
        eng_free = {}
        readyq = {}
        for i in range(n):
            if npred[i] == 0:
                heapq.heappush(readyq.setdefault(ops[i][1], []), (-blevel[i], i))
        order = []
        done = 0
        while done < n:
            best = None
            for e, hq in readyq.items():
                if not hq:
                    continue
                tfree = eng_free.get(e, 0.0)
                cand = None
                top = heapq.nsmallest(6, hq)
                for pr, i in top:
                    st_ = max(tfree, ready_t[i])
                    key = (st_, pr)
                    if cand is None or key < cand[0]:
                        cand = (key, i)
                if best is None or cand[0] < best[0]:
                    best = (cand[0], cand[1], e)
            (st_, pr), i, e = best
            hq = readyq[e]
            hq.remove((-blevel[i], i))
            heapq.heapify(hq)
            eng_free[e] = st_ + occ[i]
            fin = st_ + lat[i]
            order.append(i)
            done += 1
            for s_ in succs[i]:
                rt = fin + (0.0 if ops[s_][1] == e else SYNC)
                if rt > ready_t[s_]:
                    ready_t[s_] = rt
                npred[s_] -= 1
                if npred[s_] == 0:
                    heapq.heappush(readyq.setdefault(ops[s_][1], []), (-blevel[s_], s_))
        for i in order:
            kind, e, fn, reads, writes, cost = ops[i]
            if kind == "dma":
                out, in_, key, kw = fn
                self._emit_dma(e, out, in_, key, **kw)
            else:
                self._emit_op(e, fn, reads, writes)

    def barrier(self):
        self.flush()
        for e in self.engs:
            for f in self.engs:
                if f != e and self.cnt[f] > 0:
                    self._wait(e, (self.sem[f], self.cnt[f], f))
            for kb in self.dsems:
                if kb.dcount > 0:
                    self._wait(e, (kb.dsem, 16 * kb.dcount, "dma"))

    @staticmethod
    def _fsz(v):
        shp = _ap(v).shape
        n = 1
        for d_ in shp[1:]:
            n *= d_
        return n

    def _ecost(self, e, out, in_):
        n = self._fsz(out)
        if e == "pool":
            return 150.0 + 2.0 * n
        if e == "act":
            return 220.0 + 0.72 * n
        return 100.0 + (1.05 * n if (_ap(in_).dtype == F32 or _ap(out).dtype == F32) else 0.6 * n)

    def mm(self, out, lhsT, rhs, start=True, stop=True):
        n = self._fsz(out)
        c = 40.0 + n * (1.9 if _ap(lhsT).dtype == F32 else 0.45)
        self.op("pe", lambda e: e.matmul(_ap(out), lhsT=_ap(lhsT), rhs=_ap(rhs), start=start, stop=stop),
                _bufs(lhsT, rhs) + ([] if start else _bufs(out)), _bufs(out), cost=c)

    def tr(self, out, in_, ident):
        n = self._fsz(out)
        c = 40.0 + n * (1.9 if _ap(in_).dtype == F32 else 0.45)
        self.op("pe", lambda e: e.transpose(out=_ap(out), in_=_ap(in_), identity=_ap(ident)), _bufs(in_, ident), _bufs(out), cost=c)

    def act(self, out, in_, func, bias=0.0, scale=1.0, accum=None, eng="act"):
        kw = {}
        if accum is not None:
            kw["accum_out"] = _ap(accum)
        self.op("act", lambda e: e.activation(out=_ap(out), in_=_ap(in_), func=func, bias=_ap(bias), scale=_ap(scale), **kw),
                _bufs(in_, bias, scale), _bufs(out, accum), cost=self._ecost("act", out, in_))

    def ts(self, e, out, in0, s1, s2=None, op0=ALU.mult, op1=None):
        if op1 is None:
            f = lambda g: g.tensor_scalar(out=_ap(out), in0=_ap(in0), scalar1=_ap(s1), scalar2=None, op0=op0)
        else:
            f = lambda g: g.tensor_scalar(out=_ap(out), in0=_ap(in0), scalar1=_ap(s1), scalar2=_ap(s2), op0=op0, op1=op1)
        self.op(e, f, _bufs(in0, s1, s2), _bufs(out), cost=self._ecost(e, out, in0))

    def tt(self, e, out, a, b, op):
        self.op(e, lambda g: g.tensor_tensor(out=_ap(out), in0=_ap(a), in1=_ap(b), op=op), _bufs(a, b), _bufs(out), cost=self._ecost(e, out, a))

    def stt(self, e, out, in0, scalar, in1, op0, op1):
        self.op(e, lambda g: g.scalar_tensor_tensor(out=_ap(out), in0=_ap(in0), scalar=_ap(scalar), in1=_ap(in1), op0=op0, op1=op1),
                _bufs(in0, scalar, in1), _bufs(out), cost=self._ecost(e, out, in0))

    def copy(self, e, out, in_):
        if e == "act":
            self.act(out, in_, AF.Copy)
        else:
            self.op(e, lambda g: g.tensor_copy(out=_ap(out), in_=_ap(in_)), _bufs(in_), _bufs(out), cost=self._ecost(e, out, in_))

    def recip(self, out, in_):
        self.op("dve", lambda g: g.reciprocal(out=_ap(out), in_=_ap(in_)), _bufs(in_), _bufs(out), cost=100.0 + 6.3 * self._fsz(out))

    def memset(self, e, out, val):
        self.op(e, lambda g: g.memset(_ap(out), val), [], _bufs(out))

    def asel(self, out, in_, pattern, cmp, fill, base, cm):
        self.op("pool", lambda g: g.affine_select(out=_ap(out), in_=_ap(in_), pattern=pattern, compare_op=cmp, fill=fill,
                                                  base=base, channel_multiplier=cm), _bufs(in_), _bufs(out))


class DBuf(Buf):
    __slots__ = ("is_dram",)

    def __init__(self, name):
        super().__init__(name)
        self.is_dram = True


def dramv(nc, name, shape, dt, kind):
    t = nc.dram_tensor(name, list(shape), dt, kind=kind)
    return V(t.ap(), DBuf(name))


def build(stage=99, dbg=None):
    nc = bass.Bass("TRN2", target_bir_lowering=False)
    k = K(nc)
    k.init_banks()
    if dbg and 'limit' in dbg:
        k.limit = dbg['limit']
    IN = lambda name, shape, dt=F32: dramv(nc, name, shape, dt, "ExternalInput")
    x_d = IN("x", [T, D])
    ctx_d = IN("ctx", [TC, D])
    cc_d = IN("cc", [128, 8, 2])
    wada_d = IN("w_ada", [D, 6 * D])
    bada_d = IN("b_ada", [128, 48])
    nrm_d = IN("norms", [128, 4, 8])
    win_d = IN("w_in", [D, INW])
    cqkv_d = IN("conv_qkv", [128, 24, 3])
    gpar_d = IN("gpar", [128, 2, 16])
    dnn_d = IN("dn_norm", [128, 1])
    wf_d = IN("w_fourier", [512, D])
    wdn_d = IN("w_dn", [D, D])
    wout_d = IN("w_out", [D, D])
    wup_d = IN("w_up", [D, 2 * DFF])
    cffn_d = IN("conv_ffn", [128, NFF, 9])
    wdown_d = IN("w_down", [DFF, D])
    cos_d = IN("dft_cos", [T, T], BF16)
    sin_d = IN("dft_sin", [T, T], BF16)
    c128_d = IN("dft128", [128, 256], BF16)
    hmask_d = IN("hmask", [128, 7, 128], BF16)
    out_d = dramv(nc, "out", [T, D], F32, "ExternalOutput")
    dbg_out = {}
    if dbg:
        for nm, shp in dbg.items():
            if nm in ("heads", "nsteps", "limit"):
                continue
            dbg_out[nm] = dramv(nc, "dbg_" + nm, shp, F32, "ExternalOutput")
    x1_d = dramv(nc, "x1_scr", [T, D], F32, "Internal")
    oT_d = dramv(nc, "oT_scr", [H, 128, T], BF16, "Internal")
    gT_d = dramv(nc, "gT_scr", [NFF, 128, T], BF16, "Internal")
    yT_d = dramv(nc, "yT_scr", [4, 128, T], BF16, "Internal")

    identf = k.sb("identf", [128, 128], F32)
    ident = k.sb("ident", [128, 128], BF16)
    onesf = k.sb("onesf", [128, 128], F32)
    onesb = k.sb("onesb", [128, 128], BF16)
    negm = [k.sb(f"negm{d}", [128, 128], F32) for d in range(2)]
    smask = [k.sb(f"smask{d}", [128, 128], F32) for d in range(2)]
    ut = [k.sb(f"ut{d}", [128, 128], F32) for d in range(2)]
    zerof = k.sb("zerof", [128, 128], F32)
    scal_t = k.sb("scal_t", [128, 8], F32)
    k.memset("pool", zerof, 0.0)
    k.memset("pool", onesf, 1.0)
    k.copy("dve", onesb, onesf)
    k.asel(identf, zerof, [[-1, 128]], ALU.not_equal, 1.0, 0, 1)
    k.copy("dve", ident, identf)
    k.asel(negm[0], zerof, [[1, 128]], ALU.is_ge, -1e5, 0, -1)
    k.asel(smask[0], onesf, [[1, 128]], ALU.is_gt, 0.0, 0, -1)
    k.asel(ut[0], onesf, [[1, 128]], ALU.is_ge, 0.0, 0, -1)
    k.asel(negm[1], zerof, [[-1, 128]], ALU.is_ge, -1e5, 0, 1)
    k.asel(smask[1], onesf, [[-1, 128]], ALU.is_gt, 0.0, 0, 1)
    k.asel(ut[1], onesf, [[-1, 128]], ALU.is_ge, 0.0, 0, 1)

    nrm = k.sb("nrm", [128, 4, 8], F32)
    k.dma("sp", nrm, nrm_d)
    cqkv = k.sb("cqkv", [128, 24, 3], F32)
    k.dma("sp", cqkv, cqkv_d)
    gpar = k.sb("gpar", [128, 2, 16], F32)
    k.dma("sp", gpar, gpar_d)
    dnn = k.sb("dnn", [128, 1], F32)
    k.dma("sp", dnn, dnn_d)
    cffn = k.sb("cffn", [128, NFF, 9], F32)
    k.dma("sp", cffn, cffn_d)
    c128 = k.sb("c128", [128, 256], BF16)
    k.dma("sp", c128, c128_d)
    hmask = k.sb("hmask", [128, 7, 128], BF16)
    k.dma("sp", hmask, hmask_d)
    bada = k.sb("bada", [128, 48], F32)
    k.dma("sp", bada, bada_d)
    cc = k.sb("cc", [128, 8, 2], F32)
    k.dma("sp", cc, cc_d)

    mod = k.sb("mod", [128, 48, 2], F32)
    scc = k.sb("scc", [128, 8, 2], F32)
    k.act(scc, cc, AF.Silu)
    with ExitStack() as st:
        k.stack = st
        wa = [k.sb(f"wa{i}", [128, 8, 512], F32) for i in range(2)]
        pm = k.pv(0, 0, 96, F32, 2)
        wv = wada_d.ap.rearrange("(k p) c -> p k c", p=128)
        for blk in range(12):
            w = wa[blk % 2]
            k.dma("sp" if blk % 2 == 0 else "pool", w, V(wv[:, :, blk * 512:(blk + 1) * 512], wada_d.buf))
            for oc in range(4):
                for kk in range(8):
                    k.mm(pm[:, blk * 4 + oc, :], w[:, kk, oc * 128:(oc + 1) * 128], scc[:, kk, :], start=(kk == 0), stop=(kk == 7))
        for j in range(2):
            k.tt("dve", mod[:, :, j], pm[:, :, j], bada, ALU.add)
        k.barrier()
    k.stack = ExitStack()
    coef = k.sb("coef", [128, 8, 8], F32)
    def modc(i, j):
        return mod[:, i * 8:(i + 1) * 8, j]
    k.stt("dve", coef[:, 0, :], modc(1, 0), 1.0, nrm[:, 0, :], ALU.add, ALU.mult)
    k.copy("dve", coef[:, 1, :], modc(0, 0))
    k.stt("dve", coef[:, 2, :], modc(1, 1), 1.0, nrm[:, 0, :], ALU.add, ALU.mult)
    k.copy("dve", coef[:, 3, :], modc(0, 1))
    k.tt("dve", coef[:, 4, :], modc(2, 0), nrm[:, 1, :], ALU.mult)
    k.stt("dve", coef[:, 5, :], modc(4, 0), 1.0, nrm[:, 2, :], ALU.add, ALU.mult)
    k.copy("dve", coef[:, 6, :], modc(3, 0))
    k.tt("dve", coef[:, 7, :], modc(5, 0), nrm[:, 3, :], ALU.mult)
    if "coef" in dbg_out:
        k.dma("sp", dbg_out["coef"], coef)
    if stage <= 0:
        return finish(nc, k, out_d)

    s_h = ExitStack()
    k.stack = s_h
    hT = k.sb("hT", [128, 8, TA], BF16)
    hbuf = [Buf(f"hT{t}") for t in range(NT)]

    def norm_tile(src_tile_v, tile_idx, ca, cb, dst, dstbufs, tm):
        nb = len(tm["sq"])
        sq = tm["sq"][tile_idx % nb]
        ss = tm["ss"][tile_idx % nb]
        xn = tm["xn"][tile_idx % nb]
        pt = tm["pt"][tile_idx % len(tm["pt"])]
        k.act(sq, src_tile_v, AF.Square, accum=ss)
        k.act(ss, ss, AF.Sqrt, bias=EPS, scale=1.0 / D)
        k.recip(ss, ss)
        k.ts("dve", xn, src_tile_v, ss[:, 0:1])
        for c in range(8):
            k.tr(pt[:, c, :], xn[:, c * 128:(c + 1) * 128], ident)
        for c in range(8):
            dv = V(dst.ap[:, c, tile_idx * 128:(tile_idx + 1) * 128], dstbufs[tile_idx] if isinstance(dstbufs, list) else dstbufs)
            if c % 2 == 0:
                k.ts("dve", dv, pt[:, c, :], coef[:, ca, c:c + 1], coef[:, cb, c:c + 1], ALU.mult, ALU.add)
            else:
                k.act(dv, pt[:, c, :], AF.Identity, bias=coef[:, cb, c:c + 1], scale=coef[:, ca, c:c + 1])

    p1 = ExitStack()
    k.stack = p1
    nt_sq = [k.sb(f"nt_sq{i}", [128, D], F32) for i in range(2)]
    nt_ss = [k.sb(f"nt_ss{i}", [128, 1], F32) for i in range(2)]
    nt_xn = [k.sb(f"nt_xn{i}", [128, D], BF16) for i in range(2)]
    xin = [k.sb(f"xin{i}", [128, D], F32) for i in range(3)]
    nt_pt = [k.pv(i, 0, 512, BF16, 128) for i in range(2)]
    tm1 = {"sq": nt_sq, "ss": nt_ss, "xn": nt_xn, "pt": nt_pt}
    for t in range(NT):
        xi = xin[t % 3]
        if t < 2:
            src = V(ctx_d.ap[t * 128:(t + 1) * 128, :], ctx_d.buf)
        else:
            src = V(x_d.ap[(t - 2) * 128:(t - 1) * 128, :], x_d.buf)
        k.dma("sp" if t % 2 == 0 else "pool", xi, src)
        norm_tile(xi, t, 2 if t < 2 else 0, 3 if t < 2 else 1, hT, hbuf, tm1)
    k.barrier()
    p1.close()
    k.stack = ExitStack()
    if "hT" in dbg_out:
        with ExitStack() as st:
            k.stack = st
            tmpf = k.sb("dbg_hT", [128, 8, 1024], F32)
            k.copy("dve", tmpf, V(hT.ap[:, :, 0:1024], None))
            k.dma("sp", dbg_out["hT"], tmpf)
            k.barrier()
        k.stack = ExitStack()
    hTv = V(hT.ap, Buf("hT_all"))
    if stage <= 1:
        return finish(nc, k, out_d)

    winv = win_d.ap.rearrange("(k p) c -> p k c", p=128)

    def bcast_t(v, n):
        return V(v.ap.unsqueeze(1).to_broadcast([128, n, 16]), v.buf)

    s_g = ExitStack()
    k.stack = s_g
    beta = k.sb("beta", [128, NT, 16], F32)
    ngc = k.sb("ngc", [128, NT, 16], F32)
    ngcb = k.sb("ngcb", [128, NT, 16], F32)
    gc = k.sb("gc", [128, NT, 16], F32)
    egc = k.sb("egc", [128, NT, 16], F32)
    ekt = k.sb("ekt", [128, NT, 16], F32)
    egl = k.sb("egl", [128, NT, 16], F32)
    with ExitStack() as st:
        k.stack = st
        wbaf = k.sb("wbaf", [128, 8, 32], F32)
        wba = k.sb("wba", [128, 8, 32], BF16)
        graw = k.sb("graw", [128, NT, 32], F32)
        gg = k.sb("gg", [128, NT, 16], F32)
        gtmp = k.sb("gtmp", [128, NT, 16], F32)
        lnb = k.sb("lnb", [128, NT, 16], F32)
        negA = k.sb("negA", [128, 16], F32)
        pg = k.pv(0, 0, 512, F32, 32)
        pc0, pc1, pt0, pt1 = k.banks[1], k.banks[2], k.banks[3], k.banks[4]
        k.dma("sp", wbaf, V(winv[:, :, OFF_B:OFF_B + 32], win_d.buf))
        k.copy("dve", wba, wbaf)
        for g0 in range(0, NT, 16):
            n = min(16, NT - g0)
            for j in range(n):
                t = g0 + j
                for kk in range(8):
                    k.mm(pg[:, j, :], hTv[:, kk, t * 128:(t + 1) * 128], wba[:, kk, :], start=(kk == 0), stop=(kk == 7))
            k.copy("act", graw[:, g0:g0 + n, :], pg[:, 0:n, :])
        k.act(beta, graw[:, :, 0:16], AF.Sigmoid)
        k.act(lnb, graw[:, :, 0:16], AF.Exp, scale=-1.0)
        k.act(lnb, lnb, AF.Ln, bias=1.0)
        k.act(negA, gpar[:, 0, :], AF.Exp)
        k.ts("dve", negA, negA, -1.0)
        k.tt("dve", gg, graw[:, :, 16:32], bcast_t(gpar[:, 1, :], NT), ALU.add)
        k.act(gg, gg, AF.Exp)
        k.act(gg, gg, AF.Ln, bias=1.0)
        k.tt("dve", gg, gg, bcast_t(negA, NT), ALU.mult)
        if "g" in dbg_out:
            k.dma("sp", dbg_out["g"], gg)
            k.dma("sp", dbg_out["beta"], beta)
        pcs = [pc0, pc1]
        pts = [pt0, pt1]
        for d in range(2):
            pcv = V(pcs[d].ap[:, 0:NT * 8].rearrange("p (t c) -> p t c", c=8), pcs[d].buf)
            ptv = V(pts[d].ap[:, 0:NT * 8].rearrange("p (t c) -> p t c", c=8), pts[d].buf)
            k.mm(pcv, ut[d], gg[:, :, d * 8:(d + 1) * 8])
            k.mm(ptv, onesf, gg[:, :, d * 8:(d + 1) * 8])
            sl = slice(d * 8, (d + 1) * 8)
            k.copy("act", gc[:, :, sl], pcv)
            k.act(ngc[:, :, sl], pcv, AF.Copy, scale=-1.0)
            k.stt("dve", ngcb[:, :, sl], pcv, -1.0, lnb[:, :, sl], ALU.mult, ALU.subtract)
            k.act(egc[:, :, sl], pcv, AF.Exp)
            k.tt("dve", gtmp[:, :, sl], ptv, gc[:, :, sl], ALU.subtract)
            k.act(ekt[:, :, sl], gtmp[:, :, sl], AF.Exp)
            k.act(egl[:, :, sl], ptv, AF.Exp)
        k.barrier()
    k.stack = ExitStack()
    if stage <= 2:
        return finish(nc, k, out_d)

    blocks = [(0, TC)] + [(TC + 512 * i, 512) for i in range(8)]
    xblocks = blocks[1:]
    poff = lambda tok: 1 + tok if tok < TC else 3 + tok
    heads = list(range(H)) if dbg is None or "heads" not in dbg else dbg["heads"]
    p4 = ExitStack()
    k.stack = p4
    praw = k.sb("praw", [128, TA + 4], BF16)
    qT = k.sb("qT", [128, TA], BF16)
    kT = k.sb("kT", [128, TA], BF16)
    vT = k.sb("vT", [128, TA], BF16)
    zs_b = [k.sb(f"zsb{i}", [128, 512], BF16) for i in range(1)]
    osum = k.sb("osum", [128, T], F32)
    wst_f = [k.sb(f"wstf{i}", [128, 8, 128], F32) for i in range(1)]
    wst_b = [k.sb(f"wstb{i}", [128, 8, 128], BF16) for i in range(1)]
    dgt = k.sb("dgt", [128, 3, 128], BF16)
    rn = [k.sb(f"rn{i}", [128, 512], F32) for i in range(1)]
    ofin_f = [k.sb(f"ofinf{i}", [128, 512], F32) for i in range(1)]
    ofin = [k.sb(f"ofin{i}", [128, 512], BF16) for i in range(2)]
    osb = [Buf(f"osum{t}") for t in range(32)]
    pbig = [k.banks[0], k.banks[1]]
    GS = 3
    rot = [[k.banks[3 * d + i] for i in range(3)] for d in range(2)]
    rcnt = [0, 0]
    def nextb(d):
        rcnt[d] += 1
        return rot[d][rcnt[d] % 3]
    def b3(bank, n, dt=F32, off=0):
        if dt == F32:
            ap = bank.ap[:, off * 128:(off + n) * 128].rearrange("p (a b) -> p a b", b=128)
        else:
            ap = bank.ap[:, off * 64:(off + n) * 64].bitcast(BF16).rearrange("p (a b) -> p a b", b=128)
        return V(ap, bank.buf)
    pvn = [k.pv(6 + d, 0, 128) for d in range(2)]
    poT = [k.pv(6 + d, 128, 256) for d in range(2)]
    pS = [k.pv(6 + d, 256, 384) for d in range(2)]
    def gtmp(name, dt, nb=1):
        return [[k.sb(f"{name}{d}_{i}", [128, GS, 128], dt) for i in range(nb)] for d in range(2)]
    g_dgc, g_E = gtmp("dgc", F32), gtmp("E", F32)
    g_eg, g_Rk0, g_Q0, g_Q0T, g_tm, g_Z = (gtmp(nm, BF16) for nm in ("eg", "Rk0", "Q0", "Q0T", "tm", "Z"))
    g_E1, g_E2, g_tm2 = gtmp("E1", BF16), gtmp("E2", BF16), gtmp("tm2", BF16)
    g_LT, g_D, g_G = gtmp("LT", BF16, 2), gtmp("D", BF16, 2), gtmp("G", BF16, 2)
    g_W, g_nwT, g_Vt, g_ktail, g_qhT, g_qkm = (gtmp(nm, BF16, 2) for nm in ("W", "nwT", "Vt", "ktail", "qhT", "qkm"))
    t_vn = [k.sb(f"vn{d}", [128, 128], BF16) for d in range(2)]
    Sf = [k.sb(f"Sf{d}", [128, 128], F32) for d in range(2)]
    Sb = [k.sb(f"Sb{d}", [128, 128], BF16) for d in range(2)]
    def bcn(v, n):
        return V(v.ap.unsqueeze(1).to_broadcast([128, n, 128]), v.buf)
    k.memset("pool", praw, 0.0)
    nbig = [0]
    def nextbig():
        nbig[0] += 1
        return pbig[nbig[0] % 2]
    wcnt = [0]

    def load_w(col0):
        i = 0
        wcnt[0] += 1
        k.dma("sp", wst_f[0], V(winv[:, :, col0:col0 + 128], win_d.buf))
        k.copy("pool", wst_b[i], wst_f[0])
        return wst_b[i]

    order_f = list(range(NT))
    order_b = [1, 0] + list(range(NT - 1, 1, -1))

    for h in heads:
        for ty in range(3):
            ci = ty * 8 + h
            wb = load_w(OFF_Q + ci * 128)
            for (s, n) in blocks:
                pp = nextbig()
                for kk in range(8):
                    k.mm(pp[:, 0:n], wb[:, kk, :], hTv[:, kk, s:s + n], start=(kk == 0), stop=(kk == 7))
                k.copy("act", praw[:, poff(s):poff(s) + n], pp[:, 0:n])
            for tap in range(3):
                k.ts("pool", dgt[:, tap, :], identf, cqkv[:, ci, tap:tap + 1])
            dst = (qT, kT, vT)[ty]
            for (s, n) in blocks:
                pp = nextbig()
                base = poff(s) - 1
                for tap in range(3):
                    k.mm(pp[:, 0:n], dgt[:, tap, :], praw[:, base + tap:base + tap + n], start=(tap == 0), stop=(tap == 2))
                k.act(dst[:, s:s + n], pp[:, 0:n], AF.Silu)
            if ty < 2:
                dstn = qT if ty == 0 else kT
                for bi, (s, n) in enumerate(blocks):
                    sqv = praw[:, 4:4 + n] if False else None
                for bi, (s, n) in enumerate(blocks):
                    pp = nextbig()
                    r = rn[0]
                    sqv = ofin[bi % 2]
                    k.tt("pool", sqv[:, 0:n], dstn[:, s:s + n], dstn[:, s:s + n], ALU.mult)
                    k.mm(pp[:, 0:n], onesb, sqv[:, 0:n])
                    k.act(r[:, 0:n], pp[:, 0:n], AF.Ln, bias=EPS)
                    k.act(r[:, 0:n], r[:, 0:n], AF.Exp, scale=-0.5)
                    k.stt("dve", dstn[:, s:s + n], dstn[:, s:s + n], (128.0 ** -0.5) if ty == 0 else 1.0, r[:, 0:n], ALU.mult, ALU.mult)
        if "qkv" in dbg_out and h == heads[0]:
            with ExitStack() as st2:
                old = k.stack
                k.stack = st2
                tf_ = k.sb("dbgqkv", [128, 3, 512], F32)
                k.copy("dve", tf_[:, 0, :], qT[:, 0:512])
                k.copy("dve", tf_[:, 1, :], kT[:, 0:512])
                k.copy("dve", tf_[:, 2, :], vT[:, 0:512])
                k.dma("sp", dbg_out["qkv"], tf_)
                k.barrier()
                k.stack = old
        for d in range(2):
            k.memset("pool", Sf[d], 0.0)
            k.memset("pool", Sb[d], 0.0)
        visited = set()
        groups = [(0, 2)] + [(2 + 3 * i, 3) for i in range(10)] + [(32, 2)]
        if dbg is not None and "nsteps" in dbg:
            groups = groups[:dbg["nsteps"]]
        gorder = [groups, [groups[0]] + groups[:0:-1]]

        def pre_gen(d, a, n, gp):
            col = d * 8 + h
            isx = a >= 2
            def bc(arr):
                return V(arr.ap[:, a:a + n, col].unsqueeze(2).to_broadcast([128, n, 128]), arr.buf)
            tl = lambda v: V(v.ap[:, a * 128:(a + n) * 128].rearrange("p (a b) -> p a b", b=128), v.buf)
            tsl = lambda i: slice((a + i) * 128, (a + i + 1) * 128)
            dgc, E, eg, Rk0, Q0, Q0T, tm_, Z = (x[d][0][:, 0:n, :] for x in (g_dgc, g_E, g_eg, g_Rk0, g_Q0, g_Q0T, g_tm, g_Z))
            LT, Dm, Gm = g_LT[d], g_D[d], g_G[d]
            W, nwT, Vt, ktail, qhT, qkm = (x[d][gp][:, 0:n, :] for x in (g_W, g_nwT, g_Vt, g_ktail, g_qhT, g_qkm))
            pA = nextb(d)
            pAk, pAv = b3(pA, n, BF16, 0), b3(pA, n, BF16, GS)
            for i in range(n):
                k.tr(pAk[:, i, :], kT[:, tsl(i)], ident)
                k.tr(pAv[:, i, :], vT[:, tsl(i)], ident)
            yield
            k.act(Vt, pAv, AF.Copy)
            k.tt("dve", Rk0, pAk, bc(egc), ALU.mult)
            k.tt("dve", ktail, pAk, bc(ekt), ALU.mult)
            yield
            E1, E2, tm2_ = g_E1[d][0][:, 0:n, :], g_E2[d][0][:, 0:n, :], g_tm2[d][0][:, 0:n, :]
            k.tt("pool", dgc, bcn(identf, n), bc(gc), ALU.mult)
            pB = nextb(d)
            pBv = b3(pB, n)
            for i in range(n):
                k.mm(pBv[:, i, :], onesf, dgc[:, i, :])
            yield
            k.tt("dve", E, pBv, bcn(negm[d], n), ALU.add)
            for i in range(n):
                k.act(E2[:, i, :], E[:, i, :], AF.Exp, bias=ngcb[:, a + i, col:col + 1])
            if isx:
                for i in range(n):
                    k.act(E1[:, i, :], E[:, i, :], AF.Exp, bias=ngc[:, a + i, col:col + 1])
                k.act(eg, pBv, AF.Exp)
                k.tt("pool", qhT, tl(qT), eg, ALU.mult)
            yield
            pC = nextb(d)
            pCv = b3(pC, n)
            for i in range(n):
                k.mm(pCv[:, i, :], kT[:, tsl(i)], kT[:, tsl(i)])
            k.tt("dve", Q0, pCv, E2, ALU.mult)
            if isx:
                pQ = nextb(d)
                pQv = b3(pQ, n)
                for i in range(n):
                    k.mm(pQv[:, i, :], kT[:, tsl(i)], qT[:, tsl(i)])
                k.tt("dve", qkm, pQv, E1, ALU.mult)
            yield
            pT = nextb(d)
            pTv = b3(pT, n, BF16, 0)
            for i in range(n):
                k.tr(pTv[:, i, :], Q0[:, i, :], ident)
            k.act(Q0T, pTv, AF.Copy)
            yield
            k.tt("dve", tm_, Q0, bcn(hmask[:, 0, :], n), ALU.mult)
            k.tt("dve", Dm[0][:, 0:n, :], bcn(ident, n), tm_, ALU.subtract)
            k.tt("pool", tm2_, Q0T, bcn(hmask[:, 0, :], n), ALU.mult)
            k.tt("pool", Gm[0][:, 0:n, :], bcn(ident, n), tm2_, ALU.subtract)
            k.tt("pool", LT[1][:, 0:n, :], Q0T, bcn(hmask[:, 1, :], n), ALU.mult)
            yield
            for m_ in range(1, 7):
                a_, b_ = (m_ - 1) % 2, m_ % 2
                Da, Ga, Lm = Dm[a_][:, 0:n, :], Gm[a_][:, 0:n, :], LT[m_ % 2][:, 0:n, :]
                pz = nextb(d)
                pzv = b3(pz, n)
                for i in range(n):
                    k.mm(pzv[:, i, :], Lm[:, i, :], Da[:, i, :])
                if m_ < 6:
                    k.tt("pool", LT[(m_ + 1) % 2][:, 0:n, :], Q0T, bcn(hmask[:, m_ + 1, :], n), ALU.mult)
                k.act(Z, pzv, AF.Copy)
                yield
                pd_ = nextb(d)
                pdv = b3(pd_, n)
                for i in range(n):
                    k.mm(pdv[:, i, :], Ga[:, i, :], Z[:, i, :])
                if m_ < 6:
                    pg_ = nextb(d)
                    pgv = b3(pg_, n)
                    for i in range(n):
                        k.mm(pgv[:, i, :], Z[:, i, :], Ga[:, i, :])
                k.tt("dve", W if m_ == 6 else Dm[b_][:, 0:n, :], Da, pdv, ALU.subtract)
                if m_ < 6:
                    k.tt("dve", Gm[b_][:, 0:n, :], Ga, pgv, ALU.subtract)
                yield
            pw = nextb(d)
            pwv = b3(pw, n)
            for i in range(n):
                k.mm(pwv[:, i, :], Rk0[:, i, :], W[:, i, :])
            k.act(nwT, pwv, AF.Copy, scale=-1.0)
            yield

        def state_gen(d, a, n, gp):
            col = d * 8 + h
            isx = a >= 2
            W, nwT, Vt, ktail, qhT, qkm = (x[d][gp] for x in (g_W, g_nwT, g_Vt, g_ktail, g_qhT, g_qkm))
            vn = t_vn[d]
            for i in (range(n) if d == 0 else range(n - 1, -1, -1)):
                t = a + i
                sc = lambda arr: arr[:, t, col:col + 1]
                k.mm(pvn[d], W[:, i, :], Vt[:, i, :], start=True, stop=False)
                k.mm(pvn[d], nwT[:, i, :], Sb[d], start=False, stop=True)
                k.act(vn, pvn[d], AF.Identity, scale=sc(beta))
                yield
                if isx:
                    xt = t - 2
                    ov = V(osum.ap[:, xt * 128:(xt + 1) * 128], osb[xt])
                    k.mm(poT[d], Sb[d], qhT[:, i, :], start=True, stop=False)
                    k.mm(poT[d], vn, qkm[:, i, :], start=False, stop=True)
                k.mm(pS[d], ktail[:, i, :], vn)
                if isx:
                    if xt not in visited:
                        visited.add(xt)
                        k.act(ov, poT[d], AF.Copy)
                    else:
                        k.tt("dve", ov, poT[d], ov, ALU.add)
                k.stt("dve", Sf[d], Sf[d], sc(egl), pS[d], ALU.mult, ALU.add)
                k.act(Sb[d], Sf[d], AF.Copy)
                yield

        def run_all(gens):
            gens = list(gens)
            while gens:
                for g_ in list(gens):
                    try:
                        next(g_)
                    except StopIteration:
                        gens.remove(g_)

        ng = len(groups)
        run_all([pre_gen(0, *gorder[0][0], 0), pre_gen(1, *gorder[1][0], 0)])
        for gi in range(ng):
            gens = [state_gen(0, *gorder[0][gi], gi % 2), state_gen(1, *gorder[1][gi], gi % 2)]
            if gi + 1 < ng:
                gens += [pre_gen(0, *gorder[0][gi + 1], (gi + 1) % 2), pre_gen(1, *gorder[1][gi + 1], (gi + 1) % 2)]
            run_all(gens)
        if "S" in dbg_out and h == heads[0]:
            k.dma("sp", dbg_out["S"][0], Sf[0])
            k.dma("sp", dbg_out["S"][1], Sf[1])
        for bi in range(8):
            s = bi * 512
            ovs = [V(osum.ap[:, s:s + 512], osb[bi * 4 + j]) for j in range(4)]
            class _M:
                pass
            ovall = V(osum.ap[:, s:s + 512], osb[bi * 4])
            extra = [osb[bi * 4 + j] for j in range(1, 4)]
            if bi == 0:
                wbz = load_w(OFF_Z + h * 128)
            pz_ = nextbig()
            zsb = zs_b[0]
            for kk in range(8):
                k.mm(pz_, wbz[:, kk, :], hTv[:, kk, TC + s:TC + s + 512], start=(kk == 0), stop=(kk == 7))
            k.act(zsb, pz_, AF.Silu)
            pp = nextbig()
            sqv = ofin[bi % 2]
            r = rn[0]
            of_ = ofin_f[0]
            k.op("pool", lambda g, sqv=sqv, s=s: g.tensor_tensor(out=sqv.ap, in0=osum.ap[:, s:s + 512], in1=osum.ap[:, s:s + 512], op=ALU.mult),
                 [osb[bi * 4 + j] for j in range(4)], [sqv.buf])
            k.mm(pp, onesb, sqv)
            k.act(r, pp, AF.Ln, bias=EPS, scale=1.0 / 128)
            k.act(r, r, AF.Exp, scale=-0.5)
            k.op("dve", lambda g, of_=of_, r=r, s=s: g.scalar_tensor_tensor(out=of_.ap, in0=osum.ap[:, s:s + 512], scalar=dnn.ap[:, 0:1], in1=r.ap,
                                                                           op0=ALU.mult, op1=ALU.mult),
                 [osb[bi * 4 + j] for j in range(4)] + [dnn.buf, r.buf], [of_.buf])
            if "o0" in dbg_out and h == heads[0]:
                k.tt("pool", of_, of_, zsb, ALU.mult)
                k.dma("sp", V(dbg_out["o0"].ap[:, s:s + 512], dbg_out["o0"].buf), of_)
                k.copy("pool", sqv, of_)
            else:
                k.tt("pool", sqv, of_, zsb, ALU.mult)
            k.dma("sp", V(oT_d.ap[h, :, s:s + 512], oT_d.buf), sqv)
    k.barrier()
    p4.close()
    s_g.close()
    k.stack = ExitStack()
    if stage <= 4:
        return finish(nc, k, out_d)

    def wchunk_loader(stf, stb):
        cnt = [0]
        def load(src_v, K):
            i = cnt[0] % len(stf)
            j = cnt[0] % len(stb)
            cnt[0] += 1
            k.dma("sp" if i == 0 else "pool", stf[i][:, 0:K, :], src_v)
            k.copy("pool", stb[j][:, 0:K, :], stf[i][:, 0:K, :])
            return stb[j]
        return load

    def load_resident(dst_bf, src_ap, src_buf, K, ncols, stg):
        i = 0
        for k0 in range(0, K, 8):
            kn = min(8, K - k0)
            for c0 in range(0, ncols, 512):
                cn = min(512, ncols - c0)
                st_ = stg[i % len(stg)]
                k.dma("sp" if i % 2 == 0 else "pool", st_[:, 0:kn, 0:cn], V(src_ap[:, k0:k0 + kn, c0:c0 + cn], src_buf))
                k.copy("pool" if i % 2 == 0 else "dve", dst_bf[:, k0:k0 + kn, c0:c0 + cn], st_[:, 0:kn, 0:cn])
                i += 1

    with ExitStack() as st:
        k.stack = st
        FCS = k.sb("FCS", [128, 32, 4, 256], BF16)
        fT = [k.sb(f"fT{i}", [128, 512], BF16) for i in range(2)]
        stf = [k.sb(f"p3stf{i}", [128, 8, 128], F32) for i in range(2)]
        stb = [k.sb(f"p3stb{i}", [128, 8, 128], BF16) for i in range(2)]
        ctab = [k.sb(f"ctab{i}", [128, 4, 512], BF16) for i in range(2)]
        stab = [k.sb(f"stab{i}", [128, 4, 512], BF16) for i in range(2)]
        yblk = [k.sb(f"yblk{i}", [128, 4, 512], BF16) for i in range(2)]
        lw = wchunk_loader(stf, stb)
        nb_ = 0
        for g in range(4):
            wb = lw(V(winv[:, :, OFF_F + g * 128:OFF_F + (g + 1) * 128], win_d.buf), 8)
            for bi, (s, n) in enumerate(xblocks):
                pp = k.banks[nb_ % 2]
                ft = fT[nb_ % 2]
                nb_ += 1
                for kk in range(8):
                    k.mm(pp, wb[:, kk, :], hTv[:, kk, s:s + n], start=(kk == 0), stop=(kk == 7))
                k.act(ft, pp, AF.Copy)
                pf = k.banks[2 + (nb_ % 2)]
                pfv = V(pf.ap.rearrange("p (a b) -> p a b", b=256), pf.buf)
                for j2 in range(2):
                    for jj in range(2):
                        j = j2 * 2 + jj
                        k.mm(pfv[:, jj, :], ft[:, j * 128:(j + 1) * 128], c128)
                    t0 = bi * 4 + j2 * 2
                    if j2 == 0:
                        k.act(FCS[:, t0:t0 + 2, g, :], pfv, AF.Copy)
                    else:
                        k.copy("dve", FCS[:, t0:t0 + 2, g, :], pfv)
        k.barrier()
        cosv = cos_d.ap.rearrange("(tt p) f -> p tt f", p=128)
        sinv = sin_d.ap.rearrange("(tt p) f -> p tt f", p=128)
        ld = 0
        for kb in range(8):
            for t4 in range(8):
                ct, st_ = ctab[ld % 2], stab[ld % 2]
                ld += 1
                k.dma("sp", ct, V(cosv[:, t4 * 4:(t4 + 1) * 4, kb * 512:(kb + 1) * 512], cos_d.buf))
                k.dma("pool", st_, V(sinv[:, t4 * 4:(t4 + 1) * 4, kb * 512:(kb + 1) * 512], sin_d.buf))
                for ti in range(4):
                    tt_ = t4 * 4 + ti
                    for g in range(4):
                        k.mm(k.banks[4 + g], FCS[:, tt_, g, 0:128], ct[:, ti, :], start=(tt_ == 0), stop=False)
                        k.mm(k.banks[4 + g], FCS[:, tt_, g, 128:256], st_[:, ti, :], start=False, stop=(tt_ == 31))
            yb = yblk[kb % 2]
            for g in range(4):
                if g % 2 == 0:
                    k.act(yb[:, g, :], k.banks[4 + g], AF.Copy)
                else:
                    k.copy("dve", yb[:, g, :], k.banks[4 + g])
            k.dma("sp", V(yT_d.ap[:, :, kb * 512:(kb + 1) * 512].rearrange("g p f -> p g f"), yT_d.buf), yb)
        if "fm" in dbg_out:
            pass
        k.barrier()
    k.stack = ExitStack()
    if stage <= 5:
        return finish(nc, k, out_d)

    with ExitStack() as st:
        k.stack = st
        wg = k.sb("wg", [128, 8, 2048], BF16)
        wf4 = k.sb("wf4", [128, 4, 1024], BF16)
        wdn = k.sb("wdn", [128, 8, 1024], BF16)
        stg = [k.sb(f"p5stg{i}", [128, 8, 512], F32) for i in range(1)]
        ytb = [k.sb(f"ytb{i}", [128, 4, 512], BF16) for i in range(2)]
        otb = [k.sb(f"otb{i}", [128, 8, 512], BF16) for i in range(2)]
        g0 = [k.sb(f"g0_{i}", [128, 512], BF16) for i in range(2)]
        g1 = [k.sb(f"g1_{i}", [128, 512], BF16) for i in range(2)]
        m0 = [k.sb(f"m0_{i}", [128, 512], BF16) for i in range(2)]
        m1 = [k.sb(f"m1_{i}", [128, 512], BF16) for i in range(2)]
        mixs = k.sb("mixs", [128, 2, 8, 512], BF16)
        mixs_buf = [Buf("mixs0"), Buf("mixs1")]
        load_resident(wg, winv[:, :, OFF_G:OFF_G + 2048], win_d.buf, 8, 2048, stg)
        load_resident(wf4, wf_d.ap.rearrange("(g p) d -> p g d", p=128), wf_d.buf, 4, 1024, stg)
        load_resident(wdn, wdn_d.ap.rearrange("(h p) d -> p h d", p=128), wdn_d.buf, 8, 1024, stg)
        it = 0
        for mt in range(8):
            s = TC + mt * 512
            yt, ot = ytb[mt % 2], otb[mt % 2]
            k.dma("sp", yt, V(yT_d.ap[:, :, mt * 512:(mt + 1) * 512].rearrange("g p f -> p g f"), yT_d.buf))
            k.dma("pool", ot, V(oT_d.ap[:, :, mt * 512:(mt + 1) * 512].rearrange("h p f -> p h f"), oT_d.buf))
            for dc in range(8):
                dsl = slice(dc * 128, (dc + 1) * 128)
                i2 = it % 2
                it += 1
                pb = [k.banks[4 * i2 + j] for j in range(4)]
                for g in range(4):
                    k.mm(pb[0], wf4[:, g, dsl], yt[:, g, :], start=(g == 0), stop=(g == 3))
                for kk in range(8):
                    k.mm(pb[1], wg[:, kk, dc * 128:(dc + 1) * 128], hTv[:, kk, s:s + 512], start=(kk == 0), stop=(kk == 7))
                for hh in range(8):
                    k.mm(pb[2], wdn[:, hh, dsl], ot[:, hh, :], start=(hh == 0), stop=(hh == 7))
                for kk in range(8):
                    k.mm(pb[3], wg[:, kk, 1024 + dc * 128:1024 + (dc + 1) * 128], hTv[:, kk, s:s + 512], start=(kk == 0), stop=(kk == 7))
                k.act(g0[i2], pb[1], AF.Sigmoid)
                k.act(g1[i2], pb[3], AF.Sigmoid)
                k.tt("dve", m0[i2], pb[0], g0[i2], ALU.mult)
                k.tt("dve", m1[i2], pb[2], g1[i2], ALU.mult)
                k.tt("pool", V(mixs.ap[:, mt % 2, dc, :], mixs_buf[mt % 2]), m0[i2], m1[i2], ALU.add)
            k.copy("pool", hTv[:, :, s:s + 512], V(mixs.ap[:, mt % 2, :, :], mixs_buf[mt % 2]))
        k.barrier()
    k.stack = ExitStack()
    if stage <= 6:
        return finish(nc, k, out_d)

    def branch_tail(mt, producer, cidx, resid_d, final, tb):
        yx, sq, rst, xin_, x1t = tb["yx"], tb["sq"], tb["rst"], tb["xin"], tb["x1t"]
        for dc in range(8):
            pb = k.banks[dc % 2]
            producer(dc, pb)
            k.act(yx[:, dc, :], pb, AF.Copy)
            k.act(sq[:, dc, :], pb, AF.Square)
        pss = k.banks[2]
        for dc in range(8):
            k.mm(pss, onesb, sq[:, dc, :], start=(dc == 0), stop=(dc == 7))
        k.act(rst, pss, AF.Ln, bias=EPS, scale=1.0 / D)
        k.act(rst, rst, AF.Exp, scale=-0.5)
        for dc in range(8):
            k.stt("dve", yx[:, dc, :], yx[:, dc, :], coef[:, cidx, dc:dc + 1], rst, ALU.mult, ALU.mult)
        for j in range(4):
            tok0 = mt * 512 + j * 128
            xi = xin_[j % len(xin_)]
            xo = x1t[j % len(x1t)]
            k.dma("sp" if j % 2 == 0 else "pool", xi, V(resid_d.ap[tok0:tok0 + 128, :], resid_d.buf))
            ba, bb = k.banks[3 + 2 * (j % 2)], k.banks[4 + 2 * (j % 2)]
            for dc in range(8):
                bk = ba if dc < 4 else bb
                k.tr(bk[:, (dc % 4) * 128:(dc % 4 + 1) * 128], yx[:, dc, j * 128:(j + 1) * 128], identf)
            k.tt("dve", xo[:, 0:512], ba, xi[:, 0:512], ALU.add)
            k.tt("dve", xo[:, 512:1024], bb, xi[:, 512:1024], ALU.add)
            if final:
                k.dma("sp", V(out_d.ap[tok0:tok0 + 128, :], out_d.buf), xo)
            else:
                k.dma("sp", V(x1_d.ap[tok0:tok0 + 128, :], x1_d.buf), xo)
                norm_tile(xo, 2 + mt * 4 + j, 5, 6, hT, hTv.buf, tb["tm"])

    def tail_bufs(nbuf):
        return {"yx": k.sb("yx", [128, 8, 512], F32), "sq": k.sb("sqb", [128, 8, 512], BF16), "rst": k.sb("rst", [128, 512], F32),
                "xin": [k.sb(f"rxin{i}", [128, D], F32) for i in range(nbuf)], "x1t": [k.sb(f"x1t{i}", [128, D], F32) for i in range(nbuf)],
                "tm": {"sq": [k.sb("t_sq", [128, D], F32)], "ss": [k.sb("t_ss", [128, 1], F32)], "xn": [k.sb("t_xn", [128, D], BF16)],
                       "pt": [k.pv(7, 0, 512, BF16, 128)]}}

    with ExitStack() as st:
        k.stack = st
        wout = k.sb("wout", [128, 8, 1024], BF16)
        stg = [k.sb(f"p5bstg{i}", [128, 8, 512], F32) for i in range(1)]
        load_resident(wout, wout_d.ap.rearrange("(c p) d -> p c d", p=128), wout_d.buf, 8, 1024, stg)
        tb = tail_bufs(2)
        mixl = k.sb("mixl", [128, 8, 512], BF16)
        for mt in range(8):
            s = TC + mt * 512
            k.copy("pool", mixl, hTv[:, :, s:s + 512])
            def prod(dc, pb):
                for c in range(8):
                    k.mm(pb, wout[:, c, dc * 128:(dc + 1) * 128], mixl[:, c, :], start=(c == 0), stop=(c == 7))
            branch_tail(mt, prod, 4, x_d, False, tb)
        k.barrier()
    k.stack = ExitStack()
    if stage <= 7:
        return finish(nc, k, out_d)

    wupv = wup_d.ap.rearrange("(k p) c -> p k c", p=128)
    with ExitStack() as st:
        k.stack = st
        stf = [k.sb(f"p6stf{i}", [128, 8, 128], F32) for i in range(2)]
        stb = [k.sb(f"p6stb{i}", [128, 8, 128], BF16) for i in range(4)]
        lw = wchunk_loader(stf, stb)
        apad_b = [k.sb(f"apad{i}", [128, 66, 66], BF16) for i in range(2)]
        dg9_b = [k.sb(f"dg9_{i}", [128, 9, 128], BF16) for i in range(2)]
        sa = [k.sb(f"sa{i}", [128, 512], BF16) for i in range(4)]
        gtc = [k.sb(f"gtc{i}", [128, T], BF16) for i in range(2)]
        k.memset("pool", apad_b[0], 0.0)
        k.memset("pool", apad_b[1], 0.0)
        nb_ = 0
        for c in range(NFF):
            apad, dg9 = apad_b[c % 2], dg9_b[c % 2]
            wa = lw(V(wupv[:, :, c * 128:(c + 1) * 128], wup_d.buf), 8)
            wu = lw(V(wupv[:, :, DFF + c * 128:DFF + (c + 1) * 128], wup_d.buf), 8)
            for tap in range(9):
                k.ts("pool", dg9[:, tap, :], identf, cffn[:, c, tap:tap + 1])
            for bi in range(8):
                s = TC + bi * 512
                pp = k.banks[(0, 1, 6, 7)[nb_ % 4]]
                nb_ += 1
                for kk in range(8):
                    k.mm(pp, wa[:, kk, :], hTv[:, kk, s:s + 512], start=(kk == 0), stop=(kk == 7))
                k.act(apad[:, 1 + bi * 8:1 + bi * 8 + 8, 1:65], V(pp.ap.rearrange("p (r c) -> p r c", c=64), pp.buf), AF.Copy)
            gt = gtc[c % 2]
            for bi in range(8):
                s = TC + bi * 512
                pc = k.banks[2 + (bi % 2)]
                pu = k.banks[4 + (bi % 2)]
                pcv = V(pc.ap.rearrange("p (r c) -> p r c", c=64), pc.buf)
                for tap in range(9):
                    dr, dcc = tap // 3, tap % 3
                    k.mm(pcv, dg9[:, tap, :], apad[:, bi * 8 + dr:bi * 8 + dr + 8, dcc:dcc + 64], start=(tap == 0), stop=(tap == 8))
                k.act(sa[bi % 4], pc, AF.Silu)
                for kk in range(8):
                    k.mm(pu, wu[:, kk, :], hTv[:, kk, s:s + 512], start=(kk == 0), stop=(kk == 7))
                k.tt("dve", gt[:, bi * 512:(bi + 1) * 512], pu, sa[bi % 4], ALU.mult)
            k.dma("sp" if c % 2 == 0 else "pool", V(gT_d.ap[c], gT_d.buf), gt)
        k.barrier()
    k.stack = ExitStack()
    s_h.close()
    k.stack = ExitStack()
    if stage <= 8:
        return finish(nc, k, out_d)

    with ExitStack() as st:
        k.stack = st
        wdown = k.sb("wdown", [128, NFF, 1024], BF16)
        stg = [k.sb(f"p7stg{i}", [128, 8, 512], F32) for i in range(2)]
        load_resident(wdown, wdown_d.ap.rearrange("(c p) d -> p c d", p=128), wdown_d.buf, NFF, 1024, stg)
        gbl = [k.sb(f"gbl{i}", [128, NFF, 512], BF16) for i in range(2)]
        tb = tail_bufs(2)
        for mt in range(8):
            gb = gbl[mt % 2]
            k.dma("sp", gb[:, 0:11, :], V(gT_d.ap[0:11, :, mt * 512:(mt + 1) * 512].rearrange("c p f -> p c f"), gT_d.buf))
            k.dma("pool", gb[:, 11:NFF, :], V(gT_d.ap[11:NFF, :, mt * 512:(mt + 1) * 512].rearrange("c p f -> p c f"), gT_d.buf))
            def prod(dc, pb, gb=gb):
                for c in range(NFF):
                    k.mm(pb, wdown[:, c, dc * 128:(dc + 1) * 128], gb[:, c, :], start=(c == 0), stop=(c == NFF - 1))
            branch_tail(mt, prod, 7, x1_d, True, tb)
        k.barrier()
    k.stack = ExitStack()
    return finish(nc, k, out_d)


def finish(nc, k, out_d):
    k.barrier()
    return nc


def prep_inputs(inp, b):
    f = lambda a: np.ascontiguousarray(a, dtype=np.float32)
    colmajor = lambda v: f(np.asarray(v).reshape(-1, 128).T)
    m = {}
    m["x"] = f(inp["x"][b])
    m["ctx"] = f(inp["ctx"][b])
    m["cc"] = f(np.stack([colmajor(inp["c"][b]), colmajor(inp["c_ctx"])], axis=-1))
    m["w_ada"] = f(inp["w_ada"][0])
    m["b_ada"] = colmajor(inp["b_ada"][0])
    m["norms"] = f(np.stack([colmajor(inp[n][0]) for n in ("norm_pre_mix", "norm_post_mix", "norm_pre_ffn", "norm_post_ffn")], axis=1))
    m["w_in"] = f(inp["w_in"][0])
    cq = np.asarray(inp["conv_qkv"][0])
    m["conv_qkv"] = f(cq.T.reshape(24, 128, 3).transpose(1, 0, 2))
    gp = np.stack([np.asarray(inp["a_log"][0]).reshape(16), np.asarray(inp["dt_bias"][0]).reshape(16)], 0)
    m["gpar"] = f(np.broadcast_to(gp[None], (128, 2, 16)))
    m["dn_norm"] = f(np.asarray(inp["dn_norm"][0]).reshape(128, 1))
    m["w_fourier"] = f(inp["w_fourier"][0])
    m["w_dn"] = f(inp["w_dn"][0])
    m["w_out"] = f(inp["w_out"][0])
    m["w_up"] = f(inp["w_up"][0])
    cf = np.asarray(inp["conv_ffn"][0]).reshape(9, DFF)
    m["conv_ffn"] = f(cf.T.reshape(NFF, 128, 9).transpose(1, 0, 2))
    m["w_down"] = f(inp["w_down"][0])
    return m


_CONST = {}


def consts():
    if not _CONST:
        idx = np.arange(T, dtype=np.int64)
        ang = (2.0 * np.pi / T) * ((idx[:, None] * idx[None, :]) % T).astype(np.float64)
        s = 1.0 / np.sqrt(float(T) * 128.0)
        _CONST["dft_cos"] = (np.cos(ang) * s).astype(ml_dtypes.bfloat16)
        _CONST["dft_sin"] = (np.sin(ang) * s).astype(ml_dtypes.bfloat16)
        i8 = np.arange(128, dtype=np.int64)
        a8 = (2.0 * np.pi / 128) * ((i8[:, None] * i8[None, :]) % 128).astype(np.float64)
        j8 = np.arange(128)
        hm = []
        for m_ in range(7):
            s_ = 2 ** m_
            blk2 = (j8[:, None] // (2 * s_)) == (j8[None, :] // (2 * s_))
            half = (j8[:, None] // s_) != (j8[None, :] // s_)
            hm.append((blk2 & half).astype(np.float32))
        _CONST["hmask"] = np.stack(hm, axis=1).astype(ml_dtypes.bfloat16)
        _CONST["dft128"] = np.concatenate([np.cos(a8), -np.sin(a8)], axis=1).astype(ml_dtypes.bfloat16)
    return _CONST


def kernel(**inputs):
    inp = {k_: np.asarray(v) for k_, v in inputs.items()}
    nc = build()
    cst = consts()
    in_maps = []
    for b in range(8):
        m = prep_inputs(inp, b)
        m.update(cst)
        in_maps.append(m)
    res = run_bass_kernel_spmd(nc, in_maps, core_ids=list(range(8)))
    return np.stack([np.asarray(r["out"], dtype=np.float32) for r in res.results], axis=0)
```

```python
import os
from contextlib import ExitStack
import numpy as np
import ml_dtypes
import concourse.bass as bass
import concourse.mybir as mybir
from concourse.bass_utils import run_bass_kernel_spmd

F32 = mybir.dt.float32
BF16 = mybir.dt.bfloat16
AF = mybir.ActivationFunctionType
ALU = mybir.AluOpType

D = 1024
T = 4096
TC = 256
TA = TC + T
NT = TA // 128
H = 8
OFF_F, OFF_Q, OFF_K, OFF_V, OFF_Z, OFF_B, OFF_A, OFF_G = 0, 512, 1536, 2560, 3584, 4608, 4624, 4640
INW = 6688
DFF = 2816
NFF = DFF // 128
EPS = 1e-6


class Buf:
    __slots__ = ("name", "last_w", "readers", "dsem", "dcount", "excl")

    def __init__(self, name, excl=False):
        self.name = name
        self.excl = excl
        self.last_w = None
        self.readers = []
        self.dsem = None
        self.dcount = 0


class V:
    __slots__ = ("ap", "buf")

    def __init__(self, ap, buf):
        self.ap = ap
        self.buf = buf

    def __getitem__(self, idx):
        return V(self.ap[idx], self.buf)

    def sub(self, idx, buf):
        return V(self.ap[idx], buf)


def _bufs(*vs):
    out = []
    for v in vs:
        if isinstance(v, V) and v.buf is not None:
            for b in (v.buf if isinstance(v.buf, (list, tuple)) else (v.buf,)):
                if b not in out:
                    out.append(b)
    return out


def _ap(v):
    return v.ap if isinstance(v, V) else v


class K:
    def __init__(self, nc):
        self.nc = nc
        self.engs = {"pe": nc.tensor, "act": nc.scalar, "dve": nc.vector, "pool": nc.gpsimd, "sp": nc.sync}
        self.sem = {n: nc.alloc_semaphore(f"s_{n}") for n in self.engs}
        self.cnt = {n: 0 for n in self.engs}
        self.known = {n: {} for n in self.engs}
        self.dsems = []
        self.nins = 0
        self.nwaits = 0
        self.stack = ExitStack()
        self.limit = None
        self.log = []
        self.sched = os.environ.get('KSCHED', '1') == '1'
        self.pending = []

    def _uid(self):
        self.uid = getattr(self, 'uid', 0) + 1
        return self.uid

    def sb(self, name, shape, dt, nbuf=None):
        t = self.stack.enter_context(self.nc.sbuf_tensor(f"sb{self._uid()}_" + name, list(shape), dt))
        return V(t[:] if hasattr(t, "__getitem__") else t.ap(), Buf(name) if nbuf is None else nbuf)

    def init_banks(self):
        self.banks = []
        for i in range(8):
            t = self.nc.psum_tensor(f"ps_bank{i}", [128, 512], F32).__enter__()
            self.banks.append(V(t[:], Buf(f"bank{i}", excl=True)))

    def pv(self, bank, lo, hi, dt=F32, inner=None):
        b = self.banks[bank]
        ap = b.ap[:, lo:hi]
        if dt != F32:
            ap = ap.bitcast(dt)
        if inner is not None:
            ap = ap.rearrange("p (a b) -> p a b", b=inner)
        return V(ap, b.buf)

    def _wait(self, e, ev):
        sem, val, src = ev
        if src == "pe" and e == "pe":
            return
        kn = self.known[e]
        if kn.get(sem.num, 0) >= val:
            return
        kn[sem.num] = val
        self.engs[e].wait_ge(sem, val)
        self.nwaits += 1

    def _deps(self, e, reads, writes):
        best = {}
        def add(ev):
            s = ev[0].num
            if s not in best or best[s][1] < ev[1]:
                best[s] = ev
        for b in reads:
            if b.last_w is not None:
                add(b.last_w)
        for b in writes:
            if b.last_w is not None:
                add(b.last_w)
            for ev in b.readers:
                add(ev)
        for ev in best.values():
            self._wait(e, ev)

    def _record(self, ev, reads, writes):
        for b in reads:
            if b in writes:
                continue
            b.readers.append(ev)
            if len(b.readers) > 10:
                best = {}
                for x in b.readers:
                    s = x[0].num
                    if s not in best or best[s][1] < x[1]:
                        best[s] = x
                b.readers = list(best.values())
        for b in writes:
            b.last_w = ev
            b.readers = []

    def op(self, e, fn, reads, writes, cost=300.0):
        if self.sched:
            self.pending.append(("op", e, fn, list(reads), list(writes), float(cost)))
            return
        self._emit_op(e, fn, reads, writes)

    def _emit_op(self, e, fn, reads, writes):
        if self.limit is not None and self.nins >= self.limit:
            return
        ex = [b for b in reads if b.excl and b not in writes]
        if ex:
            writes = list(writes) + ex
        self._deps(e, reads, writes)
        ins = fn(self.engs[e])
        if os.environ.get('PRINS') and self.nins in range(int(os.environ.get('PRINS','0')), int(os.environ.get('PRINS','0')) + 4):
            print('INS', self.nins, ins.concise())
        self.cnt[e] += 1
        ins.then_inc(self.sem[e], 1)
        self._record((self.sem[e], self.cnt[e], e), reads, writes)
        self.nins += 1

    def dma(self, q, out, in_, key=None, nbytes=None, **kw):
        if self.sched:
            if nbytes is None:
                shp = _ap(out).shape
                nbytes = 4
                for d_ in shp:
                    nbytes *= d_
            self.pending.append(("dma", q, (out, in_, key, kw), _bufs(in_), _bufs(out), 2000.0 + nbytes / 100.0))
            return
        self._emit_dma(q, out, in_, key, **kw)

    def _emit_dma(self, q, out, in_, key=None, **kw):
        if self.limit is not None and self.nins >= self.limit:
            return
        reads, writes = _bufs(in_), _bufs(out)
        self._deps(q, reads, writes)
        kb = key.buf if key is not None else (out.buf if not isinstance(out.buf, DBuf) else in_.buf)
        if isinstance(kb, (list, tuple)):
            kb = kb[0]
        if kb.dsem is None:
            kb.dsem = self.nc.alloc_semaphore(f"d{self._uid()}_{kb.name}")
            self.dsems.append(kb)
        ins = self.engs[q].dma_start(out=_ap(out), in_=_ap(in_), **kw)
        kb.dcount += 1
        ins.then_inc(kb.dsem, 16)
        self._record((kb.dsem, 16 * kb.dcount, "dma"), reads, writes)
        self.nins += 1

    def flush(self):
        ops = self.pending
        self.pending = []
        n = len(ops)
        if n == 0:
            return
        SYNC = 200.0
        preds = [[] for _ in range(n)]
        lastw = {}
        rdrs = {}
        for i, (kind, e, fn, reads, writes, cost) in enumerate(ops):
            wr = list(writes) + [b for b in reads if b.excl and b not in writes]
            ps = set()
            for b in reads:
                if b in lastw:
                    ps.add(lastw[b])
            for b in wr:
                if b in lastw:
                    ps.add(lastw[b])
                for r_ in rdrs.get(b, ()):
                    ps.add(r_)
            ps.discard(i)
            preds[i] = list(ps)
            for b in reads:
                if b not in wr:
                    rdrs.setdefault(b, []).append(i)
            for b in wr:
                lastw[b] = i
                rdrs[b] = []
        succs = [[] for _ in range(n)]
        for i in range(n):
            for p in preds[i]:
                succs[p].append(i)
        occ = [0.0] * n
        lat = [0.0] * n
        for i, (kind, e, fn, reads, writes, cost) in enumerate(ops):
            if kind == "dma":
                occ[i] = 60.0
                lat[i] = cost
            else:
                occ[i] = cost
                lat[i] = cost
        blevel = [0.0] * n
        for i in range(n - 1, -1, -1):
            m_ = 0.0
            for s_ in succs[i]:
                if blevel[s_] > m_:
                    m_ = blevel[s_]
            blevel[i] = lat[i] + m_
        import heapq
        npred = [len(p) for p in preds]
        ready_t = [0.0] * n
        eng_free = {}
        readyq = {}
        for i in range(n):
            if npred[i] == 0:
                heapq.heappush(readyq.setdefault(ops[i][1], []), (-blevel[i], i))
        order = []
        done = 0
        while done < n:
            best = None
            for e, hq in readyq.items():
                if not hq:
                    continue
                tfree = eng_free.get(e, 0.0)
                cand = None
                top = heapq.nsmallest(6, hq)
                for pr, i in top:
                    st_ = max(tfree, ready_t[i])
                    key = (st_, pr)
                    if cand is None or key < cand[0]:
                        cand = (key, i)
                if best is None or cand[0] < best[0]:
                    best = (cand[0], cand[1], e)
            (st_, pr), i, e = best
            hq = readyq[e]
            hq.remove((-blevel[i], i))
            heapq.heapify(hq)
            eng_free[e] = st_ + occ[i]
            fin = st_ + lat[i]
            order.append(i)
            done += 1
            for s_ in succs[i]:
                rt = fin + (0.0 if ops[s_][1] == e else SYNC)
                if rt > ready_t[s_]:
                    ready_t[s_] = rt
                npred[s_] -= 1
                if npred[s_] == 0:
                    heapq.heappush(readyq.setdefault(ops[s_][1], []), (-blevel[s_], s_))
        for i in order:
            kind, e, fn, reads, writes, cost = ops[i]
            if kind == "dma":
                out, in_, key, kw = fn
                self._emit_dma(e, out, in_, key, **kw)
            else:
                self._emit_op(e, fn, reads, writes)

    def barrier(self):
        self.flush()
        for e in self.engs:
            for f in self.engs:
                if f != e and self.cnt[f] > 0:
                    self._wait(e, (self.sem[f], self.cnt[f], f))
            for kb in self.dsems:
                if kb.dcount > 0:
                    self._wait(e, (kb.dsem, 16 * kb.dcount, "dma"))

    @staticmethod
    def _fsz(v):
        shp = _ap(v).shape
        n = 1
        for d_ in shp[1:]:
            n *= d_
        return n

    def _ecost(self, e, out, in_):
        n = self._fsz(out)
        if e == "pool":
            return 150.0 + 2.0 * n
        if e == "act":
            return 220.0 + 0.72 * n
        return 100.0 + (1.05 * n if (_ap(in_).dtype == F32 or _ap(out).dtype == F32) else 0.6 * n)

    def mm(self, out, lhsT, rhs, start=True, stop=True):
        n = self._fsz(out)
        c = 40.0 + n * (1.9 if _ap(lhsT).dtype == F32 else 0.45)
        self.op("pe", lambda e: e.matmul(_ap(out), lhsT=_ap(lhsT), rhs=_ap(rhs), start=start, stop=stop),
                _bufs(lhsT, rhs) + ([] if start else _bufs(out)), _bufs(out), cost=c)

    def tr(self, out, in_, ident):
        n = self._fsz(out)
        c = 40.0 + n * (1.9 if _ap(in_).dtype == F32 else 0.45)
        self.op("pe", lambda e: e.transpose(out=_ap(out), in_=_ap(in_), identity=_ap(ident)), _bufs(in_, ident), _bufs(out), cost=c)

    def act(self, out, in_, func, bias=0.0, scale=1.0, accum=None, eng="act"):
        kw = {}
        if accum is not None:
            kw["accum_out"] = _ap(accum)
        self.op("act", lambda e: e.activation(out=_ap(out), in_=_ap(in_), func=func, bias=_ap(bias), scale=_ap(scale), **kw),
                _bufs(in_, bias, scale), _bufs(out, accum), cost=self._ecost("act", out, in_))

    def ts(self, e, out, in0, s1, s2=None, op0=ALU.mult, op1=None):
        if op1 is None:
            f = lambda g: g.tensor_scalar(out=_ap(out), in0=_ap(in0), scalar1=_ap(s1), scalar2=None, op0=op0)
        else:
            f = lambda g: g.tensor_scalar(out=_ap(out), in0=_ap(in0), scalar1=_ap(s1), scalar2=_ap(s2), op0=op0, op1=op1)
        self.op(e, f, _bufs(in0, s1, s2), _bufs(out), cost=self._ecost(e, out, in0))

    def tt(self, e, out, a, b, op):
        self.op(e, lambda g: g.tensor_tensor(out=_ap(out), in0=_ap(a), in1=_ap(b), op=op), _bufs(a, b), _bufs(out), cost=self._ecost(e, out, a))

    def stt(self, e, out, in0, scalar, in1, op0, op1):
        self.op(e, lambda g: g.scalar_tensor_tensor(out=_ap(out), in0=_ap(in0), scalar=_ap(scalar), in1=_ap(in1), op0=op0, op1=op1),
                _bufs(in0, scalar, in1), _bufs(out), cost=self._ecost(e, out, in0))

    def copy(self, e, out, in_):
        if e == "act":
            self.act(out, in_, AF.Copy)
        else:
            self.op(e, lambda g: g.tensor_copy(out=_ap(out), in_=_ap(in_)), _bufs(in_), _bufs(out), cost=self._ecost(e, out, in_))

    def recip(self, out, in_):
        self.op("dve", lambda g: g.reciprocal(out=_ap(out), in_=_ap(in_)), _bufs(in_), _bufs(out), cost=100.0 + 6.3 * self._fsz(out))

    def memset(self, e, out, val):
        self.op(e, lambda g: g.memset(_ap(out), val), [], _bufs(out))

    def asel(self, out, in_, pattern, cmp, fill, base, cm):
        self.op("pool", lambda g: g.affine_select(out=_ap(out), in_=_ap(in_), pattern=pattern, compare_op=cmp, fill=fill,
                                                  base=base, channel_multiplier=cm), _bufs(in_), _bufs(out))


class DBuf(Buf):
    __slots__ = ("is_dram",)

    def __init__(self, name):
        super().__init__(name)
        self.is_dram = True


def dramv(nc, name, shape, dt, kind):
    t = nc.dram_tensor(name, list(shape), dt, kind=kind)
    return V(t.ap(), DBuf(name))


def build(stage=99, dbg=None):
    nc = bass.Bass("TRN2", target_bir_lowering=False)
    k = K(nc)
    k.init_banks()
    if dbg and 'limit' in dbg:
        k.limit = dbg['limit']
    IN = lambda name, shape, dt=F32: dramv(nc, name, shape, dt, "ExternalInput")
    x_d = IN("x", [T, D])
    ctx_d = IN("ctx", [TC, D])
    cc_d = IN("cc", [128, 8, 2])
    wada_d = IN("w_ada", [D, 6 * D])
    bada_d = IN("b_ada", [128, 48])
    nrm_d = IN("norms", [128, 4, 8])
    win_d = IN("w_in", [D, INW])
    cqkv_d = IN("conv_qkv", [128, 24, 3])
    gpar_d = IN("gpar", [128, 2, 16])
    dnn_d = IN("dn_norm", [128, 1])
    wf_d = IN("w_fourier", [512, D])
    wdn_d = IN("w_dn", [D, D])
    wout_d = IN("w_out", [D, D])
    wup_d = IN("w_up", [D, 2 * DFF])
    cffn_d = IN("conv_ffn", [128, NFF, 9])
    wdown_d = IN("w_down", [DFF, D])
    cos_d = IN("dft_cos", [T, T], BF16)
    sin_d = IN("dft_sin", [T, T], BF16)
    c128_d = IN("dft128", [128, 256], BF16)
    hmask_d = IN("hmask", [128, 7, 128], BF16)
    out_d = dramv(nc, "out", [T, D], F32, "ExternalOutput")
    dbg_out = {}
    if dbg:
        for nm, shp in dbg.items():
            if nm in ("heads", "nsteps", "limit"):
                continue
            dbg_out[nm] = dramv(nc, "dbg_" + nm, shp, F32, "ExternalOutput")
    x1_d = dramv(nc, "x1_scr", [T, D], F32, "Internal")
    oT_d = dramv(nc, "oT_scr", [H, 128, T], BF16, "Internal")
    gT_d = dramv(nc, "gT_scr", [NFF, 128, T], BF16, "Internal")
    yT_d = dramv(nc, "yT_scr", [4, 128, T], BF16, "Internal")

    identf = k.sb("identf", [128, 128], F32)
    ident = k.sb("ident", [128, 128], BF16)
    onesf = k.sb("onesf", [128, 128], F32)
    onesb = k.sb("onesb", [128, 128], BF16)
    negm = [k.sb(f"negm{d}", [128, 128], F32) for d in range(2)]
    smask = [k.sb(f"smask{d}", [128, 128], F32) for d in range(2)]
    ut = [k.sb(f"ut{d}", [128, 128], F32) for d in range(2)]
    zerof = k.sb("zerof", [128, 128], F32)
    scal_t = k.sb("scal_t", [128, 8], F32)
    k.memset("pool", zerof, 0.0)
    k.memset("pool", onesf, 1.0)
    k.copy("dve", onesb, onesf)
    k.asel(identf, zerof, [[-1, 128]], ALU.not_equal, 1.0, 0, 1)
    k.copy("dve", ident, identf)
    k.asel(negm[0], zerof, [[1, 128]], ALU.is_ge, -1e5, 0, -1)
    k.asel(smask[0], onesf, [[1, 128]], ALU.is_gt, 0.0, 0, -1)
    k.asel(ut[0], onesf, [[1, 128]], ALU.is_ge, 0.0, 0, -1)
    k.asel(negm[1], zerof, [[-1, 128]], ALU.is_ge, -1e5, 0, 1)
    k.asel(smask[1], onesf, [[-1, 128]], ALU.is_gt, 0.0, 0, 1)
    k.asel(ut[1], onesf, [[-1, 128]], ALU.is_ge, 0.0, 0, 1)

    nrm = k.sb("nrm", [128, 4, 8], F32)
    k.dma("sp", nrm, nrm_d)
    cqkv = k.sb("cqkv", [128, 24, 3], F32)
    k.dma("sp", cqkv, cqkv_d)
    gpar = k.sb("gpar", [128, 2, 16], F32)
    k.dma("sp", gpar, gpar_d)
    dnn = k.sb("dnn", [128, 1], F32)
    k.dma("sp", dnn, dnn_d)
    cffn = k.sb("cffn", [128, NFF, 9], F32)
    k.dma("sp", cffn, cffn_d)
    c128 = k.sb("c128", [128, 256], BF16)
    k.dma("sp", c128, c128_d)
    hmask = k.sb("hmask", [128, 7, 128], BF16)
    k.dma("sp", hmask, hmask_d)
    bada = k.sb("bada", [128, 48], F32)
    k.dma("sp", bada, bada_d)
    cc = k.sb("cc", [128, 8, 2], F32)
    k.dma("sp", cc, cc_d)

    mod = k.sb("mod", [128, 48, 2], F32)
    scc = k.sb("scc", [128, 8, 2], F32)
    k.act(scc, cc, AF.Silu)
    with ExitStack() as st:
        k.stack = st
        wa = [k.sb(f"wa{i}", [128, 8, 512], F32) for i in range(2)]
        pm = k.pv(0, 0, 96, F32, 2)
        wv = wada_d.ap.rearrange("(k p) c -> p k c", p=128)
        for blk in range(12):
            w = wa[blk % 2]
            k.dma("sp" if blk % 2 == 0 else "pool", w, V(wv[:, :, blk * 512:(blk + 1) * 512], wada_d.buf))
            for oc in range(4):
                for kk in range(8):
                    k.mm(pm[:, blk * 4 + oc, :], w[:, kk, oc * 128:(oc + 1) * 128], scc[:, kk, :], start=(kk == 0), stop=(kk == 7))
        for j in range(2):
            k.tt("dve", mod[:, :, j], pm[:, :, j], bada, ALU.add)
        k.barrier()
    k.stack = ExitStack()
    coef = k.sb("coef", [128, 8, 8], F32)
    def modc(i, j):
        return mod[:, i * 8:(i + 1) * 8, j]
    k.stt("dve", coef[:, 0, :], modc(1, 0), 1.0, nrm[:, 0, :], ALU.add, ALU.mult)
    k.copy("dve", coef[:, 1, :], modc(0, 0))
    k.stt("dve", coef[:, 2, :], modc(1, 1), 1.0, nrm[:, 0, :], ALU.add, ALU.mult)
    k.copy("dve", coef[:, 3, :], modc(0, 1))
    k.tt("dve", coef[:, 4, :], modc(2, 0), nrm[:, 1, :], ALU.mult)
    k.stt("dve", coef[:, 5, :], modc(4, 0), 1.0, nrm[:, 2, :], ALU.add, ALU.mult)
    k.copy("dve", coef[:, 6, :], modc(3, 0))
    k.tt("dve", coef[:, 7, :], modc(5, 0), nrm[:, 3, :], ALU.mult)
    if "coef" in dbg_out:
        k.dma("sp", dbg_out["coef"], coef)
    if stage <= 0:
        return finish(nc, k, out_d)

    s_h = ExitStack()
    k.stack = s_h
    hT = k.sb("hT", [128, 8, TA], BF16)
    hbuf = [Buf(f"hT{t}") for t in range(NT)]

    def norm_tile(src_tile_v, tile_idx, ca, cb, dst, dstbufs, tm):
        nb = len(tm["sq"])
        sq = tm["sq"][tile_idx % nb]
        ss = tm["ss"][tile_idx % nb]
        xn = tm["xn"][tile_idx % nb]
        pt = tm["pt"][tile_idx % len(tm["pt"])]
        k.act(sq, src_tile_v, AF.Square, accum=ss)
        k.act(ss, ss, AF.Sqrt, bias=EPS, scale=1.0 / D)
        k.recip(ss, ss)
        k.ts("dve", xn, src_tile_v, ss[:, 0:1])
        for c in range(8):
            k.tr(pt[:, c, :], xn[:, c * 128:(c + 1) * 128], ident)
        for c in range(8):
            dv = V(dst.ap[:, c, tile_idx * 128:(tile_idx + 1) * 128], dstbufs[tile_idx] if isinstance(dstbufs, list) else dstbufs)
            if c % 2 == 0:
                k.ts("dve", dv, pt[:, c, :], coef[:, ca, c:c + 1], coef[:, cb, c:c + 1], ALU.mult, ALU.add)
            else:
                k.act(dv, pt[:, c, :], AF.Identity, bias=coef[:, cb, c:c + 1], scale=coef[:, ca, c:c + 1])

    p1 = ExitStack()
    k.stack = p1
    nt_sq = [k.sb(f"nt_sq{i}", [128, D], F32) for i in range(2)]
    nt_ss = [k.sb(f"nt_ss{i}", [128, 1], F32) for i in range(2)]
    nt_xn = [k.sb(f"nt_xn{i}", [128, D], BF16) for i in range(2)]
    xin = [k.sb(f"xin{i}", [128, D], F32) for i in range(3)]
    nt_pt = [k.pv(i, 0, 512, BF16, 128) for i in range(2)]
    tm1 = {"sq": nt_sq, "ss": nt_ss, "xn": nt_xn, "pt": nt_pt}
    for t in range(NT):
        xi = xin[t % 3]
        if t < 2:
            src = V(ctx_d.ap[t * 128:(t + 1) * 128, :], ctx_d.buf)
        else:
            src = V(x_d.ap[(t - 2) * 128:(t - 1) * 128, :], x_d.buf)
        k.dma("sp" if t % 2 == 0 else "pool", xi, src)
        norm_tile(xi, t, 2 if t < 2 else 0, 3 if t < 2 else 1, hT, hbuf, tm1)
    k.barrier()
    p1.close()
    k.stack = ExitStack()
    if "hT" in dbg_out:
        with ExitStack() as st:
            k.stack = st
            tmpf = k.sb("dbg_hT", [128, 8, 1024], F32)
            k.copy("dve", tmpf, V(hT.ap[:, :, 0:1024], None))
            k.dma("sp", dbg_out["hT"], tmpf)
            k.barrier()
        k.stack = ExitStack()
    hTv = V(hT.ap, Buf("hT_all"))
    if stage <= 1:
        return finish(nc, k, out_d)

    winv = win_d.ap.rearrange("(k p) c -> p k c", p=128)

    def bcast_t(v, n):
        return V(v.ap.unsqueeze(1).to_broadcast([128, n, 16]), v.buf)

    s_g = ExitStack()
    k.stack = s_g
    beta = k.sb("beta", [128, NT, 16], F32)
    ngc = k.sb("ngc", [128, NT, 16], F32)
    ngcb = k.sb("ngcb", [128, NT, 16], F32)
    egc = k.sb("egc", [128, NT, 16], F32)
    ekt = k.sb("ekt", [128, NT, 16], F32)
    egl = k.sb("egl", [128, NT, 16], F32)
    with ExitStack() as st:
        k.stack = st
        wbaf = k.sb("wbaf", [128, 8, 32], F32)
        wba = k.sb("wba", [128, 8, 32], BF16)
        graw = k.sb("graw", [128, NT, 32], F32)
        gg = k.sb("gg", [128, NT, 16], F32)
        gtmp = k.sb("gtmp", [128, NT, 16], F32)
        lnb = k.sb("lnb", [128, NT, 16], F32)
        negA = k.sb("negA", [128, 16], F32)
        pg = k.pv(0, 0, 512, F32, 32)
        pc0, pc1, pt0, pt1 = k.banks[1], k.banks[2], k.banks[3], k.banks[4]
        k.dma("sp", wbaf, V(winv[:, :, OFF_B:OFF_B + 32], win_d.buf))
        k.copy("dve", wba, wbaf)
        for g0 in range(0, NT, 16):
            n = min(16, NT - g0)
            for j in range(n):
                t = g0 + j
                for kk in range(8):
                    k.mm(pg[:, j, :], hTv[:, kk, t * 128:(t + 1) * 128], wba[:, kk, :], start=(kk == 0), stop=(kk == 7))
            k.copy("act", graw[:, g0:g0 + n, :], pg[:, 0:n, :])
        k.act(beta, graw[:, :, 0:16], AF.Sigmoid)
        k.act(lnb, graw[:, :, 0:16], AF.Exp, scale=-1.0)
        k.act(lnb, lnb, AF.Ln, bias=1.0)
        k.act(negA, gpar[:, 0, :], AF.Exp)
        k.ts("dve", negA, negA, -1.0)
        k.tt("dve", gg, graw[:, :, 16:32], bcast_t(gpar[:, 1, :], NT), ALU.add)
        k.act(gg, gg, AF.Exp)
        k.act(gg, gg, AF.Ln, bias=1.0)
        k.tt("dve", gg, gg, bcast_t(negA, NT), ALU.mult)
        if "g" in dbg_out:
            k.dma("sp", dbg_out["g"], gg)
            k.dma("sp", dbg_out["beta"], beta)
        pcs = [pc0, pc1]
        pts = [pt0, pt1]
        for d in range(2):
            pcv = V(pcs[d].ap[:, 0:NT * 8].rearrange("p (t c) -> p t c", c=8), pcs[d].buf)
            ptv = V(pts[d].ap[:, 0:NT * 8].rearrange("p (t c) -> p t c", c=8), pts[d].buf)
            k.mm(pcv, ut[d], gg[:, :, d * 8:(d + 1) * 8])
            k.mm(ptv, onesf, gg[:, :, d * 8:(d + 1) * 8])
            sl = slice(d * 8, (d + 1) * 8)
            k.act(ngc[:, :, sl], pcv, AF.Copy, scale=-1.0)
            k.stt("dve", ngcb[:, :, sl], pcv, -1.0, lnb[:, :, sl], ALU.mult, ALU.subtract)
            k.act(egc[:, :, sl], pcv, AF.Exp)
            k.tt("dve", gtmp[:, :, sl], ptv, ngc[:, :, sl], ALU.add)
            k.act(ekt[:, :, sl], gtmp[:, :, sl], AF.Exp)
            k.act(egl[:, :, sl], ptv, AF.Exp)
        k.barrier()
    k.stack = ExitStack()
    if stage <= 2:
        return finish(nc, k, out_d)

    blocks = [(0, TC)] + [(TC + 512 * i, 512) for i in range(8)]
    xblocks = blocks[1:]
    poff = lambda tok: 1 + tok if tok < TC else 3 + tok
    heads = list(range(H)) if dbg is None or "heads" not in dbg else dbg["heads"]
    p4 = ExitStack()
    k.stack = p4
    ring = [[k.sb(f"ring{ty}_{i}", [128, 514], BF16) for i in range(3)] for ty in range(3)]
    qT = k.sb("qT", [128, TA], BF16)
    kT = k.sb("kT", [128, TA], BF16)
    vT = k.sb("vT", [128, TA], BF16)
    zs_b = [k.sb(f"zsb{i}", [128, 512], BF16) for i in range(1)]
    osum = k.sb("osum", [128, T], F32)
    wst_f = [k.sb(f"wstf{i}", [128, 8, 128], F32) for i in range(1)]
    wst_b = [k.sb(f"wstb{i}", [128, 8, 128], BF16) for i in range(3)]
    dgt3 = [k.sb(f"dgt{ty}", [128, 3, 128], BF16) for ty in range(3)]
    wbz = k.sb("wbz", [128, 8, 128], BF16)
    rn = [k.sb(f"rn{i}", [128, 512], F32) for i in range(1)]
    ofin = [k.sb(f"ofin{i}", [128, 512], BF16) for i in range(2)]
    osb = [Buf(f"osum{t}") for t in range(32)]
    pbig = [k.banks[0], k.banks[1]]
    GS = 3
    rot = [[k.banks[3 * d + i] for i in range(3)] for d in range(2)]
    rcnt = [0, 0]
    def nextb(d):
        rcnt[d] += 1
        return rot[d][rcnt[d] % 3]
    def b3(bank, n, dt=F32, off=0):
        if dt == F32:
            ap = bank.ap[:, off * 128:(off + n) * 128].rearrange("p (a b) -> p a b", b=128)
        else:
            ap = bank.ap[:, off * 64:(off + n) * 64].bitcast(BF16).rearrange("p (a b) -> p a b", b=128)
        return V(ap, bank.buf)
    pvn = [k.pv(6 + d, 0, 128) for d in range(2)]
    poT = [k.pv(6 + d, 128, 256) for d in range(2)]
    pS = [k.pv(6 + d, 256, 384) for d in range(2)]
    def gtmp(name, dt, nb=1):
        return [[k.sb(f"{name}{d}_{i}", [128, GS, 128], dt) for i in range(nb)] for d in range(2)]
    g_dgc, g_E = gtmp("dgc", F32), gtmp("E", F32)
    g_eg, g_Rk0, g_Q0, g_Q0T, g_Z = (gtmp(nm, BF16) for nm in ("eg", "Rk0", "Q0", "Q0T", "Z"))
    g_E1, g_E2 = gtmp("E1", BF16), gtmp("E2", BF16)
    g_LT, g_D, g_G = gtmp("LT", BF16, 2), gtmp("D", BF16, 2), gtmp("G", BF16, 2)
    g_W, g_nwT, g_Vt, g_ktail, g_qhT, g_qkm = (gtmp(nm, BF16, 2) for nm in ("W", "nwT", "Vt", "ktail", "qhT", "qkm"))
    t_vn = [k.sb(f"vn{d}", [128, 128], BF16) for d in range(2)]
    Sf = [k.sb(f"Sf{d}", [128, 128], F32) for d in range(2)]
    Sb = [k.sb(f"Sb{d}", [128, 128], BF16) for d in range(2)]
    def bcn(v, n):
        return V(v.ap.unsqueeze(1).to_broadcast([128, n, 128]), v.buf)
    nbig = [0]
    def nextbig():
        nbig[0] += 1
        return pbig[nbig[0] % 2]
    wcnt = [0]

    def load_w(col0):
        i = wcnt[0] % 3
        wcnt[0] += 1
        k.dma("sp", wst_f[0], V(winv[:, :, col0:col0 + 128], win_d.buf))
        k.copy("pool", wst_b[i], wst_f[0])
        return wst_b[i]

    order_f = list(range(NT))
    order_b = [1, 0] + list(range(NT - 1, 1, -1))

    for h in heads:
        blkbuf = [[Buf(f"qkv{ty}_{b_}") for b_ in range(9)] for ty in range(3)]
        blk_of = lambda t: 0 if t < 2 else 1 + (t - 2) // 4
        dsts = (qT, kT, vT)
        def tv(ty, t):
            return V(dsts[ty].ap[:, t * 128:(t + 1) * 128], blkbuf[ty][blk_of(t)])
        def tlv(ty, a_, n_):
            bl = []
            for t_ in range(a_, a_ + n_):
                if blkbuf[ty][blk_of(t_)] not in bl:
                    bl.append(blkbuf[ty][blk_of(t_)])
            return V(dsts[ty].ap[:, a_ * 128:(a_ + n_) * 128].rearrange("p (a b) -> p a b", b=128), bl)
        pcnt = [0]
        def nextp():
            pcnt[0] += 1
            return k.banks[pcnt[0] % 6]
        wbs = [load_w(OFF_Q + (ty * 8 + h) * 128) for ty in range(3)]
        for ty in range(3):
            for tap in range(3):
                k.ts("pool", dgt3[ty][:, tap, :], identf, cqkv[:, ty * 8 + h, tap:tap + 1])

        def conv_block(ty, b_):
            s, n = blocks[b_]
            slot = ring[ty][b_ % 3]
            pp = nextp()
            for tap in range(3):
                k.mm(pp[:, 0:n], dgt3[ty][:, tap, :], slot[:, tap:tap + n], start=(tap == 0), stop=(tap == 2))
            dv = V(dsts[ty].ap[:, s:s + n], blkbuf[ty][b_])
            k.act(dv, pp[:, 0:n], AF.Silu)
            if ty < 2:
                pp2 = nextp()
                r = rn[0]
                sqv = ofin[(b_ + ty) % 2]
                k.tt("pool", sqv[:, 0:n], dv, dv, ALU.mult)
                k.mm(pp2[:, 0:n], onesb, sqv[:, 0:n])
                k.act(r[:, 0:n], pp2[:, 0:n], AF.Ln, bias=EPS)
                k.act(r[:, 0:n], r[:, 0:n], AF.Exp, scale=-0.5)
                k.stt("dve", dv, dv, (128.0 ** -0.5) if ty == 0 else 1.0, r[:, 0:n], ALU.mult, ALU.mult)

        for b_ in range(9):
            s, n = blocks[b_]
            for ty in range(3):
                slot = ring[ty][b_ % 3]
                prev = ring[ty][(b_ - 1) % 3]
                pp = nextp()
                for kk in range(8):
                    k.mm(pp[:, 0:n], wbs[ty][:, kk, :], hTv[:, kk, s:s + n], start=(kk == 0), stop=(kk == 7))
                k.copy("dve", slot[:, 1:1 + n], pp[:, 0:n])
                if b_ in (0, 1):
                    k.memset("pool", slot[:, 0:1], 0.0)
                else:
                    k.copy("pool", slot[:, 0:1], prev[:, 512:513])
                if b_ in (0, 8):
                    k.memset("pool", slot[:, n + 1:n + 2], 0.0)
                if b_ >= 2:
                    k.copy("pool", prev[:, 513:514], slot[:, 1:2])
                if b_ == 0:
                    conv_block(ty, 0)
                elif b_ >= 2:
                    conv_block(ty, b_ - 1)
                if b_ == 8:
                    conv_block(ty, 8)
        if "qkv" in dbg_out and h == heads[0]:
            with ExitStack() as st2:
                old = k.stack
                k.stack = st2
                tf_ = k.sb("dbgqkv", [128, 3, 512], F32)
                k.copy("dve", tf_[:, 0, :], qT[:, 0:512])
                k.copy("dve", tf_[:, 1, :], kT[:, 0:512])
                k.copy("dve", tf_[:, 2, :], vT[:, 0:512])
                k.dma("sp", dbg_out["qkv"], tf_)
                k.barrier()
                k.stack = old
        for d in range(2):
            k.memset("pool", Sf[d], 0.0)
            k.memset("pool", Sb[d], 0.0)
        visited = set()
        groups = [(0, 2)] + [(2 + 3 * i, 3) for i in range(10)] + [(32, 2)]
        if dbg is not None and "nsteps" in dbg:
            groups = groups[:dbg["nsteps"]]
        gorder = [groups, [groups[0]] + groups[:0:-1]]

        def pre_gen(d, a, n, gp):
            col = d * 8 + h
            isx = a >= 2
            def bc(arr):
                return V(arr.ap[:, a:a + n, col].unsqueeze(2).to_broadcast([128, n, 128]), arr.buf)
            tsl = lambda i: slice((a + i) * 128, (a + i + 1) * 128)
            dgc, E, eg, Rk0, Q0, Q0T, Z = (x[d][0][:, 0:n, :] for x in (g_dgc, g_E, g_eg, g_Rk0, g_Q0, g_Q0T, g_Z))
            LT, Dm, Gm = g_LT[d], g_D[d], g_G[d]
            tm_ = LT[0][:, 0:n, :]
            W, nwT, Vt, ktail, qhT, qkm = (x[d][gp][:, 0:n, :] for x in (g_W, g_nwT, g_Vt, g_ktail, g_qhT, g_qkm))
            pA = nextb(d)
            pAk, pAv = b3(pA, n, BF16, 0), b3(pA, n, BF16, GS)
            for i in range(n):
                k.tr(pAk[:, i, :], tv(1, a + i), ident)
                k.tr(pAv[:, i, :], tv(2, a + i), ident)
            yield
            k.act(Vt, pAv, AF.Copy)
            k.tt("dve", Rk0, pAk, bc(egc), ALU.mult)
            k.tt("dve", ktail, pAk, bc(ekt), ALU.mult)
            yield
            E1, E2, tm2_ = g_E1[d][0][:, 0:n, :], g_E2[d][0][:, 0:n, :], Z
            k.tt("pool", dgc, bcn(identf, n), bc(ngc), ALU.mult)
            pB = nextb(d)
            pBv = b3(pB, n)
            for i in range(n):
                k.mm(pBv[:, i, :], onesf, dgc[:, i, :])
            yield
            k.tt("dve", E, bcn(negm[d], n), pBv, ALU.subtract)
            for i in range(n):
                k.act(E2[:, i, :], E[:, i, :], AF.Exp, bias=ngcb[:, a + i, col:col + 1])
            if isx:
                for i in range(n):
                    k.act(E1[:, i, :], E[:, i, :], AF.Exp, bias=ngc[:, a + i, col:col + 1])
                k.act(eg, pBv, AF.Exp, scale=-1.0)
                k.tt("pool", qhT, tlv(0, a, n), eg, ALU.mult)
            yield
            pC = nextb(d)
            pCv = b3(pC, n)
            for i in range(n):
                k.mm(pCv[:, i, :], tv(1, a + i), tv(1, a + i))
            k.tt("dve", Q0, pCv, E2, ALU.mult)
            if isx:
                pQ = nextb(d)
                pQv = b3(pQ, n)
                for i in range(n):
                    k.mm(pQv[:, i, :], tv(1, a + i), tv(0, a + i))
                k.tt("dve", qkm, pQv, E1, ALU.mult)
            yield
            pT = nextb(d)
            pTv = b3(pT, n, BF16, 0)
            for i in range(n):
                k.tr(pTv[:, i, :], Q0[:, i, :], ident)
            k.act(Q0T, pTv, AF.Copy)
            yield
            k.tt("dve", tm_, Q0, bcn(hmask[:, 0, :], n), ALU.mult)
            k.tt("dve", Dm[0][:, 0:n, :], bcn(ident, n), tm_, ALU.subtract)
            k.tt("pool", tm2_, Q0T, bcn(hmask[:, 0, :], n), ALU.mult)
            k.tt("pool", Gm[0][:, 0:n, :], bcn(ident, n), tm2_, ALU.subtract)
            k.tt("pool", LT[1][:, 0:n, :], Q0T, bcn(hmask[:, 1, :], n), ALU.mult)
            yield
            for m_ in range(1, 7):
                a_, b_ = (m_ - 1) % 2, m_ % 2
                Da, Ga, Lm = Dm[a_][:, 0:n, :], Gm[a_][:, 0:n, :], LT[m_ % 2][:, 0:n, :]
                pz = nextb(d)
                pzv = b3(pz, n)
                for i in range(n):
                    k.mm(pzv[:, i, :], Lm[:, i, :], Da[:, i, :])
                if m_ < 6:
                    k.tt("pool", LT[(m_ + 1) % 2][:, 0:n, :], Q0T, bcn(hmask[:, m_ + 1, :], n), ALU.mult)
                k.act(Z, pzv, AF.Copy)
                yield
                pd_ = nextb(d)
                pdv = b3(pd_, n)
                for i in range(n):
                    k.mm(pdv[:, i, :], Ga[:, i, :], Z[:, i, :])
                if m_ < 6:
                    pg_ = nextb(d)
                    pgv = b3(pg_, n)
                    for i in range(n):
                        k.mm(pgv[:, i, :], Z[:, i, :], Ga[:, i, :])
                k.tt("dve", W if m_ == 6 else Dm[b_][:, 0:n, :], Da, pdv, ALU.subtract)
                if m_ < 6:
                    k.tt("dve", Gm[b_][:, 0:n, :], Ga, pgv, ALU.subtract)
                yield
            pw = nextb(d)
            pwv = b3(pw, n)
            for i in range(n):
                k.mm(pwv[:, i, :], Rk0[:, i, :], W[:, i, :])
            k.act(nwT, pwv, AF.Copy, scale=-1.0)
            yield

        def state_gen(d, a, n, gp):
            col = d * 8 + h
            isx = a >= 2
            W, nwT, Vt, ktail, qhT, qkm = (x[d][gp] for x in (g_W, g_nwT, g_Vt, g_ktail, g_qhT, g_qkm))
            vn = t_vn[d]
            for i in (range(n) if d == 0 else range(n - 1, -1, -1)):
                t = a + i
                sc = lambda arr: arr[:, t, col:col + 1]
                k.mm(pvn[d], W[:, i, :], Vt[:, i, :], start=True, stop=False)
                k.mm(pvn[d], nwT[:, i, :], Sb[d], start=False, stop=True)
                k.act(vn, pvn[d], AF.Identity, scale=sc(beta))
                yield
                if isx:
                    xt = t - 2
                    ov = V(osum.ap[:, xt * 128:(xt + 1) * 128], osb[xt])
                    k.mm(poT[d], Sb[d], qhT[:, i, :], start=True, stop=False)
                    k.mm(poT[d], vn, qkm[:, i, :], start=False, stop=True)
                k.mm(pS[d], ktail[:, i, :], vn)
                if isx:
                    if xt not in visited:
                        visited.add(xt)
                        k.act(ov, poT[d], AF.Copy)
                    else:
                        k.tt("dve", ov, poT[d], ov, ALU.add)
                k.stt("dve", Sf[d], Sf[d], sc(egl), pS[d], ALU.mult, ALU.add)
                k.act(Sb[d], Sf[d], AF.Copy)
                yield

        def run_all(gens):
            gens = list(gens)
            while gens:
                for g_ in list(gens):
                    try:
                        next(g_)
                    except StopIteration:
                        gens.remove(g_)

        ng = len(groups)
        run_all([pre_gen(0, *gorder[0][0], 0), pre_gen(1, *gorder[1][0], 0)])
        for gi in range(ng):
            gens = [state_gen(0, *gorder[0][gi], gi % 2), state_gen(1, *gorder[1][gi], gi % 2)]
            if gi + 1 < ng:
                gens += [pre_gen(0, *gorder[0][gi + 1], (gi + 1) % 2), pre_gen(1, *gorder[1][gi + 1], (gi + 1) % 2)]
            run_all(gens)
        if "S" in dbg_out and h == heads[0]:
            k.dma("sp", dbg_out["S"][0], Sf[0])
            k.dma("sp", dbg_out["S"][1], Sf[1])
        for bi in range(8):
            s = bi * 512
            ovs = [V(osum.ap[:, s:s + 512], osb[bi * 4 + j]) for j in range(4)]
            class _M:
                pass
            ovall = V(osum.ap[:, s:s + 512], osb[bi * 4])
            extra = [osb[bi * 4 + j] for j in range(1, 4)]
            if bi == 0:
                k.dma("sp", wst_f[0], V(winv[:, :, OFF_Z + h * 128:OFF_Z + (h + 1) * 128], win_d.buf))
                k.copy("pool", wbz, wst_f[0])
            pz_ = nextbig()
            zsb = zs_b[0]
            for kk in range(8):
                k.mm(pz_, wbz[:, kk, :], hTv[:, kk, TC + s:TC + s + 512], start=(kk == 0), stop=(kk == 7))
            k.act(zsb, pz_, AF.Silu)
            pp = nextbig()
            sqv = ofin[bi % 2]
            r = rn[0]
            of_ = V(osum.ap[:, s:s + 512], [osb[bi * 4 + j] for j in range(4)])
            k.op("pool", lambda g, sqv=sqv, s=s: g.tensor_tensor(out=sqv.ap, in0=osum.ap[:, s:s + 512], in1=osum.ap[:, s:s + 512], op=ALU.mult),
                 [osb[bi * 4 + j] for j in range(4)], [sqv.buf])
            k.mm(pp, onesb, sqv)
            k.act(r, pp, AF.Ln, bias=EPS, scale=1.0 / 128)
            k.act(r, r, AF.Exp, scale=-0.5)
            k.stt("dve", of_, of_, dnn[:, 0:1], r, ALU.mult, ALU.mult)
            if "o0" in dbg_out and h == heads[0]:
                k.tt("pool", of_, of_, zsb, ALU.mult)
                k.dma("sp", V(dbg_out["o0"].ap[:, s:s + 512], dbg_out["o0"].buf), of_)
                k.copy("pool", sqv, of_)
            else:
                k.tt("pool", sqv, of_, zsb, ALU.mult)
            k.dma("sp", V(oT_d.ap[h, :, s:s + 512], oT_d.buf), sqv)
    k.barrier()
    p4.close()
    s_g.close()
    k.stack = ExitStack()
    if stage <= 4:
        return finish(nc, k, out_d)

    def wchunk_loader(stf, stb):
        cnt = [0]
        def load(src_v, K):
            i = cnt[0] % len(stf)
            j = cnt[0] % len(stb)
            cnt[0] += 1
            k.dma("sp" if i == 0 else "pool", stf[i][:, 0:K, :], src_v)
            k.copy("pool", stb[j][:, 0:K, :], stf[i][:, 0:K, :])
            return stb[j]
        return load

    def load_resident(dst_bf, src_ap, src_buf, K, ncols, stg):
        i = 0
        for k0 in range(0, K, 8):
            kn = min(8, K - k0)
            for c0 in range(0, ncols, 512):
                cn = min(512, ncols - c0)
                st_ = stg[i % len(stg)]
                k.dma("sp" if i % 2 == 0 else "pool", st_[:, 0:kn, 0:cn], V(src_ap[:, k0:k0 + kn, c0:c0 + cn], src_buf))
                k.copy("pool" if i % 2 == 0 else "dve", dst_bf[:, k0:k0 + kn, c0:c0 + cn], st_[:, 0:kn, 0:cn])
                i += 1

    with ExitStack() as st:
        k.stack = st
        FCS = k.sb("FCS", [128, 32, 4, 256], BF16)
        fT = [k.sb(f"fT{i}", [128, 512], BF16) for i in range(2)]
        stf = [k.sb(f"p3stf{i}", [128, 8, 128], F32) for i in range(2)]
        stb = [k.sb(f"p3stb{i}", [128, 8, 128], BF16) for i in range(2)]
        ctab = [k.sb(f"ctab{i}", [128, 4, 512], BF16) for i in range(2)]
        stab = [k.sb(f"stab{i}", [128, 4, 512], BF16) for i in range(2)]
        yblk = [k.sb(f"yblk{i}", [128, 4, 512], BF16) for i in range(2)]
        lw = wchunk_loader(stf, stb)
        nb_ = 0
        for g in range(4):
            wb = lw(V(winv[:, :, OFF_F + g * 128:OFF_F + (g + 1) * 128], win_d.buf), 8)
            for bi, (s, n) in enumerate(xblocks):
                pp = k.banks[nb_ % 2]
                ft = fT[nb_ % 2]
                nb_ += 1
                for kk in range(8):
                    k.mm(pp, wb[:, kk, :], hTv[:, kk, s:s + n], start=(kk == 0), stop=(kk == 7))
                k.act(ft, pp, AF.Copy)
                pf = k.banks[2 + (nb_ % 2)]
                pfv = V(pf.ap.rearrange("p (a b) -> p a b", b=256), pf.buf)
                for j2 in range(2):
                    for jj in range(2):
                        j = j2 * 2 + jj
                        k.mm(pfv[:, jj, :], ft[:, j * 128:(j + 1) * 128], c128)
                    t0 = bi * 4 + j2 * 2
                    if j2 == 0:
                        k.act(FCS[:, t0:t0 + 2, g, :], pfv, AF.Copy)
                    else:
                        k.copy("dve", FCS[:, t0:t0 + 2, g, :], pfv)
        k.barrier()
        cosv = cos_d.ap.rearrange("(tt p) f -> p tt f", p=128)
        sinv = sin_d.ap.rearrange("(tt p) f -> p tt f", p=128)
        ld = 0
        for kb in range(8):
            for t4 in range(8):
                ct, st_ = ctab[ld % 2], stab[ld % 2]
                ld += 1
                k.dma("sp", ct, V(cosv[:, t4 * 4:(t4 + 1) * 4, kb * 512:(kb + 1) * 512], cos_d.buf))
                k.dma("pool", st_, V(sinv[:, t4 * 4:(t4 + 1) * 4, kb * 512:(kb + 1) * 512], sin_d.buf))
                for ti in range(4):
                    tt_ = t4 * 4 + ti
                    for g in range(4):
                        k.mm(k.banks[4 + g], FCS[:, tt_, g, 0:128], ct[:, ti, :], start=(tt_ == 0), stop=False)
                        k.mm(k.banks[4 + g], FCS[:, tt_, g, 128:256], st_[:, ti, :], start=False, stop=(tt_ == 31))
            yb = yblk[kb % 2]
            for g in range(4):
                if g % 2 == 0:
                    k.act(yb[:, g, :], k.banks[4 + g], AF.Copy)
                else:
                    k.copy("dve", yb[:, g, :], k.banks[4 + g])
            k.dma("sp", V(yT_d.ap[:, :, kb * 512:(kb + 1) * 512].rearrange("g p f -> p g f"), yT_d.buf), yb)
        if "fm" in dbg_out:
            pass
        k.barrier()
    k.stack = ExitStack()
    if stage <= 5:
        return finish(nc, k, out_d)

    with ExitStack() as st:
        k.stack = st
        wg = k.sb("wg", [128, 8, 2048], BF16)
        wf4 = k.sb("wf4", [128, 4, 1024], BF16)
        wdn = k.sb("wdn", [128, 8, 1024], BF16)
        stg = [k.sb(f"p5stg{i}", [128, 8, 512], F32) for i in range(1)]
        ytb = [k.sb(f"ytb{i}", [128, 4, 512], BF16) for i in range(2)]
        otb = [k.sb(f"otb{i}", [128, 8, 512], BF16) for i in range(2)]
        g0 = [k.sb(f"g0_{i}", [128, 512], BF16) for i in range(2)]
        g1 = [k.sb(f"g1_{i}", [128, 512], BF16) for i in range(2)]
        m0 = [k.sb(f"m0_{i}", [128, 512], BF16) for i in range(2)]
        m1 = [k.sb(f"m1_{i}", [128, 512], BF16) for i in range(2)]
        mixs = k.sb("mixs", [128, 2, 8, 512], BF16)
        mixs_buf = [Buf("mixs0"), Buf("mixs1")]
        load_resident(wg, winv[:, :, OFF_G:OFF_G + 2048], win_d.buf, 8, 2048, stg)
        load_resident(wf4, wf_d.ap.rearrange("(g p) d -> p g d", p=128), wf_d.buf, 4, 1024, stg)
        load_resident(wdn, wdn_d.ap.rearrange("(h p) d -> p h d", p=128), wdn_d.buf, 8, 1024, stg)
        it = 0
        for mt in range(8):
            s = TC + mt * 512
            yt, ot = ytb[mt % 2], otb[mt % 2]
            k.dma("sp", yt, V(yT_d.ap[:, :, mt * 512:(mt + 1) * 512].rearrange("g p f -> p g f"), yT_d.buf))
            k.dma("pool", ot, V(oT_d.ap[:, :, mt * 512:(mt + 1) * 512].rearrange("h p f -> p h f"), oT_d.buf))
            for dc in range(8):
                dsl = slice(dc * 128, (dc + 1) * 128)
                i2 = it % 2
                it += 1
                pb = [k.banks[4 * i2 + j] for j in range(4)]
                for g in range(4):
                    k.mm(pb[0], wf4[:, g, dsl], yt[:, g, :], start=(g == 0), stop=(g == 3))
                for kk in range(8):
                    k.mm(pb[1], wg[:, kk, dc * 128:(dc + 1) * 128], hTv[:, kk, s:s + 512], start=(kk == 0), stop=(kk == 7))
                for hh in range(8):
                    k.mm(pb[2], wdn[:, hh, dsl], ot[:, hh, :], start=(hh == 0), stop=(hh == 7))
                for kk in range(8):
                    k.mm(pb[3], wg[:, kk, 1024 + dc * 128:1024 + (dc + 1) * 128], hTv[:, kk, s:s + 512], start=(kk == 0), stop=(kk == 7))
                k.act(g0[i2], pb[1], AF.Sigmoid)
                k.act(g1[i2], pb[3], AF.Sigmoid)
                k.tt("dve", m0[i2], pb[0], g0[i2], ALU.mult)
                k.tt("dve", m1[i2], pb[2], g1[i2], ALU.mult)
                k.tt("pool", V(mixs.ap[:, mt % 2, dc, :], mixs_buf[mt % 2]), m0[i2], m1[i2], ALU.add)
            k.copy("pool", hTv[:, :, s:s + 512], V(mixs.ap[:, mt % 2, :, :], mixs_buf[mt % 2]))
        k.barrier()
    k.stack = ExitStack()
    if stage <= 6:
        return finish(nc, k, out_d)

    def branch_tail(mt, producer, cidx, resid_d, final, tb):
        yx, sq, rst, xin_, x1t = tb["yx"], tb["sq"], tb["rst"], tb["xin"], tb["x1t"]
        for dc in range(8):
            pb = k.banks[dc % 2]
            producer(dc, pb)
            k.act(yx[:, dc, :], pb, AF.Copy)
            k.act(sq[:, dc, :], pb, AF.Square)
        pss = k.banks[2]
        for dc in range(8):
            k.mm(pss, onesb, sq[:, dc, :], start=(dc == 0), stop=(dc == 7))
        k.act(rst, pss, AF.Ln, bias=EPS, scale=1.0 / D)
        k.act(rst, rst, AF.Exp, scale=-0.5)
        for dc in range(8):
            k.stt("dve", yx[:, dc, :], yx[:, dc, :], coef[:, cidx, dc:dc + 1], rst, ALU.mult, ALU.mult)
        for j in range(4):
            tok0 = mt * 512 + j * 128
            xi = xin_[j % len(xin_)]
            xo = x1t[j % len(x1t)]
            k.dma("sp" if j % 2 == 0 else "pool", xi, V(resid_d.ap[tok0:tok0 + 128, :], resid_d.buf))
            ba, bb = k.banks[3 + 2 * (j % 2)], k.banks[4 + 2 * (j % 2)]
            for dc in range(8):
                bk = ba if dc < 4 else bb
                k.tr(bk[:, (dc % 4) * 128:(dc % 4 + 1) * 128], yx[:, dc, j * 128:(j + 1) * 128], identf)
            k.tt("dve", xo[:, 0:512], ba, xi[:, 0:512], ALU.add)
            k.tt("dve", xo[:, 512:1024], bb, xi[:, 512:1024], ALU.add)
            if final:
                k.dma("sp", V(out_d.ap[tok0:tok0 + 128, :], out_d.buf), xo)
            else:
                k.dma("sp", V(x1_d.ap[tok0:tok0 + 128, :], x1_d.buf), xo)
                norm_tile(xo, 2 + mt * 4 + j, 5, 6, hT, hTv.buf, tb["tm"])

    def tail_bufs(nbuf):
        return {"yx": k.sb("yx", [128, 8, 512], F32), "sq": k.sb("sqb", [128, 8, 512], BF16), "rst": k.sb("rst", [128, 512], F32),
                "xin": [k.sb(f"rxin{i}", [128, D], F32) for i in range(nbuf)], "x1t": [k.sb(f"x1t{i}", [128, D], F32) for i in range(nbuf)],
                "tm": {"sq": [k.sb("t_sq", [128, D], F32)], "ss": [k.sb("t_ss", [128, 1], F32)], "xn": [k.sb("t_xn", [128, D], BF16)],
                       "pt": [k.pv(7, 0, 512, BF16, 128)]}}

    with ExitStack() as st:
        k.stack = st
        wout = k.sb("wout", [128, 8, 1024], BF16)
        stg = [k.sb(f"p5bstg{i}", [128, 8, 512], F32) for i in range(1)]
        load_resident(wout, wout_d.ap.rearrange("(c p) d -> p c d", p=128), wout_d.buf, 8, 1024, stg)
        tb = tail_bufs(2)
        mixl = k.sb("mixl", [128, 8, 512], BF16)
        for mt in range(8):
            s = TC + mt * 512
            k.copy("pool", mixl, hTv[:, :, s:s + 512])
            def prod(dc, pb):
                for c in range(8):
                    k.mm(pb, wout[:, c, dc * 128:(dc + 1) * 128], mixl[:, c, :], start=(c == 0), stop=(c == 7))
            branch_tail(mt, prod, 4, x_d, False, tb)
        k.barrier()
    k.stack = ExitStack()
    if stage <= 7:
        return finish(nc, k, out_d)

    wupv = wup_d.ap.rearrange("(k p) c -> p k c", p=128)
    with ExitStack() as st:
        k.stack = st
        stf = [k.sb(f"p6stf{i}", [128, 8, 128], F32) for i in range(2)]
        stb = [k.sb(f"p6stb{i}", [128, 8, 128], BF16) for i in range(4)]
        lw = wchunk_loader(stf, stb)
        apad_b = [k.sb(f"apad{i}", [128, 66, 66], BF16) for i in range(2)]
        dg9_b = [k.sb(f"dg9_{i}", [128, 9, 128], BF16) for i in range(2)]
        sa = [k.sb(f"sa{i}", [128, 512], BF16) for i in range(4)]
        gtc = [k.sb(f"gtc{i}", [128, T], BF16) for i in range(2)]
        k.memset("pool", apad_b[0], 0.0)
        k.memset("pool", apad_b[1], 0.0)
        nb_ = 0
        for c in range(NFF):
            apad, dg9 = apad_b[c % 2], dg9_b[c % 2]
            wa = lw(V(wupv[:, :, c * 128:(c + 1) * 128], wup_d.buf), 8)
            wu = lw(V(wupv[:, :, DFF + c * 128:DFF + (c + 1) * 128], wup_d.buf), 8)
            for tap in range(9):
                k.ts("pool", dg9[:, tap, :], identf, cffn[:, c, tap:tap + 1])
            for bi in range(8):
                s = TC + bi * 512
                pp = k.banks[(0, 1, 6, 7)[nb_ % 4]]
                nb_ += 1
                for kk in range(8):
                    k.mm(pp, wa[:, kk, :], hTv[:, kk, s:s + 512], start=(kk == 0), stop=(kk == 7))
                k.act(apad[:, 1 + bi * 8:1 + bi * 8 + 8, 1:65], V(pp.ap.rearrange("p (r c) -> p r c", c=64), pp.buf), AF.Copy)
            gt = gtc[c % 2]
            for bi in range(8):
                s = TC + bi * 512
                pc = k.banks[2 + (bi % 2)]
                pu = k.banks[4 + (bi % 2)]
                pcv = V(pc.ap.rearrange("p (r c) -> p r c", c=64), pc.buf)
                for tap in range(9):
                    dr, dcc = tap // 3, tap % 3
                    k.mm(pcv, dg9[:, tap, :], apad[:, bi * 8 + dr:bi * 8 + dr + 8, dcc:dcc + 64], start=(tap == 0), stop=(tap == 8))
                k.act(sa[bi % 4], pc, AF.Silu)
                for kk in range(8):
                    k.mm(pu, wu[:, kk, :], hTv[:, kk, s:s + 512], start=(kk == 0), stop=(kk == 7))
                k.tt("dve", gt[:, bi * 512:(bi + 1) * 512], pu, sa[bi % 4], ALU.mult)
            k.dma("sp" if c % 2 == 0 else "pool", V(gT_d.ap[c], gT_d.buf), gt)
        k.barrier()
    k.stack = ExitStack()
    s_h.close()
    k.stack = ExitStack()
    if stage <= 8:
        return finish(nc, k, out_d)

    with ExitStack() as st:
        k.stack = st
        wdown = k.sb("wdown", [128, NFF, 1024], BF16)
        stg = [k.sb(f"p7stg{i}", [128, 8, 512], F32) for i in range(2)]
        load_resident(wdown, wdown_d.ap.rearrange("(c p) d -> p c d", p=128), wdown_d.buf, NFF, 1024, stg)
        gbl = [k.sb(f"gbl{i}", [128, NFF, 512], BF16) for i in range(2)]
        tb = tail_bufs(2)
        for mt in range(8):
            gb = gbl[mt % 2]
            k.dma("sp", gb[:, 0:11, :], V(gT_d.ap[0:11, :, mt * 512:(mt + 1) * 512].rearrange("c p f -> p c f"), gT_d.buf))
            k.dma("pool", gb[:, 11:NFF, :], V(gT_d.ap[11:NFF, :, mt * 512:(mt + 1) * 512].rearrange("c p f -> p c f"), gT_d.buf))
            def prod(dc, pb, gb=gb):
                for c in range(NFF):
                    k.mm(pb, wdown[:, c, dc * 128:(dc + 1) * 128], gb[:, c, :], start=(c == 0), stop=(c == NFF - 1))
            branch_tail(mt, prod, 7, x1_d, True, tb)
        k.barrier()
    k.stack = ExitStack()
    return finish(nc, k, out_d)


def finish(nc, k, out_d):
    k.barrier()
    return nc


def prep_inputs(inp, b):
    f = lambda a: np.ascontiguousarray(a, dtype=np.float32)
    colmajor = lambda v: f(np.asarray(v).reshape(-1, 128).T)
    m = {}
    m["x"] = f(inp["x"][b])
    m["ctx"] = f(inp["ctx"][b])
    m["cc"] = f(np.stack([colmajor(inp["c"][b]), colmajor(inp["c_ctx"])], axis=-1))
    m["w_ada"] = f(inp["w_ada"][0])
    m["b_ada"] = colmajor(inp["b_ada"][0])
    m["norms"] = f(np.stack([colmajor(inp[n][0]) for n in ("norm_pre_mix", "norm_post_mix", "norm_pre_ffn", "norm_post_ffn")], axis=1))
    m["w_in"] = f(inp["w_in"][0])
    cq = np.asarray(inp["conv_qkv"][0])
    m["conv_qkv"] = f(cq.T.reshape(24, 128, 3).transpose(1, 0, 2))
    gp = np.stack([np.asarray(inp["a_log"][0]).reshape(16), np.asarray(inp["dt_bias"][0]).reshape(16)], 0)
    m["gpar"] = f(np.broadcast_to(gp[None], (128, 2, 16)))
    m["dn_norm"] = f(np.asarray(inp["dn_norm"][0]).reshape(128, 1))
    m["w_fourier"] = f(inp["w_fourier"][0])
    m["w_dn"] = f(inp["w_dn"][0])
    m["w_out"] = f(inp["w_out"][0])
    m["w_up"] = f(inp["w_up"][0])
    cf = np.asarray(inp["conv_ffn"][0]).reshape(9, DFF)
    m["conv_ffn"] = f(cf.T.reshape(NFF, 128, 9).transpose(1, 0, 2))
    m["w_down"] = f(inp["w_down"][0])
    return m


_CONST = {}


def consts():
    if not _CONST:
        idx = np.arange(T, dtype=np.int64)
        ang = (2.0 * np.pi / T) * ((idx[:, None] * idx[None, :]) % T).astype(np.float64)
        s = 1.0 / np.sqrt(float(T) * 128.0)
        _CONST["dft_cos"] = (np.cos(ang) * s).astype(ml_dtypes.bfloat16)
        _CONST["dft_sin"] = (np.sin(ang) * s).astype(ml_dtypes.bfloat16)
        i8 = np.arange(128, dtype=np.int64)
        a8 = (2.0 * np.pi / 128) * ((i8[:, None] * i8[None, :]) % 128).astype(np.float64)
        j8 = np.arange(128)
        hm = []
        for m_ in range(7):
            s_ = 2 ** m_
            blk2 = (j8[:, None] // (2 * s_)) == (j8[None, :] // (2 * s_))
            half = (j8[:, None] // s_) != (j8[None, :] // s_)
            hm.append((blk2 & half).astype(np.float32))
        _CONST["hmask"] = np.stack(hm, axis=1).astype(ml_dtypes.bfloat16)
        _CONST["dft128"] = np.concatenate([np.cos(a8), -np.sin(a8)], axis=1).astype(ml_dtypes.bfloat16)
    return _CONST


def kernel(**inputs):
    inp = {k_: np.asarray(v) for k_, v in inputs.items()}
    nc = build()
    cst = consts()
    in_maps = []
    for b in range(8):
        m = prep_inputs(inp, b)
        m.update(cst)
        in_maps.append(m)
    res = run_bass_kernel_spmd(nc, in_maps, core_ids=list(range(8)))
    return np.stack([np.asarray(r["out"], dtype=np.float32) for r in res.results], axis=0)
```

```python
import os
from contextlib import ExitStack
import numpy as np
import ml_dtypes
import concourse.bass as bass
import concourse.mybir as mybir
from concourse.bass_utils import run_bass_kernel_spmd

F32 = mybir.dt.float32
BF16 = mybir.dt.bfloat16
AF = mybir.ActivationFunctionType
ALU = mybir.AluOpType

D = 1024
T = 4096
TC = 256
TA = TC + T
NT = TA // 128
H = 8
OFF_F, OFF_Q, OFF_K, OFF_V, OFF_Z, OFF_B, OFF_A, OFF_G = 0, 512, 1536, 2560, 3584, 4608, 4624, 4640
INW = 6688
DFF = 2816
NFF = DFF // 128
EPS = 1e-6


class Buf:
    __slots__ = ("name", "last_w", "readers", "dsem", "dcount", "excl")

    def __init__(self, name, excl=False):
        self.name = name
        self.excl = excl
        self.last_w = None
        self.readers = []
        self.dsem = None
        self.dcount = 0


class V:
    __slots__ = ("ap", "buf")

    def __init__(self, ap, buf):
        self.ap = ap
        self.buf = buf

    def __getitem__(self, idx):
        return V(self.ap[idx], self.buf)

    def sub(self, idx, buf):
        return V(self.ap[idx], buf)


def _bufs(*vs):
    out = []
    for v in vs:
        if isinstance(v, V) and v.buf is not None:
            for b in (v.buf if isinstance(v.buf, (list, tuple)) else (v.buf,)):
                if b not in out:
                    out.append(b)
    return out


def _ap(v):
    return v.ap if isinstance(v, V) else v


class K:
    def __init__(self, nc):
        self.nc = nc
        self.engs = {"pe": nc.tensor, "act": nc.scalar, "dve": nc.vector, "pool": nc.gpsimd, "sp": nc.sync}
        self.sem = {n: nc.alloc_semaphore(f"s_{n}") for n in self.engs}
        self.cnt = {n: 0 for n in self.engs}
        self.known = {n: {} for n in self.engs}
        self.dsems = []
        self.nins = 0
        self.nwaits = 0
        self.stack = ExitStack()
        self.limit = None
        self.log = []
        self.sched = os.environ.get('KSCHED', '1') == '1'
        self.pending = []

    def _uid(self):
        self.uid = getattr(self, 'uid', 0) + 1
        return self.uid

    def sb(self, name, shape, dt, nbuf=None):
        t = self.stack.enter_context(self.nc.sbuf_tensor(f"sb{self._uid()}_" + name, list(shape), dt))
        return V(t[:] if hasattr(t, "__getitem__") else t.ap(), Buf(name) if nbuf is None else nbuf)

    def init_banks(self):
        self.banks = []
        for i in range(8):
            t = self.nc.psum_tensor(f"ps_bank{i}", [128, 512], F32).__enter__()
            self.banks.append(V(t[:], Buf(f"bank{i}", excl=True)))

    def pv(self, bank, lo, hi, dt=F32, inner=None):
        b = self.banks[bank]
        ap = b.ap[:, lo:hi]
        if dt != F32:
            ap = ap.bitcast(dt)
        if inner is not None:
            ap = ap.rearrange("p (a b) -> p a b", b=inner)
        return V(ap, b.buf)

    def _wait(self, e, ev):
        sem, val, src = ev
        if src == "pe" and e == "pe":
            return
        kn = self.known[e]
        if kn.get(sem.num, 0) >= val:
            return
        kn[sem.num] = val
        self.engs[e].wait_ge(sem, val)
        self.nwaits += 1

    def _deps(self, e, reads, writes):
        best = {}
        def add(ev):
            s = ev[0].num
            if s not in best or best[s][1] < ev[1]:
                best[s] = ev
        for b in reads:
            if b.last_w is not None:
                add(b.last_w)
        for b in writes:
            if b.last_w is not None:
                add(b.last_w)
            for ev in b.readers:
                add(ev)
        for ev in best.values():
            self._wait(e, ev)

    def _record(self, ev, reads, writes):
        for b in reads:
            if b in writes:
                continue
            b.readers.append(ev)
            if len(b.readers) > 10:
                best = {}
                for x in b.readers:
                    s = x[0].num
                    if s not in best or best[s][1] < x[1]:
                        best[s] = x
                b.readers = list(best.values())
        for b in writes:
            b.last_w = ev
            b.readers = []

    def op(self, e, fn, reads, writes, cost=300.0):
        if self.sched:
            self.pending.append(("op", e, fn, list(reads), list(writes), float(cost)))
            return
        self._emit_op(e, fn, reads, writes)

    def _emit_op(self, e, fn, reads, writes):
        if self.limit is not None and self.nins >= self.limit:
            return
        ex = [b for b in reads if b.excl and b not in writes]
        if ex:
            writes = list(writes) + ex
        self._deps(e, reads, writes)
        ins = fn(self.engs[e])
        if os.environ.get('PRINS') and self.nins in range(int(os.environ.get('PRINS','0')), int(os.environ.get('PRINS','0')) + 4):
            print('INS', self.nins, ins.concise())
        self.cnt[e] += 1
        ins.then_inc(self.sem[e], 1)
        self._record((self.sem[e], self.cnt[e], e), reads, writes)
        self.nins += 1

    def dma(self, q, out, in_, key=None, nbytes=None, **kw):
        if self.sched:
            if nbytes is None:
                shp = _ap(out).shape
                nbytes = 4
                for d_ in shp:
                    nbytes *= d_
            self.pending.append(("dma", q, (out, in_, key, kw), _bufs(in_), _bufs(out), 2000.0 + nbytes / 100.0))
            return
        self._emit_dma(q, out, in_, key, **kw)

    def _emit_dma(self, q, out, in_, key=None, **kw):
        if self.limit is not None and self.nins >= self.limit:
            return
        reads, writes = _bufs(in_), _bufs(out)
        self._deps(q, reads, writes)
        kb = key.buf if key is not None else (out.buf if not isinstance(out.buf, DBuf) else in_.buf)
        if isinstance(kb, (list, tuple)):
            kb = kb[0]
        if kb.dsem is None:
            kb.dsem = self.nc.alloc_semaphore(f"d{self._uid()}_{kb.name}")
            self.dsems.append(kb)
        ins = self.engs[q].dma_start(out=_ap(out), in_=_ap(in_), **kw)
        kb.dcount += 1
        ins.then_inc(kb.dsem, 16)
        self._record((kb.dsem, 16 * kb.dcount, "dma"), reads, writes)
        self.nins += 1

    def flush(self):
        ops = self.pending
        self.pending = []
        n = len(ops)
        if n == 0:
            return
        SYNC = 200.0
        preds = [[] for _ in range(n)]
        lastw = {}
        rdrs = {}
        for i, (kind, e, fn, reads, writes, cost) in enumerate(ops):
            wr = list(writes) + [b for b in reads if b.excl and b not in writes]
            ps = set()
            for b in reads:
                if b in lastw:
                    ps.add(lastw[b])
            for b in wr:
                if b in lastw:
                    ps.add(lastw[b])
                for r_ in rdrs.get(b, ()):
                    ps.add(r_)
            ps.discard(i)
            preds[i] = list(ps)
            for b in reads:
                if b not in wr:
                    rdrs.setdefault(b, []).append(i)
            for b in wr:
                lastw[b] = i
                rdrs[b] = []
        succs = [[] for _ in range(n)]
        for i in range(n):
            for p in preds[i]:
                succs[p].append(i)
        occ = [0.0] * n
        lat = [0.0] * n
        for i, (kind, e, fn, reads, writes, cost) in enumerate(ops):
            if kind == "dma":
                occ[i] = 60.0
                lat[i] = cost
            else:
                occ[i] = cost
                lat[i] = cost
        blevel = [0.0] * n
        for i in range(n - 1, -1, -1):
            m_ = 0.0
            for s_ in succs[i]:
                if blevel[s_] > m_:
                    m_ = blevel[s_]
            blevel[i] = lat[i] + m_
        import heapq
        npred = [len(p) for p in preds]
        ready_t = [0.0] * n
        eng_free = {}
        readyq = {}
        for i in range(n):
            if npred[i] == 0:
                heapq.heappush(readyq.setdefault(ops[i][1], []), (-blevel[i], i))
        order = []
        done = 0
        while done < n:
            best = None
            for e, hq in readyq.items():
                if not hq:
                    continue
                tfree = eng_free.get(e, 0.0)
                cand = None
                top = heapq.nsmallest(6, hq)
                for pr, i in top:
                    st_ = max(tfree, ready_t[i])
                    key = (st_, pr)
                    if cand is None or key < cand[0]:
                        cand = (key, i)
                if best is None or cand[0] < best[0]:
                    best = (cand[0], cand[1], e)
            (st_, pr), i, e = best
            hq = readyq[e]
            hq.remove((-blevel[i], i))
            heapq.heapify(hq)
            eng_free[e] = st_ + occ[i]
            fin = st_ + lat[i]
            order.append(i)
            done += 1
            for s_ in succs[i]:
                rt = fin + (0.0 if ops[s_][1] == e else SYNC)
                if rt > ready_t[s_]:
                    ready_t[s_] = rt
                npred[s_] -= 1
                if npred[s_] == 0:
                    heapq.heappush(readyq.setdefault(ops[s_][1], []), (-blevel[s_], s_))
        for i in order:
            kind, e, fn, reads, writes, cost = ops[i]
            if kind == "dma":
                out, in_, key, kw = fn
                self._emit_dma(e, out, in_, key, **kw)
            else:
                self._emit_op(e, fn, reads, writes)

    def barrier(self):
        self.flush()
        for e in self.engs:
            for f in self.engs:
                if f != e and self.cnt[f] > 0:
                    self._wait(e, (self.sem[f], self.cnt[f], f))
            for kb in self.dsems:
                if kb.dcount > 0:
                    self._wait(e, (kb.dsem, 16 * kb.dcount, "dma"))

    @staticmethod
    def _fsz(v):
        shp = _ap(v).shape
        n = 1
        for d_ in shp[1:]:
            n *= d_
        return n

    def _ecost(self, e, out, in_):
        n = self._fsz(out)
        if e == "pool":
            return 150.0 + 2.0 * n
        if e == "act":
            return 220.0 + 0.72 * n
        return 100.0 + (1.05 * n if (_ap(in_).dtype == F32 or _ap(out).dtype == F32) else 0.6 * n)

    def mm(self, out, lhsT, rhs, start=True, stop=True):
        n = self._fsz(out)
        c = 40.0 + n * (1.9 if _ap(lhsT).dtype == F32 else 0.45)
        self.op("pe", lambda e: e.matmul(_ap(out), lhsT=_ap(lhsT), rhs=_ap(rhs), start=start, stop=stop),
                _bufs(lhsT, rhs) + ([] if start else _bufs(out)), _bufs(out), cost=c)

    def tr(self, out, in_, ident):
        n = self._fsz(out)
        c = 40.0 + n * (1.9 if _ap(in_).dtype == F32 else 0.45)
        self.op("pe", lambda e: e.transpose(out=_ap(out), in_=_ap(in_), identity=_ap(ident)), _bufs(in_, ident), _bufs(out), cost=c)

    def act(self, out, in_, func, bias=0.0, scale=1.0, accum=None, eng="act"):
        kw = {}
        if accum is not None:
            kw["accum_out"] = _ap(accum)
        self.op("act", lambda e: e.activation(out=_ap(out), in_=_ap(in_), func=func, bias=_ap(bias), scale=_ap(scale), **kw),
                _bufs(in_, bias, scale), _bufs(out, accum), cost=self._ecost("act", out, in_))

    def ts(self, e, out, in0, s1, s2=None, op0=ALU.mult, op1=None):
        if op1 is None:
            f = lambda g: g.tensor_scalar(out=_ap(out), in0=_ap(in0), scalar1=_ap(s1), scalar2=None, op0=op0)
        else:
            f = lambda g: g.tensor_scalar(out=_ap(out), in0=_ap(in0), scalar1=_ap(s1), scalar2=_ap(s2), op0=op0, op1=op1)
        self.op(e, f, _bufs(in0, s1, s2), _bufs(out), cost=self._ecost(e, out, in0))

    def tt(self, e, out, a, b, op):
        self.op(e, lambda g: g.tensor_tensor(out=_ap(out), in0=_ap(a), in1=_ap(b), op=op), _bufs(a, b), _bufs(out), cost=self._ecost(e, out, a))

    def stt(self, e, out, in0, scalar, in1, op0, op1):
        self.op(e, lambda g: g.scalar_tensor_tensor(out=_ap(out), in0=_ap(in0), scalar=_ap(scalar), in1=_ap(in1), op0=op0, op1=op1),
                _bufs(in0, scalar, in1), _bufs(out), cost=self._ecost(e, out, in0))

    def copy(self, e, out, in_):
        if e == "act":
            self.act(out, in_, AF.Copy)
        else:
            self.op(e, lambda g: g.tensor_copy(out=_ap(out), in_=_ap(in_)), _bufs(in_), _bufs(out), cost=self._ecost(e, out, in_))

    def recip(self, out, in_):
        self.op("dve", lambda g: g.reciprocal(out=_ap(out), in_=_ap(in_)), _bufs(in_), _bufs(out), cost=100.0 + 6.3 * self._fsz(out))

    def memset(self, e, out, val):
        self.op(e, lambda g: g.memset(_ap(out), val), [], _bufs(out))

    def asel(self, out, in_, pattern, cmp, fill, base, cm):
        self.op("pool", lambda g: g.affine_select(out=_ap(out), in_=_ap(in_), pattern=pattern, compare_op=cmp, fill=fill,
                                                  base=base, channel_multiplier=cm), _bufs(in_), _bufs(out))


class DBuf(Buf):
    __slots__ = ("is_dram",)

    def __init__(self, name):
        super().__init__(name)
        self.is_dram = True


def dramv(nc, name, shape, dt, kind):
    t = nc.dram_tensor(name, list(shape), dt, kind=kind)
    return V(t.ap(), DBuf(name))


def build(stage=99, dbg=None):
    nc = bass.Bass("TRN2", target_bir_lowering=False)
    k = K(nc)
    k.init_banks()
    if dbg and 'limit' in dbg:
        k.limit = dbg['limit']
    IN = lambda name, shape, dt=F32: dramv(nc, name, shape, dt, "ExternalInput")
    x_d = IN("x", [T, D])
    ctx_d = IN("ctx", [TC, D])
    cc_d = IN("cc", [128, 8, 2])
    wada_d = IN("w_ada", [D, 6 * D])
    bada_d = IN("b_ada", [128, 48])
    nrm_d = IN("norms", [128, 4, 8])
    win_d = IN("w_in", [D, INW])
    cqkv_d = IN("conv_qkv", [128, 24, 3])
    gpar_d = IN("gpar", [128, 2, 16])
    dnn_d = IN("dn_norm", [128, 1])
    wf_d = IN("w_fourier", [512, D])
    wdn_d = IN("w_dn", [D, D])
    wout_d = IN("w_out", [D, D])
    wup_d = IN("w_up", [D, 2 * DFF])
    cffn_d = IN("conv_ffn", [128, NFF, 9])
    wdown_d = IN("w_down", [DFF, D])
    cos_d = IN("dft_cos", [T, T], BF16)
    sin_d = IN("dft_sin", [T, T], BF16)
    c128_d = IN("dft128", [128, 256], BF16)
    hmask_d = IN("hmask", [128, 7, 128], BF16)
    out_d = dramv(nc, "out", [T, D], F32, "ExternalOutput")
    dbg_out = {}
    if dbg:
        for nm, shp in dbg.items():
            if nm in ("heads", "nsteps", "limit"):
                continue
            dbg_out[nm] = dramv(nc, "dbg_" + nm, shp, F32, "ExternalOutput")
    x1_d = dramv(nc, "x1_scr", [T, D], F32, "Internal")
    oT_d = dramv(nc, "oT_scr", [H, 128, T], BF16, "Internal")
    gT_d = dramv(nc, "gT_scr", [NFF, 128, T], BF16, "Internal")
    yT_d = dramv(nc, "yT_scr", [4, 128, T], BF16, "Internal")

    identf = k.sb("identf", [128, 128], F32)
    ident = k.sb("ident", [128, 128], BF16)
    onesf = k.sb("onesf", [128, 128], F32)
    onesb = k.sb("onesb", [128, 128], BF16)
    negm = [k.sb(f"negm{d}", [128, 128], F32) for d in range(2)]
    smask = [k.sb(f"smask{d}", [128, 128], F32) for d in range(2)]
    ut = [k.sb(f"ut{d}", [128, 128], F32) for d in range(2)]
    zerof = k.sb("zerof", [128, 128], F32)
    scal_t = k.sb("scal_t", [128, 8], F32)
    k.memset("pool", zerof, 0.0)
    k.memset("pool", onesf, 1.0)
    k.copy("dve", onesb, onesf)
    k.asel(identf, zerof, [[-1, 128]], ALU.not_equal, 1.0, 0, 1)
    k.copy("dve", ident, identf)
    k.asel(negm[0], zerof, [[1, 128]], ALU.is_ge, -1e5, 0, -1)
    k.asel(smask[0], onesf, [[1, 128]], ALU.is_gt, 0.0, 0, -1)
    k.asel(ut[0], onesf, [[1, 128]], ALU.is_ge, 0.0, 0, -1)
    k.asel(negm[1], zerof, [[-1, 128]], ALU.is_ge, -1e5, 0, 1)
    k.asel(smask[1], onesf, [[-1, 128]], ALU.is_gt, 0.0, 0, 1)
    k.asel(ut[1], onesf, [[-1, 128]], ALU.is_ge, 0.0, 0, 1)

    nrm = k.sb("nrm", [128, 4, 8], F32)
    k.dma("sp", nrm, nrm_d)
    cqkv = k.sb("cqkv", [128, 24, 3], F32)
    k.dma("sp", cqkv, cqkv_d)
    gpar = k.sb("gpar", [128, 2, 16], F32)
    k.dma("sp", gpar, gpar_d)
    dnn = k.sb("dnn", [128, 1], F32)
    k.dma("sp", dnn, dnn_d)
    cffn = k.sb("cffn", [128, NFF, 9], F32)
    k.dma("sp", cffn, cffn_d)
    c128 = k.sb("c128", [128, 256], BF16)
    k.dma("sp", c128, c128_d)
    hmask = k.sb("hmask", [128, 7, 128], BF16)
    k.dma("sp", hmask, hmask_d)
    bada = k.sb("bada", [128, 48], F32)
    k.dma("sp", bada, bada_d)
    cc = k.sb("cc", [128, 8, 2], F32)
    k.dma("sp", cc, cc_d)

    mod = k.sb("mod", [128, 48, 2], F32)
    scc = k.sb("scc", [128, 8, 2], F32)
    k.act(scc, cc, AF.Silu)
    with ExitStack() as st:
        k.stack = st
        wa = [k.sb(f"wa{i}", [128, 8, 512], F32) for i in range(2)]
        pm = k.pv(0, 0, 96, F32, 2)
        wv = wada_d.ap.rearrange("(k p) c -> p k c", p=128)
        for blk in range(12):
            w = wa[blk % 2]
            k.dma("sp" if blk % 2 == 0 else "pool", w, V(wv[:, :, blk * 512:(blk + 1) * 512], wada_d.buf))
            for oc in range(4):
                for kk in range(8):
                    k.mm(pm[:, blk * 4 + oc, :], w[:, kk, oc * 128:(oc + 1) * 128], scc[:, kk, :], start=(kk == 0), stop=(kk == 7))
        for j in range(2):
            k.tt("dve", mod[:, :, j], pm[:, :, j], bada, ALU.add)
        k.barrier()
    k.stack = ExitStack()
    coef = k.sb("coef", [128, 8, 8], F32)
    def modc(i, j):
        return mod[:, i * 8:(i + 1) * 8, j]
    k.stt("dve", coef[:, 0, :], modc(1, 0), 1.0, nrm[:, 0, :], ALU.add, ALU.mult)
    k.copy("dve", coef[:, 1, :], modc(0, 0))
    k.stt("dve", coef[:, 2, :], modc(1, 1), 1.0, nrm[:, 0, :], ALU.add, ALU.mult)
    k.copy("dve", coef[:, 3, :], modc(0, 1))
    k.tt("dve", coef[:, 4, :], modc(2, 0), nrm[:, 1, :], ALU.mult)
    k.stt("dve", coef[:, 5, :], modc(4, 0), 1.0, nrm[:, 2, :], ALU.add, ALU.mult)
    k.copy("dve", coef[:, 6, :], modc(3, 0))
    k.tt("dve", coef[:, 7, :], modc(5, 0), nrm[:, 3, :], ALU.mult)
    if "coef" in dbg_out:
        k.dma("sp", dbg_out["coef"], coef)
    if stage <= 0:
        return finish(nc, k, out_d)

    s_h = ExitStack()
    k.stack = s_h
    hT = k.sb("hT", [128, 8, TA], BF16)
    hbuf = [Buf(f"hT{t}") for t in range(NT)]

    def norm_tile(src_tile_v, tile_idx, ca, cb, dst, dstbufs, tm):
        nb = len(tm["sq"])
        sq = tm["sq"][tile_idx % nb]
        ss = tm["ss"][tile_idx % nb]
        xn = tm["xn"][tile_idx % nb]
        pt = tm["pt"][tile_idx % len(tm["pt"])]
        k.act(sq, src_tile_v, AF.Square, accum=ss)
        k.act(ss, ss, AF.Sqrt, bias=EPS, scale=1.0 / D)
        k.recip(ss, ss)
        k.ts("dve", xn, src_tile_v, ss[:, 0:1])
        for c in range(8):
            k.tr(pt[:, c, :], xn[:, c * 128:(c + 1) * 128], ident)
        for c in range(8):
            dv = V(dst.ap[:, c, tile_idx * 128:(tile_idx + 1) * 128], dstbufs[tile_idx] if isinstance(dstbufs, list) else dstbufs)
            if c % 2 == 0:
                k.ts("dve", dv, pt[:, c, :], coef[:, ca, c:c + 1], coef[:, cb, c:c + 1], ALU.mult, ALU.add)
            else:
                k.act(dv, pt[:, c, :], AF.Identity, bias=coef[:, cb, c:c + 1], scale=coef[:, ca, c:c + 1])

    p1 = ExitStack()
    k.stack = p1
    nt_sq = [k.sb(f"nt_sq{i}", [128, D], F32) for i in range(2)]
    nt_ss = [k.sb(f"nt_ss{i}", [128, 1], F32) for i in range(2)]
    nt_xn = [k.sb(f"nt_xn{i}", [128, D], BF16) for i in range(2)]
    xin = [k.sb(f"xin{i}", [128, D], F32) for i in range(3)]
    nt_pt = [k.pv(i, 0, 512, BF16, 128) for i in range(2)]
    tm1 = {"sq": nt_sq, "ss": nt_ss, "xn": nt_xn, "pt": nt_pt}
    for t in range(NT):
        xi = xin[t % 3]
        if t < 2:
            src = V(ctx_d.ap[t * 128:(t + 1) * 128, :], ctx_d.buf)
        else:
            src = V(x_d.ap[(t - 2) * 128:(t - 1) * 128, :], x_d.buf)
        k.dma("sp" if t % 2 == 0 else "pool", xi, src)
        norm_tile(xi, t, 2 if t < 2 else 0, 3 if t < 2 else 1, hT, hbuf, tm1)
    k.barrier()
    p1.close()
    k.stack = ExitStack()
    if "hT" in dbg_out:
        with ExitStack() as st:
            k.stack = st
            tmpf = k.sb("dbg_hT", [128, 8, 1024], F32)
            k.copy("dve", tmpf, V(hT.ap[:, :, 0:1024], None))
            k.dma("sp", dbg_out["hT"], tmpf)
            k.barrier()
        k.stack = ExitStack()
    hTv = V(hT.ap, Buf("hT_all"))
    if stage <= 1:
        return finish(nc, k, out_d)

    winv = win_d.ap.rearrange("(k p) c -> p k c", p=128)

    def bcast_t(v, n):
        return V(v.ap.unsqueeze(1).to_broadcast([128, n, 16]), v.buf)

    s_g = ExitStack()
    k.stack = s_g
    beta = k.sb("beta", [128, NT, 16], F32)
    ngc = k.sb("ngc", [128, NT, 16], F32)
    ngcb = k.sb("ngcb", [128, NT, 16], F32)
    egc = k.sb("egc", [128, NT, 16], F32)
    ekt = k.sb("ekt", [128, NT, 16], F32)
    egl = k.sb("egl", [128, NT, 16], F32)
    with ExitStack() as st:
        k.stack = st
        wbaf = k.sb("wbaf", [128, 8, 32], F32)
        wba = k.sb("wba", [128, 8, 32], BF16)
        graw = k.sb("graw", [128, NT, 32], F32)
        gg = k.sb("gg", [128, NT, 16], F32)
        gtmp = k.sb("gtmp", [128, NT, 16], F32)
        lnb = k.sb("lnb", [128, NT, 16], F32)
        negA = k.sb("negA", [128, 16], F32)
        pg = k.pv(0, 0, 512, F32, 32)
        pc0, pc1, pt0, pt1 = k.banks[1], k.banks[2], k.banks[3], k.banks[4]
        k.dma("sp", wbaf, V(winv[:, :, OFF_B:OFF_B + 32], win_d.buf))
        k.copy("dve", wba, wbaf)
        for g0 in range(0, NT, 16):
            n = min(16, NT - g0)
            for j in range(n):
                t = g0 + j
                for kk in range(8):
                    k.mm(pg[:, j, :], hTv[:, kk, t * 128:(t + 1) * 128], wba[:, kk, :], start=(kk == 0), stop=(kk == 7))
            k.copy("act", graw[:, g0:g0 + n, :], pg[:, 0:n, :])
        k.act(beta, graw[:, :, 0:16], AF.Sigmoid)
        k.act(lnb, graw[:, :, 0:16], AF.Exp, scale=-1.0)
        k.act(lnb, lnb, AF.Ln, bias=1.0)
        k.act(negA, gpar[:, 0, :], AF.Exp)
        k.ts("dve", negA, negA, -1.0)
        k.tt("dve", gg, graw[:, :, 16:32], bcast_t(gpar[:, 1, :], NT), ALU.add)
        k.act(gg, gg, AF.Exp)
        k.act(gg, gg, AF.Ln, bias=1.0)
        k.tt("dve", gg, gg, bcast_t(negA, NT), ALU.mult)
        if "g" in dbg_out:
            k.dma("sp", dbg_out["g"], gg)
            k.dma("sp", dbg_out["beta"], beta)
        pcs = [pc0, pc1]
        pts = [pt0, pt1]
        for d in range(2):
            pcv = V(pcs[d].ap[:, 0:NT * 8].rearrange("p (t c) -> p t c", c=8), pcs[d].buf)
            ptv = V(pts[d].ap[:, 0:NT * 8].rearrange("p (t c) -> p t c", c=8), pts[d].buf)
            k.mm(pcv, ut[d], gg[:, :, d * 8:(d + 1) * 8])
            k.mm(ptv, onesf, gg[:, :, d * 8:(d + 1) * 8])
            sl = slice(d * 8, (d + 1) * 8)
            k.act(ngc[:, :, sl], pcv, AF.Copy, scale=-1.0)
            k.stt("dve", ngcb[:, :, sl], pcv, -1.0, lnb[:, :, sl], ALU.mult, ALU.subtract)
            k.act(egc[:, :, sl], pcv, AF.Exp)
            k.tt("dve", gtmp[:, :, sl], ptv, ngc[:, :, sl], ALU.add)
            k.act(ekt[:, :, sl], gtmp[:, :, sl], AF.Exp)
            k.act(egl[:, :, sl], ptv, AF.Exp)
        k.barrier()
    k.stack = ExitStack()
    if stage <= 2:
        return finish(nc, k, out_d)

    blocks = [(0, TC)] + [(TC + 512 * i, 512) for i in range(8)]
    xblocks = blocks[1:]
    poff = lambda tok: 1 + tok if tok < TC else 3 + tok
    heads = list(range(H)) if dbg is None or "heads" not in dbg else dbg["heads"]
    p4 = ExitStack()
    k.stack = p4
    ring = [[k.sb(f"ring{ty}_{i}", [128, 514], BF16) for i in range(3)] for ty in range(3)]
    qT = k.sb("qT", [128, TA], BF16)
    kT = k.sb("kT", [128, TA], BF16)
    vT = k.sb("vT", [128, TA], BF16)
    zs_b = [k.sb(f"zsb{i}", [128, 512], BF16) for i in range(1)]
    osum = k.sb("osum", [128, T], F32)
    wst_f = [k.sb(f"wstf{i}", [128, 8, 128], F32) for i in range(1)]
    wst_b = [k.sb(f"wstb{i}", [128, 8, 128], BF16) for i in range(3)]
    dgt3 = [k.sb(f"dgt{ty}", [128, 3, 128], BF16) for ty in range(3)]
    wbz = k.sb("wbz", [128, 8, 128], BF16)
    rn = [k.sb(f"rn{i}", [128, 512], F32) for i in range(1)]
    ofin = [k.sb(f"ofin{i}", [128, 512], BF16) for i in range(2)]
    osb = [Buf(f"osum{t}") for t in range(32)]
    pbig = [k.banks[0], k.banks[1]]
    GS = 3
    rot = [[k.banks[3 * d + i] for i in range(3)] for d in range(2)]
    rcnt = [0, 0]
    def nextb(d):
        rcnt[d] += 1
        return rot[d][rcnt[d] % 3]
    def b3(bank, n, dt=F32, off=0):
        if dt == F32:
            ap = bank.ap[:, off * 128:(off + n) * 128].rearrange("p (a b) -> p a b", b=128)
        else:
            ap = bank.ap[:, off * 64:(off + n) * 64].bitcast(BF16).rearrange("p (a b) -> p a b", b=128)
        return V(ap, bank.buf)
    pvn = [k.pv(6 + d, 0, 128) for d in range(2)]
    poT = [k.pv(6 + d, 128, 256) for d in range(2)]
    pS = [k.pv(6 + d, 256, 384) for d in range(2)]
    def gtmp(name, dt, nb=1):
        return [[k.sb(f"{name}{d}_{i}", [128, GS, 128], dt) for i in range(nb)] for d in range(2)]
    g_dgc, g_E = gtmp("dgc", F32), gtmp("E", F32)
    g_eg, g_Rk0, g_Q0, g_Q0T, g_Z = (gtmp(nm, BF16) for nm in ("eg", "Rk0", "Q0", "Q0T", "Z"))
    g_E1, g_E2 = gtmp("E1", BF16), gtmp("E2", BF16)
    g_LT, g_D, g_G = gtmp("LT", BF16, 2), gtmp("D", BF16, 2), gtmp("G", BF16, 2)
    g_W, g_nwT, g_Vt, g_ktail, g_qhT, g_qkm = (gtmp(nm, BF16, 2) for nm in ("W", "nwT", "Vt", "ktail", "qhT", "qkm"))
    t_vn = [k.sb(f"vn{d}", [128, 128], BF16) for d in range(2)]
    Sf = [k.sb(f"Sf{d}", [128, 128], F32) for d in range(2)]
    Sb = [k.sb(f"Sb{d}", [128, 128], BF16) for d in range(2)]
    def bcn(v, n):
        return V(v.ap.unsqueeze(1).to_broadcast([128, n, 128]), v.buf)
    nbig = [0]
    def nextbig():
        nbig[0] += 1
        return pbig[nbig[0] % 2]
    wcnt = [0]

    def load_w(col0):
        i = wcnt[0] % 3
        wcnt[0] += 1
        k.dma("sp", wst_f[0], V(winv[:, :, col0:col0 + 128], win_d.buf))
        k.copy("pool", wst_b[i], wst_f[0])
        return wst_b[i]

    order_f = list(range(NT))
    order_b = [1, 0] + list(range(NT - 1, 1, -1))

    for h in heads:
        blkbuf = [[Buf(f"qkv{ty}_{b_}") for b_ in range(9)] for ty in range(3)]
        blk_of = lambda t: 0 if t < 2 else 1 + (t - 2) // 4
        dsts = (qT, kT, vT)
        def tv(ty, t):
            return V(dsts[ty].ap[:, t * 128:(t + 1) * 128], blkbuf[ty][blk_of(t)])
        def tlv(ty, a_, n_):
            bl = []
            for t_ in range(a_, a_ + n_):
                if blkbuf[ty][blk_of(t_)] not in bl:
                    bl.append(blkbuf[ty][blk_of(t_)])
            return V(dsts[ty].ap[:, a_ * 128:(a_ + n_) * 128].rearrange("p (a b) -> p a b", b=128), bl)
        pcnt = [0]
        def nextp():
            pcnt[0] += 1
            return k.banks[pcnt[0] % 6]
        wbs = [load_w(OFF_Q + (ty * 8 + h) * 128) for ty in range(3)]
        for ty in range(3):
            for tap in range(3):
                k.ts("pool", dgt3[ty][:, tap, :], identf, cqkv[:, ty * 8 + h, tap:tap + 1])

        def conv_block(ty, b_):
            s, n = blocks[b_]
            slot = ring[ty][b_ % 3]
            pp = nextp()
            for tap in range(3):
                k.mm(pp[:, 0:n], dgt3[ty][:, tap, :], slot[:, tap:tap + n], start=(tap == 0), stop=(tap == 2))
            dv = V(dsts[ty].ap[:, s:s + n], blkbuf[ty][b_])
            k.act(dv, pp[:, 0:n], AF.Silu)
            if ty < 2:
                pp2 = nextp()
                r = rn[0]
                sqv = ofin[(b_ + ty) % 2]
                k.tt("pool", sqv[:, 0:n], dv, dv, ALU.mult)
                k.mm(pp2[:, 0:n], onesb, sqv[:, 0:n])
                k.act(r[:, 0:n], pp2[:, 0:n], AF.Ln, bias=EPS)
                k.act(r[:, 0:n], r[:, 0:n], AF.Exp, scale=-0.5)
                k.stt("dve", dv, dv, (128.0 ** -0.5) if ty == 0 else 1.0, r[:, 0:n], ALU.mult, ALU.mult)

        for b_ in range(9):
            s, n = blocks[b_]
            for ty in range(3):
                slot = ring[ty][b_ % 3]
                prev = ring[ty][(b_ - 1) % 3]
                pp = nextp()
                for kk in range(8):
                    k.mm(pp[:, 0:n], wbs[ty][:, kk, :], hTv[:, kk, s:s + n], start=(kk == 0), stop=(kk == 7))
                k.copy("dve", slot[:, 1:1 + n], pp[:, 0:n])
                if b_ in (0, 1):
                    k.memset("pool", slot[:, 0:1], 0.0)
                else:
                    k.copy("pool", slot[:, 0:1], prev[:, 512:513])
                if b_ in (0, 8):
                    k.memset("pool", slot[:, n + 1:n + 2], 0.0)
                if b_ >= 2:
                    k.copy("pool", prev[:, 513:514], slot[:, 1:2])
                if b_ == 0:
                    conv_block(ty, 0)
                elif b_ >= 2:
                    conv_block(ty, b_ - 1)
                if b_ == 8:
                    conv_block(ty, 8)
        if "qkv" in dbg_out and h == heads[0]:
            with ExitStack() as st2:
                old = k.stack
                k.stack = st2
                tf_ = k.sb("dbgqkv", [128, 3, 512], F32)
                k.copy("dve", tf_[:, 0, :], qT[:, 0:512])
                k.copy("dve", tf_[:, 1, :], kT[:, 0:512])
                k.copy("dve", tf_[:, 2, :], vT[:, 0:512])
                k.dma("sp", dbg_out["qkv"], tf_)
                k.barrier()
                k.stack = old
        for d in range(2):
            k.memset("pool", Sf[d], 0.0)
            k.memset("pool", Sb[d], 0.0)
        visited = set()
        groups = [(0, 2)] + [(2 + 3 * i, 3) for i in range(10)] + [(32, 2)]
        if dbg is not None and "nsteps" in dbg:
            groups = groups[:dbg["nsteps"]]
        gorder = [groups, [groups[0]] + groups[:0:-1]]

        def pre_gen(d, a, n, gp):
            col = d * 8 + h
            isx = a >= 2
            def bc(arr):
                return V(arr.ap[:, a:a + n, col].unsqueeze(2).to_broadcast([128, n, 128]), arr.buf)
            tsl = lambda i: slice((a + i) * 128, (a + i + 1) * 128)
            dgc, E, eg, Rk0, Q0, Q0T, Z = (x[d][0][:, 0:n, :] for x in (g_dgc, g_E, g_eg, g_Rk0, g_Q0, g_Q0T, g_Z))
            LT, Dm, Gm = g_LT[d], g_D[d], g_G[d]
            tm_ = LT[0][:, 0:n, :]
            W, nwT, Vt, ktail, qhT, qkm = (x[d][gp][:, 0:n, :] for x in (g_W, g_nwT, g_Vt, g_ktail, g_qhT, g_qkm))
            pA = nextb(d)
            pAk, pAv = b3(pA, n, BF16, 0), b3(pA, n, BF16, GS)
            for i in range(n):
                k.tr(pAk[:, i, :], tv(1, a + i), ident)
                k.tr(pAv[:, i, :], tv(2, a + i), ident)
            yield
            k.act(Vt, pAv, AF.Copy)
            k.tt("dve", Rk0, pAk, bc(egc), ALU.mult)
            k.tt("dve", ktail, pAk, bc(ekt), ALU.mult)
            yield
            E1, E2, tm2_ = g_E1[d][0][:, 0:n, :], g_E2[d][0][:, 0:n, :], Z
            k.tt("pool", dgc, bcn(identf, n), bc(ngc), ALU.mult)
            pB = nextb(d)
            pBv = b3(pB, n)
            for i in range(n):
                k.mm(pBv[:, i, :], onesf, dgc[:, i, :])
            yield
            k.tt("dve", E, bcn(negm[d], n), pBv, ALU.subtract)
            for i in range(n):
                k.act(E2[:, i, :], E[:, i, :], AF.Exp, bias=ngcb[:, a + i, col:col + 1])
            if isx:
                for i in range(n):
                    k.act(E1[:, i, :], E[:, i, :], AF.Exp, bias=ngc[:, a + i, col:col + 1])
                k.act(eg, pBv, AF.Exp, scale=-1.0)
                k.tt("pool", qhT, tlv(0, a, n), eg, ALU.mult)
            yield
            pC = nextb(d)
            pCv = b3(pC, n)
            for i in range(n):
                k.mm(pCv[:, i, :], tv(1, a + i), tv(1, a + i))
            k.tt("dve", Q0, pCv, E2, ALU.mult)
            if isx:
                pQ = nextb(d)
                pQv = b3(pQ, n)
                for i in range(n):
                    k.mm(pQv[:, i, :], tv(1, a + i), tv(0, a + i))
                k.tt("dve", qkm, pQv, E1, ALU.mult)
            yield
            pT = nextb(d)
            pTv = b3(pT, n, BF16, 0)
            for i in range(n):
                k.tr(pTv[:, i, :], Q0[:, i, :], ident)
            k.act(Q0T, pTv, AF.Copy)
            yield
            k.tt("dve", tm_, Q0, bcn(hmask[:, 0, :], n), ALU.mult)
            k.tt("dve", Dm[0][:, 0:n, :], bcn(ident, n), tm_, ALU.subtract)
            k.tt("pool", tm2_, Q0T, bcn(hmask[:, 0, :], n), ALU.mult)
            k.tt("pool", Gm[0][:, 0:n, :], bcn(ident, n), tm2_, ALU.subtract)
            k.tt("pool", LT[1][:, 0:n, :], Q0T, bcn(hmask[:, 1, :], n), ALU.mult)
            yield
            for m_ in range(1, 7):
                a_, b_ = (m_ - 1) % 2, m_ % 2
                Da, Ga, Lm = Dm[a_][:, 0:n, :], Gm[a_][:, 0:n, :], LT[m_ % 2][:, 0:n, :]
                pz = nextb(d)
                pzv = b3(pz, n)
                for i in range(n):
                    k.mm(pzv[:, i, :], Lm[:, i, :], Da[:, i, :])
                if m_ < 6:
                    k.tt("pool", LT[(m_ + 1) % 2][:, 0:n, :], Q0T, bcn(hmask[:, m_ + 1, :], n), ALU.mult)
                k.act(Z, pzv, AF.Copy)
                yield
                pd_ = nextb(d)
                pdv = b3(pd_, n)
                for i in range(n):
                    k.mm(pdv[:, i, :], Ga[:, i, :], Z[:, i, :])
                if m_ < 6:
                    pg_ = nextb(d)
                    pgv = b3(pg_, n)
                    for i in range(n):
                        k.mm(pgv[:, i, :], Z[:, i, :], Ga[:, i, :])
                k.tt("dve", W if m_ == 6 else Dm[b_][:, 0:n, :], Da, pdv, ALU.subtract)
                if m_ < 6:
                    k.tt("dve", Gm[b_][:, 0:n, :], Ga, pgv, ALU.subtract)
                yield
            pw = nextb(d)
            pwv = b3(pw, n)
            for i in range(n):
                k.mm(pwv[:, i, :], Rk0[:, i, :], W[:, i, :])
            k.act(nwT, pwv, AF.Copy, scale=-1.0)
            yield

        def state_gen(d, a, n, gp):
            col = d * 8 + h
            isx = a >= 2
            W, nwT, Vt, ktail, qhT, qkm = (x[d][gp] for x in (g_W, g_nwT, g_Vt, g_ktail, g_qhT, g_qkm))
            vn = t_vn[d]
            for i in (range(n) if d == 0 else range(n - 1, -1, -1)):
                t = a + i
                sc = lambda arr: arr[:, t, col:col + 1]
                k.mm(pvn[d], W[:, i, :], Vt[:, i, :], start=True, stop=False)
                k.mm(pvn[d], nwT[:, i, :], Sb[d], start=False, stop=True)
                k.act(vn, pvn[d], AF.Identity, scale=sc(beta))
                yield
                if isx:
                    xt = t - 2
                    ov = V(osum.ap[:, xt * 128:(xt + 1) * 128], osb[xt])
                    k.mm(poT[d], Sb[d], qhT[:, i, :], start=True, stop=False)
                    k.mm(poT[d], vn, qkm[:, i, :], start=False, stop=True)
                k.mm(pS[d], ktail[:, i, :], vn)
                if isx:
                    if xt not in visited:
                        visited.add(xt)
                        k.act(ov, poT[d], AF.Copy)
                    else:
                        k.tt("dve", ov, poT[d], ov, ALU.add)
                k.stt("dve", Sf[d], Sf[d], sc(egl), pS[d], ALU.mult, ALU.add)
                k.act(Sb[d], Sf[d], AF.Copy)
                yield

        def run_all(gens):
            gens = list(gens)
            while gens:
                for g_ in list(gens):
                    try:
                        next(g_)
                    except StopIteration:
                        gens.remove(g_)

        ng = len(groups)
        run_all([pre_gen(0, *gorder[0][0], 0), pre_gen(1, *gorder[1][0], 0)])
        for gi in range(ng):
            gens = [state_gen(0, *gorder[0][gi], gi % 2), state_gen(1, *gorder[1][gi], gi % 2)]
            if gi + 1 < ng:
                gens += [pre_gen(0, *gorder[0][gi + 1], (gi + 1) % 2), pre_gen(1, *gorder[1][gi + 1], (gi + 1) % 2)]
            run_all(gens)
        if "S" in dbg_out and h == heads[0]:
            k.dma("sp", dbg_out["S"][0], Sf[0])
            k.dma("sp", dbg_out["S"][1], Sf[1])
        for bi in range(8):
            s = bi * 512
            ovs = [V(osum.ap[:, s:s + 512], osb[bi * 4 + j]) for j in range(4)]
            class _M:
                pass
            ovall = V(osum.ap[:, s:s + 512], osb[bi * 4])
            extra = [osb[bi * 4 + j] for j in range(1, 4)]
            if bi == 0:
                k.dma("sp", wst_f[0], V(winv[:, :, OFF_Z + h * 128:OFF_Z + (h + 1) * 128], win_d.buf))
                k.copy("pool", wbz, wst_f[0])
            pz_ = nextbig()
            zsb = zs_b[0]
            for kk in range(8):
                k.mm(pz_, wbz[:, kk, :], hTv[:, kk, TC + s:TC + s + 512], start=(kk == 0), stop=(kk == 7))
            k.act(zsb, pz_, AF.Silu)
            pp = nextbig()
            sqv = ofin[bi % 2]
            r = rn[0]
            of_ = V(osum.ap[:, s:s + 512], [osb[bi * 4 + j] for j in range(4)])
            k.op("pool", lambda g, sqv=sqv, s=s: g.tensor_tensor(out=sqv.ap, in0=osum.ap[:, s:s + 512], in1=osum.ap[:, s:s + 512], op=ALU.mult),
                 [osb[bi * 4 + j] for j in range(4)], [sqv.buf])
            k.mm(pp, onesb, sqv)
            k.act(r, pp, AF.Ln, bias=EPS, scale=1.0 / 128)
            k.act(r, r, AF.Exp, scale=-0.5)
            k.stt("dve", of_, of_, dnn[:, 0:1], r, ALU.mult, ALU.mult)
            if "o0" in dbg_out and h == heads[0]:
                k.tt("pool", of_, of_, zsb, ALU.mult)
                k.dma("sp", V(dbg_out["o0"].ap[:, s:s + 512], dbg_out["o0"].buf), of_)
                k.copy("pool", sqv, of_)
            else:
                k.tt("pool", sqv, of_, zsb, ALU.mult)
            k.dma("sp", V(oT_d.ap[h, :, s:s + 512], DBuf("st")), sqv)
    k.barrier()
    p4.close()
    s_g.close()
    k.stack = ExitStack()
    if stage <= 4:
        return finish(nc, k, out_d)

    def wchunk_loader(stf, stb):
        cnt = [0]
        def load(src_v, K):
            i = cnt[0] % len(stf)
            j = cnt[0] % len(stb)
            cnt[0] += 1
            k.dma("sp" if i == 0 else "pool", stf[i][:, 0:K, :], src_v)
            k.copy("pool", stb[j][:, 0:K, :], stf[i][:, 0:K, :])
            return stb[j]
        return load

    def load_resident(dst_bf, src_ap, src_buf, K, ncols, stg):
        i = 0
        for k0 in range(0, K, 8):
            kn = min(8, K - k0)
            for c0 in range(0, ncols, 512):
                cn = min(512, ncols - c0)
                st_ = stg[i % len(stg)]
                k.dma("sp" if i % 2 == 0 else "pool", st_[:, 0:kn, 0:cn], V(src_ap[:, k0:k0 + kn, c0:c0 + cn], src_buf))
                k.copy("pool" if i % 2 == 0 else "dve", dst_bf[:, k0:k0 + kn, c0:c0 + cn], st_[:, 0:kn, 0:cn])
                i += 1

    with ExitStack() as st:
        k.stack = st
        FCS = k.sb("FCS", [128, 32, 4, 256], BF16)
        fT = [k.sb(f"fT{i}", [128, 512], BF16) for i in range(2)]
        stf = [k.sb(f"p3stf{i}", [128, 8, 128], F32) for i in range(2)]
        stb = [k.sb(f"p3stb{i}", [128, 8, 128], BF16) for i in range(2)]
        ctab = [k.sb(f"ctab{i}", [128, 4, 512], BF16) for i in range(2)]
        stab = [k.sb(f"stab{i}", [128, 4, 512], BF16) for i in range(2)]
        yblk = [k.sb(f"yblk{i}", [128, 4, 512], BF16) for i in range(2)]
        lw = wchunk_loader(stf, stb)
        nb_ = 0
        for g in range(4):
            wb = lw(V(winv[:, :, OFF_F + g * 128:OFF_F + (g + 1) * 128], win_d.buf), 8)
            for bi, (s, n) in enumerate(xblocks):
                pp = k.banks[nb_ % 2]
                ft = fT[nb_ % 2]
                nb_ += 1
                for kk in range(8):
                    k.mm(pp, wb[:, kk, :], hTv[:, kk, s:s + n], start=(kk == 0), stop=(kk == 7))
                k.act(ft, pp, AF.Copy)
                pf = k.banks[2 + (nb_ % 2)]
                pfv = V(pf.ap.rearrange("p (a b) -> p a b", b=256), pf.buf)
                for j2 in range(2):
                    for jj in range(2):
                        j = j2 * 2 + jj
                        k.mm(pfv[:, jj, :], ft[:, j * 128:(j + 1) * 128], c128)
                    t0 = bi * 4 + j2 * 2
                    if j2 == 0:
                        k.act(FCS[:, t0:t0 + 2, g, :], pfv, AF.Copy)
                    else:
                        k.copy("dve", FCS[:, t0:t0 + 2, g, :], pfv)
        k.barrier()
        cosv = cos_d.ap.rearrange("(tt p) f -> p tt f", p=128)
        sinv = sin_d.ap.rearrange("(tt p) f -> p tt f", p=128)
        ld = 0
        for kb in range(8):
            for t4 in range(8):
                ct, st_ = ctab[ld % 2], stab[ld % 2]
                ld += 1
                k.dma("sp", ct, V(cosv[:, t4 * 4:(t4 + 1) * 4, kb * 512:(kb + 1) * 512], cos_d.buf))
                k.dma("pool", st_, V(sinv[:, t4 * 4:(t4 + 1) * 4, kb * 512:(kb + 1) * 512], sin_d.buf))
                for ti in range(4):
                    tt_ = t4 * 4 + ti
                    for g in range(4):
                        k.mm(k.banks[4 + g], FCS[:, tt_, g, 0:128], ct[:, ti, :], start=(tt_ == 0), stop=False)
                        k.mm(k.banks[4 + g], FCS[:, tt_, g, 128:256], st_[:, ti, :], start=False, stop=(tt_ == 31))
            yb = yblk[kb % 2]
            for g in range(4):
                if g % 2 == 0:
                    k.act(yb[:, g, :], k.banks[4 + g], AF.Copy)
                else:
                    k.copy("dve", yb[:, g, :], k.banks[4 + g])
            k.dma("sp", V(yT_d.ap[:, :, kb * 512:(kb + 1) * 512].rearrange("g p f -> p g f"), DBuf("st")), yb)
        if "fm" in dbg_out:
            pass
        k.barrier()
    k.stack = ExitStack()
    if stage <= 5:
        return finish(nc, k, out_d)

    with ExitStack() as st:
        k.stack = st
        wg = k.sb("wg", [128, 8, 2048], BF16)
        wf4 = k.sb("wf4", [128, 4, 1024], BF16)
        wdn = k.sb("wdn", [128, 8, 1024], BF16)
        stg = [k.sb(f"p5stg{i}", [128, 8, 512], F32) for i in range(1)]
        ytb = [k.sb(f"ytb{i}", [128, 4, 512], BF16) for i in range(2)]
        otb = [k.sb(f"otb{i}", [128, 8, 512], BF16) for i in range(2)]
        g0 = [k.sb(f"g0_{i}", [128, 512], BF16) for i in range(2)]
        g1 = [k.sb(f"g1_{i}", [128, 512], BF16) for i in range(2)]
        m0 = [k.sb(f"m0_{i}", [128, 512], BF16) for i in range(2)]
        m1 = [k.sb(f"m1_{i}", [128, 512], BF16) for i in range(2)]
        mixs = k.sb("mixs", [128, 2, 8, 512], BF16)
        mixs_buf = [Buf("mixs0"), Buf("mixs1")]
        load_resident(wg, winv[:, :, OFF_G:OFF_G + 2048], win_d.buf, 8, 2048, stg)
        load_resident(wf4, wf_d.ap.rearrange("(g p) d -> p g d", p=128), wf_d.buf, 4, 1024, stg)
        load_resident(wdn, wdn_d.ap.rearrange("(h p) d -> p h d", p=128), wdn_d.buf, 8, 1024, stg)
        it = 0
        hmt = [Buf(f"hTm{mt}") for mt in range(8)]
        for mt in range(8):
            s = TC + mt * 512
            yt, ot = ytb[mt % 2], otb[mt % 2]
            k.dma("sp", yt, V(yT_d.ap[:, :, mt * 512:(mt + 1) * 512].rearrange("g p f -> p g f"), yT_d.buf))
            k.dma("pool", ot, V(oT_d.ap[:, :, mt * 512:(mt + 1) * 512].rearrange("h p f -> p h f"), oT_d.buf))
            for dc in range(8):
                dsl = slice(dc * 128, (dc + 1) * 128)
                i2 = it % 2
                it += 1
                pb = [k.banks[4 * i2 + j] for j in range(4)]
                for g in range(4):
                    k.mm(pb[0], wf4[:, g, dsl], yt[:, g, :], start=(g == 0), stop=(g == 3))
                for kk in range(8):
                    k.mm(pb[1], wg[:, kk, dc * 128:(dc + 1) * 128], V(hT.ap[:, kk, s:s + 512], hmt[mt]), start=(kk == 0), stop=(kk == 7))
                for hh in range(8):
                    k.mm(pb[2], wdn[:, hh, dsl], ot[:, hh, :], start=(hh == 0), stop=(hh == 7))
                for kk in range(8):
                    k.mm(pb[3], wg[:, kk, 1024 + dc * 128:1024 + (dc + 1) * 128], V(hT.ap[:, kk, s:s + 512], hmt[mt]), start=(kk == 0), stop=(kk == 7))
                k.act(g0[i2], pb[1], AF.Sigmoid)
                k.act(g1[i2], pb[3], AF.Sigmoid)
                k.tt("dve", m0[i2], pb[0], g0[i2], ALU.mult)
                k.tt("dve", m1[i2], pb[2], g1[i2], ALU.mult)
                k.tt("pool", V(mixs.ap[:, mt % 2, dc, :], mixs_buf[mt % 2]), m0[i2], m1[i2], ALU.add)
            k.copy("act" if mt % 2 == 0 else "dve", V(hT.ap[:, :, s:s + 512], hmt[mt]), V(mixs.ap[:, mt % 2, :, :], mixs_buf[mt % 2]))
        k.barrier()
    k.stack = ExitStack()
    if stage <= 6:
        return finish(nc, k, out_d)

    def branch_tail(mt, producer, cidx, resid_d, final, tb):
        yx, sq, rst, xin_, x1t = tb["yx"][mt % 2], tb["sq"][mt % 2], tb["rst"][mt % 2], tb["xin"], tb["x1t"]
        for dc in range(8):
            pb = k.banks[dc % 2]
            producer(dc, pb)
            k.act(yx[:, dc, :], pb, AF.Copy)
            k.act(sq[:, dc, :], pb, AF.Square)
        pss = k.banks[2]
        for dc in range(8):
            k.mm(pss, onesb, sq[:, dc, :], start=(dc == 0), stop=(dc == 7))
        k.act(rst, pss, AF.Ln, bias=EPS, scale=1.0 / D)
        k.act(rst, rst, AF.Exp, scale=-0.5)
        for dc in range(8):
            k.stt("dve", yx[:, dc, :], yx[:, dc, :], coef[:, cidx, dc:dc + 1], rst, ALU.mult, ALU.mult)
        for j in range(4):
            tok0 = mt * 512 + j * 128
            xi = xin_[j % len(xin_)]
            xo = x1t[j % len(x1t)]
            k.dma("sp" if j % 2 == 0 else "pool", xi, V(resid_d.ap[tok0:tok0 + 128, :], resid_d.buf))
            ba, bb = k.banks[3 + 2 * (j % 2)], k.banks[4 + 2 * (j % 2)]
            for dc in range(8):
                bk = ba if dc < 4 else bb
                k.tr(bk[:, (dc % 4) * 128:(dc % 4 + 1) * 128], yx[:, dc, j * 128:(j + 1) * 128], identf)
            k.tt("dve", xo[:, 0:512], ba, xi[:, 0:512], ALU.add)
            k.tt("dve", xo[:, 512:1024], bb, xi[:, 512:1024], ALU.add)
            if final:
                k.dma("sp", V(out_d.ap[tok0:tok0 + 128, :], DBuf("st")), xo)
            else:
                k.dma("sp", V(x1_d.ap[tok0:tok0 + 128, :], DBuf("st")), xo)
                norm_tile(xo, 2 + mt * 4 + j, 5, 6, hT, tb["hbuf"], tb["tm"])

    def tail_bufs(nbuf, need_tm=True):
        if not need_tm:
            return {"yx": [k.sb(f"yx{i}", [128, 8, 512], F32) for i in range(2)], "sq": [k.sb(f"sqb{i}", [128, 8, 512], BF16) for i in range(2)],
                    "rst": [k.sb(f"rst{i}", [128, 512], F32) for i in range(2)],
                    "xin": [k.sb(f"rxin{i}", [128, D], F32) for i in range(nbuf)], "x1t": [k.sb(f"x1t{i}", [128, D], F32) for i in range(nbuf)], "tm": None}
        return {"yx": [k.sb(f"yx{i}", [128, 8, 512], F32) for i in range(2)], "sq": [k.sb(f"sqb{i}", [128, 8, 512], BF16) for i in range(2)],
                "rst": [k.sb(f"rst{i}", [128, 512], F32) for i in range(2)],
                "xin": [k.sb(f"rxin{i}", [128, D], F32) for i in range(nbuf)], "x1t": [k.sb(f"x1t{i}", [128, D], F32) for i in range(nbuf)],
                "tm": {"sq": [k.sb(f"t_sq{i}", [128, D], F32) for i in range(2)], "ss": [k.sb(f"t_ss{i}", [128, 1], F32) for i in range(2)],
                       "xn": [k.sb(f"t_xn{i}", [128, D], BF16) for i in range(2)], "pt": [k.pv(7, 0, 512, BF16, 128)]}}

    with ExitStack() as st:
        k.stack = st
        wout = k.sb("wout", [128, 8, 1024], BF16)
        stg = [k.sb(f"p5bstg{i}", [128, 8, 512], F32) for i in range(1)]
        load_resident(wout, wout_d.ap.rearrange("(c p) d -> p c d", p=128), wout_d.buf, 8, 1024, stg)
        tb = tail_bufs(2)
        hmt2 = [Buf(f"hTn{mt}") for mt in range(8)]
        for mt in range(8):
            s = TC + mt * 512
            def prod(dc, pb, s=s, mt=mt):
                for c in range(8):
                    k.mm(pb, wout[:, c, dc * 128:(dc + 1) * 128], V(hT.ap[:, c, s:s + 512], hmt2[mt]), start=(c == 0), stop=(c == 7))
            tb["hbuf"] = hmt2[mt]
            branch_tail(mt, prod, 4, x_d, False, tb)
        k.barrier()
    k.stack = ExitStack()
    if stage <= 7:
        return finish(nc, k, out_d)

    wupv = wup_d.ap.rearrange("(k p) c -> p k c", p=128)
    with ExitStack() as st:
        k.stack = st
        stf = [k.sb(f"p6stf{i}", [128, 8, 128], F32) for i in range(2)]
        stb = [k.sb(f"p6stb{i}", [128, 8, 128], BF16) for i in range(4)]
        lw = wchunk_loader(stf, stb)
        apad_b = [k.sb(f"apad{i}", [128, 66, 66], BF16) for i in range(2)]
        dg9_b = [k.sb(f"dg9_{i}", [128, 9, 128], BF16) for i in range(2)]
        sa = [k.sb(f"sa{i}", [128, 512], BF16) for i in range(4)]
        gtc = [k.sb(f"gtc{i}", [128, T], BF16) for i in range(2)]
        k.memset("pool", apad_b[0], 0.0)
        k.memset("pool", apad_b[1], 0.0)
        nb_ = 0
        for c in range(NFF):
            apad, dg9 = apad_b[c % 2], dg9_b[c % 2]
            wa = lw(V(wupv[:, :, c * 128:(c + 1) * 128], wup_d.buf), 8)
            wu = lw(V(wupv[:, :, DFF + c * 128:DFF + (c + 1) * 128], wup_d.buf), 8)
            for tap in range(9):
                k.ts("pool", dg9[:, tap, :], identf, cffn[:, c, tap:tap + 1])
            for bi in range(8):
                s = TC + bi * 512
                pp = k.banks[(0, 1, 6, 7)[nb_ % 4]]
                nb_ += 1
                for kk in range(8):
                    k.mm(pp, wa[:, kk, :], hTv[:, kk, s:s + 512], start=(kk == 0), stop=(kk == 7))
                k.act(apad[:, 1 + bi * 8:1 + bi * 8 + 8, 1:65], V(pp.ap.rearrange("p (r c) -> p r c", c=64), pp.buf), AF.Copy)
            gt = gtc[c % 2]
            for bi in range(8):
                s = TC + bi * 512
                pc = k.banks[2 + (bi % 2)]
                pu = k.banks[4 + (bi % 2)]
                pcv = V(pc.ap.rearrange("p (r c) -> p r c", c=64), pc.buf)
                for tap in range(9):
                    dr, dcc = tap // 3, tap % 3
                    k.mm(pcv, dg9[:, tap, :], apad[:, bi * 8 + dr:bi * 8 + dr + 8, dcc:dcc + 64], start=(tap == 0), stop=(tap == 8))
                k.act(sa[bi % 4], pc, AF.Silu)
                for kk in range(8):
                    k.mm(pu, wu[:, kk, :], hTv[:, kk, s:s + 512], start=(kk == 0), stop=(kk == 7))
                k.tt("dve", gt[:, bi * 512:(bi + 1) * 512], pu, sa[bi % 4], ALU.mult)
            k.dma("sp" if c % 2 == 0 else "pool", V(gT_d.ap[c], DBuf("st")), gt)
        k.barrier()
    k.stack = ExitStack()
    s_h.close()
    k.stack = ExitStack()
    if stage <= 8:
        return finish(nc, k, out_d)

    with ExitStack() as st:
        k.stack = st
        wdown = k.sb("wdown", [128, NFF, 1024], BF16)
        stg = [k.sb(f"p7stg{i}", [128, 8, 512], F32) for i in range(2)]
        load_resident(wdown, wdown_d.ap.rearrange("(c p) d -> p c d", p=128), wdown_d.buf, NFF, 1024, stg)
        gbl = [k.sb(f"gbl{i}", [128, NFF, 512], BF16) for i in range(2)]
        tb = tail_bufs(2, need_tm=False)
        for mt in range(8):
            gb = gbl[mt % 2]
            k.dma("sp", gb[:, 0:11, :], V(gT_d.ap[0:11, :, mt * 512:(mt + 1) * 512].rearrange("c p f -> p c f"), gT_d.buf))
            k.dma("pool", gb[:, 11:NFF, :], V(gT_d.ap[11:NFF, :, mt * 512:(mt + 1) * 512].rearrange("c p f -> p c f"), gT_d.buf))
            def prod(dc, pb, gb=gb):
                for c in range(NFF):
                    k.mm(pb, wdown[:, c, dc * 128:(dc + 1) * 128], gb[:, c, :], start=(c == 0), stop=(c == NFF - 1))
            branch_tail(mt, prod, 7, x1_d, True, tb)
        k.barrier()
    k.stack = ExitStack()
    return finish(nc, k, out_d)


def finish(nc, k, out_d):
    k.barrier()
    return nc


def prep_inputs(inp, b):
    f = lambda a: np.ascontiguousarray(a, dtype=np.float32)
    colmajor = lambda v: f(np.asarray(v).reshape(-1, 128).T)
    m = {}
    m["x"] = f(inp["x"][b])
    m["ctx"] = f(inp["ctx"][b])
    m["cc"] = f(np.stack([colmajor(inp["c"][b]), colmajor(inp["c_ctx"])], axis=-1))
    m["w_ada"] = f(inp["w_ada"][0])
    m["b_ada"] = colmajor(inp["b_ada"][0])
    m["norms"] = f(np.stack([colmajor(inp[n][0]) for n in ("norm_pre_mix", "norm_post_mix", "norm_pre_ffn", "norm_post_ffn")], axis=1))
    m["w_in"] = f(inp["w_in"][0])
    cq = np.asarray(inp["conv_qkv"][0])
    m["conv_qkv"] = f(cq.T.reshape(24, 128, 3).transpose(1, 0, 2))
    gp = np.stack([np.asarray(inp["a_log"][0]).reshape(16), np.asarray(inp["dt_bias"][0]).reshape(16)], 0)
    m["gpar"] = f(np.broadcast_to(gp[None], (128, 2, 16)))
    m["dn_norm"] = f(np.asarray(inp["dn_norm"][0]).reshape(128, 1))
    m["w_fourier"] = f(inp["w_fourier"][0])
    m["w_dn"] = f(inp["w_dn"][0])
    m["w_out"] = f(inp["w_out"][0])
    m["w_up"] = f(inp["w_up"][0])
    cf = np.asarray(inp["conv_ffn"][0]).reshape(9, DFF)
    m["conv_ffn"] = f(cf.T.reshape(NFF, 128, 9).transpose(1, 0, 2))
    m["w_down"] = f(inp["w_down"][0])
    return m


_CONST = {}


def consts():
    if not _CONST:
        idx = np.arange(T, dtype=np.int64)
        ang = (2.0 * np.pi / T) * ((idx[:, None] * idx[None, :]) % T).astype(np.float64)
        s = 1.0 / np.sqrt(float(T) * 128.0)
        _CONST["dft_cos"] = (np.cos(ang) * s).astype(ml_dtypes.bfloat16)
        _CONST["dft_sin"] = (np.sin(ang) * s).astype(ml_dtypes.bfloat16)
        i8 = np.arange(128, dtype=np.int64)
        a8 = (2.0 * np.pi / 128) * ((i8[:, None] * i8[None, :]) % 128).astype(np.float64)
        j8 = np.arange(128)
        hm = []
        for m_ in range(7):
            s_ = 2 ** m_
            blk2 = (j8[:, None] // (2 * s_)) == (j8[None, :] // (2 * s_))
            half = (j8[:, None] // s_) != (j8[None, :] // s_)
            hm.append((blk2 & half).astype(np.float32))
        _CONST["hmask"] = np.stack(hm, axis=1).astype(ml_dtypes.bfloat16)
        _CONST["dft128"] = np.concatenate([np.cos(a8), -np.sin(a8)], axis=1).astype(ml_dtypes.bfloat16)
    return _CONST


def kernel(**inputs):
    inp = {k_: np.asarray(v) for k_, v in inputs.items()}
    nc = build()
    cst = consts()
    in_maps = []
    for b in range(8):
        m = prep_inputs(inp, b)
        m.update(cst)
        in_maps.append(m)
    res = run_bass_kernel_spmd(nc, in_maps, core_ids=list(range(8)))
    return np.stack([np.asarray(r["out"], dtype=np.float32) for r in res.results], axis=0)
```

```python
import os
from contextlib import ExitStack
import numpy as np
import ml_dtypes
import concourse.bass as bass
import concourse.mybir as mybir
from concourse.bass_utils import run_bass_kernel_spmd

F32 = mybir.dt.float32
BF16 = mybir.dt.bfloat16
AF = mybir.ActivationFunctionType
ALU = mybir.AluOpType

D = 1024
T = 4096
TC = 256
TA = TC + T
NT = TA // 128
H = 8
OFF_F, OFF_Q, OFF_K, OFF_V, OFF_Z, OFF_B, OFF_A, OFF_G = 0, 512, 1536, 2560, 3584, 4608, 4624, 4640
INW = 6688
DFF = 2816
NFF = DFF // 128
EPS = 1e-6


class Buf:
    __slots__ = ("name", "last_w", "readers", "dsem", "dcount", "excl")

    def __init__(self, name, excl=False):
        self.name = name
        self.excl = excl
        self.last_w = None
        self.readers = []
        self.dsem = None
        self.dcount = 0


class V:
    __slots__ = ("ap", "buf")

    def __init__(self, ap, buf):
        self.ap = ap
        self.buf = buf

    def __getitem__(self, idx):
        return V(self.ap[idx], self.buf)

    def sub(self, idx, buf):
        return V(self.ap[idx], buf)


def _bufs(*vs):
    out = []
    for v in vs:
        if isinstance(v, V) and v.buf is not None:
            for b in (v.buf if isinstance(v.buf, (list, tuple)) else (v.buf,)):
                if b not in out:
                    out.append(b)
    return out


def _ap(v):
    return v.ap if isinstance(v, V) else v


class K:
    def __init__(self, nc):
        self.nc = nc
        self.engs = {"pe": nc.tensor, "act": nc.scalar, "dve": nc.vector, "pool": nc.gpsimd, "sp": nc.sync}
        self.sem = {n: nc.alloc_semaphore(f"s_{n}") for n in self.engs}
        self.cnt = {n: 0 for n in self.engs}
        self.known = {n: {} for n in self.engs}
        self.dsems = []
        self.nins = 0
        self.nwaits = 0
        self.stack = ExitStack()
        self.limit = None
        self.log = []
        self.sched = os.environ.get('KSCHED', '1') == '1'
        self.pending = []

    def _uid(self):
        self.uid = getattr(self, 'uid', 0) + 1
        return self.uid

    def sb(self, name, shape, dt, nbuf=None):
        t = self.stack.enter_context(self.nc.sbuf_tensor(f"sb{self._uid()}_" + name, list(shape), dt))
        return V(t[:] if hasattr(t, "__getitem__") else t.ap(), Buf(name) if nbuf is None else nbuf)

    def init_banks(self):
        self.banks = []
        for i in range(8):
            t = self.nc.psum_tensor(f"ps_bank{i}", [128, 512], F32).__enter__()
            self.banks.append(V(t[:], Buf(f"bank{i}", excl=True)))

    def pv(self, bank, lo, hi, dt=F32, inner=None):
        b = self.banks[bank]
        ap = b.ap[:, lo:hi]
        if dt != F32:
            ap = ap.bitcast(dt)
        if inner is not None:
            ap = ap.rearrange("p (a b) -> p a b", b=inner)
        return V(ap, b.buf)

    def _wait(self, e, ev):
        sem, val, src = ev
        if src == "pe" and e == "pe":
            return
        kn = self.known[e]
        if kn.get(sem.num, 0) >= val:
            return
        kn[sem.num] = val
        self.engs[e].wait_ge(sem, val)
        self.nwaits += 1

    def _deps(self, e, reads, writes):
        best = {}
        def add(ev):
            s = ev[0].num
            if s not in best or best[s][1] < ev[1]:
                best[s] = ev
        for b in reads:
            if b.last_w is not None:
                add(b.last_w)
        for b in writes:
            if b.last_w is not None:
                add(b.last_w)
            for ev in b.readers:
                add(ev)
        for ev in best.values():
            self._wait(e, ev)

    def _record(self, ev, reads, writes):
        for b in reads:
            if b in writes:
                continue
            b.readers.append(ev)
            if len(b.readers) > 10:
                best = {}
                for x in b.readers:
                    s = x[0].num
                    if s not in best or best[s][1] < x[1]:
                        best[s] = x
                b.readers = list(best.values())
        for b in writes:
            b.last_w = ev
            b.readers = []

    def op(self, e, fn, reads, writes, cost=300.0):
        if self.sched:
            self.pending.append(("op", e, fn, list(reads), list(writes), float(cost)))
            return
        self._emit_op(e, fn, reads, writes)

    def _emit_op(self, e, fn, reads, writes):
        if self.limit is not None and self.nins >= self.limit:
            return
        ex = [b for b in reads if b.excl and b not in writes]
        if ex:
            writes = list(writes) + ex
        self._deps(e, reads, writes)
        ins = fn(self.engs[e])
        if os.environ.get('PRINS') and self.nins in range(int(os.environ.get('PRINS','0')), int(os.environ.get('PRINS','0')) + 4):
            print('INS', self.nins, ins.concise())
        self.cnt[e] += 1
        ins.then_inc(self.sem[e], 1)
        self._record((self.sem[e], self.cnt[e], e), reads, writes)
        self.nins += 1

    def dma(self, q, out, in_, key=None, nbytes=None, **kw):
        if self.sched:
            if nbytes is None:
                shp = _ap(out).shape
                nbytes = 4
                for d_ in shp:
                    nbytes *= d_
            self.pending.append(("dma", q, (out, in_, key, kw), _bufs(in_), _bufs(out), 2000.0 + nbytes / 100.0))
            return
        self._emit_dma(q, out, in_, key, **kw)

    def _emit_dma(self, q, out, in_, key=None, **kw):
        if self.limit is not None and self.nins >= self.limit:
            return
        reads, writes = _bufs(in_), _bufs(out)
        self._deps(q, reads, writes)
        kb = key.buf if key is not None else (out.buf if not isinstance(out.buf, DBuf) else in_.buf)
        if isinstance(kb, (list, tuple)):
            kb = kb[0]
        if kb.dsem is None:
            kb.dsem = self.nc.alloc_semaphore(f"d{self._uid()}_{kb.name}")
            self.dsems.append(kb)
        ins = self.engs[q].dma_start(out=_ap(out), in_=_ap(in_), **kw)
        kb.dcount += 1
        ins.then_inc(kb.dsem, 16)
        self._record((kb.dsem, 16 * kb.dcount, "dma"), reads, writes)
        self.nins += 1

    def flush(self):
        ops = self.pending
        self.pending = []
        n = len(ops)
        if n == 0:
            return
        SYNC = 200.0
        preds = [[] for _ in range(n)]
        lastw = {}
        rdrs = {}
        for i, (kind, e, fn, reads, writes, cost) in enumerate(ops):
            wr = list(writes) + [b for b in reads if b.excl and b not in writes]
            ps = set()
            for b in reads:
                if b in lastw:
                    ps.add(lastw[b])
            for b in wr:
                if b in lastw:
                    ps.add(lastw[b])
                for r_ in rdrs.get(b, ()):
                    ps.add(r_)
            ps.discard(i)
            preds[i] = list(ps)
            for b in reads:
                if b not in wr:
                    rdrs.setdefault(b, []).append(i)
            for b in wr:
                lastw[b] = i
                rdrs[b] = []
        succs = [[] for _ in range(n)]
        for i in range(n):
            for p in preds[i]:
                succs[p].append(i)
        occ = [0.0] * n
        lat = [0.0] * n
        for i, (kind, e, fn, reads, writes, cost) in enumerate(ops):
            if kind == "dma":
                occ[i] = 60.0
                lat[i] = cost
            else:
                occ[i] = cost
                lat[i] = cost
        blevel = [0.0] * n
        for i in range(n - 1, -1, -1):
            m_ = 0.0
            for s_ in succs[i]:
                if blevel[s_] > m_:
                    m_ = blevel[s_]
            blevel[i] = lat[i] + m_
        import heapq
        npred = [len(p) for p in preds]
        ready_t = [0.0] * n
        eng_free = {}
        readyq = {}
        for i in range(n):
            if npred[i] == 0:
                heapq.heappush(readyq.setdefault(ops[i][1], []), (-blevel[i], i))
        order = []
        done = 0
        while done < n:
            best = None
            for e, hq in readyq.items():
                if not hq:
                    continue
                tfree = eng_free.get(e, 0.0)
                cand = None
                top = heapq.nsmallest(6, hq)
                for pr, i in top:
                    st_ = max(tfree, ready_t[i])
                    key = (st_, pr)
                    if cand is None or key < cand[0]:
                        cand = (key, i)
                if best is None or cand[0] < best[0]:
                    best = (cand[0], cand[1], e)
            (st_, pr), i, e = best
            hq = readyq[e]
            hq.remove((-blevel[i], i))
            heapq.heapify(hq)
            eng_free[e] = st_ + occ[i]
            fin = st_ + lat[i]
            order.append(i)
            done += 1
            for s_ in succs[i]:
                rt = fin + (0.0 if ops[s_][1] == e else SYNC)
                if rt > ready_t[s_]:
                    ready_t[s_] = rt
                npred[s_] -= 1
                if npred[s_] == 0:
                    heapq.heappush(readyq.setdefault(ops[s_][1], []), (-blevel[s_], s_))
        if os.environ.get('KSIM'):
            print('flush n=%d simulated makespan %.1f us' % (n, max(eng_free.values()) / 1e3), {e_: round(v_ / 1e3) for e_, v_ in eng_free.items()})
        for i in order:
            kind, e, fn, reads, writes, cost = ops[i]
            if kind == "dma":
                out, in_, key, kw = fn
                self._emit_dma(e, out, in_, key, **kw)
            else:
                self._emit_op(e, fn, reads, writes)

    def barrier(self):
        self.flush()
        for e in self.engs:
            for f in self.engs:
                if f != e and self.cnt[f] > 0:
                    self._wait(e, (self.sem[f], self.cnt[f], f))
            for kb in self.dsems:
                if kb.dcount > 0:
                    self._wait(e, (kb.dsem, 16 * kb.dcount, "dma"))

    @staticmethod
    def _fsz(v):
        shp = _ap(v).shape
        n = 1
        for d_ in shp[1:]:
            n *= d_
        return n

    def _ecost(self, e, out, in_):
        n = self._fsz(out)
        if e == "pool":
            return 150.0 + 2.0 * n
        if e == "act":
            return 220.0 + 0.72 * n
        return 100.0 + (1.05 * n if (_ap(in_).dtype == F32 or _ap(out).dtype == F32) else 0.6 * n)

    def mm(self, out, lhsT, rhs, start=True, stop=True):
        n = self._fsz(out)
        c = 40.0 + n * (1.9 if _ap(lhsT).dtype == F32 else 0.45)
        self.op("pe", lambda e: e.matmul(_ap(out), lhsT=_ap(lhsT), rhs=_ap(rhs), start=start, stop=stop),
                _bufs(lhsT, rhs) + ([] if start else _bufs(out)), _bufs(out), cost=c)

    def tr(self, out, in_, ident):
        n = self._fsz(out)
        c = 40.0 + n * (1.9 if _ap(in_).dtype == F32 else 0.45)
        self.op("pe", lambda e: e.transpose(out=_ap(out), in_=_ap(in_), identity=_ap(ident)), _bufs(in_, ident), _bufs(out), cost=c)

    def act(self, out, in_, func, bias=0.0, scale=1.0, accum=None, eng="act"):
        kw = {}
        if accum is not None:
            kw["accum_out"] = _ap(accum)
        self.op("act", lambda e: e.activation(out=_ap(out), in_=_ap(in_), func=func, bias=_ap(bias), scale=_ap(scale), **kw),
                _bufs(in_, bias, scale), _bufs(out, accum), cost=self._ecost("act", out, in_))

    def ts(self, e, out, in0, s1, s2=None, op0=ALU.mult, op1=None):
        if op1 is None:
            f = lambda g: g.tensor_scalar(out=_ap(out), in0=_ap(in0), scalar1=_ap(s1), scalar2=None, op0=op0)
        else:
            f = lambda g: g.tensor_scalar(out=_ap(out), in0=_ap(in0), scalar1=_ap(s1), scalar2=_ap(s2), op0=op0, op1=op1)
        self.op(e, f, _bufs(in0, s1, s2), _bufs(out), cost=self._ecost(e, out, in0))

    def tt(self, e, out, a, b, op):
        self.op(e, lambda g: g.tensor_tensor(out=_ap(out), in0=_ap(a), in1=_ap(b), op=op), _bufs(a, b), _bufs(out), cost=self._ecost(e, out, a))

    def stt(self, e, out, in0, scalar, in1, op0, op1):
        self.op(e, lambda g: g.scalar_tensor_tensor(out=_ap(out), in0=_ap(in0), scalar=_ap(scalar), in1=_ap(in1), op0=op0, op1=op1),
                _bufs(in0, scalar, in1), _bufs(out), cost=self._ecost(e, out, in0))

    def copy(self, e, out, in_):
        if e == "act":
            self.act(out, in_, AF.Copy)
        else:
            self.op(e, lambda g: g.tensor_copy(out=_ap(out), in_=_ap(in_)), _bufs(in_), _bufs(out), cost=self._ecost(e, out, in_))

    def recip(self, out, in_):
        self.op("dve", lambda g: g.reciprocal(out=_ap(out), in_=_ap(in_)), _bufs(in_), _bufs(out), cost=100.0 + 6.3 * self._fsz(out))

    def memset(self, e, out, val):
        self.op(e, lambda g: g.memset(_ap(out), val), [], _bufs(out))

    def asel(self, out, in_, pattern, cmp, fill, base, cm):
        self.op("pool", lambda g: g.affine_select(out=_ap(out), in_=_ap(in_), pattern=pattern, compare_op=cmp, fill=fill,
                                                  base=base, channel_multiplier=cm), _bufs(in_), _bufs(out))


class DBuf(Buf):
    __slots__ = ("is_dram",)

    def __init__(self, name):
        super().__init__(name)
        self.is_dram = True


def dramv(nc, name, shape, dt, kind):
    t = nc.dram_tensor(name, list(shape), dt, kind=kind)
    return V(t.ap(), DBuf(name))


def build(stage=99, dbg=None):
    nc = bass.Bass("TRN2", target_bir_lowering=False)
    k = K(nc)
    k.init_banks()
    if dbg and 'limit' in dbg:
        k.limit = dbg['limit']
    IN = lambda name, shape, dt=F32: dramv(nc, name, shape, dt, "ExternalInput")
    x_d = IN("x", [T, D])
    ctx_d = IN("ctx", [TC, D])
    cc_d = IN("cc", [128, 8, 2])
    wada_d = IN("w_ada", [D, 6 * D])
    bada_d = IN("b_ada", [128, 48])
    nrm_d = IN("norms", [128, 4, 8])
    win_d = IN("w_in", [D, INW])
    cqkv_d = IN("conv_qkv", [128, 24, 3])
    gpar_d = IN("gpar", [128, 2, 16])
    dnn_d = IN("dn_norm", [128, 1])
    wf_d = IN("w_fourier", [512, D])
    wdn_d = IN("w_dn", [D, D])
    wout_d = IN("w_out", [D, D])
    wup_d = IN("w_up", [D, 2 * DFF])
    cffn_d = IN("conv_ffn", [128, NFF, 9])
    wdown_d = IN("w_down", [DFF, D])
    cos_d = IN("dft_cos", [T, T], BF16)
    sin_d = IN("dft_sin", [T, T], BF16)
    c128_d = IN("dft128", [128, 256], BF16)
    hmask_d = IN("hmask", [128, 7, 128], BF16)
    out_d = dramv(nc, "out", [T, D], F32, "ExternalOutput")
    dbg_out = {}
    if dbg:
        for nm, shp in dbg.items():
            if nm in ("heads", "nsteps", "limit"):
                continue
            dbg_out[nm] = dramv(nc, "dbg_" + nm, shp, F32, "ExternalOutput")
    x1_d = dramv(nc, "x1_scr", [T, D], F32, "Internal")
    oT_d = dramv(nc, "oT_scr", [H, 128, T], BF16, "Internal")
    gT_d = dramv(nc, "gT_scr", [NFF, 128, T], BF16, "Internal")
    yT_d = dramv(nc, "yT_scr", [4, 128, T], BF16, "Internal")

    identf = k.sb("identf", [128, 128], F32)
    ident = k.sb("ident", [128, 128], BF16)
    onesf = k.sb("onesf", [128, 128], F32)
    onesb = k.sb("onesb", [128, 128], BF16)
    negm = [k.sb(f"negm{d}", [128, 128], F32) for d in range(2)]
    smask = [k.sb(f"smask{d}", [128, 128], F32) for d in range(2)]
    ut = [k.sb(f"ut{d}", [128, 128], F32) for d in range(2)]
    zerof = k.sb("zerof", [128, 128], F32)
    scal_t = k.sb("scal_t", [128, 8], F32)
    k.memset("pool", zerof, 0.0)
    k.memset("pool", onesf, 1.0)
    k.copy("dve", onesb, onesf)
    k.asel(identf, zerof, [[-1, 128]], ALU.not_equal, 1.0, 0, 1)
    k.copy("dve", ident, identf)
    k.asel(negm[0], zerof, [[1, 128]], ALU.is_ge, -1e5, 0, -1)
    k.asel(smask[0], onesf, [[1, 128]], ALU.is_gt, 0.0, 0, -1)
    k.asel(ut[0], onesf, [[1, 128]], ALU.is_ge, 0.0, 0, -1)
    k.asel(negm[1], zerof, [[-1, 128]], ALU.is_ge, -1e5, 0, 1)
    k.asel(smask[1], onesf, [[-1, 128]], ALU.is_gt, 0.0, 0, 1)
    k.asel(ut[1], onesf, [[-1, 128]], ALU.is_ge, 0.0, 0, 1)

    nrm = k.sb("nrm", [128, 4, 8], F32)
    k.dma("sp", nrm, nrm_d)
    cqkv = k.sb("cqkv", [128, 24, 3], F32)
    k.dma("sp", cqkv, cqkv_d)
    gpar = k.sb("gpar", [128, 2, 16], F32)
    k.dma("sp", gpar, gpar_d)
    dnn = k.sb("dnn", [128, 1], F32)
    k.dma("sp", dnn, dnn_d)
    cffn = k.sb("cffn", [128, NFF, 9], F32)
    k.dma("sp", cffn, cffn_d)
    c128 = k.sb("c128", [128, 256], BF16)
    k.dma("sp", c128, c128_d)
    hmask = k.sb("hmask", [128, 7, 128], BF16)
    k.dma("sp", hmask, hmask_d)
    bada = k.sb("bada", [128, 48], F32)
    k.dma("sp", bada, bada_d)
    cc = k.sb("cc", [128, 8, 2], F32)
    k.dma("sp", cc, cc_d)

    mod = k.sb("mod", [128, 48, 2], F32)
    scc = k.sb("scc", [128, 8, 2], F32)
    k.act(scc, cc, AF.Silu)
    with ExitStack() as st:
        k.stack = st
        wa = [k.sb(f"wa{i}", [128, 8, 512], F32) for i in range(2)]
        pm = k.pv(0, 0, 96, F32, 2)
        wv = wada_d.ap.rearrange("(k p) c -> p k c", p=128)
        for blk in range(12):
            w = wa[blk % 2]
            k.dma("sp" if blk % 2 == 0 else "pool", w, V(wv[:, :, blk * 512:(blk + 1) * 512], wada_d.buf))
            for oc in range(4):
                for kk in range(8):
                    k.mm(pm[:, blk * 4 + oc, :], w[:, kk, oc * 128:(oc + 1) * 128], scc[:, kk, :], start=(kk == 0), stop=(kk == 7))
        for j in range(2):
            k.tt("dve", mod[:, :, j], pm[:, :, j], bada, ALU.add)
        k.barrier()
    k.stack = ExitStack()
    coef = k.sb("coef", [128, 8, 8], F32)
    def modc(i, j):
        return mod[:, i * 8:(i + 1) * 8, j]
    k.stt("dve", coef[:, 0, :], modc(1, 0), 1.0, nrm[:, 0, :], ALU.add, ALU.mult)
    k.copy("dve", coef[:, 1, :], modc(0, 0))
    k.stt("dve", coef[:, 2, :], modc(1, 1), 1.0, nrm[:, 0, :], ALU.add, ALU.mult)
    k.copy("dve", coef[:, 3, :], modc(0, 1))
    k.tt("dve", coef[:, 4, :], modc(2, 0), nrm[:, 1, :], ALU.mult)
    k.stt("dve", coef[:, 5, :], modc(4, 0), 1.0, nrm[:, 2, :], ALU.add, ALU.mult)
    k.copy("dve", coef[:, 6, :], modc(3, 0))
    k.tt("dve", coef[:, 7, :], modc(5, 0), nrm[:, 3, :], ALU.mult)
    if "coef" in dbg_out:
        k.dma("sp", dbg_out["coef"], coef)
    if stage <= 0:
        return finish(nc, k, out_d)

    s_h = ExitStack()
    k.stack = s_h
    hT = k.sb("hT", [128, 8, TA], BF16)
    hbuf = [Buf(f"hT{t}") for t in range(NT)]

    def norm_tile(src_tile_v, tile_idx, ca, cb, dst, dstbufs, tm):
        nb = len(tm["sq"])
        sq = tm["sq"][tile_idx % nb]
        ss = tm["ss"][tile_idx % nb]
        xn = tm["xn"][tile_idx % nb]
        pt = tm["pt"][tile_idx % len(tm["pt"])]
        k.act(sq, src_tile_v, AF.Square, accum=ss)
        k.act(ss, ss, AF.Sqrt, bias=EPS, scale=1.0 / D)
        k.recip(ss, ss)
        k.ts("dve", xn, src_tile_v, ss[:, 0:1])
        for c in range(8):
            k.tr(pt[:, c, :], xn[:, c * 128:(c + 1) * 128], ident)
        for c in range(8):
            dv = V(dst.ap[:, c, tile_idx * 128:(tile_idx + 1) * 128], dstbufs[tile_idx] if isinstance(dstbufs, list) else dstbufs)
            if c % 2 == 0:
                k.ts("dve", dv, pt[:, c, :], coef[:, ca, c:c + 1], coef[:, cb, c:c + 1], ALU.mult, ALU.add)
            else:
                k.act(dv, pt[:, c, :], AF.Identity, bias=coef[:, cb, c:c + 1], scale=coef[:, ca, c:c + 1])

    p1 = ExitStack()
    k.stack = p1
    nt_sq = [k.sb(f"nt_sq{i}", [128, D], F32) for i in range(2)]
    nt_ss = [k.sb(f"nt_ss{i}", [128, 1], F32) for i in range(2)]
    nt_xn = [k.sb(f"nt_xn{i}", [128, D], BF16) for i in range(2)]
    xin = [k.sb(f"xin{i}", [128, D], F32) for i in range(3)]
    nt_pt = [k.pv(i, 0, 512, BF16, 128) for i in range(2)]
    tm1 = {"sq": nt_sq, "ss": nt_ss, "xn": nt_xn, "pt": nt_pt}
    for t in range(NT):
        xi = xin[t % 3]
        if t < 2:
            src = V(ctx_d.ap[t * 128:(t + 1) * 128, :], ctx_d.buf)
        else:
            src = V(x_d.ap[(t - 2) * 128:(t - 1) * 128, :], x_d.buf)
        k.dma("sp" if t % 2 == 0 else "pool", xi, src)
        norm_tile(xi, t, 2 if t < 2 else 0, 3 if t < 2 else 1, hT, hbuf, tm1)
    k.barrier()
    p1.close()
    k.stack = ExitStack()
    if "hT" in dbg_out:
        with ExitStack() as st:
            k.stack = st
            tmpf = k.sb("dbg_hT", [128, 8, 1024], F32)
            k.copy("dve", tmpf, V(hT.ap[:, :, 0:1024], None))
            k.dma("sp", dbg_out["hT"], tmpf)
            k.barrier()
        k.stack = ExitStack()
    hTv = V(hT.ap, Buf("hT_all"))
    if stage <= 1:
        return finish(nc, k, out_d)

    winv = win_d.ap.rearrange("(k p) c -> p k c", p=128)

    def bcast_t(v, n):
        return V(v.ap.unsqueeze(1).to_broadcast([128, n, 16]), v.buf)

    s_g = ExitStack()
    k.stack = s_g
    beta = k.sb("beta", [128, NT, 16], F32)
    ngc = k.sb("ngc", [128, NT, 16], F32)
    ngcb = k.sb("ngcb", [128, NT, 16], F32)
    egc = k.sb("egc", [128, NT, 16], F32)
    ekt = k.sb("ekt", [128, NT, 16], F32)
    egl = k.sb("egl", [128, NT, 16], F32)
    with ExitStack() as st:
        k.stack = st
        wbaf = k.sb("wbaf", [128, 8, 32], F32)
        wba = k.sb("wba", [128, 8, 32], BF16)
        graw = k.sb("graw", [128, NT, 32], F32)
        gg = k.sb("gg", [128, NT, 16], F32)
        gtmp = k.sb("gtmp", [128, NT, 16], F32)
        lnb = k.sb("lnb", [128, NT, 16], F32)
        negA = k.sb("negA", [128, 16], F32)
        pg = k.pv(0, 0, 512, F32, 32)
        pc0, pc1, pt0, pt1 = k.banks[1], k.banks[2], k.banks[3], k.banks[4]
        k.dma("pool", wba, V(winv[:, :, OFF_B:OFF_B + 32], win_d.buf))
        for g0 in range(0, NT, 16):
            n = min(16, NT - g0)
            for j in range(n):
                t = g0 + j
                for kk in range(8):
                    k.mm(pg[:, j, :], hTv[:, kk, t * 128:(t + 1) * 128], wba[:, kk, :], start=(kk == 0), stop=(kk == 7))
            k.copy("act", graw[:, g0:g0 + n, :], pg[:, 0:n, :])
        k.act(beta, graw[:, :, 0:16], AF.Sigmoid)
        k.act(lnb, graw[:, :, 0:16], AF.Exp, scale=-1.0)
        k.act(lnb, lnb, AF.Ln, bias=1.0)
        k.act(negA, gpar[:, 0, :], AF.Exp)
        k.ts("dve", negA, negA, -1.0)
        k.tt("dve", gg, graw[:, :, 16:32], bcast_t(gpar[:, 1, :], NT), ALU.add)
        k.act(gg, gg, AF.Exp)
        k.act(gg, gg, AF.Ln, bias=1.0)
        k.tt("dve", gg, gg, bcast_t(negA, NT), ALU.mult)
        if "g" in dbg_out:
            k.dma("sp", dbg_out["g"], gg)
            k.dma("sp", dbg_out["beta"], beta)
        pcs = [pc0, pc1]
        pts = [pt0, pt1]
        for d in range(2):
            pcv = V(pcs[d].ap[:, 0:NT * 8].rearrange("p (t c) -> p t c", c=8), pcs[d].buf)
            ptv = V(pts[d].ap[:, 0:NT * 8].rearrange("p (t c) -> p t c", c=8), pts[d].buf)
            k.mm(pcv, ut[d], gg[:, :, d * 8:(d + 1) * 8])
            k.mm(ptv, onesf, gg[:, :, d * 8:(d + 1) * 8])
            sl = slice(d * 8, (d + 1) * 8)
            k.act(ngc[:, :, sl], pcv, AF.Copy, scale=-1.0)
            k.stt("dve", ngcb[:, :, sl], pcv, -1.0, lnb[:, :, sl], ALU.mult, ALU.subtract)
            k.act(egc[:, :, sl], pcv, AF.Exp)
            k.tt("dve", gtmp[:, :, sl], ptv, ngc[:, :, sl], ALU.add)
            k.act(ekt[:, :, sl], gtmp[:, :, sl], AF.Exp)
            k.act(egl[:, :, sl], ptv, AF.Exp)
        k.barrier()
    k.stack = ExitStack()
    if stage <= 2:
        return finish(nc, k, out_d)

    blocks = [(0, TC)] + [(TC + 512 * i, 512) for i in range(8)]
    xblocks = blocks[1:]
    poff = lambda tok: 1 + tok if tok < TC else 3 + tok
    heads = list(range(H)) if dbg is None or "heads" not in dbg else dbg["heads"]
    p4 = ExitStack()
    k.stack = p4
    ring = [[k.sb(f"ring{ty}_{i}", [128, 514], BF16) for i in range(3)] for ty in range(3)]
    qT = k.sb("qT", [128, TA], BF16)
    kT = k.sb("kT", [128, TA], BF16)
    vT = k.sb("vT", [128, TA], BF16)
    zs_b = [k.sb(f"zsb{i}", [128, 512], BF16) for i in range(1)]
    osum = k.sb("osum", [128, T], F32)
    wst_b = [k.sb(f"wstb{i}", [128, 8, 128], BF16) for i in range(3)]
    dgt3 = [k.sb(f"dgt{ty}", [128, 3, 128], BF16) for ty in range(3)]
    wbz = k.sb("wbz", [128, 8, 128], BF16)
    rn = [k.sb(f"rn{i}", [128, 512], F32) for i in range(1)]
    ofin = [k.sb(f"ofin{i}", [128, 512], BF16) for i in range(2)]
    osb = [Buf(f"osum{t}") for t in range(32)]
    pbig = [k.banks[0], k.banks[1]]
    GS = 3
    rot = [[k.banks[3 * d + i] for i in range(3)] for d in range(2)]
    rcnt = [0, 0]
    def nextb(d):
        rcnt[d] += 1
        return rot[d][rcnt[d] % 3]
    def b3(bank, n, dt=F32, off=0):
        if dt == F32:
            ap = bank.ap[:, off * 128:(off + n) * 128].rearrange("p (a b) -> p a b", b=128)
        else:
            ap = bank.ap[:, off * 64:(off + n) * 64].bitcast(BF16).rearrange("p (a b) -> p a b", b=128)
        return V(ap, bank.buf)
    pvn = [k.pv(6 + d, 0, 128) for d in range(2)]
    poT = [k.pv(6 + d, 128, 256) for d in range(2)]
    pS = [k.pv(6 + d, 256, 384) for d in range(2)]
    def gtmp(name, dt, nb=1):
        return [[k.sb(f"{name}{d}_{i}", [128, GS, 128], dt) for i in range(nb)] for d in range(2)]
    g_dgc, g_E = gtmp("dgc", F32), gtmp("E", F32)
    g_eg, g_Rk0, g_Q0, g_Q0T, g_Z = (gtmp(nm, BF16) for nm in ("eg", "Rk0", "Q0", "Q0T", "Z"))
    g_E1, g_E2 = gtmp("E1", BF16), gtmp("E2", BF16)
    g_LT, g_D, g_G = gtmp("LT", BF16, 2), gtmp("D", BF16, 2), gtmp("G", BF16, 2)
    g_W, g_nwT, g_Vt, g_ktail, g_qhT, g_qkm = (gtmp(nm, BF16, 2) for nm in ("W", "nwT", "Vt", "ktail", "qhT", "qkm"))
    t_vn = [k.sb(f"vn{d}", [128, 128], BF16) for d in range(2)]
    Sf = [k.sb(f"Sf{d}", [128, 128], F32) for d in range(2)]
    Sb = [k.sb(f"Sb{d}", [128, 128], BF16) for d in range(2)]
    def bcn(v, n):
        return V(v.ap.unsqueeze(1).to_broadcast([128, n, 128]), v.buf)
    nbig = [0]
    def nextbig():
        nbig[0] += 1
        return pbig[nbig[0] % 2]
    wcnt = [0]

    def load_w(col0):
        i = wcnt[0] % 3
        wcnt[0] += 1
        k.dma("pool", wst_b[i], V(winv[:, :, col0:col0 + 128], win_d.buf))
        return wst_b[i]

    order_f = list(range(NT))
    order_b = [1, 0] + list(range(NT - 1, 1, -1))

    for h in heads:
        blkbuf = [[Buf(f"qkv{ty}_{b_}") for b_ in range(9)] for ty in range(3)]
        blk_of = lambda t: 0 if t < 2 else 1 + (t - 2) // 4
        dsts = (qT, kT, vT)
        def tv(ty, t):
            return V(dsts[ty].ap[:, t * 128:(t + 1) * 128], blkbuf[ty][blk_of(t)])
        def tlv(ty, a_, n_):
            bl = []
            for t_ in range(a_, a_ + n_):
                if blkbuf[ty][blk_of(t_)] not in bl:
                    bl.append(blkbuf[ty][blk_of(t_)])
            return V(dsts[ty].ap[:, a_ * 128:(a_ + n_) * 128].rearrange("p (a b) -> p a b", b=128), bl)
        pcnt = [0]
        def nextp():
            pcnt[0] += 1
            return k.banks[pcnt[0] % 6]
        wbs = [load_w(OFF_Q + (ty * 8 + h) * 128) for ty in range(3)]
        for ty in range(3):
            for tap in range(3):
                k.ts("pool", dgt3[ty][:, tap, :], identf, cqkv[:, ty * 8 + h, tap:tap + 1])

        def conv_block(ty, b_):
            s, n = blocks[b_]
            slot = ring[ty][b_ % 3]
            pp = nextp()
            for tap in range(3):
                k.mm(pp[:, 0:n], dgt3[ty][:, tap, :], slot[:, tap:tap + n], start=(tap == 0), stop=(tap == 2))
            dv = V(dsts[ty].ap[:, s:s + n], blkbuf[ty][b_])
            k.act(dv, pp[:, 0:n], AF.Silu)
            if ty < 2:
                pp2 = nextp()
                r = rn[0]
                sqv = ofin[(b_ + ty) % 2]
                k.tt("pool", sqv[:, 0:n], dv, dv, ALU.mult)
                k.mm(pp2[:, 0:n], onesb, sqv[:, 0:n])
                k.act(r[:, 0:n], pp2[:, 0:n], AF.Ln, bias=EPS)
                k.act(r[:, 0:n], r[:, 0:n], AF.Exp, scale=-0.5)
                k.stt("dve", dv, dv, (128.0 ** -0.5) if ty == 0 else 1.0, r[:, 0:n], ALU.mult, ALU.mult)

        for b_ in range(9):
            s, n = blocks[b_]
            for ty in range(3):
                slot = ring[ty][b_ % 3]
                prev = ring[ty][(b_ - 1) % 3]
                pp = nextp()
                for kk in range(8):
                    k.mm(pp[:, 0:n], wbs[ty][:, kk, :], hTv[:, kk, s:s + n], start=(kk == 0), stop=(kk == 7))
                k.copy("dve", slot[:, 1:1 + n], pp[:, 0:n])
                if b_ in (0, 1):
                    k.memset("pool", slot[:, 0:1], 0.0)
                else:
                    k.copy("pool", slot[:, 0:1], prev[:, 512:513])
                if b_ in (0, 8):
                    k.memset("pool", slot[:, n + 1:n + 2], 0.0)
                if b_ >= 2:
                    k.copy("pool", prev[:, 513:514], slot[:, 1:2])
                if b_ == 0:
                    conv_block(ty, 0)
                elif b_ >= 2:
                    conv_block(ty, b_ - 1)
                if b_ == 8:
                    conv_block(ty, 8)
        if "qkv" in dbg_out and h == heads[0]:
            with ExitStack() as st2:
                old = k.stack
                k.stack = st2
                tf_ = k.sb("dbgqkv", [128, 3, 512], F32)
                k.copy("dve", tf_[:, 0, :], qT[:, 0:512])
                k.copy("dve", tf_[:, 1, :], kT[:, 0:512])
                k.copy("dve", tf_[:, 2, :], vT[:, 0:512])
                k.dma("sp", dbg_out["qkv"], tf_)
                k.barrier()
                k.stack = old
        for d in range(2):
            k.memset("pool", Sf[d], 0.0)
            k.memset("pool", Sb[d], 0.0)
        visited = set()
        groups = [(0, 2)] + [(2 + 3 * i, 3) for i in range(10)] + [(32, 2)]
        if dbg is not None and "nsteps" in dbg:
            groups = groups[:dbg["nsteps"]]
        gorder = [groups, [groups[0]] + groups[:0:-1]]

        def pre_gen(d, a, n, gp):
            col = d * 8 + h
            isx = a >= 2
            def bc(arr):
                return V(arr.ap[:, a:a + n, col].unsqueeze(2).to_broadcast([128, n, 128]), arr.buf)
            tsl = lambda i: slice((a + i) * 128, (a + i + 1) * 128)
            dgc, E, eg, Rk0, Q0, Q0T, Z = (x[d][0][:, 0:n, :] for x in (g_dgc, g_E, g_eg, g_Rk0, g_Q0, g_Q0T, g_Z))
            LT, Dm, Gm = g_LT[d], g_D[d], g_G[d]
            tm_ = LT[0][:, 0:n, :]
            W, nwT, Vt, ktail, qhT, qkm = (x[d][gp][:, 0:n, :] for x in (g_W, g_nwT, g_Vt, g_ktail, g_qhT, g_qkm))
            pA = nextb(d)
            pAk, pAv = b3(pA, n, BF16, 0), b3(pA, n, BF16, GS)
            for i in range(n):
                k.tr(pAk[:, i, :], tv(1, a + i), ident)
                k.tr(pAv[:, i, :], tv(2, a + i), ident)
            yield
            k.act(Vt, pAv, AF.Copy)
            k.tt("dve", Rk0, pAk, bc(egc), ALU.mult)
            k.tt("dve", ktail, pAk, bc(ekt), ALU.mult)
            yield
            E1, E2, tm2_ = g_E1[d][0][:, 0:n, :], g_E2[d][0][:, 0:n, :], Z
            k.tt("pool", dgc, bcn(identf, n), bc(ngc), ALU.mult)
            pB = nextb(d)
            pBv = b3(pB, n)
            for i in range(n):
                k.mm(pBv[:, i, :], onesf, dgc[:, i, :])
            yield
            k.tt("dve", E, bcn(negm[d], n), pBv, ALU.subtract)
            for i in range(n):
                k.act(E2[:, i, :], E[:, i, :], AF.Exp, bias=ngcb[:, a + i, col:col + 1])
            if isx:
                for i in range(n):
                    k.act(E1[:, i, :], E[:, i, :], AF.Exp, bias=ngc[:, a + i, col:col + 1])
                k.act(eg, pBv, AF.Exp, scale=-1.0)
                k.tt("pool", qhT, tlv(0, a, n), eg, ALU.mult)
            yield
            pC = nextb(d)
            pCv = b3(pC, n)
            for i in range(n):
                k.mm(pCv[:, i, :], tv(1, a + i), tv(1, a + i))
            k.tt("dve", Q0, pCv, E2, ALU.mult)
            if isx:
                pQ = nextb(d)
                pQv = b3(pQ, n)
                for i in range(n):
                    k.mm(pQv[:, i, :], tv(1, a + i), tv(0, a + i))
                k.tt("dve", qkm, pQv, E1, ALU.mult)
            yield
            pT = nextb(d)
            pTv = b3(pT, n, BF16, 0)
            for i in range(n):
                k.tr(pTv[:, i, :], Q0[:, i, :], ident)
            k.act(Q0T, pTv, AF.Copy)
            yield
            k.tt("dve", tm_, Q0, bcn(hmask[:, 0, :], n), ALU.mult)
            k.tt("dve", Dm[0][:, 0:n, :], bcn(ident, n), tm_, ALU.subtract)
            k.tt("pool", tm2_, Q0T, bcn(hmask[:, 0, :], n), ALU.mult)
            k.tt("pool", Gm[0][:, 0:n, :], bcn(ident, n), tm2_, ALU.subtract)
            k.tt("pool", LT[1][:, 0:n, :], Q0T, bcn(hmask[:, 1, :], n), ALU.mult)
            yield
            for m_ in range(1, 7):
                a_, b_ = (m_ - 1) % 2, m_ % 2
                Da, Ga, Lm = Dm[a_][:, 0:n, :], Gm[a_][:, 0:n, :], LT[m_ % 2][:, 0:n, :]
                pz = nextb(d)
                pzv = b3(pz, n)
                for i in range(n):
                    k.mm(pzv[:, i, :], Lm[:, i, :], Da[:, i, :])
                if m_ < 6:
                    k.tt("pool", LT[(m_ + 1) % 2][:, 0:n, :], Q0T, bcn(hmask[:, m_ + 1, :], n), ALU.mult)
                k.act(Z, pzv, AF.Copy)
                yield
                pd_ = nextb(d)
                pdv = b3(pd_, n)
                for i in range(n):
                    k.mm(pdv[:, i, :], Ga[:, i, :], Z[:, i, :])
                if m_ < 6:
                    pg_ = nextb(d)
                    pgv = b3(pg_, n)
                    for i in range(n):
                        k.mm(pgv[:, i, :], Z[:, i, :], Ga[:, i, :])
                k.tt("dve", W if m_ == 6 else Dm[b_][:, 0:n, :], Da, pdv, ALU.subtract)
                if m_ < 6:
                    k.tt("dve", Gm[b_][:, 0:n, :], Ga, pgv, ALU.subtract)
                yield
            pw = nextb(d)
            pwv = b3(pw, n)
            for i in range(n):
                k.mm(pwv[:, i, :], Rk0[:, i, :], W[:, i, :])
            k.act(nwT, pwv, AF.Copy, scale=-1.0)
            yield

        def state_gen(d, a, n, gp):
            col = d * 8 + h
            isx = a >= 2
            W, nwT, Vt, ktail, qhT, qkm = (x[d][gp] for x in (g_W, g_nwT, g_Vt, g_ktail, g_qhT, g_qkm))
            vn = t_vn[d]
            for i in (range(n) if d == 0 else range(n - 1, -1, -1)):
                t = a + i
                sc = lambda arr: arr[:, t, col:col + 1]
                k.mm(pvn[d], W[:, i, :], Vt[:, i, :], start=True, stop=False)
                k.mm(pvn[d], nwT[:, i, :], Sb[d], start=False, stop=True)
                k.act(vn, pvn[d], AF.Identity, scale=sc(beta))
                yield
                if isx:
                    xt = t - 2
                    ov = V(osum.ap[:, xt * 128:(xt + 1) * 128], osb[xt])
                    k.mm(poT[d], Sb[d], qhT[:, i, :], start=True, stop=False)
                    k.mm(poT[d], vn, qkm[:, i, :], start=False, stop=True)
                k.mm(pS[d], ktail[:, i, :], vn)
                if isx:
                    if xt not in visited:
                        visited.add(xt)
                        k.act(ov, poT[d], AF.Copy)
                    else:
                        k.tt("dve", ov, poT[d], ov, ALU.add)
                k.stt("dve", Sf[d], Sf[d], sc(egl), pS[d], ALU.mult, ALU.add)
                k.act(Sb[d], Sf[d], AF.Copy)
                yield

        def run_all(gens):
            gens = list(gens)
            while gens:
                for g_ in list(gens):
                    try:
                        next(g_)
                    except StopIteration:
                        gens.remove(g_)

        ng = len(groups)
        run_all([pre_gen(0, *gorder[0][0], 0), pre_gen(1, *gorder[1][0], 0)])
        for gi in range(ng):
            gens = [state_gen(0, *gorder[0][gi], gi % 2), state_gen(1, *gorder[1][gi], gi % 2)]
            if gi + 1 < ng:
                gens += [pre_gen(0, *gorder[0][gi + 1], (gi + 1) % 2), pre_gen(1, *gorder[1][gi + 1], (gi + 1) % 2)]
            run_all(gens)
        if "S" in dbg_out and h == heads[0]:
            k.dma("sp", dbg_out["S"][0], Sf[0])
            k.dma("sp", dbg_out["S"][1], Sf[1])
        for bi in range(8):
            s = bi * 512
            ovs = [V(osum.ap[:, s:s + 512], osb[bi * 4 + j]) for j in range(4)]
            class _M:
                pass
            ovall = V(osum.ap[:, s:s + 512], osb[bi * 4])
            extra = [osb[bi * 4 + j] for j in range(1, 4)]
            if bi == 0:
                k.dma("pool", wbz, V(winv[:, :, OFF_Z + h * 128:OFF_Z + (h + 1) * 128], win_d.buf))
            pz_ = nextbig()
            zsb = zs_b[0]
            for kk in range(8):
                k.mm(pz_, wbz[:, kk, :], hTv[:, kk, TC + s:TC + s + 512], start=(kk == 0), stop=(kk == 7))
            k.act(zsb, pz_, AF.Silu)
            pp = nextbig()
            sqv = ofin[bi % 2]
            r = rn[0]
            of_ = V(osum.ap[:, s:s + 512], [osb[bi * 4 + j] for j in range(4)])
            k.op("pool", lambda g, sqv=sqv, s=s: g.tensor_tensor(out=sqv.ap, in0=osum.ap[:, s:s + 512], in1=osum.ap[:, s:s + 512], op=ALU.mult),
                 [osb[bi * 4 + j] for j in range(4)], [sqv.buf])
            k.mm(pp, onesb, sqv)
            k.act(r, pp, AF.Ln, bias=EPS, scale=1.0 / 128)
            k.act(r, r, AF.Exp, scale=-0.5)
            k.stt("dve", of_, of_, dnn[:, 0:1], r, ALU.mult, ALU.mult)
            if "o0" in dbg_out and h == heads[0]:
                k.tt("pool", of_, of_, zsb, ALU.mult)
                k.dma("sp", V(dbg_out["o0"].ap[:, s:s + 512], dbg_out["o0"].buf), of_)
                k.copy("pool", sqv, of_)
            else:
                k.tt("pool", sqv, of_, zsb, ALU.mult)
            k.dma("sp", V(oT_d.ap[h, :, s:s + 512], DBuf("st")), sqv)
    k.barrier()
    p4.close()
    s_g.close()
    k.stack = ExitStack()
    if stage <= 4:
        return finish(nc, k, out_d)

    def wchunk_loader(stf, stb):
        cnt = [0]
        def load(src_v, K):
            j = cnt[0] % len(stb)
            cnt[0] += 1
            k.dma("pool", stb[j][:, 0:K, :], src_v)
            return stb[j]
        return load

    def load_resident(dst_bf, src_ap, src_buf, K, ncols, stg):
        i = 0
        for k0 in range(0, K, 8):
            kn = min(8, K - k0)
            for c0 in range(0, ncols, 512):
                cn = min(512, ncols - c0)
                k.dma("pool", dst_bf[:, k0:k0 + kn, c0:c0 + cn], V(src_ap[:, k0:k0 + kn, c0:c0 + cn], src_buf))
                i += 1

    with ExitStack() as st:
        k.stack = st
        FCS = k.sb("FCS", [128, 32, 4, 256], BF16)
        fT = [k.sb(f"fT{i}", [128, 512], BF16) for i in range(2)]
        stf = None
        stb = [k.sb(f"p3stb{i}", [128, 8, 128], BF16) for i in range(2)]
        ctab = [k.sb(f"ctab{i}", [128, 4, 512], BF16) for i in range(2)]
        stab = [k.sb(f"stab{i}", [128, 4, 512], BF16) for i in range(2)]
        yblk = [k.sb(f"yblk{i}", [128, 4, 512], BF16) for i in range(2)]
        lw = wchunk_loader(stf, stb)
        nb_ = 0
        for g in range(4):
            wb = lw(V(winv[:, :, OFF_F + g * 128:OFF_F + (g + 1) * 128], win_d.buf), 8)
            for bi, (s, n) in enumerate(xblocks):
                pp = k.banks[nb_ % 2]
                ft = fT[nb_ % 2]
                nb_ += 1
                for kk in range(8):
                    k.mm(pp, wb[:, kk, :], hTv[:, kk, s:s + n], start=(kk == 0), stop=(kk == 7))
                k.act(ft, pp, AF.Copy)
                pf = k.banks[2 + (nb_ % 2)]
                pfv = V(pf.ap.rearrange("p (a b) -> p a b", b=256), pf.buf)
                for j2 in range(2):
                    for jj in range(2):
                        j = j2 * 2 + jj
                        k.mm(pfv[:, jj, :], ft[:, j * 128:(j + 1) * 128], c128)
                    t0 = bi * 4 + j2 * 2
                    if j2 == 0:
                        k.act(FCS[:, t0:t0 + 2, g, :], pfv, AF.Copy)
                    else:
                        k.copy("dve", FCS[:, t0:t0 + 2, g, :], pfv)
        k.barrier()
        cosv = cos_d.ap.rearrange("(tt p) f -> p tt f", p=128)
        sinv = sin_d.ap.rearrange("(tt p) f -> p tt f", p=128)
        ld = 0
        for kb in range(8):
            for t4 in range(8):
                ct, st_ = ctab[ld % 2], stab[ld % 2]
                ld += 1
                k.dma("sp", ct, V(cosv[:, t4 * 4:(t4 + 1) * 4, kb * 512:(kb + 1) * 512], cos_d.buf))
                k.dma("pool", st_, V(sinv[:, t4 * 4:(t4 + 1) * 4, kb * 512:(kb + 1) * 512], sin_d.buf))
                for ti in range(4):
                    tt_ = t4 * 4 + ti
                    for g in range(4):
                        k.mm(k.banks[4 + g], FCS[:, tt_, g, 0:128], ct[:, ti, :], start=(tt_ == 0), stop=False)
                        k.mm(k.banks[4 + g], FCS[:, tt_, g, 128:256], st_[:, ti, :], start=False, stop=(tt_ == 31))
            yb = yblk[kb % 2]
            for g in range(4):
                if g % 2 == 0:
                    k.act(yb[:, g, :], k.banks[4 + g], AF.Copy)
                else:
                    k.copy("dve", yb[:, g, :], k.banks[4 + g])
            k.dma("sp", V(yT_d.ap[:, :, kb * 512:(kb + 1) * 512].rearrange("g p f -> p g f"), DBuf("st")), yb)
        if "fm" in dbg_out:
            pass
        k.barrier()
    k.stack = ExitStack()
    if stage <= 5:
        return finish(nc, k, out_d)

    with ExitStack() as st:
        k.stack = st
        wg = k.sb("wg", [128, 8, 2048], BF16)
        wf4 = k.sb("wf4", [128, 4, 1024], BF16)
        wdn = k.sb("wdn", [128, 8, 1024], BF16)
        stg = None
        ytb = [k.sb(f"ytb{i}", [128, 4, 512], BF16) for i in range(2)]
        otb = [k.sb(f"otb{i}", [128, 8, 512], BF16) for i in range(2)]
        g0 = [k.sb(f"g0_{i}", [128, 512], BF16) for i in range(2)]
        g1 = [k.sb(f"g1_{i}", [128, 512], BF16) for i in range(2)]
        m0 = [k.sb(f"m0_{i}", [128, 512], BF16) for i in range(2)]
        m1 = [k.sb(f"m1_{i}", [128, 512], BF16) for i in range(2)]
        mixs = k.sb("mixs", [128, 2, 8, 512], BF16)
        mixs_buf = [Buf("mixs0"), Buf("mixs1")]
        load_resident(wg, winv[:, :, OFF_G:OFF_G + 2048], win_d.buf, 8, 2048, stg)
        load_resident(wf4, wf_d.ap.rearrange("(g p) d -> p g d", p=128), wf_d.buf, 4, 1024, stg)
        load_resident(wdn, wdn_d.ap.rearrange("(h p) d -> p h d", p=128), wdn_d.buf, 8, 1024, stg)
        it = 0
        hmt = [Buf(f"hTm{mt}") for mt in range(8)]
        for mt in range(8):
            s = TC + mt * 512
            yt, ot = ytb[mt % 2], otb[mt % 2]
            k.dma("sp", yt, V(yT_d.ap[:, :, mt * 512:(mt + 1) * 512].rearrange("g p f -> p g f"), yT_d.buf))
            k.dma("pool", ot, V(oT_d.ap[:, :, mt * 512:(mt + 1) * 512].rearrange("h p f -> p h f"), oT_d.buf))
            for dc in range(8):
                dsl = slice(dc * 128, (dc + 1) * 128)
                i2 = it % 2
                it += 1
                pb = [k.banks[4 * i2 + j] for j in range(4)]
                for g in range(4):
                    k.mm(pb[0], wf4[:, g, dsl], yt[:, g, :], start=(g == 0), stop=(g == 3))
                for kk in range(8):
                    k.mm(pb[1], wg[:, kk, dc * 128:(dc + 1) * 128], V(hT.ap[:, kk, s:s + 512], hmt[mt]), start=(kk == 0), stop=(kk == 7))
                for hh in range(8):
                    k.mm(pb[2], wdn[:, hh, dsl], ot[:, hh, :], start=(hh == 0), stop=(hh == 7))
                for kk in range(8):
                    k.mm(pb[3], wg[:, kk, 1024 + dc * 128:1024 + (dc + 1) * 128], V(hT.ap[:, kk, s:s + 512], hmt[mt]), start=(kk == 0), stop=(kk == 7))
                k.act(g0[i2], pb[1], AF.Sigmoid)
                k.act(g1[i2], pb[3], AF.Sigmoid)
                k.tt("dve", m0[i2], pb[0], g0[i2], ALU.mult)
                k.tt("dve", m1[i2], pb[2], g1[i2], ALU.mult)
                k.tt("pool", V(mixs.ap[:, mt % 2, dc, :], mixs_buf[mt % 2]), m0[i2], m1[i2], ALU.add)
            k.copy("act" if mt % 2 == 0 else "dve", V(hT.ap[:, :, s:s + 512], hmt[mt]), V(mixs.ap[:, mt % 2, :, :], mixs_buf[mt % 2]))
        k.barrier()
    k.stack = ExitStack()
    if stage <= 6:
        return finish(nc, k, out_d)

    def branch_tail(mt, producer, cidx, resid_d, final, tb):
        yx, sq, rst, xin_, x1t = tb["yx"][mt % 2], tb["sq"][mt % 2], tb["rst"][mt % 2], tb["xin"], tb["x1t"]
        for dc in range(8):
            pb = k.banks[dc % 2]
            producer(dc, pb)
            k.act(yx[:, dc, :], pb, AF.Copy)
            k.act(sq[:, dc, :], pb, AF.Square)
        pss = k.banks[2]
        for dc in range(8):
            k.mm(pss, onesb, sq[:, dc, :], start=(dc == 0), stop=(dc == 7))
        k.act(rst, pss, AF.Ln, bias=EPS, scale=1.0 / D)
        k.act(rst, rst, AF.Exp, scale=-0.5)
        for dc in range(8):
            k.stt("dve", yx[:, dc, :], yx[:, dc, :], coef[:, cidx, dc:dc + 1], rst, ALU.mult, ALU.mult)
        for j in range(4):
            tok0 = mt * 512 + j * 128
            xi = xin_[j % len(xin_)]
            xo = x1t[j % len(x1t)]
            k.dma("sp" if j % 2 == 0 else "pool", xi, V(resid_d.ap[tok0:tok0 + 128, :], resid_d.buf))
            ba, bb = k.banks[3 + 2 * (j % 2)], k.banks[4 + 2 * (j % 2)]
            for dc in range(8):
                bk = ba if dc < 4 else bb
                k.tr(bk[:, (dc % 4) * 128:(dc % 4 + 1) * 128], yx[:, dc, j * 128:(j + 1) * 128], identf)
            k.tt("dve", xo[:, 0:512], ba, xi[:, 0:512], ALU.add)
            k.tt("dve", xo[:, 512:1024], bb, xi[:, 512:1024], ALU.add)
            if final:
                k.dma("sp", V(out_d.ap[tok0:tok0 + 128, :], DBuf("st")), xo)
            else:
                k.dma("sp", V(x1_d.ap[tok0:tok0 + 128, :], DBuf("st")), xo)
                norm_tile(xo, 2 + mt * 4 + j, 5, 6, hT, tb["hbuf"], tb["tm"])

    def tail_bufs(nbuf, need_tm=True):
        if not need_tm:
            return {"yx": [k.sb(f"yx{i}", [128, 8, 512], F32) for i in range(2)], "sq": [k.sb(f"sqb{i}", [128, 8, 512], BF16) for i in range(2)],
                    "rst": [k.sb(f"rst{i}", [128, 512], F32) for i in range(2)],
                    "xin": [k.sb(f"rxin{i}", [128, D], F32) for i in range(nbuf)], "x1t": [k.sb(f"x1t{i}", [128, D], F32) for i in range(nbuf)], "tm": None}
        return {"yx": [k.sb(f"yx{i}", [128, 8, 512], F32) for i in range(2)], "sq": [k.sb(f"sqb{i}", [128, 8, 512], BF16) for i in range(2)],
                "rst": [k.sb(f"rst{i}", [128, 512], F32) for i in range(2)],
                "xin": [k.sb(f"rxin{i}", [128, D], F32) for i in range(nbuf)], "x1t": [k.sb(f"x1t{i}", [128, D], F32) for i in range(nbuf)],
                "tm": {"sq": [k.sb(f"t_sq{i}", [128, D], F32) for i in range(2)], "ss": [k.sb(f"t_ss{i}", [128, 1], F32) for i in range(2)],
                       "xn": [k.sb(f"t_xn{i}", [128, D], BF16) for i in range(2)], "pt": [k.pv(7, 0, 512, BF16, 128)]}}

    with ExitStack() as st:
        k.stack = st
        wout = k.sb("wout", [128, 8, 1024], BF16)
        stg = None
        load_resident(wout, wout_d.ap.rearrange("(c p) d -> p c d", p=128), wout_d.buf, 8, 1024, stg)
        tb = tail_bufs(2)
        hmt2 = [Buf(f"hTn{mt}") for mt in range(8)]
        for mt in range(8):
            s = TC + mt * 512
            def prod(dc, pb, s=s, mt=mt):
                for c in range(8):
                    k.mm(pb, wout[:, c, dc * 128:(dc + 1) * 128], V(hT.ap[:, c, s:s + 512], hmt2[mt]), start=(c == 0), stop=(c == 7))
            tb["hbuf"] = hmt2[mt]
            branch_tail(mt, prod, 4, x_d, False, tb)
        k.barrier()
    k.stack = ExitStack()
    if stage <= 7:
        return finish(nc, k, out_d)

    wupv = wup_d.ap.rearrange("(k p) c -> p k c", p=128)
    with ExitStack() as st:
        k.stack = st
        stf = None
        stb = [k.sb(f"p6stb{i}", [128, 8, 128], BF16) for i in range(4)]
        lw = wchunk_loader(stf, stb)
        apad_b = [k.sb(f"apad{i}", [128, 66, 66], BF16) for i in range(2)]
        dg9_b = [k.sb(f"dg9_{i}", [128, 9, 128], BF16) for i in range(2)]
        sa = [k.sb(f"sa{i}", [128, 512], BF16) for i in range(4)]
        gtc = [k.sb(f"gtc{i}", [128, T], BF16) for i in range(2)]
        k.memset("pool", apad_b[0], 0.0)
        k.memset("pool", apad_b[1], 0.0)
        nb_ = 0
        for c in range(NFF):
            apad, dg9 = apad_b[c % 2], dg9_b[c % 2]
            wa = lw(V(wupv[:, :, c * 128:(c + 1) * 128], wup_d.buf), 8)
            wu = lw(V(wupv[:, :, DFF + c * 128:DFF + (c + 1) * 128], wup_d.buf), 8)
            for tap in range(9):
                k.ts("pool", dg9[:, tap, :], identf, cffn[:, c, tap:tap + 1])
            for bi in range(8):
                s = TC + bi * 512
                pp = k.banks[(0, 1, 6, 7)[nb_ % 4]]
                nb_ += 1
                for kk in range(8):
                    k.mm(pp, wa[:, kk, :], hTv[:, kk, s:s + 512], start=(kk == 0), stop=(kk == 7))
                k.act(apad[:, 1 + bi * 8:1 + bi * 8 + 8, 1:65], V(pp.ap.rearrange("p (r c) -> p r c", c=64), pp.buf), AF.Copy)
            gt = gtc[c % 2]
            for bi in range(8):
                s = TC + bi * 512
                pc = k.banks[2 + (bi % 2)]
                pu = k.banks[4 + (bi % 2)]
                pcv = V(pc.ap.rearrange("p (r c) -> p r c", c=64), pc.buf)
                for tap in range(9):
                    dr, dcc = tap // 3, tap % 3
                    k.mm(pcv, dg9[:, tap, :], apad[:, bi * 8 + dr:bi * 8 + dr + 8, dcc:dcc + 64], start=(tap == 0), stop=(tap == 8))
                k.act(sa[bi % 4], pc, AF.Silu)
                for kk in range(8):
                    k.mm(pu, wu[:, kk, :], hTv[:, kk, s:s + 512], start=(kk == 0), stop=(kk == 7))
                k.tt("dve", gt[:, bi * 512:(bi + 1) * 512], pu, sa[bi % 4], ALU.mult)
            k.dma("sp" if c % 2 == 0 else "pool", V(gT_d.ap[c], DBuf("st")), gt)
        k.barrier()
    k.stack = ExitStack()
    s_h.close()
    k.stack = ExitStack()
    if stage <= 8:
        return finish(nc, k, out_d)

    with ExitStack() as st:
        k.stack = st
        wdown = k.sb("wdown", [128, NFF, 1024], BF16)
        stg = None
        load_resident(wdown, wdown_d.ap.rearrange("(c p) d -> p c d", p=128), wdown_d.buf, NFF, 1024, stg)
        gbl = [k.sb(f"gbl{i}", [128, NFF, 512], BF16) for i in range(2)]
        tb = tail_bufs(2, need_tm=False)
        for mt in range(8):
            gb = gbl[mt % 2]
            k.dma("sp", gb[:, 0:11, :], V(gT_d.ap[0:11, :, mt * 512:(mt + 1) * 512].rearrange("c p f -> p c f"), gT_d.buf))
            k.dma("pool", gb[:, 11:NFF, :], V(gT_d.ap[11:NFF, :, mt * 512:(mt + 1) * 512].rearrange("c p f -> p c f"), gT_d.buf))
            def prod(dc, pb, gb=gb):
                for c in range(NFF):
                    k.mm(pb, wdown[:, c, dc * 128:(dc + 1) * 128], gb[:, c, :], start=(c == 0), stop=(c == NFF - 1))
            branch_tail(mt, prod, 7, x1_d, True, tb)
        k.barrier()
    k.stack = ExitStack()
    return finish(nc, k, out_d)


def finish(nc, k, out_d):
    k.barrier()
    return nc


def prep_inputs(inp, b):
    f = lambda a: np.ascontiguousarray(a, dtype=np.float32)
    colmajor = lambda v: f(np.asarray(v).reshape(-1, 128).T)
    m = {}
    m["x"] = f(inp["x"][b])
    m["ctx"] = f(inp["ctx"][b])
    m["cc"] = f(np.stack([colmajor(inp["c"][b]), colmajor(inp["c_ctx"])], axis=-1))
    m["w_ada"] = f(inp["w_ada"][0])
    m["b_ada"] = colmajor(inp["b_ada"][0])
    m["norms"] = f(np.stack([colmajor(inp[n][0]) for n in ("norm_pre_mix", "norm_post_mix", "norm_pre_ffn", "norm_post_ffn")], axis=1))
    m["w_in"] = f(inp["w_in"][0])
    cq = np.asarray(inp["conv_qkv"][0])
    m["conv_qkv"] = f(cq.T.reshape(24, 128, 3).transpose(1, 0, 2))
    gp = np.stack([np.asarray(inp["a_log"][0]).reshape(16), np.asarray(inp["dt_bias"][0]).reshape(16)], 0)
    m["gpar"] = f(np.broadcast_to(gp[None], (128, 2, 16)))
    m["dn_norm"] = f(np.asarray(inp["dn_norm"][0]).reshape(128, 1))
    m["w_fourier"] = f(inp["w_fourier"][0])
    m["w_dn"] = f(inp["w_dn"][0])
    m["w_out"] = f(inp["w_out"][0])
    m["w_up"] = f(inp["w_up"][0])
    cf = np.asarray(inp["conv_ffn"][0]).reshape(9, DFF)
    m["conv_ffn"] = f(cf.T.reshape(NFF, 128, 9).transpose(1, 0, 2))
    m["w_down"] = f(inp["w_down"][0])
    return m


_CONST = {}


def consts():
    if not _CONST:
        idx = np.arange(T, dtype=np.int64)
        ang = (2.0 * np.pi / T) * ((idx[:, None] * idx[None, :]) % T).astype(np.float64)
        s = 1.0 / np.sqrt(float(T) * 128.0)
        _CONST["dft_cos"] = (np.cos(ang) * s).astype(ml_dtypes.bfloat16)
        _CONST["dft_sin"] = (np.sin(ang) * s).astype(ml_dtypes.bfloat16)
        i8 = np.arange(128, dtype=np.int64)
        a8 = (2.0 * np.pi / 128) * ((i8[:, None] * i8[None, :]) % 128).astype(np.float64)
        j8 = np.arange(128)
        hm = []
        for m_ in range(7):
            s_ = 2 ** m_
            blk2 = (j8[:, None] // (2 * s_)) == (j8[None, :] // (2 * s_))
            half = (j8[:, None] // s_) != (j8[None, :] // s_)
            hm.append((blk2 & half).astype(np.float32))
        _CONST["hmask"] = np.stack(hm, axis=1).astype(ml_dtypes.bfloat16)
        _CONST["dft128"] = np.concatenate([np.cos(a8), -np.sin(a8)], axis=1).astype(ml_dtypes.bfloat16)
    return _CONST


def kernel(**inputs):
    inp = {k_: np.asarray(v) for k_, v in inputs.items()}
    nc = build()
    cst = consts()
    in_maps = []
    for b in range(8):
        m = prep_inputs(inp, b)
        m.update(cst)
        in_maps.append(m)
    res = run_bass_kernel_spmd(nc, in_maps, core_ids=list(range(8)))
    return np.stack([np.asarray(r["out"], dtype=np.float32) for r in res.results], axis=0)
```

```python
import os
from contextlib import ExitStack
import numpy as np
import ml_dtypes
import concourse.bass as bass
import concourse.mybir as mybir
from concourse.bass_utils import run_bass_kernel_spmd

F32 = mybir.dt.float32
BF16 = mybir.dt.bfloat16
AF = mybir.ActivationFunctionType
ALU = mybir.AluOpType

D = 1024
T = 4096
TC = 256
TA = TC + T
NT = TA // 128
H = 8
OFF_F, OFF_Q, OFF_K, OFF_V, OFF_Z, OFF_B, OFF_A, OFF_G = 0, 512, 1536, 2560, 3584, 4608, 4624, 4640
INW = 6688
DFF = 2816
NFF = DFF // 128
EPS = 1e-6


class Buf:
    __slots__ = ("name", "last_w", "readers", "dsem", "dcount", "excl")

    def __init__(self, name, excl=False):
        self.name = name
        self.excl = excl
        self.last_w = None
        self.readers = []
        self.dsem = None
        self.dcount = 0


class V:
    __slots__ = ("ap", "buf")

    def __init__(self, ap, buf):
        self.ap = ap
        self.buf = buf

    def __getitem__(self, idx):
        return V(self.ap[idx], self.buf)

    def sub(self, idx, buf):
        return V(self.ap[idx], buf)


def _bufs(*vs):
    out = []
    for v in vs:
        if isinstance(v, V) and v.buf is not None:
            for b in (v.buf if isinstance(v.buf, (list, tuple)) else (v.buf,)):
                if b not in out:
                    out.append(b)
    return out


def _ap(v):
    return v.ap if isinstance(v, V) else v


class K:
    def __init__(self, nc):
        self.nc = nc
        self.engs = {"pe": nc.tensor, "act": nc.scalar, "dve": nc.vector, "pool": nc.gpsimd, "sp": nc.sync}
        self.sem = {n: nc.alloc_semaphore(f"s_{n}") for n in self.engs}
        self.cnt = {n: 0 for n in self.engs}
        self.known = {n: {} for n in self.engs}
        self.dsems = []
        self.nins = 0
        self.nwaits = 0
        self.stack = ExitStack()
        self.limit = None
        self.log = []
        self.sched = os.environ.get('KSCHED', '1') == '1'
        self.pending = []

    def _uid(self):
        self.uid = getattr(self, 'uid', 0) + 1
        return self.uid

    def sb(self, name, shape, dt, nbuf=None):
        t = self.stack.enter_context(self.nc.sbuf_tensor(f"sb{self._uid()}_" + name, list(shape), dt))
        return V(t[:] if hasattr(t, "__getitem__") else t.ap(), Buf(name) if nbuf is None else nbuf)

    def init_banks(self):
        self.banks = []
        for i in range(8):
            t = self.nc.psum_tensor(f"ps_bank{i}", [128, 512], F32).__enter__()
            self.banks.append(V(t[:], Buf(f"bank{i}", excl=True)))

    def pv(self, bank, lo, hi, dt=F32, inner=None):
        b = self.banks[bank]
        ap = b.ap[:, lo:hi]
        if dt != F32:
            ap = ap.bitcast(dt)
        if inner is not None:
            ap = ap.rearrange("p (a b) -> p a b", b=inner)
        return V(ap, b.buf)

    def _wait(self, e, ev):
        sem, val, src = ev
        if src == "pe" and e == "pe":
            return
        kn = self.known[e]
        if kn.get(sem.num, 0) >= val:
            return
        kn[sem.num] = val
        self.engs[e].wait_ge(sem, val)
        self.nwaits += 1

    def _deps(self, e, reads, writes):
        best = {}
        def add(ev):
            s = ev[0].num
            if s not in best or best[s][1] < ev[1]:
                best[s] = ev
        for b in reads:
            if b.last_w is not None:
                add(b.last_w)
        for b in writes:
            if b.last_w is not None:
                add(b.last_w)
            for ev in b.readers:
                add(ev)
        for ev in best.values():
            self._wait(e, ev)

    def _record(self, ev, reads, writes):
        for b in reads:
            if b in writes:
                continue
            b.readers.append(ev)
            if len(b.readers) > 10:
                best = {}
                for x in b.readers:
                    s = x[0].num
                    if s not in best or best[s][1] < x[1]:
                        best[s] = x
                b.readers = list(best.values())
        for b in writes:
            b.last_w = ev
            b.readers = []

    def op(self, e, fn, reads, writes, cost=300.0):
        if self.sched:
            self.pending.append(("op", e, fn, list(reads), list(writes), float(cost)))
            return
        self._emit_op(e, fn, reads, writes)

    def _emit_op(self, e, fn, reads, writes):
        if self.limit is not None and self.nins >= self.limit:
            return
        ex = [b for b in reads if b.excl and b not in writes]
        if ex:
            writes = list(writes) + ex
        self._deps(e, reads, writes)
        ins = fn(self.engs[e])
        if os.environ.get('PRINS') and self.nins in range(int(os.environ.get('PRINS','0')), int(os.environ.get('PRINS','0')) + 4):
            print('INS', self.nins, ins.concise())
        self.cnt[e] += 1
        ins.then_inc(self.sem[e], 1)
        self._record((self.sem[e], self.cnt[e], e), reads, writes)
        self.nins += 1

    def dma(self, q, out, in_, key=None, nbytes=None, **kw):
        if self.sched:
            if nbytes is None:
                shp = _ap(out).shape
                nbytes = 4
                for d_ in shp:
                    nbytes *= d_
            self.pending.append(("dma", q, (out, in_, key, kw), _bufs(in_), _bufs(out), 2000.0 + nbytes / 100.0))
            return
        self._emit_dma(q, out, in_, key, **kw)

    def _emit_dma(self, q, out, in_, key=None, **kw):
        if self.limit is not None and self.nins >= self.limit:
            return
        reads, writes = _bufs(in_), _bufs(out)
        self._deps(q, reads, writes)
        kb = key.buf if key is not None else (out.buf if not isinstance(out.buf, DBuf) else in_.buf)
        if isinstance(kb, (list, tuple)):
            kb = kb[0]
        if kb.dsem is None:
            kb.dsem = self.nc.alloc_semaphore(f"d{self._uid()}_{kb.name}")
            self.dsems.append(kb)
        ins = self.engs[q].dma_start(out=_ap(out), in_=_ap(in_), **kw)
        kb.dcount += 1
        ins.then_inc(kb.dsem, 16)
        self._record((kb.dsem, 16 * kb.dcount, "dma"), reads, writes)
        self.nins += 1

    def flush(self):
        ops = self.pending
        self.pending = []
        n = len(ops)
        if n == 0:
            return
        SYNC = float(os.environ.get('KSYNC', '200'))
        PEF = float(os.environ.get('KPEF', '1.0'))
        preds = [[] for _ in range(n)]
        lastw = {}
        rdrs = {}
        for i, (kind, e, fn, reads, writes, cost) in enumerate(ops):
            wr = list(writes) + [b for b in reads if b.excl and b not in writes]
            ps = set()
            for b in reads:
                if b in lastw:
                    ps.add(lastw[b])
            for b in wr:
                if b in lastw:
                    ps.add(lastw[b])
                for r_ in rdrs.get(b, ()):
                    ps.add(r_)
            ps.discard(i)
            preds[i] = list(ps)
            for b in reads:
                if b not in wr:
                    rdrs.setdefault(b, []).append(i)
            for b in wr:
                lastw[b] = i
                rdrs[b] = []
        succs = [[] for _ in range(n)]
        for i in range(n):
            for p in preds[i]:
                succs[p].append(i)
        occ = [0.0] * n
        lat = [0.0] * n
        for i, (kind, e, fn, reads, writes, cost) in enumerate(ops):
            if kind == "dma":
                occ[i] = 60.0
                lat[i] = cost
            else:
                occ[i] = cost * (PEF if e == 'pe' else 1.0)
                lat[i] = occ[i]
        blevel = [0.0] * n
        for i in range(n - 1, -1, -1):
            m_ = 0.0
            for s_ in succs[i]:
                if blevel[s_] > m_:
                    m_ = blevel[s_]
            blevel[i] = lat[i] + m_
        import heapq
        npred = [len(p) for p in preds]
        ready_t = [0.0] * n
        eng_free = {}
        readyq = {}
        for i in range(n):
            if npred[i] == 0:
                heapq.heappush(readyq.setdefault(ops[i][1], []), (-blevel[i], i))
        order = []
        done = 0
        while done < n:
            best = None
            for e, hq in readyq.items():
                if not hq:
                    continue
                tfree = eng_free.get(e, 0.0)
                cand = None
                top = heapq.nsmallest(6, hq)
                for pr, i in top:
                    st_ = max(tfree, ready_t[i])
                    key = (st_, pr)
                    if cand is None or key < cand[0]:
                        cand = (key, i)
                if best is None or cand[0] < best[0]:
                    best = (cand[0], cand[1], e)
            (st_, pr), i, e = best
            hq = readyq[e]
            hq.remove((-blevel[i], i))
            heapq.heapify(hq)
            eng_free[e] = st_ + occ[i]
            fin = st_ + lat[i]
            order.append(i)
            done += 1
            for s_ in succs[i]:
                rt = fin + (0.0 if ops[s_][1] == e else SYNC)
                if rt > ready_t[s_]:
                    ready_t[s_] = rt
                npred[s_] -= 1
                if npred[s_] == 0:
                    heapq.heappush(readyq.setdefault(ops[s_][1], []), (-blevel[s_], s_))
        if os.environ.get('KSIM'):
            print('flush n=%d simulated makespan %.1f us' % (n, max(eng_free.values()) / 1e3), {e_: round(v_ / 1e3) for e_, v_ in eng_free.items()})
        for i in order:
            kind, e, fn, reads, writes, cost = ops[i]
            if kind == "dma":
                out, in_, key, kw = fn
                self._emit_dma(e, out, in_, key, **kw)
            else:
                self._emit_op(e, fn, reads, writes)

    def barrier(self):
        self.flush()
        for e in self.engs:
            for f in self.engs:
                if f != e and self.cnt[f] > 0:
                    self._wait(e, (self.sem[f], self.cnt[f], f))
            for kb in self.dsems:
                if kb.dcount > 0:
                    self._wait(e, (kb.dsem, 16 * kb.dcount, "dma"))

    @staticmethod
    def _fsz(v):
        shp = _ap(v).shape
        n = 1
        for d_ in shp[1:]:
            n *= d_
        return n

    def _ecost(self, e, out, in_):
        n = self._fsz(out)
        if e == "pool":
            return 150.0 + 2.0 * n
        if e == "act":
            return 220.0 + 0.72 * n
        return 100.0 + (1.05 * n if (_ap(in_).dtype == F32 or _ap(out).dtype == F32) else 0.6 * n)

    def mm(self, out, lhsT, rhs, start=True, stop=True):
        n = self._fsz(out)
        c = 40.0 + n * (1.9 if _ap(lhsT).dtype == F32 else 0.45)
        self.op("pe", lambda e: e.matmul(_ap(out), lhsT=_ap(lhsT), rhs=_ap(rhs), start=start, stop=stop),
                _bufs(lhsT, rhs) + ([] if start else _bufs(out)), _bufs(out), cost=c)

    def tr(self, out, in_, ident):
        n = self._fsz(out)
        c = 40.0 + n * (1.9 if _ap(in_).dtype == F32 else 0.45)
        self.op("pe", lambda e: e.transpose(out=_ap(out), in_=_ap(in_), identity=_ap(ident)), _bufs(in_, ident), _bufs(out), cost=c)

    def act(self, out, in_, func, bias=0.0, scale=1.0, accum=None, eng="act"):
        kw = {}
        if accum is not None:
            kw["accum_out"] = _ap(accum)
        self.op("act", lambda e: e.activation(out=_ap(out), in_=_ap(in_), func=func, bias=_ap(bias), scale=_ap(scale), **kw),
                _bufs(in_, bias, scale), _bufs(out, accum), cost=self._ecost("act", out, in_))

    def ts(self, e, out, in0, s1, s2=None, op0=ALU.mult, op1=None):
        if op1 is None:
            f = lambda g: g.tensor_scalar(out=_ap(out), in0=_ap(in0), scalar1=_ap(s1), scalar2=None, op0=op0)
        else:
            f = lambda g: g.tensor_scalar(out=_ap(out), in0=_ap(in0), scalar1=_ap(s1), scalar2=_ap(s2), op0=op0, op1=op1)
        self.op(e, f, _bufs(in0, s1, s2), _bufs(out), cost=self._ecost(e, out, in0))

    def tt(self, e, out, a, b, op):
        self.op(e, lambda g: g.tensor_tensor(out=_ap(out), in0=_ap(a), in1=_ap(b), op=op), _bufs(a, b), _bufs(out), cost=self._ecost(e, out, a))

    def stt(self, e, out, in0, scalar, in1, op0, op1):
        self.op(e, lambda g: g.scalar_tensor_tensor(out=_ap(out), in0=_ap(in0), scalar=_ap(scalar), in1=_ap(in1), op0=op0, op1=op1),
                _bufs(in0, scalar, in1), _bufs(out), cost=self._ecost(e, out, in0))

    def copy(self, e, out, in_):
        if e == "act":
            self.act(out, in_, AF.Copy)
        else:
            self.op(e, lambda g: g.tensor_copy(out=_ap(out), in_=_ap(in_)), _bufs(in_), _bufs(out), cost=self._ecost(e, out, in_))

    def recip(self, out, in_):
        self.op("dve", lambda g: g.reciprocal(out=_ap(out), in_=_ap(in_)), _bufs(in_), _bufs(out), cost=100.0 + 6.3 * self._fsz(out))

    def memset(self, e, out, val):
        self.op(e, lambda g: g.memset(_ap(out), val), [], _bufs(out))

    def asel(self, out, in_, pattern, cmp, fill, base, cm):
        self.op("pool", lambda g: g.affine_select(out=_ap(out), in_=_ap(in_), pattern=pattern, compare_op=cmp, fill=fill,
                                                  base=base, channel_multiplier=cm), _bufs(in_), _bufs(out))


class DBuf(Buf):
    __slots__ = ("is_dram",)

    def __init__(self, name):
        super().__init__(name)
        self.is_dram = True


def dramv(nc, name, shape, dt, kind):
    t = nc.dram_tensor(name, list(shape), dt, kind=kind)
    return V(t.ap(), DBuf(name))


def build(stage=99, dbg=None):
    nc = bass.Bass("TRN2", target_bir_lowering=False)
    k = K(nc)
    k.init_banks()
    if dbg and 'limit' in dbg:
        k.limit = dbg['limit']
    IN = lambda name, shape, dt=F32: dramv(nc, name, shape, dt, "ExternalInput")
    x_d = IN("x", [T, D])
    ctx_d = IN("ctx", [TC, D])
    cc_d = IN("cc", [128, 8, 2])
    wada_d = IN("w_ada", [D, 6 * D])
    bada_d = IN("b_ada", [128, 48])
    nrm_d = IN("norms", [128, 4, 8])
    win_d = IN("w_in", [D, INW])
    cqkv_d = IN("conv_qkv", [128, 24, 3])
    gpar_d = IN("gpar", [128, 2, 16])
    dnn_d = IN("dn_norm", [128, 1])
    wf_d = IN("w_fourier", [512, D])
    wdn_d = IN("w_dn", [D, D])
    wout_d = IN("w_out", [D, D])
    wup_d = IN("w_up", [D, 2 * DFF])
    cffn_d = IN("conv_ffn", [128, NFF, 9])
    wdown_d = IN("w_down", [DFF, D])
    cos_d = IN("dft_cos", [T, T], BF16)
    sin_d = IN("dft_sin", [T, T], BF16)
    c128_d = IN("dft128", [128, 256], BF16)
    hmask_d = IN("hmask", [128, 7, 128], BF16)
    out_d = dramv(nc, "out", [T, D], F32, "ExternalOutput")
    dbg_out = {}
    if dbg:
        for nm, shp in dbg.items():
            if nm in ("heads", "nsteps", "limit"):
                continue
            dbg_out[nm] = dramv(nc, "dbg_" + nm, shp, F32, "ExternalOutput")
    x1_d = dramv(nc, "x1_scr", [T, D], F32, "Internal")
    oT_d = dramv(nc, "oT_scr", [H, 128, T], BF16, "Internal")
    gT_d = dramv(nc, "gT_scr", [NFF, 128, T], BF16, "Internal")
    yT_d = dramv(nc, "yT_scr", [4, 128, T], BF16, "Internal")

    identf = k.sb("identf", [128, 128], F32)
    ident = k.sb("ident", [128, 128], BF16)
    onesf = k.sb("onesf", [128, 128], F32)
    onesb = k.sb("onesb", [128, 128], BF16)
    negm = [k.sb(f"negm{d}", [128, 128], F32) for d in range(2)]
    smask = [k.sb(f"smask{d}", [128, 128], F32) for d in range(2)]
    ut = [k.sb(f"ut{d}", [128, 128], F32) for d in range(2)]
    zerof = k.sb("zerof", [128, 128], F32)
    scal_t = k.sb("scal_t", [128, 8], F32)
    k.memset("pool", zerof, 0.0)
    k.memset("pool", onesf, 1.0)
    k.copy("dve", onesb, onesf)
    k.asel(identf, zerof, [[-1, 128]], ALU.not_equal, 1.0, 0, 1)
    k.copy("dve", ident, identf)
    k.asel(negm[0], zerof, [[1, 128]], ALU.is_ge, -1e5, 0, -1)
    k.asel(smask[0], onesf, [[1, 128]], ALU.is_gt, 0.0, 0, -1)
    k.asel(ut[0], onesf, [[1, 128]], ALU.is_ge, 0.0, 0, -1)
    k.asel(negm[1], zerof, [[-1, 128]], ALU.is_ge, -1e5, 0, 1)
    k.asel(smask[1], onesf, [[-1, 128]], ALU.is_gt, 0.0, 0, 1)
    k.asel(ut[1], onesf, [[-1, 128]], ALU.is_ge, 0.0, 0, 1)

    nrm = k.sb("nrm", [128, 4, 8], F32)
    k.dma("sp", nrm, nrm_d)
    cqkv = k.sb("cqkv", [128, 24, 3], F32)
    k.dma("sp", cqkv, cqkv_d)
    gpar = k.sb("gpar", [128, 2, 16], F32)
    k.dma("sp", gpar, gpar_d)
    dnn = k.sb("dnn", [128, 1], F32)
    k.dma("sp", dnn, dnn_d)
    cffn = k.sb("cffn", [128, NFF, 9], F32)
    k.dma("sp", cffn, cffn_d)
    c128 = k.sb("c128", [128, 256], BF16)
    k.dma("sp", c128, c128_d)
    hmask = k.sb("hmask", [128, 7, 128], BF16)
    k.dma("sp", hmask, hmask_d)
    bada = k.sb("bada", [128, 48], F32)
    k.dma("sp", bada, bada_d)
    cc = k.sb("cc", [128, 8, 2], F32)
    k.dma("sp", cc, cc_d)

    mod = k.sb("mod", [128, 48, 2], F32)
    scc = k.sb("scc", [128, 8, 2], F32)
    k.act(scc, cc, AF.Silu)
    with ExitStack() as st:
        k.stack = st
        wa = [k.sb(f"wa{i}", [128, 8, 512], BF16) for i in range(3)]
        sccb = k.sb("sccb", [128, 8, 2], BF16)
        k.copy("dve", sccb, scc)
        pm = k.pv(0, 0, 96, F32, 2)
        wv = wada_d.ap.rearrange("(k p) c -> p k c", p=128)
        for blk in range(12):
            w = wa[blk % 3]
            k.dma("pool", w, V(wv[:, :, blk * 512:(blk + 1) * 512], wada_d.buf))
            for oc in range(4):
                for kk in range(8):
                    k.mm(pm[:, blk * 4 + oc, :], w[:, kk, oc * 128:(oc + 1) * 128], sccb[:, kk, :], start=(kk == 0), stop=(kk == 7))
        for j in range(2):
            k.tt("dve", mod[:, :, j], pm[:, :, j], bada, ALU.add)
        k.barrier()
    k.stack = ExitStack()
    coef = k.sb("coef", [128, 8, 8], F32)
    def modc(i, j):
        return mod[:, i * 8:(i + 1) * 8, j]
    k.stt("dve", coef[:, 0, :], modc(1, 0), 1.0, nrm[:, 0, :], ALU.add, ALU.mult)
    k.copy("dve", coef[:, 1, :], modc(0, 0))
    k.stt("dve", coef[:, 2, :], modc(1, 1), 1.0, nrm[:, 0, :], ALU.add, ALU.mult)
    k.copy("dve", coef[:, 3, :], modc(0, 1))
    k.tt("dve", coef[:, 4, :], modc(2, 0), nrm[:, 1, :], ALU.mult)
    k.stt("dve", coef[:, 5, :], modc(4, 0), 1.0, nrm[:, 2, :], ALU.add, ALU.mult)
    k.copy("dve", coef[:, 6, :], modc(3, 0))
    k.tt("dve", coef[:, 7, :], modc(5, 0), nrm[:, 3, :], ALU.mult)
    if "coef" in dbg_out:
        k.dma("sp", dbg_out["coef"], coef)
    if stage <= 0:
        return finish(nc, k, out_d)

    s_h = ExitStack()
    k.stack = s_h
    hT = k.sb("hT", [128, 8, TA], BF16)
    hbuf = [Buf(f"hT{t}") for t in range(NT)]

    def norm_tile(src_tile_v, tile_idx, ca, cb, dst, dstbufs, tm):
        nb = len(tm["sq"])
        sq = tm["sq"][tile_idx % nb]
        ss = tm["ss"][tile_idx % nb]
        xn = tm["xn"][tile_idx % nb]
        pt = tm["pt"][tile_idx % len(tm["pt"])]
        k.act(sq, src_tile_v, AF.Square, accum=ss)
        k.act(ss, ss, AF.Sqrt, bias=EPS, scale=1.0 / D)
        k.recip(ss, ss)
        k.ts("dve", xn, src_tile_v, ss[:, 0:1])
        for c in range(8):
            k.tr(pt[:, c, :], xn[:, c * 128:(c + 1) * 128], ident)
        for c in range(8):
            dv = V(dst.ap[:, c, tile_idx * 128:(tile_idx + 1) * 128], dstbufs[tile_idx] if isinstance(dstbufs, list) else dstbufs)
            if c % 2 == 0:
                k.ts("dve", dv, pt[:, c, :], coef[:, ca, c:c + 1], coef[:, cb, c:c + 1], ALU.mult, ALU.add)
            else:
                k.act(dv, pt[:, c, :], AF.Identity, bias=coef[:, cb, c:c + 1], scale=coef[:, ca, c:c + 1])

    p1 = ExitStack()
    k.stack = p1
    nt_sq = [k.sb(f"nt_sq{i}", [128, D], F32) for i in range(2)]
    nt_ss = [k.sb(f"nt_ss{i}", [128, 1], F32) for i in range(2)]
    nt_xn = [k.sb(f"nt_xn{i}", [128, D], BF16) for i in range(2)]
    xin = [k.sb(f"xin{i}", [128, D], F32) for i in range(3)]
    nt_pt = [k.pv(i, 0, 512, BF16, 128) for i in range(2)]
    tm1 = {"sq": nt_sq, "ss": nt_ss, "xn": nt_xn, "pt": nt_pt}
    for t in range(NT):
        xi = xin[t % 3]
        if t < 2:
            src = V(ctx_d.ap[t * 128:(t + 1) * 128, :], ctx_d.buf)
        else:
            src = V(x_d.ap[(t - 2) * 128:(t - 1) * 128, :], x_d.buf)
        k.dma("sp" if t % 2 == 0 else "pool", xi, src)
        norm_tile(xi, t, 2 if t < 2 else 0, 3 if t < 2 else 1, hT, hbuf, tm1)
    k.barrier()
    p1.close()
    k.stack = ExitStack()
    if "hT" in dbg_out:
        with ExitStack() as st:
            k.stack = st
            tmpf = k.sb("dbg_hT", [128, 8, 1024], F32)
            k.copy("dve", tmpf, V(hT.ap[:, :, 0:1024], None))
            k.dma("sp", dbg_out["hT"], tmpf)
            k.barrier()
        k.stack = ExitStack()
    hTv = V(hT.ap, Buf("hT_all"))
    if stage <= 1:
        return finish(nc, k, out_d)

    winv = win_d.ap.rearrange("(k p) c -> p k c", p=128)

    def bcast_t(v, n):
        return V(v.ap.unsqueeze(1).to_broadcast([128, n, 16]), v.buf)

    s_g = ExitStack()
    k.stack = s_g
    beta = k.sb("beta", [128, NT, 16], F32)
    ngc = k.sb("ngc", [128, NT, 16], F32)
    ngcb = k.sb("ngcb", [128, NT, 16], F32)
    egc = k.sb("egc", [128, NT, 16], F32)
    ekt = k.sb("ekt", [128, NT, 16], F32)
    egl = k.sb("egl", [128, NT, 16], F32)
    with ExitStack() as st:
        k.stack = st
        wbaf = k.sb("wbaf", [128, 8, 32], F32)
        wba = k.sb("wba", [128, 8, 32], BF16)
        graw = k.sb("graw", [128, NT, 32], F32)
        gg = k.sb("gg", [128, NT, 16], F32)
        gtmp = k.sb("gtmp", [128, NT, 16], F32)
        lnb = k.sb("lnb", [128, NT, 16], F32)
        negA = k.sb("negA", [128, 16], F32)
        pg = k.pv(0, 0, 512, F32, 32)
        pc0, pc1, pt0, pt1 = k.banks[1], k.banks[2], k.banks[3], k.banks[4]
        k.dma("pool", wba, V(winv[:, :, OFF_B:OFF_B + 32], win_d.buf))
        for g0 in range(0, NT, 16):
            n = min(16, NT - g0)
            for j in range(n):
                t = g0 + j
                for kk in range(8):
                    k.mm(pg[:, j, :], hTv[:, kk, t * 128:(t + 1) * 128], wba[:, kk, :], start=(kk == 0), stop=(kk == 7))
            k.copy("act", graw[:, g0:g0 + n, :], pg[:, 0:n, :])
        k.act(beta, graw[:, :, 0:16], AF.Sigmoid)
        k.act(lnb, graw[:, :, 0:16], AF.Exp, scale=-1.0)
        k.act(lnb, lnb, AF.Ln, bias=1.0)
        k.act(negA, gpar[:, 0, :], AF.Exp)
        k.ts("dve", negA, negA, -1.0)
        k.tt("dve", gg, graw[:, :, 16:32], bcast_t(gpar[:, 1, :], NT), ALU.add)
        k.act(gg, gg, AF.Exp)
        k.act(gg, gg, AF.Ln, bias=1.0)
        k.tt("dve", gg, gg, bcast_t(negA, NT), ALU.mult)
        if "g" in dbg_out:
            k.dma("sp", dbg_out["g"], gg)
            k.dma("sp", dbg_out["beta"], beta)
        pcs = [pc0, pc1]
        pts = [pt0, pt1]
        for d in range(2):
            pcv = V(pcs[d].ap[:, 0:NT * 8].rearrange("p (t c) -> p t c", c=8), pcs[d].buf)
            ptv = V(pts[d].ap[:, 0:NT * 8].rearrange("p (t c) -> p t c", c=8), pts[d].buf)
            k.mm(pcv, ut[d], gg[:, :, d * 8:(d + 1) * 8])
            k.mm(ptv, onesf, gg[:, :, d * 8:(d + 1) * 8])
            sl = slice(d * 8, (d + 1) * 8)
            k.act(ngc[:, :, sl], pcv, AF.Copy, scale=-1.0)
            k.stt("dve", ngcb[:, :, sl], pcv, -1.0, lnb[:, :, sl], ALU.mult, ALU.subtract)
            k.act(egc[:, :, sl], pcv, AF.Exp)
            k.tt("dve", gtmp[:, :, sl], ptv, ngc[:, :, sl], ALU.add)
            k.act(ekt[:, :, sl], gtmp[:, :, sl], AF.Exp)
            k.act(egl[:, :, sl], ptv, AF.Exp)
        k.barrier()
    k.stack = ExitStack()
    if stage <= 2:
        return finish(nc, k, out_d)

    blocks = [(0, TC)] + [(TC + 512 * i, 512) for i in range(8)]
    xblocks = blocks[1:]
    poff = lambda tok: 1 + tok if tok < TC else 3 + tok
    heads = list(range(H)) if dbg is None or "heads" not in dbg else dbg["heads"]
    p4 = ExitStack()
    k.stack = p4
    ring = [[k.sb(f"ring{ty}_{i}", [128, 514], BF16) for i in range(3)] for ty in range(3)]
    qT = k.sb("qT", [128, TA], BF16)
    kT = k.sb("kT", [128, TA], BF16)
    vT = k.sb("vT", [128, TA], BF16)
    zs_b = [k.sb(f"zsb{i}", [128, 512], BF16) for i in range(1)]
    osum = k.sb("osum", [128, T], F32)
    wst_b = [k.sb(f"wstb{i}", [128, 8, 128], BF16) for i in range(3)]
    dgt3 = [k.sb(f"dgt{ty}", [128, 3, 128], BF16) for ty in range(3)]
    wbz = k.sb("wbz", [128, 8, 128], BF16)
    rn = [k.sb(f"rn{i}", [128, 512], F32) for i in range(1)]
    ofin = [k.sb(f"ofin{i}", [128, 512], BF16) for i in range(2)]
    osb = [Buf(f"osum{t}") for t in range(32)]
    pbig = [k.banks[0], k.banks[1]]
    GS = 3
    rot = [[k.banks[3 * d + i] for i in range(3)] for d in range(2)]
    rcnt = [0, 0]
    def nextb(d):
        rcnt[d] += 1
        return rot[d][rcnt[d] % 3]
    def b3(bank, n, dt=F32, off=0):
        if dt == F32:
            ap = bank.ap[:, off * 128:(off + n) * 128].rearrange("p (a b) -> p a b", b=128)
        else:
            ap = bank.ap[:, off * 64:(off + n) * 64].bitcast(BF16).rearrange("p (a b) -> p a b", b=128)
        return V(ap, bank.buf)
    pvn = [k.pv(6 + d, 0, 128) for d in range(2)]
    poT = [k.pv(6 + d, 128, 256) for d in range(2)]
    pS = [k.pv(6 + d, 256, 384) for d in range(2)]
    def gtmp(name, dt, nb=1):
        return [[k.sb(f"{name}{d}_{i}", [128, GS, 128], dt) for i in range(nb)] for d in range(2)]
    g_dgc, g_E = gtmp("dgc", F32), gtmp("E", F32)
    g_eg, g_Rk0, g_Q0, g_Q0T, g_Z = (gtmp(nm, BF16) for nm in ("eg", "Rk0", "Q0", "Q0T", "Z"))
    g_E1, g_E2 = gtmp("E1", BF16), gtmp("E2", BF16)
    g_LT, g_D, g_G = gtmp("LT", BF16, 2), gtmp("D", BF16, 2), gtmp("G", BF16, 2)
    g_W, g_nwT, g_Vt, g_ktail, g_qhT, g_qkm = (gtmp(nm, BF16, 2) for nm in ("W", "nwT", "Vt", "ktail", "qhT", "qkm"))
    t_vn = [k.sb(f"vn{d}", [128, 128], BF16) for d in range(2)]
    Sf = [k.sb(f"Sf{d}", [128, 128], F32) for d in range(2)]
    Sb = [k.sb(f"Sb{d}", [128, 128], BF16) for d in range(2)]
    def bcn(v, n):
        return V(v.ap.unsqueeze(1).to_broadcast([128, n, 128]), v.buf)
    nbig = [0]
    def nextbig():
        nbig[0] += 1
        return pbig[nbig[0] % 2]
    wcnt = [0]

    def load_w(col0):
        i = wcnt[0] % 3
        wcnt[0] += 1
        k.dma("pool", wst_b[i], V(winv[:, :, col0:col0 + 128], win_d.buf))
        return wst_b[i]

    order_f = list(range(NT))
    order_b = [1, 0] + list(range(NT - 1, 1, -1))

    for h in heads:
        blkbuf = [[Buf(f"qkv{ty}_{b_}") for b_ in range(9)] for ty in range(3)]
        blk_of = lambda t: 0 if t < 2 else 1 + (t - 2) // 4
        dsts = (qT, kT, vT)
        def tv(ty, t):
            return V(dsts[ty].ap[:, t * 128:(t + 1) * 128], blkbuf[ty][blk_of(t)])
        def tlv(ty, a_, n_):
            bl = []
            for t_ in range(a_, a_ + n_):
                if blkbuf[ty][blk_of(t_)] not in bl:
                    bl.append(blkbuf[ty][blk_of(t_)])
            return V(dsts[ty].ap[:, a_ * 128:(a_ + n_) * 128].rearrange("p (a b) -> p a b", b=128), bl)
        pcnt = [0]
        def nextp():
            pcnt[0] += 1
            return k.banks[pcnt[0] % 6]
        wbs = [load_w(OFF_Q + (ty * 8 + h) * 128) for ty in range(3)]
        for ty in range(3):
            for tap in range(3):
                k.ts("pool", dgt3[ty][:, tap, :], identf, cqkv[:, ty * 8 + h, tap:tap + 1])

        def conv_block(ty, b_):
            s, n = blocks[b_]
            slot = ring[ty][b_ % 3]
            pp = nextp()
            for tap in range(3):
                k.mm(pp[:, 0:n], dgt3[ty][:, tap, :], slot[:, tap:tap + n], start=(tap == 0), stop=(tap == 2))
            dv = V(dsts[ty].ap[:, s:s + n], blkbuf[ty][b_])
            k.act(dv, pp[:, 0:n], AF.Silu)
            if ty < 2:
                pp2 = nextp()
                r = rn[0]
                sqv = ofin[(b_ + ty) % 2]
                k.tt("pool", sqv[:, 0:n], dv, dv, ALU.mult)
                k.mm(pp2[:, 0:n], onesb, sqv[:, 0:n])
                k.act(r[:, 0:n], pp2[:, 0:n], AF.Ln, bias=EPS)
                k.act(r[:, 0:n], r[:, 0:n], AF.Exp, scale=-0.5)
                k.stt("dve", dv, dv, (128.0 ** -0.5) if ty == 0 else 1.0, r[:, 0:n], ALU.mult, ALU.mult)

        for b_ in range(9):
            s, n = blocks[b_]
            for ty in range(3):
                slot = ring[ty][b_ % 3]
                prev = ring[ty][(b_ - 1) % 3]
                pp = nextp()
                for kk in range(8):
                    k.mm(pp[:, 0:n], wbs[ty][:, kk, :], hTv[:, kk, s:s + n], start=(kk == 0), stop=(kk == 7))
                k.copy("dve", slot[:, 1:1 + n], pp[:, 0:n])
                if b_ in (0, 1):
                    k.memset("pool", slot[:, 0:1], 0.0)
                else:
                    k.copy("pool", slot[:, 0:1], prev[:, 512:513])
                if b_ in (0, 8):
                    k.memset("pool", slot[:, n + 1:n + 2], 0.0)
                if b_ >= 2:
                    k.copy("pool", prev[:, 513:514], slot[:, 1:2])
                if b_ == 0:
                    conv_block(ty, 0)
                elif b_ >= 2:
                    conv_block(ty, b_ - 1)
                if b_ == 8:
                    conv_block(ty, 8)
        if "qkv" in dbg_out and h == heads[0]:
            with ExitStack() as st2:
                old = k.stack
                k.stack = st2
                tf_ = k.sb("dbgqkv", [128, 3, 512], F32)
                k.copy("dve", tf_[:, 0, :], qT[:, 0:512])
                k.copy("dve", tf_[:, 1, :], kT[:, 0:512])
                k.copy("dve", tf_[:, 2, :], vT[:, 0:512])
                k.dma("sp", dbg_out["qkv"], tf_)
                k.barrier()
                k.stack = old
        for d in range(2):
            k.memset("pool", Sf[d], 0.0)
            k.memset("pool", Sb[d], 0.0)
        visited = set()
        groups = [(0, 2)] + [(2 + 3 * i, 3) for i in range(10)] + [(32, 2)]
        if dbg is not None and "nsteps" in dbg:
            groups = groups[:dbg["nsteps"]]
        gorder = [groups, [groups[0]] + groups[:0:-1]]

        def pre_gen(d, a, n, gp):
            col = d * 8 + h
            isx = a >= 2
            def bc(arr):
                return V(arr.ap[:, a:a + n, col].unsqueeze(2).to_broadcast([128, n, 128]), arr.buf)
            tsl = lambda i: slice((a + i) * 128, (a + i + 1) * 128)
            dgc, E, eg, Rk0, Q0, Q0T, Z = (x[d][0][:, 0:n, :] for x in (g_dgc, g_E, g_eg, g_Rk0, g_Q0, g_Q0T, g_Z))
            LT, Dm, Gm = g_LT[d], g_D[d], g_G[d]
            tm_ = LT[0][:, 0:n, :]
            W, nwT, Vt, ktail, qhT, qkm = (x[d][gp][:, 0:n, :] for x in (g_W, g_nwT, g_Vt, g_ktail, g_qhT, g_qkm))
            pA = nextb(d)
            pAk, pAv = b3(pA, n, BF16, 0), b3(pA, n, BF16, GS)
            for i in range(n):
                k.tr(pAk[:, i, :], tv(1, a + i), ident)
                k.tr(pAv[:, i, :], tv(2, a + i), ident)
            yield
            k.act(Vt, pAv, AF.Copy)
            k.tt("dve", Rk0, pAk, bc(egc), ALU.mult)
            k.tt("dve", ktail, pAk, bc(ekt), ALU.mult)
            yield
            E1, E2, tm2_ = g_E1[d][0][:, 0:n, :], g_E2[d][0][:, 0:n, :], Z
            k.tt("pool", dgc, bcn(identf, n), bc(ngc), ALU.mult)
            pB = nextb(d)
            pBv = b3(pB, n)
            for i in range(n):
                k.mm(pBv[:, i, :], onesf, dgc[:, i, :])
            yield
            k.tt("dve", E, bcn(negm[d], n), pBv, ALU.subtract)
            for i in range(n):
                k.act(E2[:, i, :], E[:, i, :], AF.Exp, bias=ngcb[:, a + i, col:col + 1])
            if isx:
                for i in range(n):
                    k.act(E1[:, i, :], E[:, i, :], AF.Exp, bias=ngc[:, a + i, col:col + 1])
                k.act(eg, pBv, AF.Exp, scale=-1.0)
                k.tt("pool", qhT, tlv(0, a, n), eg, ALU.mult)
            yield
            pC = nextb(d)
            pCv = b3(pC, n)
            for i in range(n):
                k.mm(pCv[:, i, :], tv(1, a + i), tv(1, a + i))
            k.tt("dve", Q0, pCv, E2, ALU.mult)
            if isx:
                pQ = nextb(d)
                pQv = b3(pQ, n)
                for i in range(n):
                    k.mm(pQv[:, i, :], tv(1, a + i), tv(0, a + i))
                k.tt("dve", qkm, pQv, E1, ALU.mult)
            yield
            pT = nextb(d)
            pTv = b3(pT, n, BF16, 0)
            for i in range(n):
                k.tr(pTv[:, i, :], Q0[:, i, :], ident)
            k.act(Q0T, pTv, AF.Copy)
            yield
            k.tt("dve", tm_, Q0, bcn(hmask[:, 0, :], n), ALU.mult)
            k.tt("dve", Dm[0][:, 0:n, :], bcn(ident, n), tm_, ALU.subtract)
            k.tt("pool", tm2_, Q0T, bcn(hmask[:, 0, :], n), ALU.mult)
            k.tt("pool", Gm[0][:, 0:n, :], bcn(ident, n), tm2_, ALU.subtract)
            k.tt("pool", LT[1][:, 0:n, :], Q0T, bcn(hmask[:, 1, :], n), ALU.mult)
            yield
            for m_ in range(1, 7):
                a_, b_ = (m_ - 1) % 2, m_ % 2
                Da, Ga, Lm = Dm[a_][:, 0:n, :], Gm[a_][:, 0:n, :], LT[m_ % 2][:, 0:n, :]
                pz = nextb(d)
                pzv = b3(pz, n)
                for i in range(n):
                    k.mm(pzv[:, i, :], Lm[:, i, :], Da[:, i, :])
                if m_ < 6:
                    k.tt("pool", LT[(m_ + 1) % 2][:, 0:n, :], Q0T, bcn(hmask[:, m_ + 1, :], n), ALU.mult)
                k.act(Z, pzv, AF.Copy)
                yield
                pd_ = nextb(d)
                pdv = b3(pd_, n)
                for i in range(n):
                    k.mm(pdv[:, i, :], Ga[:, i, :], Z[:, i, :])
                if m_ < 6:
                    pg_ = nextb(d)
                    pgv = b3(pg_, n)
                    for i in range(n):
                        k.mm(pgv[:, i, :], Z[:, i, :], Ga[:, i, :])
                k.tt("dve", W if m_ == 6 else Dm[b_][:, 0:n, :], Da, pdv, ALU.subtract)
                if m_ < 6:
                    k.tt("dve", Gm[b_][:, 0:n, :], Ga, pgv, ALU.subtract)
                yield
            pw = nextb(d)
            pwv = b3(pw, n)
            for i in range(n):
                k.mm(pwv[:, i, :], Rk0[:, i, :], W[:, i, :])
            k.act(nwT, pwv, AF.Copy, scale=-1.0)
            yield

        def state_gen(d, a, n, gp):
            col = d * 8 + h
            isx = a >= 2
            W, nwT, Vt, ktail, qhT, qkm = (x[d][gp] for x in (g_W, g_nwT, g_Vt, g_ktail, g_qhT, g_qkm))
            vn = t_vn[d]
            for i in (range(n) if d == 0 else range(n - 1, -1, -1)):
                t = a + i
                sc = lambda arr: arr[:, t, col:col + 1]
                k.mm(pvn[d], W[:, i, :], Vt[:, i, :], start=True, stop=False)
                k.mm(pvn[d], nwT[:, i, :], Sb[d], start=False, stop=True)
                k.act(vn, pvn[d], AF.Identity, scale=sc(beta))
                yield
                if isx:
                    xt = t - 2
                    ov = V(osum.ap[:, xt * 128:(xt + 1) * 128], osb[xt])
                    k.mm(poT[d], Sb[d], qhT[:, i, :], start=True, stop=False)
                    k.mm(poT[d], vn, qkm[:, i, :], start=False, stop=True)
                k.mm(pS[d], ktail[:, i, :], vn)
                if isx:
                    if xt not in visited:
                        visited.add(xt)
                        k.act(ov, poT[d], AF.Copy)
                    else:
                        k.tt("dve", ov, poT[d], ov, ALU.add)
                k.stt("dve", Sf[d], Sf[d], sc(egl), pS[d], ALU.mult, ALU.add)
                k.act(Sb[d], Sf[d], AF.Copy)
                yield

        def run_all(gens):
            gens = list(gens)
            while gens:
                for g_ in list(gens):
                    try:
                        next(g_)
                    except StopIteration:
                        gens.remove(g_)

        ng = len(groups)
        run_all([pre_gen(0, *gorder[0][0], 0), pre_gen(1, *gorder[1][0], 0)])
        for gi in range(ng):
            gens = [state_gen(0, *gorder[0][gi], gi % 2), state_gen(1, *gorder[1][gi], gi % 2)]
            if gi + 1 < ng:
                gens += [pre_gen(0, *gorder[0][gi + 1], (gi + 1) % 2), pre_gen(1, *gorder[1][gi + 1], (gi + 1) % 2)]
            run_all(gens)
        if "S" in dbg_out and h == heads[0]:
            k.dma("sp", dbg_out["S"][0], Sf[0])
            k.dma("sp", dbg_out["S"][1], Sf[1])
        for bi in range(8):
            s = bi * 512
            ovs = [V(osum.ap[:, s:s + 512], osb[bi * 4 + j]) for j in range(4)]
            class _M:
                pass
            ovall = V(osum.ap[:, s:s + 512], osb[bi * 4])
            extra = [osb[bi * 4 + j] for j in range(1, 4)]
            if bi == 0:
                k.dma("pool", wbz, V(winv[:, :, OFF_Z + h * 128:OFF_Z + (h + 1) * 128], win_d.buf))
            pz_ = nextbig()
            zsb = zs_b[0]
            for kk in range(8):
                k.mm(pz_, wbz[:, kk, :], hTv[:, kk, TC + s:TC + s + 512], start=(kk == 0), stop=(kk == 7))
            k.act(zsb, pz_, AF.Silu)
            pp = nextbig()
            sqv = ofin[bi % 2]
            r = rn[0]
            of_ = V(osum.ap[:, s:s + 512], [osb[bi * 4 + j] for j in range(4)])
            k.op("pool", lambda g, sqv=sqv, s=s: g.tensor_tensor(out=sqv.ap, in0=osum.ap[:, s:s + 512], in1=osum.ap[:, s:s + 512], op=ALU.mult),
                 [osb[bi * 4 + j] for j in range(4)], [sqv.buf])
            k.mm(pp, onesb, sqv)
            k.act(r, pp, AF.Ln, bias=EPS, scale=1.0 / 128)
            k.act(r, r, AF.Exp, scale=-0.5)
            k.stt("dve", of_, of_, dnn[:, 0:1], r, ALU.mult, ALU.mult)
            if "o0" in dbg_out and h == heads[0]:
                k.tt("pool", of_, of_, zsb, ALU.mult)
                k.dma("sp", V(dbg_out["o0"].ap[:, s:s + 512], dbg_out["o0"].buf), of_)
                k.copy("pool", sqv, of_)
            else:
                k.tt("pool", sqv, of_, zsb, ALU.mult)
            k.dma("sp", V(oT_d.ap[h, :, s:s + 512], DBuf("st")), sqv)
    k.barrier()
    p4.close()
    s_g.close()
    k.stack = ExitStack()
    if stage <= 4:
        return finish(nc, k, out_d)

    def wchunk_loader(stf, stb):
        cnt = [0]
        def load(src_v, K):
            j = cnt[0] % len(stb)
            cnt[0] += 1
            k.dma("pool", stb[j][:, 0:K, :], src_v)
            return stb[j]
        return load

    def load_resident(dst_bf, src_ap, src_buf, K, ncols, stg):
        i = 0
        for k0 in range(0, K, 8):
            kn = min(8, K - k0)
            for c0 in range(0, ncols, 512):
                cn = min(512, ncols - c0)
                k.dma("pool", dst_bf[:, k0:k0 + kn, c0:c0 + cn], V(src_ap[:, k0:k0 + kn, c0:c0 + cn], src_buf))
                i += 1

    with ExitStack() as st:
        k.stack = st
        FCS = k.sb("FCS", [128, 32, 4, 256], BF16)
        fT = [k.sb(f"fT{i}", [128, 512], BF16) for i in range(2)]
        stf = None
        stb = [k.sb(f"p3stb{i}", [128, 8, 128], BF16) for i in range(2)]
        ctab = [k.sb(f"ctab{i}", [128, 4, 512], BF16) for i in range(2)]
        stab = [k.sb(f"stab{i}", [128, 4, 512], BF16) for i in range(2)]
        yblk = [k.sb(f"yblk{i}", [128, 4, 512], BF16) for i in range(2)]
        lw = wchunk_loader(stf, stb)
        nb_ = 0
        for g in range(4):
            wb = lw(V(winv[:, :, OFF_F + g * 128:OFF_F + (g + 1) * 128], win_d.buf), 8)
            for bi, (s, n) in enumerate(xblocks):
                pp = k.banks[nb_ % 2]
                ft = fT[nb_ % 2]
                nb_ += 1
                for kk in range(8):
                    k.mm(pp, wb[:, kk, :], hTv[:, kk, s:s + n], start=(kk == 0), stop=(kk == 7))
                k.act(ft, pp, AF.Copy)
                pf = k.banks[2 + (nb_ % 2)]
                pfv = V(pf.ap.rearrange("p (a b) -> p a b", b=256), pf.buf)
                for j2 in range(2):
                    for jj in range(2):
                        j = j2 * 2 + jj
                        k.mm(pfv[:, jj, :], ft[:, j * 128:(j + 1) * 128], c128)
                    t0 = bi * 4 + j2 * 2
                    if j2 == 0:
                        k.act(FCS[:, t0:t0 + 2, g, :], pfv, AF.Copy)
                    else:
                        k.copy("dve", FCS[:, t0:t0 + 2, g, :], pfv)
        cosv = cos_d.ap.rearrange("(tt p) f -> p tt f", p=128)
        sinv = sin_d.ap.rearrange("(tt p) f -> p tt f", p=128)
        ld = 0
        for kb in range(8):
            for t4 in range(8):
                ct, st_ = ctab[ld % 2], stab[ld % 2]
                ld += 1
                k.dma("sp", ct, V(cosv[:, t4 * 4:(t4 + 1) * 4, kb * 512:(kb + 1) * 512], cos_d.buf))
                k.dma("pool", st_, V(sinv[:, t4 * 4:(t4 + 1) * 4, kb * 512:(kb + 1) * 512], sin_d.buf))
                for ti in range(4):
                    tt_ = t4 * 4 + ti
                    for g in range(4):
                        k.mm(k.banks[4 + g], FCS[:, tt_, g, 0:128], ct[:, ti, :], start=(tt_ == 0), stop=False)
                        k.mm(k.banks[4 + g], FCS[:, tt_, g, 128:256], st_[:, ti, :], start=False, stop=(tt_ == 31))
            yb = yblk[kb % 2]
            for g in range(4):
                if g % 2 == 0:
                    k.act(yb[:, g, :], k.banks[4 + g], AF.Copy)
                else:
                    k.copy("dve", yb[:, g, :], k.banks[4 + g])
            k.dma("sp", V(yT_d.ap[:, :, kb * 512:(kb + 1) * 512].rearrange("g p f -> p g f"), DBuf("st")), yb)
        if "fm" in dbg_out:
            pass
        k.barrier()
    k.stack = ExitStack()
    if stage <= 5:
        return finish(nc, k, out_d)

    with ExitStack() as st:
        k.stack = st
        wg = k.sb("wg", [128, 8, 2048], BF16)
        wf4 = k.sb("wf4", [128, 4, 1024], BF16)
        wdn = k.sb("wdn", [128, 8, 1024], BF16)
        stg = None
        ytb = [k.sb(f"ytb{i}", [128, 4, 512], BF16) for i in range(2)]
        otb = [k.sb(f"otb{i}", [128, 8, 512], BF16) for i in range(2)]
        g0 = [k.sb(f"g0_{i}", [128, 512], BF16) for i in range(2)]
        g1 = [k.sb(f"g1_{i}", [128, 512], BF16) for i in range(2)]
        m0 = [k.sb(f"m0_{i}", [128, 512], BF16) for i in range(2)]
        m1 = [k.sb(f"m1_{i}", [128, 512], BF16) for i in range(2)]
        mixs = k.sb("mixs", [128, 2, 8, 512], BF16)
        mixs_buf = [Buf("mixs0"), Buf("mixs1")]
        load_resident(wg, winv[:, :, OFF_G:OFF_G + 2048], win_d.buf, 8, 2048, stg)
        load_resident(wf4, wf_d.ap.rearrange("(g p) d -> p g d", p=128), wf_d.buf, 4, 1024, stg)
        load_resident(wdn, wdn_d.ap.rearrange("(h p) d -> p h d", p=128), wdn_d.buf, 8, 1024, stg)
        it = 0
        hmt = [Buf(f"hTm{mt}") for mt in range(8)]
        for mt in range(8):
            s = TC + mt * 512
            yt, ot = ytb[mt % 2], otb[mt % 2]
            k.dma("sp", yt, V(yT_d.ap[:, :, mt * 512:(mt + 1) * 512].rearrange("g p f -> p g f"), yT_d.buf))
            k.dma("pool", ot, V(oT_d.ap[:, :, mt * 512:(mt + 1) * 512].rearrange("h p f -> p h f"), oT_d.buf))
            for dc in range(8):
                dsl = slice(dc * 128, (dc + 1) * 128)
                i2 = it % 2
                it += 1
                pb = [k.banks[4 * i2 + j] for j in range(4)]
                for g in range(4):
                    k.mm(pb[0], wf4[:, g, dsl], yt[:, g, :], start=(g == 0), stop=(g == 3))
                for kk in range(8):
                    k.mm(pb[1], wg[:, kk, dc * 128:(dc + 1) * 128], V(hT.ap[:, kk, s:s + 512], hmt[mt]), start=(kk == 0), stop=(kk == 7))
                for hh in range(8):
                    k.mm(pb[2], wdn[:, hh, dsl], ot[:, hh, :], start=(hh == 0), stop=(hh == 7))
                for kk in range(8):
                    k.mm(pb[3], wg[:, kk, 1024 + dc * 128:1024 + (dc + 1) * 128], V(hT.ap[:, kk, s:s + 512], hmt[mt]), start=(kk == 0), stop=(kk == 7))
                k.act(g0[i2], pb[1], AF.Sigmoid)
                k.act(g1[i2], pb[3], AF.Sigmoid)
                k.tt("dve", m0[i2], pb[0], g0[i2], ALU.mult)
                k.tt("dve", m1[i2], pb[2], g1[i2], ALU.mult)
                k.tt("pool", V(mixs.ap[:, mt % 2, dc, :], mixs_buf[mt % 2]), m0[i2], m1[i2], ALU.add)
            k.copy("act" if mt % 2 == 0 else "dve", V(hT.ap[:, :, s:s + 512], hmt[mt]), V(mixs.ap[:, mt % 2, :, :], mixs_buf[mt % 2]))
        k.barrier()
    k.stack = ExitStack()
    if stage <= 6:
        return finish(nc, k, out_d)

    def branch_tail(mt, producer, cidx, resid_d, final, tb):
        yx, sq, rst, xin_, x1t = tb["yx"][mt % 2], tb["sq"][mt % 2], tb["rst"][mt % 2], tb["xin"], tb["x1t"]
        for dc in range(8):
            pb = k.banks[dc % 2]
            producer(dc, pb)
            k.act(yx[:, dc, :], pb, AF.Copy)
            k.act(sq[:, dc, :], pb, AF.Square)
        pss = k.banks[2]
        for dc in range(8):
            k.mm(pss, onesb, sq[:, dc, :], start=(dc == 0), stop=(dc == 7))
        k.act(rst, pss, AF.Ln, bias=EPS, scale=1.0 / D)
        k.act(rst, rst, AF.Exp, scale=-0.5)
        for dc in range(8):
            k.stt("dve", yx[:, dc, :], yx[:, dc, :], coef[:, cidx, dc:dc + 1], rst, ALU.mult, ALU.mult)
        for j in range(4):
            tok0 = mt * 512 + j * 128
            xi = xin_[j % len(xin_)]
            xo = x1t[j % len(x1t)]
            k.dma("sp" if j % 2 == 0 else "pool", xi, V(resid_d.ap[tok0:tok0 + 128, :], resid_d.buf))
            ba, bb = k.banks[3 + 2 * (j % 2)], k.banks[4 + 2 * (j % 2)]
            for dc in range(8):
                bk = ba if dc < 4 else bb
                k.tr(bk[:, (dc % 4) * 128:(dc % 4 + 1) * 128], yx[:, dc, j * 128:(j + 1) * 128], identf)
            k.tt("dve", xo[:, 0:512], ba, xi[:, 0:512], ALU.add)
            k.tt("dve", xo[:, 512:1024], bb, xi[:, 512:1024], ALU.add)
            if final:
                k.dma("sp", V(out_d.ap[tok0:tok0 + 128, :], DBuf("st")), xo)
            else:
                k.dma("sp", V(x1_d.ap[tok0:tok0 + 128, :], DBuf("st")), xo)
                norm_tile(xo, 2 + mt * 4 + j, 5, 6, hT, tb["hbuf"], tb["tm"])

    def tail_bufs(nbuf, need_tm=True):
        if not need_tm:
            return {"yx": [k.sb(f"yx{i}", [128, 8, 512], F32) for i in range(2)], "sq": [k.sb(f"sqb{i}", [128, 8, 512], BF16) for i in range(2)],
                    "rst": [k.sb(f"rst{i}", [128, 512], F32) for i in range(2)],
                    "xin": [k.sb(f"rxin{i}", [128, D], F32) for i in range(nbuf)], "x1t": [k.sb(f"x1t{i}", [128, D], F32) for i in range(nbuf)], "tm": None}
        return {"yx": [k.sb(f"yx{i}", [128, 8, 512], F32) for i in range(2)], "sq": [k.sb(f"sqb{i}", [128, 8, 512], BF16) for i in range(2)],
                "rst": [k.sb(f"rst{i}", [128, 512], F32) for i in range(2)],
                "xin": [k.sb(f"rxin{i}", [128, D], F32) for i in range(nbuf)], "x1t": [k.sb(f"x1t{i}", [128, D], F32) for i in range(nbuf)],
                "tm": {"sq": [k.sb(f"t_sq{i}", [128, D], F32) for i in range(2)], "ss": [k.sb(f"t_ss{i}", [128, 1], F32) for i in range(2)],
                       "xn": [k.sb(f"t_xn{i}", [128, D], BF16) for i in range(2)], "pt": [k.pv(7, 0, 512, BF16, 128)]}}

    with ExitStack() as st:
        k.stack = st
        wout = k.sb("wout", [128, 8, 1024], BF16)
        stg = None
        load_resident(wout, wout_d.ap.rearrange("(c p) d -> p c d", p=128), wout_d.buf, 8, 1024, stg)
        tb = tail_bufs(2)
        hmt2 = [Buf(f"hTn{mt}") for mt in range(8)]
        for mt in range(8):
            s = TC + mt * 512
            def prod(dc, pb, s=s, mt=mt):
                for c in range(8):
                    k.mm(pb, wout[:, c, dc * 128:(dc + 1) * 128], V(hT.ap[:, c, s:s + 512], hmt2[mt]), start=(c == 0), stop=(c == 7))
            tb["hbuf"] = hmt2[mt]
            branch_tail(mt, prod, 4, x_d, False, tb)
        k.barrier()
    k.stack = ExitStack()
    if stage <= 7:
        return finish(nc, k, out_d)

    wupv = wup_d.ap.rearrange("(k p) c -> p k c", p=128)
    with ExitStack() as st:
        k.stack = st
        stf = None
        stb = [k.sb(f"p6stb{i}", [128, 8, 128], BF16) for i in range(4)]
        lw = wchunk_loader(stf, stb)
        apad_b = [k.sb(f"apad{i}", [128, 66, 66], BF16) for i in range(2)]
        dg9_b = [k.sb(f"dg9_{i}", [128, 9, 128], BF16) for i in range(2)]
        sa = [k.sb(f"sa{i}", [128, 512], BF16) for i in range(4)]
        gtc = [k.sb(f"gtc{i}", [128, T], BF16) for i in range(2)]
        k.memset("pool", apad_b[0], 0.0)
        k.memset("pool", apad_b[1], 0.0)
        nb_ = 0
        for c in range(NFF):
            apad, dg9 = apad_b[c % 2], dg9_b[c % 2]
            wa = lw(V(wupv[:, :, c * 128:(c + 1) * 128], wup_d.buf), 8)
            wu = lw(V(wupv[:, :, DFF + c * 128:DFF + (c + 1) * 128], wup_d.buf), 8)
            for tap in range(9):
                k.ts("pool", dg9[:, tap, :], identf, cffn[:, c, tap:tap + 1])
            for bi in range(8):
                s = TC + bi * 512
                pp = k.banks[(0, 1, 6, 7)[nb_ % 4]]
                nb_ += 1
                for kk in range(8):
                    k.mm(pp, wa[:, kk, :], hTv[:, kk, s:s + 512], start=(kk == 0), stop=(kk == 7))
                k.act(apad[:, 1 + bi * 8:1 + bi * 8 + 8, 1:65], V(pp.ap.rearrange("p (r c) -> p r c", c=64), pp.buf), AF.Copy)
            gt = gtc[c % 2]
            for bi in range(8):
                s = TC + bi * 512
                pc = k.banks[2 + (bi % 2)]
                pu = k.banks[4 + (bi % 2)]
                pcv = V(pc.ap.rearrange("p (r c) -> p r c", c=64), pc.buf)
                for tap in range(9):
                    dr, dcc = tap // 3, tap % 3
                    k.mm(pcv, dg9[:, tap, :], apad[:, bi * 8 + dr:bi * 8 + dr + 8, dcc:dcc + 64], start=(tap == 0), stop=(tap == 8))
                k.act(sa[bi % 4], pc, AF.Silu)
                for kk in range(8):
                    k.mm(pu, wu[:, kk, :], hTv[:, kk, s:s + 512], start=(kk == 0), stop=(kk == 7))
                k.tt("dve", gt[:, bi * 512:(bi + 1) * 512], pu, sa[bi % 4], ALU.mult)
            k.dma("sp" if c % 2 == 0 else "pool", V(gT_d.ap[c], DBuf("st")), gt)
        k.barrier()
    k.stack = ExitStack()
    s_h.close()
    k.stack = ExitStack()
    if stage <= 8:
        return finish(nc, k, out_d)

    with ExitStack() as st:
        k.stack = st
        wdown = k.sb("wdown", [128, NFF, 1024], BF16)
        stg = None
        load_resident(wdown, wdown_d.ap.rearrange("(c p) d -> p c d", p=128), wdown_d.buf, NFF, 1024, stg)
        gbl = [k.sb(f"gbl{i}", [128, NFF, 512], BF16) for i in range(2)]
        tb = tail_bufs(2, need_tm=False)
        for mt in range(8):
            gb = gbl[mt % 2]
            k.dma("sp", gb[:, 0:11, :], V(gT_d.ap[0:11, :, mt * 512:(mt + 1) * 512].rearrange("c p f -> p c f"), gT_d.buf))
            k.dma("pool", gb[:, 11:NFF, :], V(gT_d.ap[11:NFF, :, mt * 512:(mt + 1) * 512].rearrange("c p f -> p c f"), gT_d.buf))
            def prod(dc, pb, gb=gb):
                for c in range(NFF):
                    k.mm(pb, wdown[:, c, dc * 128:(dc + 1) * 128], gb[:, c, :], start=(c == 0), stop=(c == NFF - 1))
            branch_tail(mt, prod, 7, x1_d, True, tb)
        k.barrier()
    k.stack = ExitStack()
    return finish(nc, k, out_d)


def finish(nc, k, out_d):
    k.barrier()
    return nc


def prep_inputs(inp, b):
    f = lambda a: np.ascontiguousarray(a, dtype=np.float32)
    colmajor = lambda v: f(np.asarray(v).reshape(-1, 128).T)
    m = {}
    m["x"] = f(inp["x"][b])
    m["ctx"] = f(inp["ctx"][b])
    m["cc"] = f(np.stack([colmajor(inp["c"][b]), colmajor(inp["c_ctx"])], axis=-1))
    m["w_ada"] = f(inp["w_ada"][0])
    m["b_ada"] = colmajor(inp["b_ada"][0])
    m["norms"] = f(np.stack([colmajor(inp[n][0]) for n in ("norm_pre_mix", "norm_post_mix", "norm_pre_ffn", "norm_post_ffn")], axis=1))
    m["w_in"] = f(inp["w_in"][0])
    cq = np.asarray(inp["conv_qkv"][0])
    m["conv_qkv"] = f(cq.T.reshape(24, 128, 3).transpose(1, 0, 2))
    gp = np.stack([np.asarray(inp["a_log"][0]).reshape(16), np.asarray(inp["dt_bias"][0]).reshape(16)], 0)
    m["gpar"] = f(np.broadcast_to(gp[None], (128, 2, 16)))
    m["dn_norm"] = f(np.asarray(inp["dn_norm"][0]).reshape(128, 1))
    m["w_fourier"] = f(inp["w_fourier"][0])
    m["w_dn"] = f(inp["w_dn"][0])
    m["w_out"] = f(inp["w_out"][0])
    m["w_up"] = f(inp["w_up"][0])
    cf = np.asarray(inp["conv_ffn"][0]).reshape(9, DFF)
    m["conv_ffn"] = f(cf.T.reshape(NFF, 128, 9).transpose(1, 0, 2))
    m["w_down"] = f(inp["w_down"][0])
    return m


_CONST = {}


def consts():
    if not _CONST:
        idx = np.arange(T, dtype=np.int64)
        ang = (2.0 * np.pi / T) * ((idx[:, None] * idx[None, :]) % T).astype(np.float64)
        s = 1.0 / np.sqrt(float(T) * 128.0)
        _CONST["dft_cos"] = (np.cos(ang) * s).astype(ml_dtypes.bfloat16)
        _CONST["dft_sin"] = (np.sin(ang) * s).astype(ml_dtypes.bfloat16)
        i8 = np.arange(128, dtype=np.int64)
        a8 = (2.0 * np.pi / 128) * ((i8[:, None] * i8[None, :]) % 128).astype(np.float64)
        j8 = np.arange(128)
        hm = []
        for m_ in range(7):
            s_ = 2 ** m_
            blk2 = (j8[:, None] // (2 * s_)) == (j8[None, :] // (2 * s_))
            half = (j8[:, None] // s_) != (j8[None, :] // s_)
            hm.append((blk2 & half).astype(np.float32))
        _CONST["hmask"] = np.stack(hm, axis=1).astype(ml_dtypes.bfloat16)
        _CONST["dft128"] = np.concatenate([np.cos(a8), -np.sin(a8)], axis=1).astype(ml_dtypes.bfloat16)
    return _CONST


def kernel(**inputs):
    inp = {k_: np.asarray(v) for k_, v in inputs.items()}
    nc = build()
    cst = consts()
    in_maps = []
    for b in range(8):
        m = prep_inputs(inp, b)
        m.update(cst)
        in_maps.append(m)
    res = run_bass_kernel_spmd(nc, in_maps, core_ids=list(range(8)))
    return np.stack([np.asarray(r["out"], dtype=np.float32) for r in res.results], axis=0)
```

```python
import os
from contextlib import ExitStack
import numpy as np
import ml_dtypes
import concourse.bass as bass
import concourse.mybir as mybir
from concourse.bass_utils import run_bass_kernel_spmd

F32 = mybir.dt.float32
BF16 = mybir.dt.bfloat16
AF = mybir.ActivationFunctionType
ALU = mybir.AluOpType

D = 1024
T = 4096
TC = 256
TA = TC + T
NT = TA // 128
H = 8
OFF_F, OFF_Q, OFF_K, OFF_V, OFF_Z, OFF_B, OFF_A, OFF_G = 0, 512, 1536, 2560, 3584, 4608, 4624, 4640
INW = 6688
DFF = 2816
NFF = DFF // 128
EPS = 1e-6


class Buf:
    __slots__ = ("name", "last_w", "readers", "dsem", "dcount", "excl")

    def __init__(self, name, excl=False):
        self.name = name
        self.excl = excl
        self.last_w = None
        self.readers = []
        self.dsem = None
        self.dcount = 0


class V:
    __slots__ = ("ap", "buf")

    def __init__(self, ap, buf):
        self.ap = ap
        self.buf = buf

    def __getitem__(self, idx):
        return V(self.ap[idx], self.buf)

    def sub(self, idx, buf):
        return V(self.ap[idx], buf)


def _bufs(*vs):
    out = []
    for v in vs:
        if isinstance(v, V) and v.buf is not None:
            for b in (v.buf if isinstance(v.buf, (list, tuple)) else (v.buf,)):
                if b not in out:
                    out.append(b)
    return out


def _ap(v):
    return v.ap if isinstance(v, V) else v


class K:
    def __init__(self, nc):
        self.nc = nc
        self.engs = {"pe": nc.tensor, "act": nc.scalar, "dve": nc.vector, "pool": nc.gpsimd, "sp": nc.sync}
        self.sem = {n: nc.alloc_semaphore(f"s_{n}") for n in self.engs}
        self.cnt = {n: 0 for n in self.engs}
        self.known = {n: {} for n in self.engs}
        self.dsems = []
        self.nins = 0
        self.nwaits = 0
        self.stack = ExitStack()
        self.limit = None
        self.log = []
        self.sched = os.environ.get('KSCHED', '1') == '1'
        self.pending = []
        self.ptags = []

    def _uid(self):
        self.uid = getattr(self, 'uid', 0) + 1
        return self.uid

    def sb(self, name, shape, dt, nbuf=None):
        t = self.stack.enter_context(self.nc.sbuf_tensor(f"sb{self._uid()}_" + name, list(shape), dt))
        return V(t[:] if hasattr(t, "__getitem__") else t.ap(), Buf(name) if nbuf is None else nbuf)

    def init_banks(self):
        self.banks = []
        for i in range(8):
            t = self.nc.psum_tensor(f"ps_bank{i}", [128, 512], F32).__enter__()
            self.banks.append(V(t[:], Buf(f"bank{i}", excl=True)))

    def pv(self, bank, lo, hi, dt=F32, inner=None):
        b = self.banks[bank]
        ap = b.ap[:, lo:hi]
        if dt != F32:
            ap = ap.bitcast(dt)
        if inner is not None:
            ap = ap.rearrange("p (a b) -> p a b", b=inner)
        return V(ap, b.buf)

    def _wait(self, e, ev):
        sem, val, src = ev
        if src == "pe" and e == "pe":
            return
        kn = self.known[e]
        if kn.get(sem.num, 0) >= val:
            return
        kn[sem.num] = val
        self.engs[e].wait_ge(sem, val)
        self.nwaits += 1

    def _deps(self, e, reads, writes):
        best = {}
        def add(ev):
            s = ev[0].num
            if s not in best or best[s][1] < ev[1]:
                best[s] = ev
        for b in reads:
            if b.last_w is not None:
                add(b.last_w)
        for b in writes:
            if b.last_w is not None:
                add(b.last_w)
            for ev in b.readers:
                add(ev)
        for ev in best.values():
            self._wait(e, ev)

    def _record(self, ev, reads, writes):
        for b in reads:
            if b in writes:
                continue
            b.readers.append(ev)
            if len(b.readers) > 10:
                best = {}
                for x in b.readers:
                    s = x[0].num
                    if s not in best or best[s][1] < x[1]:
                        best[s] = x
                b.readers = list(best.values())
        for b in writes:
            b.last_w = ev
            b.readers = []

    def op(self, e, fn, reads, writes, cost=300.0, tag=0):
        if self.sched:
            self.pending.append(("op", e, fn, list(reads), list(writes), float(cost)))
            self.ptags.append(tag)
            return
        self._emit_op(e, fn, reads, writes)

    def _emit_op(self, e, fn, reads, writes):
        if self.limit is not None and self.nins >= self.limit:
            return
        ex = [b for b in reads if b.excl and b not in writes]
        if ex:
            writes = list(writes) + ex
        self._deps(e, reads, writes)
        ins = fn(self.engs[e])
        if os.environ.get('PRINS') and self.nins in range(int(os.environ.get('PRINS','0')), int(os.environ.get('PRINS','0')) + 4):
            print('INS', self.nins, ins.concise())
        self.cnt[e] += 1
        ins.then_inc(self.sem[e], 1)
        self._record((self.sem[e], self.cnt[e], e), reads, writes)
        self.nins += 1

    def dma(self, q, out, in_, key=None, nbytes=None, **kw):
        if self.sched:
            if nbytes is None:
                shp = _ap(out).shape
                nbytes = 4
                for d_ in shp:
                    nbytes *= d_
            self.pending.append(("dma", q, (out, in_, key, kw), _bufs(in_), _bufs(out), 2000.0 + nbytes / 100.0))
            self.ptags.append(0)
            return
        self._emit_dma(q, out, in_, key, **kw)

    def _emit_dma(self, q, out, in_, key=None, **kw):
        if self.limit is not None and self.nins >= self.limit:
            return
        reads, writes = _bufs(in_), _bufs(out)
        self._deps(q, reads, writes)
        kb = key.buf if key is not None else (out.buf if not isinstance(out.buf, DBuf) else in_.buf)
        if isinstance(kb, (list, tuple)):
            kb = kb[0]
        if kb.dsem is None:
            kb.dsem = self.nc.alloc_semaphore(f"d{self._uid()}_{kb.name}")
            self.dsems.append(kb)
        ins = self.engs[q].dma_start(out=_ap(out), in_=_ap(in_), **kw)
        kb.dcount += 1
        ins.then_inc(kb.dsem, 16)
        self._record((kb.dsem, 16 * kb.dcount, "dma"), reads, writes)
        self.nins += 1

    def flush(self):
        ops = self.pending
        tags = self.ptags
        self.pending = []
        self.ptags = []
        n = len(ops)
        if n == 0:
            return
        SYNC = float(os.environ.get('KSYNC', '200'))
        PEF = float(os.environ.get('KPEF', '1.0'))
        preds = [[] for _ in range(n)]
        lastw = {}
        rdrs = {}
        for i, (kind, e, fn, reads, writes, cost) in enumerate(ops):
            wr = list(writes) + [b for b in reads if b.excl and b not in writes]
            ps = set()
            for b in reads:
                if b in lastw:
                    ps.add(lastw[b])
            for b in wr:
                if b in lastw:
                    ps.add(lastw[b])
                for r_ in rdrs.get(b, ()):
                    ps.add(r_)
            ps.discard(i)
            preds[i] = list(ps)
            for b in reads:
                if b not in wr:
                    rdrs.setdefault(b, []).append(i)
            for b in wr:
                lastw[b] = i
                rdrs[b] = []
        succs = [[] for _ in range(n)]
        for i in range(n):
            for p in preds[i]:
                succs[p].append(i)
        occ = [0.0] * n
        lat = [0.0] * n
        for i, (kind, e, fn, reads, writes, cost) in enumerate(ops):
            if kind == "dma":
                occ[i] = 60.0
                lat[i] = cost
            else:
                occ[i] = cost * (PEF if e == 'pe' else 1.0)
                lat[i] = occ[i]
        blevel = [0.0] * n
        for i in range(n - 1, -1, -1):
            m_ = 0.0
            for s_ in succs[i]:
                if blevel[s_] > m_:
                    m_ = blevel[s_]
            blevel[i] = lat[i] + m_
        import heapq
        npred = [len(p) for p in preds]
        ready_t = [0.0] * n
        eng_free = {}
        readyq = {}
        for i in range(n):
            if npred[i] == 0:
                heapq.heappush(readyq.setdefault(ops[i][1], []), (-blevel[i], i))
        order = []
        done = 0
        cur_tab = [0]
        while done < n:
            best = None
            for e, hq in readyq.items():
                if not hq:
                    continue
                tfree = eng_free.get(e, 0.0)
                cand = None
                top = heapq.nsmallest(12 if e == 'act' else 6, hq)
                for pr, i in top:
                    st_ = max(tfree, ready_t[i])
                    if e == "act" and tags[i] != 0 and tags[i] != cur_tab[0]:
                        st_ += 1300.0
                    key = (st_, pr)
                    if cand is None or key < cand[0]:
                        cand = (key, i)
                if best is None or cand[0] < best[0]:
                    best = (cand[0], cand[1], e)
            (st_, pr), i, e = best
            if e == "act" and tags[i] != 0:
                cur_tab[0] = tags[i]
            hq = readyq[e]
            hq.remove((-blevel[i], i))
            heapq.heapify(hq)
            eng_free[e] = st_ + occ[i]
            fin = st_ + lat[i]
            order.append(i)
            done += 1
            for s_ in succs[i]:
                rt = fin + (0.0 if ops[s_][1] == e else SYNC)
                if rt > ready_t[s_]:
                    ready_t[s_] = rt
                npred[s_] -= 1
                if npred[s_] == 0:
                    heapq.heappush(readyq.setdefault(ops[s_][1], []), (-blevel[s_], s_))
        if os.environ.get('KSIM'):
            print('flush n=%d simulated makespan %.1f us' % (n, max(eng_free.values()) / 1e3), {e_: round(v_ / 1e3) for e_, v_ in eng_free.items()})
        for i in order:
            kind, e, fn, reads, writes, cost = ops[i]
            if kind == "dma":
                out, in_, key, kw = fn
                self._emit_dma(e, out, in_, key, **kw)
            else:
                self._emit_op(e, fn, reads, writes)

    def barrier(self):
        self.flush()
        for e in self.engs:
            for f in self.engs:
                if f != e and self.cnt[f] > 0:
                    self._wait(e, (self.sem[f], self.cnt[f], f))
            for kb in self.dsems:
                if kb.dcount > 0:
                    self._wait(e, (kb.dsem, 16 * kb.dcount, "dma"))

    @staticmethod
    def _fsz(v):
        shp = _ap(v).shape
        n = 1
        for d_ in shp[1:]:
            n *= d_
        return n

    def _ecost(self, e, out, in_):
        n = self._fsz(out)
        if e == "pool":
            return 150.0 + 2.0 * n
        if e == "act":
            return 220.0 + 0.72 * n
        return 100.0 + (1.05 * n if (_ap(in_).dtype == F32 or _ap(out).dtype == F32) else 0.6 * n)

    def mm(self, out, lhsT, rhs, start=True, stop=True):
        n = self._fsz(out)
        c = 40.0 + n * (1.9 if _ap(lhsT).dtype == F32 else 0.45)
        self.op("pe", lambda e: e.matmul(_ap(out), lhsT=_ap(lhsT), rhs=_ap(rhs), start=start, stop=stop),
                _bufs(lhsT, rhs) + ([] if start else _bufs(out)), _bufs(out), cost=c)

    def tr(self, out, in_, ident):
        n = self._fsz(out)
        c = 40.0 + n * (1.9 if _ap(in_).dtype == F32 else 0.45)
        self.op("pe", lambda e: e.transpose(out=_ap(out), in_=_ap(in_), identity=_ap(ident)), _bufs(in_, ident), _bufs(out), cost=c)

    def act(self, out, in_, func, bias=0.0, scale=1.0, accum=None, eng="act"):
        kw = {}
        if accum is not None:
            kw["accum_out"] = _ap(accum)
        self.op("act", lambda e: e.activation(out=_ap(out), in_=_ap(in_), func=func, bias=_ap(bias), scale=_ap(scale), **kw),
                _bufs(in_, bias, scale), _bufs(out, accum), cost=self._ecost("act", out, in_),
                tag={AF.Silu: 1, AF.Ln: 2, AF.Exp: 2, AF.Sigmoid: 3, AF.Sqrt: 4}.get(func, 0))

    def ts(self, e, out, in0, s1, s2=None, op0=ALU.mult, op1=None):
        if op1 is None:
            f = lambda g: g.tensor_scalar(out=_ap(out), in0=_ap(in0), scalar1=_ap(s1), scalar2=None, op0=op0)
        else:
            f = lambda g: g.tensor_scalar(out=_ap(out), in0=_ap(in0), scalar1=_ap(s1), scalar2=_ap(s2), op0=op0, op1=op1)
        self.op(e, f, _bufs(in0, s1, s2), _bufs(out), cost=self._ecost(e, out, in0))

    def tt(self, e, out, a, b, op):
        self.op(e, lambda g: g.tensor_tensor(out=_ap(out), in0=_ap(a), in1=_ap(b), op=op), _bufs(a, b), _bufs(out), cost=self._ecost(e, out, a))

    def stt(self, e, out, in0, scalar, in1, op0, op1):
        self.op(e, lambda g: g.scalar_tensor_tensor(out=_ap(out), in0=_ap(in0), scalar=_ap(scalar), in1=_ap(in1), op0=op0, op1=op1),
                _bufs(in0, scalar, in1), _bufs(out), cost=self._ecost(e, out, in0))

    def copy(self, e, out, in_):
        if e == "act":
            self.act(out, in_, AF.Copy)
        else:
            self.op(e, lambda g: g.tensor_copy(out=_ap(out), in_=_ap(in_)), _bufs(in_), _bufs(out), cost=self._ecost(e, out, in_))

    def recip(self, out, in_):
        self.op("dve", lambda g: g.reciprocal(out=_ap(out), in_=_ap(in_)), _bufs(in_), _bufs(out), cost=100.0 + 6.3 * self._fsz(out))

    def memset(self, e, out, val):
        self.op(e, lambda g: g.memset(_ap(out), val), [], _bufs(out))

    def asel(self, out, in_, pattern, cmp, fill, base, cm):
        self.op("pool", lambda g: g.affine_select(out=_ap(out), in_=_ap(in_), pattern=pattern, compare_op=cmp, fill=fill,
                                                  base=base, channel_multiplier=cm), _bufs(in_), _bufs(out))


class DBuf(Buf):
    __slots__ = ("is_dram",)

    def __init__(self, name):
        super().__init__(name)
        self.is_dram = True


def dramv(nc, name, shape, dt, kind):
    t = nc.dram_tensor(name, list(shape), dt, kind=kind)
    return V(t.ap(), DBuf(name))


def build(stage=99, dbg=None):
    nc = bass.Bass("TRN2", target_bir_lowering=False)
    k = K(nc)
    k.init_banks()
    if dbg and 'limit' in dbg:
        k.limit = dbg['limit']
    IN = lambda name, shape, dt=F32: dramv(nc, name, shape, dt, "ExternalInput")
    x_d = IN("x", [T, D])
    ctx_d = IN("ctx", [TC, D])
    cc_d = IN("cc", [128, 8, 2])
    wada_d = IN("w_ada", [D, 6 * D])
    bada_d = IN("b_ada", [128, 48])
    nrm_d = IN("norms", [128, 4, 8])
    win_d = IN("w_in", [D, INW])
    cqkv_d = IN("conv_qkv", [128, 24, 3])
    gpar_d = IN("gpar", [128, 2, 16])
    dnn_d = IN("dn_norm", [128, 1])
    wf_d = IN("w_fourier", [512, D])
    wdn_d = IN("w_dn", [D, D])
    wout_d = IN("w_out", [D, D])
    wup_d = IN("w_up", [D, 2 * DFF])
    cffn_d = IN("conv_ffn", [128, NFF, 9])
    wdown_d = IN("w_down", [DFF, D])
    cos_d = IN("dft_cos", [T, T], BF16)
    sin_d = IN("dft_sin", [T, T], BF16)
    c128_d = IN("dft128", [128, 256], BF16)
    hmask_d = IN("hmask", [128, 7, 128], BF16)
    out_d = dramv(nc, "out", [T, D], F32, "ExternalOutput")
    dbg_out = {}
    if dbg:
        for nm, shp in dbg.items():
            if nm in ("heads", "nsteps", "limit"):
                continue
            dbg_out[nm] = dramv(nc, "dbg_" + nm, shp, F32, "ExternalOutput")
    x1_d = dramv(nc, "x1_scr", [T, D], F32, "Internal")
    oT_d = dramv(nc, "oT_scr", [H, 128, T], BF16, "Internal")
    gT_d = dramv(nc, "gT_scr", [NFF, 128, T], BF16, "Internal")
    yT_d = dramv(nc, "yT_scr", [4, 128, T], BF16, "Internal")

    identf = k.sb("identf", [128, 128], F32)
    ident = k.sb("ident", [128, 128], BF16)
    onesf = k.sb("onesf", [128, 128], F32)
    onesb = k.sb("onesb", [128, 128], BF16)
    negm = [k.sb(f"negm{d}", [128, 128], F32) for d in range(2)]
    smask = [k.sb(f"smask{d}", [128, 128], F32) for d in range(2)]
    ut = [k.sb(f"ut{d}", [128, 128], F32) for d in range(2)]
    zerof = k.sb("zerof", [128, 128], F32)
    scal_t = k.sb("scal_t", [128, 8], F32)
    k.memset("pool", zerof, 0.0)
    k.memset("pool", onesf, 1.0)
    k.copy("dve", onesb, onesf)
    k.asel(identf, zerof, [[-1, 128]], ALU.not_equal, 1.0, 0, 1)
    k.copy("dve", ident, identf)
    k.asel(negm[0], zerof, [[1, 128]], ALU.is_ge, -1e5, 0, -1)
    k.asel(smask[0], onesf, [[1, 128]], ALU.is_gt, 0.0, 0, -1)
    k.asel(ut[0], onesf, [[1, 128]], ALU.is_ge, 0.0, 0, -1)
    k.asel(negm[1], zerof, [[-1, 128]], ALU.is_ge, -1e5, 0, 1)
    k.asel(smask[1], onesf, [[-1, 128]], ALU.is_gt, 0.0, 0, 1)
    k.asel(ut[1], onesf, [[-1, 128]], ALU.is_ge, 0.0, 0, 1)

    nrm = k.sb("nrm", [128, 4, 8], F32)
    k.dma("sp", nrm, nrm_d)
    cqkv = k.sb("cqkv", [128, 24, 3], F32)
    k.dma("sp", cqkv, cqkv_d)
    gpar = k.sb("gpar", [128, 2, 16], F32)
    k.dma("sp", gpar, gpar_d)
    dnn = k.sb("dnn", [128, 1], F32)
    k.dma("sp", dnn, dnn_d)
    cffn = k.sb("cffn", [128, NFF, 9], F32)
    k.dma("sp", cffn, cffn_d)
    c128 = k.sb("c128", [128, 256], BF16)
    k.dma("sp", c128, c128_d)
    hmask = k.sb("hmask", [128, 7, 128], BF16)
    k.dma("sp", hmask, hmask_d)
    bada = k.sb("bada", [128, 48], F32)
    k.dma("sp", bada, bada_d)
    cc = k.sb("cc", [128, 8, 2], F32)
    k.dma("sp", cc, cc_d)

    mod = k.sb("mod", [128, 48, 2], F32)
    scc = k.sb("scc", [128, 8, 2], F32)
    k.act(scc, cc, AF.Silu)
    with ExitStack() as st:
        k.stack = st
        wa = [k.sb(f"wa{i}", [128, 8, 512], BF16) for i in range(3)]
        sccb = k.sb("sccb", [128, 8, 2], BF16)
        k.copy("dve", sccb, scc)
        pm = k.pv(0, 0, 96, F32, 2)
        wv = wada_d.ap.rearrange("(k p) c -> p k c", p=128)
        for blk in range(12):
            w = wa[blk % 3]
            k.dma("pool", w, V(wv[:, :, blk * 512:(blk + 1) * 512], wada_d.buf))
            for oc in range(4):
                for kk in range(8):
                    k.mm(pm[:, blk * 4 + oc, :], w[:, kk, oc * 128:(oc + 1) * 128], sccb[:, kk, :], start=(kk == 0), stop=(kk == 7))
        for j in range(2):
            k.tt("dve", mod[:, :, j], pm[:, :, j], bada, ALU.add)
        k.barrier()
    k.stack = ExitStack()
    coef = k.sb("coef", [128, 8, 8], F32)
    def modc(i, j):
        return mod[:, i * 8:(i + 1) * 8, j]
    k.stt("dve", coef[:, 0, :], modc(1, 0), 1.0, nrm[:, 0, :], ALU.add, ALU.mult)
    k.copy("dve", coef[:, 1, :], modc(0, 0))
    k.stt("dve", coef[:, 2, :], modc(1, 1), 1.0, nrm[:, 0, :], ALU.add, ALU.mult)
    k.copy("dve", coef[:, 3, :], modc(0, 1))
    k.tt("dve", coef[:, 4, :], modc(2, 0), nrm[:, 1, :], ALU.mult)
    k.stt("dve", coef[:, 5, :], modc(4, 0), 1.0, nrm[:, 2, :], ALU.add, ALU.mult)
    k.copy("dve", coef[:, 6, :], modc(3, 0))
    k.tt("dve", coef[:, 7, :], modc(5, 0), nrm[:, 3, :], ALU.mult)
    if "coef" in dbg_out:
        k.dma("sp", dbg_out["coef"], coef)
    if stage <= 0:
        return finish(nc, k, out_d)

    s_h = ExitStack()
    k.stack = s_h
    hT = k.sb("hT", [128, 8, TA], BF16)
    hbuf = [Buf(f"hT{t}") for t in range(NT)]

    def norm_tile(src_tile_v, tile_idx, ca, cb, dst, dstbufs, tm):
        nb = len(tm["sq"])
        sq = tm["sq"][tile_idx % nb]
        ss = tm["ss"][tile_idx % nb]
        xn = tm["xn"][tile_idx % nb]
        pt = tm["pt"][tile_idx % len(tm["pt"])]
        k.act(sq, src_tile_v, AF.Square, accum=ss)
        k.act(ss, ss, AF.Sqrt, bias=EPS, scale=1.0 / D)
        k.recip(ss, ss)
        k.ts("dve", xn, src_tile_v, ss[:, 0:1])
        for c in range(8):
            k.tr(pt[:, c, :], xn[:, c * 128:(c + 1) * 128], ident)
        for c in range(8):
            dv = V(dst.ap[:, c, tile_idx * 128:(tile_idx + 1) * 128], dstbufs[tile_idx] if isinstance(dstbufs, list) else dstbufs)
            if c % 2 == 0:
                k.ts("dve", dv, pt[:, c, :], coef[:, ca, c:c + 1], coef[:, cb, c:c + 1], ALU.mult, ALU.add)
            else:
                k.act(dv, pt[:, c, :], AF.Identity, bias=coef[:, cb, c:c + 1], scale=coef[:, ca, c:c + 1])

    p1 = ExitStack()
    k.stack = p1
    nt_sq = [k.sb(f"nt_sq{i}", [128, D], F32) for i in range(2)]
    nt_ss = [k.sb(f"nt_ss{i}", [128, 1], F32) for i in range(2)]
    nt_xn = [k.sb(f"nt_xn{i}", [128, D], BF16) for i in range(2)]
    xin = [k.sb(f"xin{i}", [128, D], F32) for i in range(3)]
    nt_pt = [k.pv(i, 0, 512, BF16, 128) for i in range(2)]
    tm1 = {"sq": nt_sq, "ss": nt_ss, "xn": nt_xn, "pt": nt_pt}
    for t in range(NT):
        xi = xin[t % 3]
        if t < 2:
            src = V(ctx_d.ap[t * 128:(t + 1) * 128, :], ctx_d.buf)
        else:
            src = V(x_d.ap[(t - 2) * 128:(t - 1) * 128, :], x_d.buf)
        k.dma("sp" if t % 2 == 0 else "pool", xi, src)
        norm_tile(xi, t, 2 if t < 2 else 0, 3 if t < 2 else 1, hT, hbuf, tm1)
    k.barrier()
    p1.close()
    k.stack = ExitStack()
    if "hT" in dbg_out:
        with ExitStack() as st:
            k.stack = st
            tmpf = k.sb("dbg_hT", [128, 8, 1024], F32)
            k.copy("dve", tmpf, V(hT.ap[:, :, 0:1024], None))
            k.dma("sp", dbg_out["hT"], tmpf)
            k.barrier()
        k.stack = ExitStack()
    hTv = V(hT.ap, Buf("hT_all"))
    if stage <= 1:
        return finish(nc, k, out_d)

    winv = win_d.ap.rearrange("(k p) c -> p k c", p=128)

    def bcast_t(v, n):
        return V(v.ap.unsqueeze(1).to_broadcast([128, n, 16]), v.buf)

    s_g = ExitStack()
    k.stack = s_g
    beta = k.sb("beta", [128, NT, 16], F32)
    ngc = k.sb("ngc", [128, NT, 16], F32)
    ngcb = k.sb("ngcb", [128, NT, 16], F32)
    egc = k.sb("egc", [128, NT, 16], F32)
    ekt = k.sb("ekt", [128, NT, 16], F32)
    egl = k.sb("egl", [128, NT, 16], F32)
    with ExitStack() as st:
        k.stack = st
        wbaf = k.sb("wbaf", [128, 8, 32], F32)
        wba = k.sb("wba", [128, 8, 32], BF16)
        graw = k.sb("graw", [128, NT, 32], F32)
        gg = k.sb("gg", [128, NT, 16], F32)
        gtmp = k.sb("gtmp", [128, NT, 16], F32)
        lnb = k.sb("lnb", [128, NT, 16], F32)
        negA = k.sb("negA", [128, 16], F32)
        pg = k.pv(0, 0, 512, F32, 32)
        pc0, pc1, pt0, pt1 = k.banks[1], k.banks[2], k.banks[3], k.banks[4]
        k.dma("pool", wba, V(winv[:, :, OFF_B:OFF_B + 32], win_d.buf))
        for g0 in range(0, NT, 16):
            n = min(16, NT - g0)
            for j in range(n):
                t = g0 + j
                for kk in range(8):
                    k.mm(pg[:, j, :], hTv[:, kk, t * 128:(t + 1) * 128], wba[:, kk, :], start=(kk == 0), stop=(kk == 7))
            k.copy("act", graw[:, g0:g0 + n, :], pg[:, 0:n, :])
        k.act(beta, graw[:, :, 0:16], AF.Sigmoid)
        k.act(lnb, graw[:, :, 0:16], AF.Exp, scale=-1.0)
        k.act(lnb, lnb, AF.Ln, bias=1.0)
        k.act(negA, gpar[:, 0, :], AF.Exp)
        k.ts("dve", negA, negA, -1.0)
        k.tt("dve", gg, graw[:, :, 16:32], bcast_t(gpar[:, 1, :], NT), ALU.add)
        k.act(gg, gg, AF.Exp)
        k.act(gg, gg, AF.Ln, bias=1.0)
        k.tt("dve", gg, gg, bcast_t(negA, NT), ALU.mult)
        if "g" in dbg_out:
            k.dma("sp", dbg_out["g"], gg)
            k.dma("sp", dbg_out["beta"], beta)
        pcs = [pc0, pc1]
        pts = [pt0, pt1]
        for d in range(2):
            pcv = V(pcs[d].ap[:, 0:NT * 8].rearrange("p (t c) -> p t c", c=8), pcs[d].buf)
            ptv = V(pts[d].ap[:, 0:NT * 8].rearrange("p (t c) -> p t c", c=8), pts[d].buf)
            k.mm(pcv, ut[d], gg[:, :, d * 8:(d + 1) * 8])
            k.mm(ptv, onesf, gg[:, :, d * 8:(d + 1) * 8])
            sl = slice(d * 8, (d + 1) * 8)
            k.act(ngc[:, :, sl], pcv, AF.Copy, scale=-1.0)
            k.stt("dve", ngcb[:, :, sl], pcv, -1.0, lnb[:, :, sl], ALU.mult, ALU.subtract)
            k.act(egc[:, :, sl], pcv, AF.Exp)
            k.tt("dve", gtmp[:, :, sl], ptv, ngc[:, :, sl], ALU.add)
            k.act(ekt[:, :, sl], gtmp[:, :, sl], AF.Exp)
            k.act(egl[:, :, sl], ptv, AF.Exp)
        k.barrier()
    k.stack = ExitStack()
    if stage <= 2:
        return finish(nc, k, out_d)

    blocks = [(0, TC)] + [(TC + 512 * i, 512) for i in range(8)]
    xblocks = blocks[1:]
    poff = lambda tok: 1 + tok if tok < TC else 3 + tok
    heads = list(range(H)) if dbg is None or "heads" not in dbg else dbg["heads"]
    p4 = ExitStack()
    k.stack = p4
    ring = [[k.sb(f"ring{ty}_{i}", [128, 514], BF16) for i in range(3)] for ty in range(3)]
    qT = k.sb("qT", [128, TA], BF16)
    kT = k.sb("kT", [128, TA], BF16)
    vT = k.sb("vT", [128, TA], BF16)
    zs_b = [k.sb(f"zsb{i}", [128, 512], BF16) for i in range(1)]
    osum = k.sb("osum", [128, T], F32)
    wst_b = [k.sb(f"wstb{i}", [128, 8, 128], BF16) for i in range(3)]
    dgt3 = [k.sb(f"dgt{ty}", [128, 3, 128], BF16) for ty in range(3)]
    wbz = k.sb("wbz", [128, 8, 128], BF16)
    rn = [k.sb(f"rn{i}", [128, 512], F32) for i in range(1)]
    ofin = [k.sb(f"ofin{i}", [128, 512], BF16) for i in range(2)]
    osb = [Buf(f"osum{t}") for t in range(32)]
    pbig = [k.banks[0], k.banks[1]]
    GS = 3
    rot = [[k.banks[3 * d + i] for i in range(3)] for d in range(2)]
    rcnt = [0, 0]
    def nextb(d):
        rcnt[d] += 1
        return rot[d][rcnt[d] % 3]
    def b3(bank, n, dt=F32, off=0):
        if dt == F32:
            ap = bank.ap[:, off * 128:(off + n) * 128].rearrange("p (a b) -> p a b", b=128)
        else:
            ap = bank.ap[:, off * 64:(off + n) * 64].bitcast(BF16).rearrange("p (a b) -> p a b", b=128)
        return V(ap, bank.buf)
    pvn = [k.pv(6 + d, 0, 128) for d in range(2)]
    poT = [k.pv(6 + d, 128, 256) for d in range(2)]
    pS = [k.pv(6 + d, 256, 384) for d in range(2)]
    def gtmp(name, dt, nb=1):
        return [[k.sb(f"{name}{d}_{i}", [128, GS, 128], dt) for i in range(nb)] for d in range(2)]
    g_dgc, g_E = gtmp("dgc", F32), gtmp("E", F32)
    g_eg, g_Rk0, g_Q0, g_Q0T, g_Z = (gtmp(nm, BF16) for nm in ("eg", "Rk0", "Q0", "Q0T", "Z"))
    g_E1, g_E2 = gtmp("E1", BF16), gtmp("E2", BF16)
    g_LT, g_D, g_G = gtmp("LT", BF16, 2), gtmp("D", BF16, 2), gtmp("G", BF16, 2)
    g_W, g_nwT, g_Vt, g_ktail, g_qhT, g_qkm = (gtmp(nm, BF16, 2) for nm in ("W", "nwT", "Vt", "ktail", "qhT", "qkm"))
    t_vn = [k.sb(f"vn{d}", [128, 128], BF16) for d in range(2)]
    Sf = [k.sb(f"Sf{d}", [128, 128], F32) for d in range(2)]
    Sb = [k.sb(f"Sb{d}", [128, 128], BF16) for d in range(2)]
    def bcn(v, n):
        return V(v.ap.unsqueeze(1).to_broadcast([128, n, 128]), v.buf)
    nbig = [0]
    def nextbig():
        nbig[0] += 1
        return pbig[nbig[0] % 2]
    wcnt = [0]

    def load_w(col0):
        i = wcnt[0] % 3
        wcnt[0] += 1
        k.dma("pool", wst_b[i], V(winv[:, :, col0:col0 + 128], win_d.buf))
        return wst_b[i]

    order_f = list(range(NT))
    order_b = [1, 0] + list(range(NT - 1, 1, -1))

    for h in heads:
        blkbuf = [[Buf(f"qkv{ty}_{b_}") for b_ in range(9)] for ty in range(3)]
        blk_of = lambda t: 0 if t < 2 else 1 + (t - 2) // 4
        dsts = (qT, kT, vT)
        def tv(ty, t):
            return V(dsts[ty].ap[:, t * 128:(t + 1) * 128], blkbuf[ty][blk_of(t)])
        def tlv(ty, a_, n_):
            bl = []
            for t_ in range(a_, a_ + n_):
                if blkbuf[ty][blk_of(t_)] not in bl:
                    bl.append(blkbuf[ty][blk_of(t_)])
            return V(dsts[ty].ap[:, a_ * 128:(a_ + n_) * 128].rearrange("p (a b) -> p a b", b=128), bl)
        pcnt = [0]
        def nextp():
            pcnt[0] += 1
            return k.banks[pcnt[0] % 6]
        wbs = [load_w(OFF_Q + (ty * 8 + h) * 128) for ty in range(3)]
        for ty in range(3):
            for tap in range(3):
                k.ts("pool", dgt3[ty][:, tap, :], identf, cqkv[:, ty * 8 + h, tap:tap + 1])

        def conv_block(ty, b_):
            s, n = blocks[b_]
            slot = ring[ty][b_ % 3]
            pp = nextp()
            for tap in range(3):
                k.mm(pp[:, 0:n], dgt3[ty][:, tap, :], slot[:, tap:tap + n], start=(tap == 0), stop=(tap == 2))
            dv = V(dsts[ty].ap[:, s:s + n], blkbuf[ty][b_])
            k.act(dv, pp[:, 0:n], AF.Silu)
            if ty < 2:
                pp2 = nextp()
                r = rn[0]
                sqv = ofin[(b_ + ty) % 2]
                k.tt("pool", sqv[:, 0:n], dv, dv, ALU.mult)
                k.mm(pp2[:, 0:n], onesb, sqv[:, 0:n])
                k.act(r[:, 0:n], pp2[:, 0:n], AF.Ln, bias=EPS)
                k.act(r[:, 0:n], r[:, 0:n], AF.Exp, scale=-0.5)
                k.stt("dve", dv, dv, (128.0 ** -0.5) if ty == 0 else 1.0, r[:, 0:n], ALU.mult, ALU.mult)

        for b_ in range(9):
            s, n = blocks[b_]
            for ty in range(3):
                slot = ring[ty][b_ % 3]
                prev = ring[ty][(b_ - 1) % 3]
                pp = nextp()
                for kk in range(8):
                    k.mm(pp[:, 0:n], wbs[ty][:, kk, :], hTv[:, kk, s:s + n], start=(kk == 0), stop=(kk == 7))
                k.copy("dve", slot[:, 1:1 + n], pp[:, 0:n])
                if b_ in (0, 1):
                    k.memset("pool", slot[:, 0:1], 0.0)
                else:
                    k.copy("pool", slot[:, 0:1], prev[:, 512:513])
                if b_ in (0, 8):
                    k.memset("pool", slot[:, n + 1:n + 2], 0.0)
                if b_ >= 2:
                    k.copy("pool", prev[:, 513:514], slot[:, 1:2])
                if b_ == 0:
                    conv_block(ty, 0)
                elif b_ >= 2:
                    conv_block(ty, b_ - 1)
                if b_ == 8:
                    conv_block(ty, 8)
        if "qkv" in dbg_out and h == heads[0]:
            with ExitStack() as st2:
                old = k.stack
                k.stack = st2
                tf_ = k.sb("dbgqkv", [128, 3, 512], F32)
                k.copy("dve", tf_[:, 0, :], qT[:, 0:512])
                k.copy("dve", tf_[:, 1, :], kT[:, 0:512])
                k.copy("dve", tf_[:, 2, :], vT[:, 0:512])
                k.dma("sp", dbg_out["qkv"], tf_)
                k.barrier()
                k.stack = old
        for d in range(2):
            k.memset("pool", Sf[d], 0.0)
            k.memset("pool", Sb[d], 0.0)
        visited = set()
        groups = [(0, 2)] + [(2 + 3 * i, 3) for i in range(10)] + [(32, 2)]
        if dbg is not None and "nsteps" in dbg:
            groups = groups[:dbg["nsteps"]]
        gorder = [groups, [groups[0]] + groups[:0:-1]]

        def pre_gen(d, a, n, gp):
            col = d * 8 + h
            isx = a >= 2
            def bc(arr):
                return V(arr.ap[:, a:a + n, col].unsqueeze(2).to_broadcast([128, n, 128]), arr.buf)
            tsl = lambda i: slice((a + i) * 128, (a + i + 1) * 128)
            dgc, E, eg, Rk0, Q0, Q0T, Z = (x[d][0][:, 0:n, :] for x in (g_dgc, g_E, g_eg, g_Rk0, g_Q0, g_Q0T, g_Z))
            LT, Dm, Gm = g_LT[d], g_D[d], g_G[d]
            tm_ = LT[0][:, 0:n, :]
            W, nwT, Vt, ktail, qhT, qkm = (x[d][gp][:, 0:n, :] for x in (g_W, g_nwT, g_Vt, g_ktail, g_qhT, g_qkm))
            pA = nextb(d)
            pAk, pAv = b3(pA, n, BF16, 0), b3(pA, n, BF16, GS)
            for i in range(n):
                k.tr(pAk[:, i, :], tv(1, a + i), ident)
                k.tr(pAv[:, i, :], tv(2, a + i), ident)
            yield
            k.act(Vt, pAv, AF.Copy)
            k.tt("dve", Rk0, pAk, bc(egc), ALU.mult)
            k.tt("dve", ktail, pAk, bc(ekt), ALU.mult)
            yield
            E1, E2, tm2_ = g_E1[d][0][:, 0:n, :], g_E2[d][0][:, 0:n, :], Z
            k.tt("pool", dgc, bcn(identf, n), bc(ngc), ALU.mult)
            pB = nextb(d)
            pBv = b3(pB, n)
            for i in range(n):
                k.mm(pBv[:, i, :], onesf, dgc[:, i, :])
            yield
            k.tt("dve", E, bcn(negm[d], n), pBv, ALU.subtract)
            for i in range(n):
                k.act(E2[:, i, :], E[:, i, :], AF.Exp, bias=ngcb[:, a + i, col:col + 1])
            if isx:
                for i in range(n):
                    k.act(E1[:, i, :], E[:, i, :], AF.Exp, bias=ngc[:, a + i, col:col + 1])
                k.act(eg, pBv, AF.Exp, scale=-1.0)
                k.tt("pool", qhT, tlv(0, a, n), eg, ALU.mult)
            yield
            pC = nextb(d)
            pCv = b3(pC, n)
            for i in range(n):
                k.mm(pCv[:, i, :], tv(1, a + i), tv(1, a + i))
            k.tt("dve", Q0, pCv, E2, ALU.mult)
            if isx:
                pQ = nextb(d)
                pQv = b3(pQ, n)
                for i in range(n):
                    k.mm(pQv[:, i, :], tv(1, a + i), tv(0, a + i))
                k.tt("dve", qkm, pQv, E1, ALU.mult)
            yield
            pT = nextb(d)
            pTv = b3(pT, n, BF16, 0)
            for i in range(n):
                k.tr(pTv[:, i, :], Q0[:, i, :], ident)
            k.act(Q0T, pTv, AF.Copy)
            yield
            k.tt("dve", tm_, Q0, bcn(hmask[:, 0, :], n), ALU.mult)
            k.tt("dve", Dm[0][:, 0:n, :], bcn(ident, n), tm_, ALU.subtract)
            k.tt("pool", tm2_, Q0T, bcn(hmask[:, 0, :], n), ALU.mult)
            k.tt("pool", Gm[0][:, 0:n, :], bcn(ident, n), tm2_, ALU.subtract)
            k.tt("pool", LT[1][:, 0:n, :], Q0T, bcn(hmask[:, 1, :], n), ALU.mult)
            yield
            for m_ in range(1, 7):
                a_, b_ = (m_ - 1) % 2, m_ % 2
                Da, Ga, Lm = Dm[a_][:, 0:n, :], Gm[a_][:, 0:n, :], LT[m_ % 2][:, 0:n, :]
                pz = nextb(d)
                pzv = b3(pz, n)
                for i in range(n):
                    k.mm(pzv[:, i, :], Lm[:, i, :], Da[:, i, :])
                if m_ < 6:
                    k.tt("pool", LT[(m_ + 1) % 2][:, 0:n, :], Q0T, bcn(hmask[:, m_ + 1, :], n), ALU.mult)
                k.act(Z, pzv, AF.Copy)
                yield
                pd_ = nextb(d)
                pdv = b3(pd_, n)
                for i in range(n):
                    k.mm(pdv[:, i, :], Ga[:, i, :], Z[:, i, :])
                if m_ < 6:
                    pg_ = nextb(d)
                    pgv = b3(pg_, n)
                    for i in range(n):
                        k.mm(pgv[:, i, :], Z[:, i, :], Ga[:, i, :])
                k.tt("dve", W if m_ == 6 else Dm[b_][:, 0:n, :], Da, pdv, ALU.subtract)
                if m_ < 6:
                    k.tt("dve", Gm[b_][:, 0:n, :], Ga, pgv, ALU.subtract)
                yield
            pw = nextb(d)
            pwv = b3(pw, n)
            for i in range(n):
                k.mm(pwv[:, i, :], Rk0[:, i, :], W[:, i, :])
            k.act(nwT, pwv, AF.Copy, scale=-1.0)
            yield

        def state_gen(d, a, n, gp):
            col = d * 8 + h
            isx = a >= 2
            W, nwT, Vt, ktail, qhT, qkm = (x[d][gp] for x in (g_W, g_nwT, g_Vt, g_ktail, g_qhT, g_qkm))
            vn = t_vn[d]
            for i in (range(n) if d == 0 else range(n - 1, -1, -1)):
                t = a + i
                sc = lambda arr: arr[:, t, col:col + 1]
                k.mm(pvn[d], W[:, i, :], Vt[:, i, :], start=True, stop=False)
                k.mm(pvn[d], nwT[:, i, :], Sb[d], start=False, stop=True)
                k.act(vn, pvn[d], AF.Identity, scale=sc(beta))
                yield
                if isx:
                    xt = t - 2
                    ov = V(osum.ap[:, xt * 128:(xt + 1) * 128], osb[xt])
                    k.mm(poT[d], Sb[d], qhT[:, i, :], start=True, stop=False)
                    k.mm(poT[d], vn, qkm[:, i, :], start=False, stop=True)
                k.mm(pS[d], ktail[:, i, :], vn)
                if isx:
                    if xt not in visited:
                        visited.add(xt)
                        k.act(ov, poT[d], AF.Copy)
                    else:
                        k.tt("dve", ov, poT[d], ov, ALU.add)
                k.stt("dve", Sf[d], Sf[d], sc(egl), pS[d], ALU.mult, ALU.add)
                k.act(Sb[d], Sf[d], AF.Copy)
                yield

        def run_all(gens):
            gens = list(gens)
            while gens:
                for g_ in list(gens):
                    try:
                        next(g_)
                    except StopIteration:
                        gens.remove(g_)

        ng = len(groups)
        run_all([pre_gen(0, *gorder[0][0], 0), pre_gen(1, *gorder[1][0], 0)])
        for gi in range(ng):
            gens = [state_gen(0, *gorder[0][gi], gi % 2), state_gen(1, *gorder[1][gi], gi % 2)]
            if gi + 1 < ng:
                gens += [pre_gen(0, *gorder[0][gi + 1], (gi + 1) % 2), pre_gen(1, *gorder[1][gi + 1], (gi + 1) % 2)]
            run_all(gens)
        if "S" in dbg_out and h == heads[0]:
            k.dma("sp", dbg_out["S"][0], Sf[0])
            k.dma("sp", dbg_out["S"][1], Sf[1])
        for bi in range(8):
            s = bi * 512
            ovs = [V(osum.ap[:, s:s + 512], osb[bi * 4 + j]) for j in range(4)]
            class _M:
                pass
            ovall = V(osum.ap[:, s:s + 512], osb[bi * 4])
            extra = [osb[bi * 4 + j] for j in range(1, 4)]
            if bi == 0:
                k.dma("pool", wbz, V(winv[:, :, OFF_Z + h * 128:OFF_Z + (h + 1) * 128], win_d.buf))
            pz_ = nextbig()
            zsb = zs_b[0]
            for kk in range(8):
                k.mm(pz_, wbz[:, kk, :], hTv[:, kk, TC + s:TC + s + 512], start=(kk == 0), stop=(kk == 7))
            k.act(zsb, pz_, AF.Silu)
            pp = nextbig()
            sqv = ofin[bi % 2]
            r = rn[0]
            of_ = V(osum.ap[:, s:s + 512], [osb[bi * 4 + j] for j in range(4)])
            k.op("pool", lambda g, sqv=sqv, s=s: g.tensor_tensor(out=sqv.ap, in0=osum.ap[:, s:s + 512], in1=osum.ap[:, s:s + 512], op=ALU.mult),
                 [osb[bi * 4 + j] for j in range(4)], [sqv.buf])
            k.mm(pp, onesb, sqv)
            k.act(r, pp, AF.Ln, bias=EPS, scale=1.0 / 128)
            k.act(r, r, AF.Exp, scale=-0.5)
            k.stt("dve", of_, of_, dnn[:, 0:1], r, ALU.mult, ALU.mult)
            if "o0" in dbg_out and h == heads[0]:
                k.tt("pool", of_, of_, zsb, ALU.mult)
                k.dma("sp", V(dbg_out["o0"].ap[:, s:s + 512], dbg_out["o0"].buf), of_)
                k.copy("pool", sqv, of_)
            else:
                k.tt("pool", sqv, of_, zsb, ALU.mult)
            k.dma("sp", V(oT_d.ap[h, :, s:s + 512], DBuf("st")), sqv)
    k.barrier()
    p4.close()
    s_g.close()
    k.stack = ExitStack()
    if stage <= 4:
        return finish(nc, k, out_d)

    def wchunk_loader(stf, stb):
        cnt = [0]
        def load(src_v, K):
            j = cnt[0] % len(stb)
            cnt[0] += 1
            k.dma("pool", stb[j][:, 0:K, :], src_v)
            return stb[j]
        return load

    def load_resident(dst_bf, src_ap, src_buf, K, ncols, stg):
        i = 0
        for k0 in range(0, K, 8):
            kn = min(8, K - k0)
            for c0 in range(0, ncols, 512):
                cn = min(512, ncols - c0)
                k.dma("pool", dst_bf[:, k0:k0 + kn, c0:c0 + cn], V(src_ap[:, k0:k0 + kn, c0:c0 + cn], src_buf))
                i += 1

    with ExitStack() as st:
        k.stack = st
        FCS = k.sb("FCS", [128, 32, 4, 256], BF16)
        fT = [k.sb(f"fT{i}", [128, 512], BF16) for i in range(2)]
        stf = None
        stb = [k.sb(f"p3stb{i}", [128, 8, 128], BF16) for i in range(2)]
        ctab = [k.sb(f"ctab{i}", [128, 4, 512], BF16) for i in range(2)]
        stab = [k.sb(f"stab{i}", [128, 4, 512], BF16) for i in range(2)]
        yblk = [k.sb(f"yblk{i}", [128, 4, 512], BF16) for i in range(2)]
        lw = wchunk_loader(stf, stb)
        nb_ = 0
        for g in range(4):
            wb = lw(V(winv[:, :, OFF_F + g * 128:OFF_F + (g + 1) * 128], win_d.buf), 8)
            for bi, (s, n) in enumerate(xblocks):
                pp = k.banks[nb_ % 2]
                ft = fT[nb_ % 2]
                nb_ += 1
                for kk in range(8):
                    k.mm(pp, wb[:, kk, :], hTv[:, kk, s:s + n], start=(kk == 0), stop=(kk == 7))
                k.act(ft, pp, AF.Copy)
                pf = k.banks[2 + (nb_ % 2)]
                pfv = V(pf.ap.rearrange("p (a b) -> p a b", b=256), pf.buf)
                for j2 in range(2):
                    for jj in range(2):
                        j = j2 * 2 + jj
                        k.mm(pfv[:, jj, :], ft[:, j * 128:(j + 1) * 128], c128)
                    t0 = bi * 4 + j2 * 2
                    if j2 == 0:
                        k.act(FCS[:, t0:t0 + 2, g, :], pfv, AF.Copy)
                    else:
                        k.copy("dve", FCS[:, t0:t0 + 2, g, :], pfv)
        cosv = cos_d.ap.rearrange("(tt p) f -> p tt f", p=128)
        sinv = sin_d.ap.rearrange("(tt p) f -> p tt f", p=128)
        ld = 0
        for kb in range(8):
            for t4 in range(8):
                ct, st_ = ctab[ld % 2], stab[ld % 2]
                ld += 1
                k.dma("sp", ct, V(cosv[:, t4 * 4:(t4 + 1) * 4, kb * 512:(kb + 1) * 512], cos_d.buf))
                k.dma("pool", st_, V(sinv[:, t4 * 4:(t4 + 1) * 4, kb * 512:(kb + 1) * 512], sin_d.buf))
                for ti in range(4):
                    tt_ = t4 * 4 + ti
                    for g in range(4):
                        k.mm(k.banks[4 + g], FCS[:, tt_, g, 0:128], ct[:, ti, :], start=(tt_ == 0), stop=False)
                        k.mm(k.banks[4 + g], FCS[:, tt_, g, 128:256], st_[:, ti, :], start=False, stop=(tt_ == 31))
            yb = yblk[kb % 2]
            for g in range(4):
                if g % 2 == 0:
                    k.act(yb[:, g, :], k.banks[4 + g], AF.Copy)
                else:
                    k.copy("dve", yb[:, g, :], k.banks[4 + g])
            k.dma("sp", V(yT_d.ap[:, :, kb * 512:(kb + 1) * 512].rearrange("g p f -> p g f"), DBuf("st")), yb)
        if "fm" in dbg_out:
            pass
        k.barrier()
    k.stack = ExitStack()
    if stage <= 5:
        return finish(nc, k, out_d)

    with ExitStack() as st:
        k.stack = st
        wg = k.sb("wg", [128, 8, 2048], BF16)
        wf4 = k.sb("wf4", [128, 4, 1024], BF16)
        wdn = k.sb("wdn", [128, 8, 1024], BF16)
        stg = None
        ytb = [k.sb(f"ytb{i}", [128, 4, 512], BF16) for i in range(2)]
        otb = [k.sb(f"otb{i}", [128, 8, 512], BF16) for i in range(2)]
        g0 = [k.sb(f"g0_{i}", [128, 512], BF16) for i in range(2)]
        g1 = [k.sb(f"g1_{i}", [128, 512], BF16) for i in range(2)]
        m0 = [k.sb(f"m0_{i}", [128, 512], BF16) for i in range(2)]
        m1 = [k.sb(f"m1_{i}", [128, 512], BF16) for i in range(2)]
        mixs = k.sb("mixs", [128, 2, 8, 512], BF16)
        mixs_buf = [Buf("mixs0"), Buf("mixs1")]
        load_resident(wg, winv[:, :, OFF_G:OFF_G + 2048], win_d.buf, 8, 2048, stg)
        load_resident(wf4, wf_d.ap.rearrange("(g p) d -> p g d", p=128), wf_d.buf, 4, 1024, stg)
        load_resident(wdn, wdn_d.ap.rearrange("(h p) d -> p h d", p=128), wdn_d.buf, 8, 1024, stg)
        it = 0
        hmt = [Buf(f"hTm{mt}") for mt in range(8)]
        for mt in range(8):
            s = TC + mt * 512
            yt, ot = ytb[mt % 2], otb[mt % 2]
            k.dma("sp", yt, V(yT_d.ap[:, :, mt * 512:(mt + 1) * 512].rearrange("g p f -> p g f"), yT_d.buf))
            k.dma("pool", ot, V(oT_d.ap[:, :, mt * 512:(mt + 1) * 512].rearrange("h p f -> p h f"), oT_d.buf))
            for dc in range(8):
                dsl = slice(dc * 128, (dc + 1) * 128)
                i2 = it % 2
                it += 1
                pb = [k.banks[4 * i2 + j] for j in range(4)]
                for g in range(4):
                    k.mm(pb[0], wf4[:, g, dsl], yt[:, g, :], start=(g == 0), stop=(g == 3))
                for kk in range(8):
                    k.mm(pb[1], wg[:, kk, dc * 128:(dc + 1) * 128], V(hT.ap[:, kk, s:s + 512], hmt[mt]), start=(kk == 0), stop=(kk == 7))
                for hh in range(8):
                    k.mm(pb[2], wdn[:, hh, dsl], ot[:, hh, :], start=(hh == 0), stop=(hh == 7))
                for kk in range(8):
                    k.mm(pb[3], wg[:, kk, 1024 + dc * 128:1024 + (dc + 1) * 128], V(hT.ap[:, kk, s:s + 512], hmt[mt]), start=(kk == 0), stop=(kk == 7))
                k.act(g0[i2], pb[1], AF.Sigmoid)
                k.act(g1[i2], pb[3], AF.Sigmoid)
                k.tt("dve", m0[i2], pb[0], g0[i2], ALU.mult)
                k.tt("dve", m1[i2], pb[2], g1[i2], ALU.mult)
                k.tt("pool", V(mixs.ap[:, mt % 2, dc, :], mixs_buf[mt % 2]), m0[i2], m1[i2], ALU.add)
            k.copy("act" if mt % 2 == 0 else "dve", V(hT.ap[:, :, s:s + 512], hmt[mt]), V(mixs.ap[:, mt % 2, :, :], mixs_buf[mt % 2]))
        k.barrier()
    k.stack = ExitStack()
    if stage <= 6:
        return finish(nc, k, out_d)

    def branch_tail(mt, producer, cidx, resid_d, final, tb):
        yx, sq, rst, xin_, x1t = tb["yx"][mt % 2], tb["sq"][mt % 2], tb["rst"][mt % 2], tb["xin"], tb["x1t"]
        for dc in range(8):
            pb = k.banks[dc % 2]
            producer(dc, pb)
            k.act(yx[:, dc, :], pb, AF.Copy)
            k.act(sq[:, dc, :], pb, AF.Square)
        pss = k.banks[2]
        for dc in range(8):
            k.mm(pss, onesb, sq[:, dc, :], start=(dc == 0), stop=(dc == 7))
        k.act(rst, pss, AF.Ln, bias=EPS, scale=1.0 / D)
        k.act(rst, rst, AF.Exp, scale=-0.5)
        for dc in range(8):
            k.stt("dve", yx[:, dc, :], yx[:, dc, :], coef[:, cidx, dc:dc + 1], rst, ALU.mult, ALU.mult)
        for j in range(4):
            tok0 = mt * 512 + j * 128
            xi = xin_[j % len(xin_)]
            xo = x1t[j % len(x1t)]
            k.dma("sp" if j % 2 == 0 else "pool", xi, V(resid_d.ap[tok0:tok0 + 128, :], resid_d.buf))
            ba, bb = k.banks[3 + 2 * (j % 2)], k.banks[4 + 2 * (j % 2)]
            for dc in range(8):
                bk = ba if dc < 4 else bb
                k.tr(bk[:, (dc % 4) * 128:(dc % 4 + 1) * 128], yx[:, dc, j * 128:(j + 1) * 128], identf)
            k.tt("dve", xo[:, 0:512], ba, xi[:, 0:512], ALU.add)
            k.tt("dve", xo[:, 512:1024], bb, xi[:, 512:1024], ALU.add)
            if final:
                k.dma("sp", V(out_d.ap[tok0:tok0 + 128, :], DBuf("st")), xo)
            else:
                k.dma("sp", V(x1_d.ap[tok0:tok0 + 128, :], DBuf("st")), xo)
                norm_tile(xo, 2 + mt * 4 + j, 5, 6, hT, tb["hbuf"], tb["tm"])

    def tail_bufs(nbuf, need_tm=True):
        if not need_tm:
            return {"yx": [k.sb(f"yx{i}", [128, 8, 512], F32) for i in range(2)], "sq": [k.sb(f"sqb{i}", [128, 8, 512], BF16) for i in range(2)],
                    "rst": [k.sb(f"rst{i}", [128, 512], F32) for i in range(2)],
                    "xin": [k.sb(f"rxin{i}", [128, D], F32) for i in range(nbuf)], "x1t": [k.sb(f"x1t{i}", [128, D], F32) for i in range(nbuf)], "tm": None}
        return {"yx": [k.sb(f"yx{i}", [128, 8, 512], F32) for i in range(2)], "sq": [k.sb(f"sqb{i}", [128, 8, 512], BF16) for i in range(2)],
                "rst": [k.sb(f"rst{i}", [128, 512], F32) for i in range(2)],
                "xin": [k.sb(f"rxin{i}", [128, D], F32) for i in range(nbuf)], "x1t": [k.sb(f"x1t{i}", [128, D], F32) for i in range(nbuf)],
                "tm": {"sq": [k.sb(f"t_sq{i}", [128, D], F32) for i in range(2)], "ss": [k.sb(f"t_ss{i}", [128, 1], F32) for i in range(2)],
                       "xn": [k.sb(f"t_xn{i}", [128, D], BF16) for i in range(2)], "pt": [k.pv(7, 0, 512, BF16, 128)]}}

    with ExitStack() as st:
        k.stack = st
        wout = k.sb("wout", [128, 8, 1024], BF16)
        stg = None
        load_resident(wout, wout_d.ap.rearrange("(c p) d -> p c d", p=128), wout_d.buf, 8, 1024, stg)
        tb = tail_bufs(2)
        hmt2 = [Buf(f"hTn{mt}") for mt in range(8)]
        for mt in range(8):
            s = TC + mt * 512
            def prod(dc, pb, s=s, mt=mt):
                for c in range(8):
                    k.mm(pb, wout[:, c, dc * 128:(dc + 1) * 128], V(hT.ap[:, c, s:s + 512], hmt2[mt]), start=(c == 0), stop=(c == 7))
            tb["hbuf"] = hmt2[mt]
            branch_tail(mt, prod, 4, x_d, False, tb)
        k.barrier()
    k.stack = ExitStack()
    if stage <= 7:
        return finish(nc, k, out_d)

    wupv = wup_d.ap.rearrange("(k p) c -> p k c", p=128)
    with ExitStack() as st:
        k.stack = st
        stf = None
        stb = [k.sb(f"p6stb{i}", [128, 8, 128], BF16) for i in range(4)]
        lw = wchunk_loader(stf, stb)
        apad_b = [k.sb(f"apad{i}", [128, 66, 66], BF16) for i in range(2)]
        dg9_b = [k.sb(f"dg9_{i}", [128, 9, 128], BF16) for i in range(2)]
        sa = [k.sb(f"sa{i}", [128, 512], BF16) for i in range(4)]
        gtc = [k.sb(f"gtc{i}", [128, T], BF16) for i in range(2)]
        k.memset("pool", apad_b[0], 0.0)
        k.memset("pool", apad_b[1], 0.0)
        nb_ = 0
        for c in range(NFF):
            apad, dg9 = apad_b[c % 2], dg9_b[c % 2]
            wa = lw(V(wupv[:, :, c * 128:(c + 1) * 128], wup_d.buf), 8)
            wu = lw(V(wupv[:, :, DFF + c * 128:DFF + (c + 1) * 128], wup_d.buf), 8)
            for tap in range(9):
                k.ts("pool", dg9[:, tap, :], identf, cffn[:, c, tap:tap + 1])
            for bi in range(8):
                s = TC + bi * 512
                pp = k.banks[(0, 1, 6, 7)[nb_ % 4]]
                nb_ += 1
                for kk in range(8):
                    k.mm(pp, wa[:, kk, :], hTv[:, kk, s:s + 512], start=(kk == 0), stop=(kk == 7))
                k.act(apad[:, 1 + bi * 8:1 + bi * 8 + 8, 1:65], V(pp.ap.rearrange("p (r c) -> p r c", c=64), pp.buf), AF.Copy)
            gt = gtc[c % 2]
            for bi in range(8):
                s = TC + bi * 512
                pc = k.banks[2 + (bi % 2)]
                pu = k.banks[4 + (bi % 2)]
                pcv = V(pc.ap.rearrange("p (r c) -> p r c", c=64), pc.buf)
                for tap in range(9):
                    dr, dcc = tap // 3, tap % 3
                    k.mm(pcv, dg9[:, tap, :], apad[:, bi * 8 + dr:bi * 8 + dr + 8, dcc:dcc + 64], start=(tap == 0), stop=(tap == 8))
                k.act(sa[bi % 4], pc, AF.Silu)
                for kk in range(8):
                    k.mm(pu, wu[:, kk, :], hTv[:, kk, s:s + 512], start=(kk == 0), stop=(kk == 7))
                k.tt("dve", gt[:, bi * 512:(bi + 1) * 512], pu, sa[bi % 4], ALU.mult)
            k.dma("sp" if c % 2 == 0 else "pool", V(gT_d.ap[c], DBuf("st")), gt)
        k.barrier()
    k.stack = ExitStack()
    s_h.close()
    k.stack = ExitStack()
    if stage <= 8:
        return finish(nc, k, out_d)

    with ExitStack() as st:
        k.stack = st
        wdown = k.sb("wdown", [128, NFF, 1024], BF16)
        stg = None
        load_resident(wdown, wdown_d.ap.rearrange("(c p) d -> p c d", p=128), wdown_d.buf, NFF, 1024, stg)
        gbl = [k.sb(f"gbl{i}", [128, NFF, 512], BF16) for i in range(2)]
        tb = tail_bufs(2, need_tm=False)
        for mt in range(8):
            gb = gbl[mt % 2]
            k.dma("sp", gb[:, 0:11, :], V(gT_d.ap[0:11, :, mt * 512:(mt + 1) * 512].rearrange("c p f -> p c f"), gT_d.buf))
            k.dma("pool", gb[:, 11:NFF, :], V(gT_d.ap[11:NFF, :, mt * 512:(mt + 1) * 512].rearrange("c p f -> p c f"), gT_d.buf))
            def prod(dc, pb, gb=gb):
                for c in range(NFF):
                    k.mm(pb, wdown[:, c, dc * 128:(dc + 1) * 128], gb[:, c, :], start=(c == 0), stop=(c == NFF - 1))
            branch_tail(mt, prod, 7, x1_d, True, tb)
        k.barrier()
    k.stack = ExitStack()
    return finish(nc, k, out_d)


def finish(nc, k, out_d):
    k.barrier()
    return nc


def prep_inputs(inp, b):
    f = lambda a: np.ascontiguousarray(a, dtype=np.float32)
    colmajor = lambda v: f(np.asarray(v).reshape(-1, 128).T)
    m = {}
    m["x"] = f(inp["x"][b])
    m["ctx"] = f(inp["ctx"][b])
    m["cc"] = f(np.stack([colmajor(inp["c"][b]), colmajor(inp["c_ctx"])], axis=-1))
    m["w_ada"] = f(inp["w_ada"][0])
    m["b_ada"] = colmajor(inp["b_ada"][0])
    m["norms"] = f(np.stack([colmajor(inp[n][0]) for n in ("norm_pre_mix", "norm_post_mix", "norm_pre_ffn", "norm_post_ffn")], axis=1))
    m["w_in"] = f(inp["w_in"][0])
    cq = np.asarray(inp["conv_qkv"][0])
    m["conv_qkv"] = f(cq.T.reshape(24, 128, 3).transpose(1, 0, 2))
    gp = np.stack([np.asarray(inp["a_log"][0]).reshape(16), np.asarray(inp["dt_bias"][0]).reshape(16)], 0)
    m["gpar"] = f(np.broadcast_to(gp[None], (128, 2, 16)))
    m["dn_norm"] = f(np.asarray(inp["dn_norm"][0]).reshape(128, 1))
    m["w_fourier"] = f(inp["w_fourier"][0])
    m["w_dn"] = f(inp["w_dn"][0])
    m["w_out"] = f(inp["w_out"][0])
    m["w_up"] = f(inp["w_up"][0])
    cf = np.asarray(inp["conv_ffn"][0]).reshape(9, DFF)
    m["conv_ffn"] = f(cf.T.reshape(NFF, 128, 9).transpose(1, 0, 2))
    m["w_down"] = f(inp["w_down"][0])
    return m


_CONST = {}


def consts():
    if not _CONST:
        idx = np.arange(T, dtype=np.int64)
        ang = (2.0 * np.pi / T) * ((idx[:, None] * idx[None, :]) % T).astype(np.float64)
        s = 1.0 / np.sqrt(float(T) * 128.0)
        _CONST["dft_cos"] = (np.cos(ang) * s).astype(ml_dtypes.bfloat16)
        _CONST["dft_sin"] = (np.sin(ang) * s).astype(ml_dtypes.bfloat16)
        i8 = np.arange(128, dtype=np.int64)
        a8 = (2.0 * np.pi / 128) * ((i8[:, None] * i8[None, :]) % 128).astype(np.float64)
        j8 = np.arange(128)
        hm = []
        for m_ in range(7):
            s_ = 2 ** m_
            blk2 = (j8[:, None] // (2 * s_)) == (j8[None, :] // (2 * s_))
            half = (j8[:, None] // s_) != (j8[None, :] // s_)
            hm.append((blk2 & half).astype(np.float32))
        _CONST["hmask"] = np.stack(hm, axis=1).astype(ml_dtypes.bfloat16)
        _CONST["dft128"] = np.concatenate([np.cos(a8), -np.sin(a8)], axis=1).astype(ml_dtypes.bfloat16)
    return _CONST


def kernel(**inputs):
    inp = {k_: np.asarray(v) for k_, v in inputs.items()}
    nc = build()
    cst = consts()
    in_maps = []
    for b in range(8):
        m = prep_inputs(inp, b)
        m.update(cst)
        in_maps.append(m)
    res = run_bass_kernel_spmd(nc, in_maps, core_ids=list(range(8)))
    return np.stack([np.asarray(r["out"], dtype=np.float32) for r in res.results], axis=0)
```

```python
import os
from contextlib import ExitStack
import numpy as np
import ml_dtypes
import concourse.bass as bass
import concourse.mybir as mybir
from concourse.bass_utils import run_bass_kernel_spmd

F32 = mybir.dt.float32
BF16 = mybir.dt.bfloat16
AF = mybir.ActivationFunctionType
ALU = mybir.AluOpType

D = 1024
T = 4096
TC = 256
TA = TC + T
NT = TA // 128
H = 8
OFF_F, OFF_Q, OFF_K, OFF_V, OFF_Z, OFF_B, OFF_A, OFF_G = 0, 512, 1536, 2560, 3584, 4608, 4624, 4640
INW = 6688
DFF = 2816
NFF = DFF // 128
EPS = 1e-6


class Buf:
    __slots__ = ("name", "last_w", "readers", "dsem", "dcount", "excl")

    def __init__(self, name, excl=False):
        self.name = name
        self.excl = excl
        self.last_w = None
        self.readers = []
        self.dsem = None
        self.dcount = 0


class V:
    __slots__ = ("ap", "buf")

    def __init__(self, ap, buf):
        self.ap = ap
        self.buf = buf

    def __getitem__(self, idx):
        return V(self.ap[idx], self.buf)

    def sub(self, idx, buf):
        return V(self.ap[idx], buf)


def _bufs(*vs):
    out = []
    for v in vs:
        if isinstance(v, V) and v.buf is not None:
            for b in (v.buf if isinstance(v.buf, (list, tuple)) else (v.buf,)):
                if b not in out:
                    out.append(b)
    return out


def _ap(v):
    return v.ap if isinstance(v, V) else v


class K:
    def __init__(self, nc):
        self.nc = nc
        self.engs = {"pe": nc.tensor, "act": nc.scalar, "dve": nc.vector, "pool": nc.gpsimd, "sp": nc.sync}
        self.sem = {n: nc.alloc_semaphore(f"s_{n}") for n in self.engs}
        self.cnt = {n: 0 for n in self.engs}
        self.known = {n: {} for n in self.engs}
        self.dsems = []
        self.nins = 0
        self.nwaits = 0
        self.stack = ExitStack()
        self.limit = None
        self.log = []
        self.sched = os.environ.get('KSCHED', '1') == '1'
        self.pending = []
        self.ptags = []

    def _uid(self):
        self.uid = getattr(self, 'uid', 0) + 1
        return self.uid

    def sb(self, name, shape, dt, nbuf=None):
        t = self.stack.enter_context(self.nc.sbuf_tensor(f"sb{self._uid()}_" + name, list(shape), dt))
        return V(t[:] if hasattr(t, "__getitem__") else t.ap(), Buf(name) if nbuf is None else nbuf)

    def init_banks(self):
        self.banks = []
        for i in range(8):
            t = self.nc.psum_tensor(f"ps_bank{i}", [128, 512], F32).__enter__()
            self.banks.append(V(t[:], Buf(f"bank{i}", excl=True)))

    def pv(self, bank, lo, hi, dt=F32, inner=None):
        b = self.banks[bank]
        ap = b.ap[:, lo:hi]
        if dt != F32:
            ap = ap.bitcast(dt)
        if inner is not None:
            ap = ap.rearrange("p (a b) -> p a b", b=inner)
        return V(ap, b.buf)

    def _wait(self, e, ev):
        sem, val, src = ev
        if src == "pe" and e == "pe":
            return
        kn = self.known[e]
        if kn.get(sem.num, 0) >= val:
            return
        kn[sem.num] = val
        self.engs[e].wait_ge(sem, val)
        self.nwaits += 1

    def _deps(self, e, reads, writes):
        best = {}
        def add(ev):
            s = ev[0].num
            if s not in best or best[s][1] < ev[1]:
                best[s] = ev
        for b in reads:
            if b.last_w is not None:
                add(b.last_w)
        for b in writes:
            if b.last_w is not None:
                add(b.last_w)
            for ev in b.readers:
                add(ev)
        for ev in best.values():
            self._wait(e, ev)

    def _record(self, ev, reads, writes):
        for b in reads:
            if b in writes:
                continue
            b.readers.append(ev)
            if len(b.readers) > 10:
                best = {}
                for x in b.readers:
                    s = x[0].num
                    if s not in best or best[s][1] < x[1]:
                        best[s] = x
                b.readers = list(best.values())
        for b in writes:
            b.last_w = ev
            b.readers = []

    def op(self, e, fn, reads, writes, cost=300.0, tag=0):
        if self.sched:
            self.pending.append(("op", e, fn, list(reads), list(writes), float(cost)))
            self.ptags.append(tag)
            return
        self._emit_op(e, fn, reads, writes)

    def _emit_op(self, e, fn, reads, writes):
        if self.limit is not None and self.nins >= self.limit:
            return
        ex = [b for b in reads if b.excl and b not in writes]
        if ex:
            writes = list(writes) + ex
        self._deps(e, reads, writes)
        ins = fn(self.engs[e])
        if os.environ.get('PRINS') and self.nins in range(int(os.environ.get('PRINS','0')), int(os.environ.get('PRINS','0')) + 4):
            print('INS', self.nins, ins.concise())
        self.cnt[e] += 1
        ins.then_inc(self.sem[e], 1)
        self._record((self.sem[e], self.cnt[e], e), reads, writes)
        self.nins += 1

    def dma(self, q, out, in_, key=None, nbytes=None, **kw):
        if self.sched:
            if nbytes is None:
                shp = _ap(out).shape
                nbytes = 4
                for d_ in shp:
                    nbytes *= d_
            self.pending.append(("dma", q, (out, in_, key, kw), _bufs(in_), _bufs(out), 2000.0 + nbytes / 100.0))
            self.ptags.append(0)
            return
        self._emit_dma(q, out, in_, key, **kw)

    def _emit_dma(self, q, out, in_, key=None, **kw):
        if self.limit is not None and self.nins >= self.limit:
            return
        reads, writes = _bufs(in_), _bufs(out)
        self._deps(q, reads, writes)
        kb = key.buf if key is not None else (out.buf if not isinstance(out.buf, DBuf) else in_.buf)
        if isinstance(kb, (list, tuple)):
            kb = kb[0]
        if kb.dsem is None:
            kb.dsem = self.nc.alloc_semaphore(f"d{self._uid()}_{kb.name}")
            self.dsems.append(kb)
        ins = self.engs[q].dma_start(out=_ap(out), in_=_ap(in_), **kw)
        kb.dcount += 1
        ins.then_inc(kb.dsem, 16)
        self._record((kb.dsem, 16 * kb.dcount, "dma"), reads, writes)
        self.nins += 1

    def flush(self):
        ops = self.pending
        tags = self.ptags
        self.pending = []
        self.ptags = []
        n = len(ops)
        if n == 0:
            return
        SYNC = float(os.environ.get('KSYNC', '400'))
        PEF = float(os.environ.get('KPEF', '1.0'))
        preds = [[] for _ in range(n)]
        lastw = {}
        rdrs = {}
        for i, (kind, e, fn, reads, writes, cost) in enumerate(ops):
            wr = list(writes) + [b for b in reads if b.excl and b not in writes]
            ps = set()
            for b in reads:
                if b in lastw:
                    ps.add(lastw[b])
            for b in wr:
                if b in lastw:
                    ps.add(lastw[b])
                for r_ in rdrs.get(b, ()):
                    ps.add(r_)
            ps.discard(i)
            preds[i] = list(ps)
            for b in reads:
                if b not in wr:
                    rdrs.setdefault(b, []).append(i)
            for b in wr:
                lastw[b] = i
                rdrs[b] = []
        succs = [[] for _ in range(n)]
        for i in range(n):
            for p in preds[i]:
                succs[p].append(i)
        occ = [0.0] * n
        lat = [0.0] * n
        for i, (kind, e, fn, reads, writes, cost) in enumerate(ops):
            if kind == "dma":
                occ[i] = 60.0
                lat[i] = cost
            else:
                occ[i] = cost * (PEF if e == 'pe' else 1.0)
                lat[i] = occ[i]
        blevel = [0.0] * n
        for i in range(n - 1, -1, -1):
            m_ = 0.0
            for s_ in succs[i]:
                if blevel[s_] > m_:
                    m_ = blevel[s_]
            blevel[i] = lat[i] + m_
        import heapq
        npred = [len(p) for p in preds]
        ready_t = [0.0] * n
        eng_free = {}
        readyq = {}
        for i in range(n):
            if npred[i] == 0:
                heapq.heappush(readyq.setdefault(ops[i][1], []), (-blevel[i], i))
        order = []
        done = 0
        cur_tab = [0]
        while done < n:
            best = None
            for e, hq in readyq.items():
                if not hq:
                    continue
                tfree = eng_free.get(e, 0.0)
                cand = None
                top = heapq.nsmallest(12 if e == 'act' else 6, hq)
                for pr, i in top:
                    st_ = max(tfree, ready_t[i])
                    if e == "act" and tags[i] != 0 and tags[i] != cur_tab[0]:
                        st_ += 1300.0
                    key = (st_, pr)
                    if cand is None or key < cand[0]:
                        cand = (key, i)
                if best is None or cand[0] < best[0]:
                    best = (cand[0], cand[1], e)
            (st_, pr), i, e = best
            if e == "act" and tags[i] != 0:
                cur_tab[0] = tags[i]
            hq = readyq[e]
            hq.remove((-blevel[i], i))
            heapq.heapify(hq)
            eng_free[e] = st_ + occ[i]
            fin = st_ + lat[i]
            order.append(i)
            done += 1
            for s_ in succs[i]:
                rt = fin + (0.0 if ops[s_][1] == e else SYNC)
                if rt > ready_t[s_]:
                    ready_t[s_] = rt
                npred[s_] -= 1
                if npred[s_] == 0:
                    heapq.heappush(readyq.setdefault(ops[s_][1], []), (-blevel[s_], s_))
        if os.environ.get('KSIM'):
            print('flush n=%d simulated makespan %.1f us' % (n, max(eng_free.values()) / 1e3), {e_: round(v_ / 1e3) for e_, v_ in eng_free.items()})
        for i in order:
            kind, e, fn, reads, writes, cost = ops[i]
            if kind == "dma":
                out, in_, key, kw = fn
                self._emit_dma(e, out, in_, key, **kw)
            else:
                self._emit_op(e, fn, reads, writes)

    def barrier(self):
        self.flush()
        for e in self.engs:
            for f in self.engs:
                if f != e and self.cnt[f] > 0:
                    self._wait(e, (self.sem[f], self.cnt[f], f))
            for kb in self.dsems:
                if kb.dcount > 0:
                    self._wait(e, (kb.dsem, 16 * kb.dcount, "dma"))

    @staticmethod
    def _fsz(v):
        shp = _ap(v).shape
        n = 1
        for d_ in shp[1:]:
            n *= d_
        return n

    def _ecost(self, e, out, in_):
        n = self._fsz(out)
        if e == "pool":
            return 150.0 + 2.0 * n
        if e == "act":
            return 220.0 + 0.72 * n
        return 100.0 + (1.05 * n if (_ap(in_).dtype == F32 or _ap(out).dtype == F32) else 0.6 * n)

    def mm(self, out, lhsT, rhs, start=True, stop=True):
        n = self._fsz(out)
        c = 40.0 + n * (1.9 if _ap(lhsT).dtype == F32 else 0.45)
        self.op("pe", lambda e: e.matmul(_ap(out), lhsT=_ap(lhsT), rhs=_ap(rhs), start=start, stop=stop),
                _bufs(lhsT, rhs) + ([] if start else _bufs(out)), _bufs(out), cost=c)

    def tr(self, out, in_, ident):
        n = self._fsz(out)
        c = 40.0 + n * (1.9 if _ap(in_).dtype == F32 else 0.45)
        self.op("pe", lambda e: e.transpose(out=_ap(out), in_=_ap(in_), identity=_ap(ident)), _bufs(in_, ident), _bufs(out), cost=c)

    def act(self, out, in_, func, bias=0.0, scale=1.0, accum=None, eng="act"):
        kw = {}
        if accum is not None:
            kw["accum_out"] = _ap(accum)
        self.op("act", lambda e: e.activation(out=_ap(out), in_=_ap(in_), func=func, bias=_ap(bias), scale=_ap(scale), **kw),
                _bufs(in_, bias, scale), _bufs(out, accum), cost=self._ecost("act", out, in_),
                tag={AF.Silu: 1, AF.Ln: 2, AF.Exp: 2, AF.Sigmoid: 3, AF.Sqrt: 4}.get(func, 0))

    def ts(self, e, out, in0, s1, s2=None, op0=ALU.mult, op1=None):
        if op1 is None:
            f = lambda g: g.tensor_scalar(out=_ap(out), in0=_ap(in0), scalar1=_ap(s1), scalar2=None, op0=op0)
        else:
            f = lambda g: g.tensor_scalar(out=_ap(out), in0=_ap(in0), scalar1=_ap(s1), scalar2=_ap(s2), op0=op0, op1=op1)
        self.op(e, f, _bufs(in0, s1, s2), _bufs(out), cost=self._ecost(e, out, in0))

    def tt(self, e, out, a, b, op):
        self.op(e, lambda g: g.tensor_tensor(out=_ap(out), in0=_ap(a), in1=_ap(b), op=op), _bufs(a, b), _bufs(out), cost=self._ecost(e, out, a))

    def stt(self, e, out, in0, scalar, in1, op0, op1):
        self.op(e, lambda g: g.scalar_tensor_tensor(out=_ap(out), in0=_ap(in0), scalar=_ap(scalar), in1=_ap(in1), op0=op0, op1=op1),
                _bufs(in0, scalar, in1), _bufs(out), cost=self._ecost(e, out, in0))

    def copy(self, e, out, in_):
        if e == "act":
            self.act(out, in_, AF.Copy)
        else:
            self.op(e, lambda g: g.tensor_copy(out=_ap(out), in_=_ap(in_)), _bufs(in_), _bufs(out), cost=self._ecost(e, out, in_))

    def recip(self, out, in_):
        self.op("dve", lambda g: g.reciprocal(out=_ap(out), in_=_ap(in_)), _bufs(in_), _bufs(out), cost=100.0 + 6.3 * self._fsz(out))

    def memset(self, e, out, val):
        self.op(e, lambda g: g.memset(_ap(out), val), [], _bufs(out))

    def asel(self, out, in_, pattern, cmp, fill, base, cm):
        self.op("pool", lambda g: g.affine_select(out=_ap(out), in_=_ap(in_), pattern=pattern, compare_op=cmp, fill=fill,
                                                  base=base, channel_multiplier=cm), _bufs(in_), _bufs(out))


class DBuf(Buf):
    __slots__ = ("is_dram",)

    def __init__(self, name):
        super().__init__(name)
        self.is_dram = True


def dramv(nc, name, shape, dt, kind):
    t = nc.dram_tensor(name, list(shape), dt, kind=kind)
    return V(t.ap(), DBuf(name))


def build(stage=99, dbg=None):
    nc = bass.Bass("TRN2", target_bir_lowering=False)
    k = K(nc)
    k.init_banks()
    if dbg and 'limit' in dbg:
        k.limit = dbg['limit']
    IN = lambda name, shape, dt=F32: dramv(nc, name, shape, dt, "ExternalInput")
    x_d = IN("x", [T, D])
    ctx_d = IN("ctx", [TC, D])
    cc_d = IN("cc", [128, 8, 2])
    wada_d = IN("w_ada", [D, 6 * D])
    bada_d = IN("b_ada", [128, 48])
    nrm_d = IN("norms", [128, 4, 8])
    win_d = IN("w_in", [D, INW])
    cqkv_d = IN("conv_qkv", [128, 24, 3])
    gpar_d = IN("gpar", [128, 2, 16])
    dnn_d = IN("dn_norm", [128, 1])
    wf_d = IN("w_fourier", [512, D])
    wdn_d = IN("w_dn", [D, D])
    wout_d = IN("w_out", [D, D])
    wup_d = IN("w_up", [D, 2 * DFF])
    cffn_d = IN("conv_ffn", [128, NFF, 9])
    wdown_d = IN("w_down", [DFF, D])
    cos_d = IN("dft_cos", [T, T], BF16)
    sin_d = IN("dft_sin", [T, T], BF16)
    c128_d = IN("dft128", [128, 256], BF16)
    hmask_d = IN("hmask", [128, 7, 128], BF16)
    out_d = dramv(nc, "out", [T, D], F32, "ExternalOutput")
    dbg_out = {}
    if dbg:
        for nm, shp in dbg.items():
            if nm in ("heads", "nsteps", "limit"):
                continue
            dbg_out[nm] = dramv(nc, "dbg_" + nm, shp, F32, "ExternalOutput")
    x1_d = dramv(nc, "x1_scr", [T, D], F32, "Internal")
    oT_d = dramv(nc, "oT_scr", [H, 128, T], BF16, "Internal")
    gT_d = dramv(nc, "gT_scr", [NFF, 128, T], BF16, "Internal")
    yT_d = dramv(nc, "yT_scr", [4, 128, T], BF16, "Internal")

    identf = k.sb("identf", [128, 128], F32)
    ident = k.sb("ident", [128, 128], BF16)
    onesf = k.sb("onesf", [128, 128], F32)
    onesb = k.sb("onesb", [128, 128], BF16)
    negm = [k.sb(f"negm{d}", [128, 128], F32) for d in range(2)]
    smask = [k.sb(f"smask{d}", [128, 128], F32) for d in range(2)]
    ut = [k.sb(f"ut{d}", [128, 128], F32) for d in range(2)]
    zerof = k.sb("zerof", [128, 128], F32)
    scal_t = k.sb("scal_t", [128, 8], F32)
    k.memset("pool", zerof, 0.0)
    k.memset("pool", onesf, 1.0)
    k.copy("dve", onesb, onesf)
    k.asel(identf, zerof, [[-1, 128]], ALU.not_equal, 1.0, 0, 1)
    k.copy("dve", ident, identf)
    k.asel(negm[0], zerof, [[1, 128]], ALU.is_ge, -1e5, 0, -1)
    k.asel(smask[0], onesf, [[1, 128]], ALU.is_gt, 0.0, 0, -1)
    k.asel(ut[0], onesf, [[1, 128]], ALU.is_ge, 0.0, 0, -1)
    k.asel(negm[1], zerof, [[-1, 128]], ALU.is_ge, -1e5, 0, 1)
    k.asel(smask[1], onesf, [[-1, 128]], ALU.is_gt, 0.0, 0, 1)
    k.asel(ut[1], onesf, [[-1, 128]], ALU.is_ge, 0.0, 0, 1)

    nrm = k.sb("nrm", [128, 4, 8], F32)
    k.dma("sp", nrm, nrm_d)
    cqkv = k.sb("cqkv", [128, 24, 3], F32)
    k.dma("sp", cqkv, cqkv_d)
    gpar = k.sb("gpar", [128, 2, 16], F32)
    k.dma("sp", gpar, gpar_d)
    dnn = k.sb("dnn", [128, 1], F32)
    k.dma("sp", dnn, dnn_d)
    cffn = k.sb("cffn", [128, NFF, 9], F32)
    k.dma("sp", cffn, cffn_d)
    c128 = k.sb("c128", [128, 256], BF16)
    k.dma("sp", c128, c128_d)
    hmask = k.sb("hmask", [128, 7, 128], BF16)
    k.dma("sp", hmask, hmask_d)
    bada = k.sb("bada", [128, 48], F32)
    k.dma("sp", bada, bada_d)
    cc = k.sb("cc", [128, 8, 2], F32)
    k.dma("sp", cc, cc_d)

    mod = k.sb("mod", [128, 48, 2], F32)
    scc = k.sb("scc", [128, 8, 2], F32)
    k.act(scc, cc, AF.Silu)
    with ExitStack() as st:
        k.stack = st
        wa = [k.sb(f"wa{i}", [128, 8, 512], BF16) for i in range(3)]
        sccb = k.sb("sccb", [128, 8, 2], BF16)
        k.copy("dve", sccb, scc)
        pm = k.pv(0, 0, 96, F32, 2)
        wv = wada_d.ap.rearrange("(k p) c -> p k c", p=128)
        for blk in range(12):
            w = wa[blk % 3]
            k.dma("pool", w, V(wv[:, :, blk * 512:(blk + 1) * 512], wada_d.buf))
            for oc in range(4):
                for kk in range(8):
                    k.mm(pm[:, blk * 4 + oc, :], w[:, kk, oc * 128:(oc + 1) * 128], sccb[:, kk, :], start=(kk == 0), stop=(kk == 7))
        for j in range(2):
            k.tt("dve", mod[:, :, j], pm[:, :, j], bada, ALU.add)
        k.barrier()
    k.stack = ExitStack()
    coef = k.sb("coef", [128, 8, 8], F32)
    def modc(i, j):
        return mod[:, i * 8:(i + 1) * 8, j]
    k.stt("dve", coef[:, 0, :], modc(1, 0), 1.0, nrm[:, 0, :], ALU.add, ALU.mult)
    k.copy("dve", coef[:, 1, :], modc(0, 0))
    k.stt("dve", coef[:, 2, :], modc(1, 1), 1.0, nrm[:, 0, :], ALU.add, ALU.mult)
    k.copy("dve", coef[:, 3, :], modc(0, 1))
    k.tt("dve", coef[:, 4, :], modc(2, 0), nrm[:, 1, :], ALU.mult)
    k.stt("dve", coef[:, 5, :], modc(4, 0), 1.0, nrm[:, 2, :], ALU.add, ALU.mult)
    k.copy("dve", coef[:, 6, :], modc(3, 0))
    k.tt("dve", coef[:, 7, :], modc(5, 0), nrm[:, 3, :], ALU.mult)
    if "coef" in dbg_out:
        k.dma("sp", dbg_out["coef"], coef)
    if stage <= 0:
        return finish(nc, k, out_d)

    s_h = ExitStack()
    k.stack = s_h
    hT = k.sb("hT", [128, 8, TA], BF16)
    hbuf = [Buf(f"hT{t}") for t in range(NT)]

    def norm_tile(src_tile_v, tile_idx, ca, cb, dst, dstbufs, tm):
        nb = len(tm["sq"])
        sq = tm["sq"][tile_idx % nb]
        ss = tm["ss"][tile_idx % nb]
        xn = tm["xn"][tile_idx % nb]
        pt = tm["pt"][tile_idx % len(tm["pt"])]
        k.act(sq, src_tile_v, AF.Square, accum=ss)
        k.act(ss, ss, AF.Sqrt, bias=EPS, scale=1.0 / D)
        k.recip(ss, ss)
        k.ts("dve", xn, src_tile_v, ss[:, 0:1])
        for c in range(8):
            k.tr(pt[:, c, :], xn[:, c * 128:(c + 1) * 128], ident)
        for c in range(8):
            dv = V(dst.ap[:, c, tile_idx * 128:(tile_idx + 1) * 128], dstbufs[tile_idx] if isinstance(dstbufs, list) else dstbufs)
            if c % 2 == 0:
                k.ts("dve", dv, pt[:, c, :], coef[:, ca, c:c + 1], coef[:, cb, c:c + 1], ALU.mult, ALU.add)
            else:
                k.act(dv, pt[:, c, :], AF.Identity, bias=coef[:, cb, c:c + 1], scale=coef[:, ca, c:c + 1])

    p1 = ExitStack()
    k.stack = p1
    nt_sq = [k.sb(f"nt_sq{i}", [128, D], F32) for i in range(2)]
    nt_ss = [k.sb(f"nt_ss{i}", [128, 1], F32) for i in range(2)]
    nt_xn = [k.sb(f"nt_xn{i}", [128, D], BF16) for i in range(2)]
    xin = [k.sb(f"xin{i}", [128, D], F32) for i in range(3)]
    nt_pt = [k.pv(i, 0, 512, BF16, 128) for i in range(2)]
    tm1 = {"sq": nt_sq, "ss": nt_ss, "xn": nt_xn, "pt": nt_pt}
    for t in range(NT):
        xi = xin[t % 3]
        if t < 2:
            src = V(ctx_d.ap[t * 128:(t + 1) * 128, :], ctx_d.buf)
        else:
            src = V(x_d.ap[(t - 2) * 128:(t - 1) * 128, :], x_d.buf)
        k.dma("sp" if t % 2 == 0 else "pool", xi, src)
        norm_tile(xi, t, 2 if t < 2 else 0, 3 if t < 2 else 1, hT, hbuf, tm1)
    k.barrier()
    p1.close()
    k.stack = ExitStack()
    if "hT" in dbg_out:
        with ExitStack() as st:
            k.stack = st
            tmpf = k.sb("dbg_hT", [128, 8, 1024], F32)
            k.copy("dve", tmpf, V(hT.ap[:, :, 0:1024], None))
            k.dma("sp", dbg_out["hT"], tmpf)
            k.barrier()
        k.stack = ExitStack()
    hTv = V(hT.ap, Buf("hT_all"))
    if stage <= 1:
        return finish(nc, k, out_d)

    winv = win_d.ap.rearrange("(k p) c -> p k c", p=128)

    def bcast_t(v, n):
        return V(v.ap.unsqueeze(1).to_broadcast([128, n, 16]), v.buf)

    s_g = ExitStack()
    k.stack = s_g
    beta = k.sb("beta", [128, NT, 16], F32)
    ngc = k.sb("ngc", [128, NT, 16], F32)
    ngcb = k.sb("ngcb", [128, NT, 16], F32)
    egc = k.sb("egc", [128, NT, 16], F32)
    ekt = k.sb("ekt", [128, NT, 16], F32)
    egl = k.sb("egl", [128, NT, 16], F32)
    with ExitStack() as st:
        k.stack = st
        wbaf = k.sb("wbaf", [128, 8, 32], F32)
        wba = k.sb("wba", [128, 8, 32], BF16)
        graw = k.sb("graw", [128, NT, 32], F32)
        gg = k.sb("gg", [128, NT, 16], F32)
        gtmp = k.sb("gtmp", [128, NT, 16], F32)
        lnb = k.sb("lnb", [128, NT, 16], F32)
        negA = k.sb("negA", [128, 16], F32)
        pg = k.pv(0, 0, 512, F32, 32)
        pc0, pc1, pt0, pt1 = k.banks[1], k.banks[2], k.banks[3], k.banks[4]
        k.dma("pool", wba, V(winv[:, :, OFF_B:OFF_B + 32], win_d.buf))
        for g0 in range(0, NT, 16):
            n = min(16, NT - g0)
            for j in range(n):
                t = g0 + j
                for kk in range(8):
                    k.mm(pg[:, j, :], hTv[:, kk, t * 128:(t + 1) * 128], wba[:, kk, :], start=(kk == 0), stop=(kk == 7))
            k.copy("act", graw[:, g0:g0 + n, :], pg[:, 0:n, :])
        k.act(beta, graw[:, :, 0:16], AF.Sigmoid)
        k.act(lnb, graw[:, :, 0:16], AF.Exp, scale=-1.0)
        k.act(lnb, lnb, AF.Ln, bias=1.0)
        k.act(negA, gpar[:, 0, :], AF.Exp)
        k.ts("dve", negA, negA, -1.0)
        k.tt("dve", gg, graw[:, :, 16:32], bcast_t(gpar[:, 1, :], NT), ALU.add)
        k.act(gg, gg, AF.Exp)
        k.act(gg, gg, AF.Ln, bias=1.0)
        k.tt("dve", gg, gg, bcast_t(negA, NT), ALU.mult)
        if "g" in dbg_out:
            k.dma("sp", dbg_out["g"], gg)
            k.dma("sp", dbg_out["beta"], beta)
        pcs = [pc0, pc1]
        pts = [pt0, pt1]
        for d in range(2):
            pcv = V(pcs[d].ap[:, 0:NT * 8].rearrange("p (t c) -> p t c", c=8), pcs[d].buf)
            ptv = V(pts[d].ap[:, 0:NT * 8].rearrange("p (t c) -> p t c", c=8), pts[d].buf)
            k.mm(pcv, ut[d], gg[:, :, d * 8:(d + 1) * 8])
            k.mm(ptv, onesf, gg[:, :, d * 8:(d + 1) * 8])
            sl = slice(d * 8, (d + 1) * 8)
            k.act(ngc[:, :, sl], pcv, AF.Copy, scale=-1.0)
            k.stt("dve", ngcb[:, :, sl], pcv, -1.0, lnb[:, :, sl], ALU.mult, ALU.subtract)
            k.act(egc[:, :, sl], pcv, AF.Exp)
            k.tt("dve", gtmp[:, :, sl], ptv, ngc[:, :, sl], ALU.add)
            k.act(ekt[:, :, sl], gtmp[:, :, sl], AF.Exp)
            k.act(egl[:, :, sl], ptv, AF.Exp)
        k.barrier()
    k.stack = ExitStack()
    if stage <= 2:
        return finish(nc, k, out_d)

    blocks = [(0, TC)] + [(TC + 512 * i, 512) for i in range(8)]
    xblocks = blocks[1:]
    poff = lambda tok: 1 + tok if tok < TC else 3 + tok
    heads = list(range(H)) if dbg is None or "heads" not in dbg else dbg["heads"]
    p4 = ExitStack()
    k.stack = p4
    ring = [[k.sb(f"ring{ty}_{i}", [128, 514], BF16) for i in range(3)] for ty in range(3)]
    qT = k.sb("qT", [128, TA], BF16)
    kT = k.sb("kT", [128, TA], BF16)
    vT = k.sb("vT", [128, TA], BF16)
    zs_b = [k.sb(f"zsb{i}", [128, 512], BF16) for i in range(1)]
    osum = k.sb("osum", [128, T], F32)
    wst_b = [k.sb(f"wstb{i}", [128, 8, 128], BF16) for i in range(3)]
    dgt3 = [k.sb(f"dgt{ty}", [128, 3, 128], BF16) for ty in range(3)]
    wbz = k.sb("wbz", [128, 8, 128], BF16)
    rn = [k.sb(f"rn{i}", [128, 512], F32) for i in range(1)]
    ofin = [k.sb(f"ofin{i}", [128, 512], BF16) for i in range(2)]
    osb = [Buf(f"osum{t}") for t in range(32)]
    pbig = [k.banks[0], k.banks[1]]
    GS = 3
    rot = [[k.banks[3 * d + i] for i in range(3)] for d in range(2)]
    rcnt = [0, 0]
    def nextb(d):
        rcnt[d] += 1
        return rot[d][rcnt[d] % 3]
    def b3(bank, n, dt=F32, off=0):
        if dt == F32:
            ap = bank.ap[:, off * 128:(off + n) * 128].rearrange("p (a b) -> p a b", b=128)
        else:
            ap = bank.ap[:, off * 64:(off + n) * 64].bitcast(BF16).rearrange("p (a b) -> p a b", b=128)
        return V(ap, bank.buf)
    pvn = [k.pv(6 + d, 0, 128) for d in range(2)]
    poT = [k.pv(6 + d, 128, 256) for d in range(2)]
    pS = [k.pv(6 + d, 256, 384) for d in range(2)]
    def gtmp(name, dt, nb=1):
        return [[k.sb(f"{name}{d}_{i}", [128, GS, 128], dt) for i in range(nb)] for d in range(2)]
    g_dgc, g_E = gtmp("dgc", F32), gtmp("E", F32)
    g_eg, g_Rk0, g_Q0, g_Q0T, g_Z = (gtmp(nm, BF16) for nm in ("eg", "Rk0", "Q0", "Q0T", "Z"))
    g_E1, g_E2 = gtmp("E1", BF16), gtmp("E2", BF16)
    g_LT, g_D, g_G = gtmp("LT", BF16, 2), gtmp("D", BF16, 2), gtmp("G", BF16, 2)
    g_W, g_nwT, g_Vt, g_ktail, g_qhT, g_qkm = (gtmp(nm, BF16, 2) for nm in ("W", "nwT", "Vt", "ktail", "qhT", "qkm"))
    t_vn = [k.sb(f"vn{d}", [128, 128], BF16) for d in range(2)]
    Sf = [k.sb(f"Sf{d}", [128, 128], F32) for d in range(2)]
    Sb = [k.sb(f"Sb{d}", [128, 128], BF16) for d in range(2)]
    def bcn(v, n):
        return V(v.ap.unsqueeze(1).to_broadcast([128, n, 128]), v.buf)
    nbig = [0]
    def nextbig():
        nbig[0] += 1
        return pbig[nbig[0] % 2]
    wcnt = [0]

    def load_w(col0):
        i = wcnt[0] % 3
        wcnt[0] += 1
        k.dma("pool", wst_b[i], V(winv[:, :, col0:col0 + 128], win_d.buf))
        return wst_b[i]

    order_f = list(range(NT))
    order_b = [1, 0] + list(range(NT - 1, 1, -1))

    for h in heads:
        blkbuf = [[Buf(f"qkv{ty}_{b_}") for b_ in range(9)] for ty in range(3)]
        blk_of = lambda t: 0 if t < 2 else 1 + (t - 2) // 4
        dsts = (qT, kT, vT)
        def tv(ty, t):
            return V(dsts[ty].ap[:, t * 128:(t + 1) * 128], blkbuf[ty][blk_of(t)])
        def tlv(ty, a_, n_):
            bl = []
            for t_ in range(a_, a_ + n_):
                if blkbuf[ty][blk_of(t_)] not in bl:
                    bl.append(blkbuf[ty][blk_of(t_)])
            return V(dsts[ty].ap[:, a_ * 128:(a_ + n_) * 128].rearrange("p (a b) -> p a b", b=128), bl)
        pcnt = [0]
        def nextp():
            pcnt[0] += 1
            return k.banks[pcnt[0] % 6]
        wbs = [load_w(OFF_Q + (ty * 8 + h) * 128) for ty in range(3)]
        for ty in range(3):
            for tap in range(3):
                k.ts("pool", dgt3[ty][:, tap, :], identf, cqkv[:, ty * 8 + h, tap:tap + 1])

        def conv_block(ty, b_):
            s, n = blocks[b_]
            slot = ring[ty][b_ % 3]
            pp = nextp()
            for tap in range(3):
                k.mm(pp[:, 0:n], dgt3[ty][:, tap, :], slot[:, tap:tap + n], start=(tap == 0), stop=(tap == 2))
            dv = V(dsts[ty].ap[:, s:s + n], blkbuf[ty][b_])
            k.act(dv, pp[:, 0:n], AF.Silu)
            if ty < 2:
                pp2 = nextp()
                r = rn[0]
                sqv = ofin[(b_ + ty) % 2]
                k.tt("pool", sqv[:, 0:n], dv, dv, ALU.mult)
                k.mm(pp2[:, 0:n], onesb, sqv[:, 0:n])
                k.act(r[:, 0:n], pp2[:, 0:n], AF.Ln, bias=EPS)
                k.act(r[:, 0:n], r[:, 0:n], AF.Exp, scale=-0.5)
                k.stt("dve", dv, dv, (128.0 ** -0.5) if ty == 0 else 1.0, r[:, 0:n], ALU.mult, ALU.mult)

        for b_ in range(9):
            s, n = blocks[b_]
            for ty in range(3):
                slot = ring[ty][b_ % 3]
                prev = ring[ty][(b_ - 1) % 3]
                pp = nextp()
                for kk in range(8):
                    k.mm(pp[:, 0:n], wbs[ty][:, kk, :], hTv[:, kk, s:s + n], start=(kk == 0), stop=(kk == 7))
                k.copy("dve", slot[:, 1:1 + n], pp[:, 0:n])
                if b_ in (0, 1):
                    k.memset("pool", slot[:, 0:1], 0.0)
                else:
                    k.copy("pool", slot[:, 0:1], prev[:, 512:513])
                if b_ in (0, 8):
                    k.memset("pool", slot[:, n + 1:n + 2], 0.0)
                if b_ >= 2:
                    k.copy("pool", prev[:, 513:514], slot[:, 1:2])
                if b_ == 0:
                    conv_block(ty, 0)
                elif b_ >= 2:
                    conv_block(ty, b_ - 1)
                if b_ == 8:
                    conv_block(ty, 8)
        if "qkv" in dbg_out and h == heads[0]:
            with ExitStack() as st2:
                old = k.stack
                k.stack = st2
                tf_ = k.sb("dbgqkv", [128, 3, 512], F32)
                k.copy("dve", tf_[:, 0, :], qT[:, 0:512])
                k.copy("dve", tf_[:, 1, :], kT[:, 0:512])
                k.copy("dve", tf_[:, 2, :], vT[:, 0:512])
                k.dma("sp", dbg_out["qkv"], tf_)
                k.barrier()
                k.stack = old
        for d in range(2):
            k.memset("pool", Sf[d], 0.0)
            k.memset("pool", Sb[d], 0.0)
        visited = set()
        groups = [(0, 2)] + [(2 + 3 * i, 3) for i in range(10)] + [(32, 2)]
        if dbg is not None and "nsteps" in dbg:
            groups = groups[:dbg["nsteps"]]
        gorder = [groups, [groups[0]] + groups[:0:-1]]

        def pre_gen(d, a, n, gp):
            col = d * 8 + h
            isx = a >= 2
            def bc(arr):
                return V(arr.ap[:, a:a + n, col].unsqueeze(2).to_broadcast([128, n, 128]), arr.buf)
            tsl = lambda i: slice((a + i) * 128, (a + i + 1) * 128)
            dgc, E, eg, Rk0, Q0, Q0T, Z = (x[d][0][:, 0:n, :] for x in (g_dgc, g_E, g_eg, g_Rk0, g_Q0, g_Q0T, g_Z))
            LT, Dm, Gm = g_LT[d], g_D[d], g_G[d]
            tm_ = LT[0][:, 0:n, :]
            W, nwT, Vt, ktail, qhT, qkm = (x[d][gp][:, 0:n, :] for x in (g_W, g_nwT, g_Vt, g_ktail, g_qhT, g_qkm))
            pA = nextb(d)
            pAk, pAv = b3(pA, n, BF16, 0), b3(pA, n, BF16, GS)
            for i in range(n):
                k.tr(pAk[:, i, :], tv(1, a + i), ident)
                k.tr(pAv[:, i, :], tv(2, a + i), ident)
            yield
            k.act(Vt, pAv, AF.Copy)
            k.tt("dve", Rk0, pAk, bc(egc), ALU.mult)
            k.tt("dve", ktail, pAk, bc(ekt), ALU.mult)
            yield
            E1, E2, tm2_ = g_E1[d][0][:, 0:n, :], g_E2[d][0][:, 0:n, :], Z
            k.tt("pool", dgc, bcn(identf, n), bc(ngc), ALU.mult)
            pB = nextb(d)
            pBv = b3(pB, n)
            for i in range(n):
                k.mm(pBv[:, i, :], onesf, dgc[:, i, :])
            yield
            k.tt("dve", E, bcn(negm[d], n), pBv, ALU.subtract)
            for i in range(n):
                k.act(E2[:, i, :], E[:, i, :], AF.Exp, bias=ngcb[:, a + i, col:col + 1])
            if isx:
                for i in range(n):
                    k.act(E1[:, i, :], E[:, i, :], AF.Exp, bias=ngc[:, a + i, col:col + 1])
                k.act(eg, pBv, AF.Exp, scale=-1.0)
                k.tt("pool", qhT, tlv(0, a, n), eg, ALU.mult)
            yield
            pC = nextb(d)
            pCv = b3(pC, n)
            for i in range(n):
                k.mm(pCv[:, i, :], tv(1, a + i), tv(1, a + i))
            k.tt("dve", Q0, pCv, E2, ALU.mult)
            if isx:
                pQ = nextb(d)
                pQv = b3(pQ, n)
                for i in range(n):
                    k.mm(pQv[:, i, :], tv(1, a + i), tv(0, a + i))
                k.tt("dve", qkm, pQv, E1, ALU.mult)
            yield
            pT = nextb(d)
            pTv = b3(pT, n, BF16, 0)
            for i in range(n):
                k.tr(pTv[:, i, :], Q0[:, i, :], ident)
            k.act(Q0T, pTv, AF.Copy)
            yield
            k.tt("dve", tm_, Q0, bcn(hmask[:, 0, :], n), ALU.mult)
            k.tt("dve", Dm[0][:, 0:n, :], bcn(ident, n), tm_, ALU.subtract)
            k.tt("pool", tm2_, Q0T, bcn(hmask[:, 0, :], n), ALU.mult)
            k.tt("pool", Gm[0][:, 0:n, :], bcn(ident, n), tm2_, ALU.subtract)
            k.tt("pool", LT[1][:, 0:n, :], Q0T, bcn(hmask[:, 1, :], n), ALU.mult)
            yield
            for m_ in range(1, 7):
                a_, b_ = (m_ - 1) % 2, m_ % 2
                Da, Ga, Lm = Dm[a_][:, 0:n, :], Gm[a_][:, 0:n, :], LT[m_ % 2][:, 0:n, :]
                pz = nextb(d)
                pzv = b3(pz, n)
                for i in range(n):
                    k.mm(pzv[:, i, :], Lm[:, i, :], Da[:, i, :])
                if m_ < 6:
                    k.tt("pool", LT[(m_ + 1) % 2][:, 0:n, :], Q0T, bcn(hmask[:, m_ + 1, :], n), ALU.mult)
                k.act(Z, pzv, AF.Copy)
                yield
                pd_ = nextb(d)
                pdv = b3(pd_, n)
                for i in range(n):
                    k.mm(pdv[:, i, :], Ga[:, i, :], Z[:, i, :])
                if m_ < 6:
                    pg_ = nextb(d)
                    pgv = b3(pg_, n)
                    for i in range(n):
                        k.mm(pgv[:, i, :], Z[:, i, :], Ga[:, i, :])
                k.tt("dve", W if m_ == 6 else Dm[b_][:, 0:n, :], Da, pdv, ALU.subtract)
                if m_ < 6:
                    k.tt("dve", Gm[b_][:, 0:n, :], Ga, pgv, ALU.subtract)
                yield
            pw = nextb(d)
            pwv = b3(pw, n)
            for i in range(n):
                k.mm(pwv[:, i, :], Rk0[:, i, :], W[:, i, :])
            k.act(nwT, pwv, AF.Copy, scale=-1.0)
            yield

        def state_gen(d, a, n, gp):
            col = d * 8 + h
            isx = a >= 2
            W, nwT, Vt, ktail, qhT, qkm = (x[d][gp] for x in (g_W, g_nwT, g_Vt, g_ktail, g_qhT, g_qkm))
            vn = t_vn[d]
            for i in (range(n) if d == 0 else range(n - 1, -1, -1)):
                t = a + i
                sc = lambda arr: arr[:, t, col:col + 1]
                k.mm(pvn[d], W[:, i, :], Vt[:, i, :], start=True, stop=False)
                k.mm(pvn[d], nwT[:, i, :], Sb[d], start=False, stop=True)
                k.act(vn, pvn[d], AF.Identity, scale=sc(beta))
                yield
                if isx:
                    xt = t - 2
                    ov = V(osum.ap[:, xt * 128:(xt + 1) * 128], osb[xt])
                    k.mm(poT[d], Sb[d], qhT[:, i, :], start=True, stop=False)
                    k.mm(poT[d], vn, qkm[:, i, :], start=False, stop=True)
                k.mm(pS[d], ktail[:, i, :], vn)
                if isx:
                    if xt not in visited:
                        visited.add(xt)
                        k.act(ov, poT[d], AF.Copy)
                    else:
                        k.tt("dve", ov, poT[d], ov, ALU.add)
                k.stt("dve", Sf[d], Sf[d], sc(egl), pS[d], ALU.mult, ALU.add)
                k.act(Sb[d], Sf[d], AF.Copy)
                yield

        def run_all(gens):
            gens = list(gens)
            while gens:
                for g_ in list(gens):
                    try:
                        next(g_)
                    except StopIteration:
                        gens.remove(g_)

        ng = len(groups)
        run_all([pre_gen(0, *gorder[0][0], 0), pre_gen(1, *gorder[1][0], 0)])
        for gi in range(ng):
            gens = [state_gen(0, *gorder[0][gi], gi % 2), state_gen(1, *gorder[1][gi], gi % 2)]
            if gi + 1 < ng:
                gens += [pre_gen(0, *gorder[0][gi + 1], (gi + 1) % 2), pre_gen(1, *gorder[1][gi + 1], (gi + 1) % 2)]
            run_all(gens)
        if "S" in dbg_out and h == heads[0]:
            k.dma("sp", dbg_out["S"][0], Sf[0])
            k.dma("sp", dbg_out["S"][1], Sf[1])
        for bi in range(8):
            s = bi * 512
            ovs = [V(osum.ap[:, s:s + 512], osb[bi * 4 + j]) for j in range(4)]
            class _M:
                pass
            ovall = V(osum.ap[:, s:s + 512], osb[bi * 4])
            extra = [osb[bi * 4 + j] for j in range(1, 4)]
            if bi == 0:
                k.dma("pool", wbz, V(winv[:, :, OFF_Z + h * 128:OFF_Z + (h + 1) * 128], win_d.buf))
            pz_ = nextbig()
            zsb = zs_b[0]
            for kk in range(8):
                k.mm(pz_, wbz[:, kk, :], hTv[:, kk, TC + s:TC + s + 512], start=(kk == 0), stop=(kk == 7))
            k.act(zsb, pz_, AF.Silu)
            pp = nextbig()
            sqv = ofin[bi % 2]
            r = rn[0]
            of_ = V(osum.ap[:, s:s + 512], [osb[bi * 4 + j] for j in range(4)])
            k.op("pool", lambda g, sqv=sqv, s=s: g.tensor_tensor(out=sqv.ap, in0=osum.ap[:, s:s + 512], in1=osum.ap[:, s:s + 512], op=ALU.mult),
                 [osb[bi * 4 + j] for j in range(4)], [sqv.buf])
            k.mm(pp, onesb, sqv)
            k.act(r, pp, AF.Ln, bias=EPS, scale=1.0 / 128)
            k.act(r, r, AF.Exp, scale=-0.5)
            k.stt("dve", of_, of_, dnn[:, 0:1], r, ALU.mult, ALU.mult)
            if "o0" in dbg_out and h == heads[0]:
                k.tt("pool", of_, of_, zsb, ALU.mult)
                k.dma("sp", V(dbg_out["o0"].ap[:, s:s + 512], dbg_out["o0"].buf), of_)
                k.copy("pool", sqv, of_)
            else:
                k.tt("pool", sqv, of_, zsb, ALU.mult)
            k.dma("sp", V(oT_d.ap[h, :, s:s + 512], DBuf("st")), sqv)
    k.barrier()
    p4.close()
    s_g.close()
    k.stack = ExitStack()
    if stage <= 4:
        return finish(nc, k, out_d)

    def wchunk_loader(stf, stb):
        cnt = [0]
        def load(src_v, K):
            j = cnt[0] % len(stb)
            cnt[0] += 1
            k.dma("pool", stb[j][:, 0:K, :], src_v)
            return stb[j]
        return load

    def load_resident(dst_bf, src_ap, src_buf, K, ncols, stg):
        i = 0
        for k0 in range(0, K, 8):
            kn = min(8, K - k0)
            for c0 in range(0, ncols, 512):
                cn = min(512, ncols - c0)
                k.dma("pool", dst_bf[:, k0:k0 + kn, c0:c0 + cn], V(src_ap[:, k0:k0 + kn, c0:c0 + cn], src_buf))
                i += 1

    with ExitStack() as st:
        k.stack = st
        FCS = k.sb("FCS", [128, 32, 4, 256], BF16)
        fT = [k.sb(f"fT{i}", [128, 512], BF16) for i in range(2)]
        stf = None
        stb = [k.sb(f"p3stb{i}", [128, 8, 128], BF16) for i in range(2)]
        ctab = [k.sb(f"ctab{i}", [128, 4, 512], BF16) for i in range(2)]
        stab = [k.sb(f"stab{i}", [128, 4, 512], BF16) for i in range(2)]
        yblk = [k.sb(f"yblk{i}", [128, 4, 512], BF16) for i in range(2)]
        lw = wchunk_loader(stf, stb)
        nb_ = 0
        for g in range(4):
            wb = lw(V(winv[:, :, OFF_F + g * 128:OFF_F + (g + 1) * 128], win_d.buf), 8)
            for bi, (s, n) in enumerate(xblocks):
                pp = k.banks[nb_ % 2]
                ft = fT[nb_ % 2]
                nb_ += 1
                for kk in range(8):
                    k.mm(pp, wb[:, kk, :], hTv[:, kk, s:s + n], start=(kk == 0), stop=(kk == 7))
                k.act(ft, pp, AF.Copy)
                pf = k.banks[2 + (nb_ % 2)]
                pfv = V(pf.ap.rearrange("p (a b) -> p a b", b=256), pf.buf)
                for j2 in range(2):
                    for jj in range(2):
                        j = j2 * 2 + jj
                        k.mm(pfv[:, jj, :], ft[:, j * 128:(j + 1) * 128], c128)
                    t0 = bi * 4 + j2 * 2
                    if j2 == 0:
                        k.act(FCS[:, t0:t0 + 2, g, :], pfv, AF.Copy)
                    else:
                        k.copy("dve", FCS[:, t0:t0 + 2, g, :], pfv)
        cosv = cos_d.ap.rearrange("(tt p) f -> p tt f", p=128)
        sinv = sin_d.ap.rearrange("(tt p) f -> p tt f", p=128)
        ld = 0
        for kb in range(8):
            for t4 in range(8):
                ct, st_ = ctab[ld % 2], stab[ld % 2]
                ld += 1
                k.dma("sp", ct, V(cosv[:, t4 * 4:(t4 + 1) * 4, kb * 512:(kb + 1) * 512], cos_d.buf))
                k.dma("pool", st_, V(sinv[:, t4 * 4:(t4 + 1) * 4, kb * 512:(kb + 1) * 512], sin_d.buf))
                for ti in range(4):
                    tt_ = t4 * 4 + ti
                    for g in range(4):
                        k.mm(k.banks[4 + g], FCS[:, tt_, g, 0:128], ct[:, ti, :], start=(tt_ == 0), stop=False)
                        k.mm(k.banks[4 + g], FCS[:, tt_, g, 128:256], st_[:, ti, :], start=False, stop=(tt_ == 31))
            yb = yblk[kb % 2]
            for g in range(4):
                if g % 2 == 0:
                    k.act(yb[:, g, :], k.banks[4 + g], AF.Copy)
                else:
                    k.copy("dve", yb[:, g, :], k.banks[4 + g])
            k.dma("sp", V(yT_d.ap[:, :, kb * 512:(kb + 1) * 512].rearrange("g p f -> p g f"), DBuf("st")), yb)
        if "fm" in dbg_out:
            pass
        k.barrier()
    k.stack = ExitStack()
    if stage <= 5:
        return finish(nc, k, out_d)

    with ExitStack() as st:
        k.stack = st
        wg = k.sb("wg", [128, 8, 2048], BF16)
        wf4 = k.sb("wf4", [128, 4, 1024], BF16)
        wdn = k.sb("wdn", [128, 8, 1024], BF16)
        stg = None
        ytb = [k.sb(f"ytb{i}", [128, 4, 512], BF16) for i in range(2)]
        otb = [k.sb(f"otb{i}", [128, 8, 512], BF16) for i in range(2)]
        g0 = [k.sb(f"g0_{i}", [128, 512], BF16) for i in range(2)]
        g1 = [k.sb(f"g1_{i}", [128, 512], BF16) for i in range(2)]
        m0 = [k.sb(f"m0_{i}", [128, 512], BF16) for i in range(2)]
        m1 = [k.sb(f"m1_{i}", [128, 512], BF16) for i in range(2)]
        mixs = k.sb("mixs", [128, 2, 8, 512], BF16)
        mixs_buf = [Buf("mixs0"), Buf("mixs1")]
        load_resident(wg, winv[:, :, OFF_G:OFF_G + 2048], win_d.buf, 8, 2048, stg)
        load_resident(wf4, wf_d.ap.rearrange("(g p) d -> p g d", p=128), wf_d.buf, 4, 1024, stg)
        load_resident(wdn, wdn_d.ap.rearrange("(h p) d -> p h d", p=128), wdn_d.buf, 8, 1024, stg)
        it = 0
        hmt = [Buf(f"hTm{mt}") for mt in range(8)]
        for mt in range(8):
            s = TC + mt * 512
            yt, ot = ytb[mt % 2], otb[mt % 2]
            k.dma("sp", yt, V(yT_d.ap[:, :, mt * 512:(mt + 1) * 512].rearrange("g p f -> p g f"), yT_d.buf))
            k.dma("pool", ot, V(oT_d.ap[:, :, mt * 512:(mt + 1) * 512].rearrange("h p f -> p h f"), oT_d.buf))
            for dc in range(8):
                dsl = slice(dc * 128, (dc + 1) * 128)
                i2 = it % 2
                it += 1
                pb = [k.banks[4 * i2 + j] for j in range(4)]
                for g in range(4):
                    k.mm(pb[0], wf4[:, g, dsl], yt[:, g, :], start=(g == 0), stop=(g == 3))
                for kk in range(8):
                    k.mm(pb[1], wg[:, kk, dc * 128:(dc + 1) * 128], V(hT.ap[:, kk, s:s + 512], hmt[mt]), start=(kk == 0), stop=(kk == 7))
                for hh in range(8):
                    k.mm(pb[2], wdn[:, hh, dsl], ot[:, hh, :], start=(hh == 0), stop=(hh == 7))
                for kk in range(8):
                    k.mm(pb[3], wg[:, kk, 1024 + dc * 128:1024 + (dc + 1) * 128], V(hT.ap[:, kk, s:s + 512], hmt[mt]), start=(kk == 0), stop=(kk == 7))
                k.act(g0[i2], pb[1], AF.Sigmoid)
                k.act(g1[i2], pb[3], AF.Sigmoid)
                k.tt("dve", m0[i2], pb[0], g0[i2], ALU.mult)
                k.tt("dve", m1[i2], pb[2], g1[i2], ALU.mult)
                k.tt("pool", V(mixs.ap[:, mt % 2, dc, :], mixs_buf[mt % 2]), m0[i2], m1[i2], ALU.add)
            k.copy("act" if mt % 2 == 0 else "dve", V(hT.ap[:, :, s:s + 512], hmt[mt]), V(mixs.ap[:, mt % 2, :, :], mixs_buf[mt % 2]))
        k.barrier()
    k.stack = ExitStack()
    if stage <= 6:
        return finish(nc, k, out_d)

    def branch_tail(mt, producer, cidx, resid_d, final, tb):
        yx, sq, rst, xin_, x1t = tb["yx"][mt % 2], tb["sq"][mt % 2], tb["rst"][mt % 2], tb["xin"], tb["x1t"]
        for dc in range(8):
            pb = k.banks[dc % 2]
            producer(dc, pb)
            k.act(yx[:, dc, :], pb, AF.Copy)
            k.act(sq[:, dc, :], pb, AF.Square)
        pss = k.banks[2]
        for dc in range(8):
            k.mm(pss, onesb, sq[:, dc, :], start=(dc == 0), stop=(dc == 7))
        k.act(rst, pss, AF.Ln, bias=EPS, scale=1.0 / D)
        k.act(rst, rst, AF.Exp, scale=-0.5)
        for dc in range(8):
            k.stt("dve", yx[:, dc, :], yx[:, dc, :], coef[:, cidx, dc:dc + 1], rst, ALU.mult, ALU.mult)
        for j in range(4):
            tok0 = mt * 512 + j * 128
            xi = xin_[j % len(xin_)]
            xo = x1t[j % len(x1t)]
            k.dma("sp" if j % 2 == 0 else "pool", xi, V(resid_d.ap[tok0:tok0 + 128, :], resid_d.buf))
            ba, bb = k.banks[3 + 2 * (j % 2)], k.banks[4 + 2 * (j % 2)]
            for dc in range(8):
                bk = ba if dc < 4 else bb
                k.tr(bk[:, (dc % 4) * 128:(dc % 4 + 1) * 128], yx[:, dc, j * 128:(j + 1) * 128], identf)
            k.tt("dve", xo[:, 0:512], ba, xi[:, 0:512], ALU.add)
            k.tt("dve", xo[:, 512:1024], bb, xi[:, 512:1024], ALU.add)
            if final:
                k.dma("sp", V(out_d.ap[tok0:tok0 + 128, :], DBuf("st")), xo)
            else:
                k.dma("sp", V(x1_d.ap[tok0:tok0 + 128, :], DBuf("st")), xo)
                norm_tile(xo, 2 + mt * 4 + j, 5, 6, hT, tb["hbuf"], tb["tm"])

    def tail_bufs(nbuf, need_tm=True):
        if not need_tm:
            return {"yx": [k.sb(f"yx{i}", [128, 8, 512], F32) for i in range(2)], "sq": [k.sb(f"sqb{i}", [128, 8, 512], BF16) for i in range(2)],
                    "rst": [k.sb(f"rst{i}", [128, 512], F32) for i in range(2)],
                    "xin": [k.sb(f"rxin{i}", [128, D], F32) for i in range(nbuf)], "x1t": [k.sb(f"x1t{i}", [128, D], F32) for i in range(nbuf)], "tm": None}
        return {"yx": [k.sb(f"yx{i}", [128, 8, 512], F32) for i in range(2)], "sq": [k.sb(f"sqb{i}", [128, 8, 512], BF16) for i in range(2)],
                "rst": [k.sb(f"rst{i}", [128, 512], F32) for i in range(2)],
                "xin": [k.sb(f"rxin{i}", [128, D], F32) for i in range(nbuf)], "x1t": [k.sb(f"x1t{i}", [128, D], F32) for i in range(nbuf)],
                "tm": {"sq": [k.sb(f"t_sq{i}", [128, D], F32) for i in range(2)], "ss": [k.sb(f"t_ss{i}", [128, 1], F32) for i in range(2)],
                       "xn": [k.sb(f"t_xn{i}", [128, D], BF16) for i in range(2)], "pt": [k.pv(7, 0, 512, BF16, 128)]}}

    with ExitStack() as st:
        k.stack = st
        wout = k.sb("wout", [128, 8, 1024], BF16)
        stg = None
        load_resident(wout, wout_d.ap.rearrange("(c p) d -> p c d", p=128), wout_d.buf, 8, 1024, stg)
        tb = tail_bufs(2)
        hmt2 = [Buf(f"hTn{mt}") for mt in range(8)]
        for mt in range(8):
            s = TC + mt * 512
            def prod(dc, pb, s=s, mt=mt):
                for c in range(8):
                    k.mm(pb, wout[:, c, dc * 128:(dc + 1) * 128], V(hT.ap[:, c, s:s + 512], hmt2[mt]), start=(c == 0), stop=(c == 7))
            tb["hbuf"] = hmt2[mt]
            branch_tail(mt, prod, 4, x_d, False, tb)
        k.barrier()
    k.stack = ExitStack()
    if stage <= 7:
        return finish(nc, k, out_d)

    wupv = wup_d.ap.rearrange("(k p) c -> p k c", p=128)
    with ExitStack() as st:
        k.stack = st
        stf = None
        stb = [k.sb(f"p6stb{i}", [128, 8, 128], BF16) for i in range(4)]
        lw = wchunk_loader(stf, stb)
        apad_b = [k.sb(f"apad{i}", [128, 66, 66], BF16) for i in range(2)]
        dg9_b = [k.sb(f"dg9_{i}", [128, 9, 128], BF16) for i in range(2)]
        sa = [k.sb(f"sa{i}", [128, 512], BF16) for i in range(4)]
        gtc = [k.sb(f"gtc{i}", [128, T], BF16) for i in range(2)]
        k.memset("pool", apad_b[0], 0.0)
        k.memset("pool", apad_b[1], 0.0)
        nb_ = 0
        for c in range(NFF):
            apad, dg9 = apad_b[c % 2], dg9_b[c % 2]
            wa = lw(V(wupv[:, :, c * 128:(c + 1) * 128], wup_d.buf), 8)
            wu = lw(V(wupv[:, :, DFF + c * 128:DFF + (c + 1) * 128], wup_d.buf), 8)
            for tap in range(9):
                k.ts("pool", dg9[:, tap, :], identf, cffn[:, c, tap:tap + 1])
            for bi in range(8):
                s = TC + bi * 512
                pp = k.banks[(0, 1, 6, 7)[nb_ % 4]]
                nb_ += 1
                for kk in range(8):
                    k.mm(pp, wa[:, kk, :], hTv[:, kk, s:s + 512], start=(kk == 0), stop=(kk == 7))
                k.act(apad[:, 1 + bi * 8:1 + bi * 8 + 8, 1:65], V(pp.ap.rearrange("p (r c) -> p r c", c=64), pp.buf), AF.Copy)
            gt = gtc[c % 2]
            for bi in range(8):
                s = TC + bi * 512
                pc = k.banks[2 + (bi % 2)]
                pu = k.banks[4 + (bi % 2)]
                pcv = V(pc.ap.rearrange("p (r c) -> p r c", c=64), pc.buf)
                for tap in range(9):
                    dr, dcc = tap // 3, tap % 3
                    k.mm(pcv, dg9[:, tap, :], apad[:, bi * 8 + dr:bi * 8 + dr + 8, dcc:dcc + 64], start=(tap == 0), stop=(tap == 8))
                k.act(sa[bi % 4], pc, AF.Silu)
                for kk in range(8):
                    k.mm(pu, wu[:, kk, :], hTv[:, kk, s:s + 512], start=(kk == 0), stop=(kk == 7))
                k.tt("dve", gt[:, bi * 512:(bi + 1) * 512], pu, sa[bi % 4], ALU.mult)
            k.dma("sp" if c % 2 == 0 else "pool", V(gT_d.ap[c], DBuf("st")), gt)
        k.barrier()
    k.stack = ExitStack()
    s_h.close()
    k.stack = ExitStack()
    if stage <= 8:
        return finish(nc, k, out_d)

    with ExitStack() as st:
        k.stack = st
        wdown = k.sb("wdown", [128, NFF, 1024], BF16)
        stg = None
        load_resident(wdown, wdown_d.ap.rearrange("(c p) d -> p c d", p=128), wdown_d.buf, NFF, 1024, stg)
        gbl = [k.sb(f"gbl{i}", [128, NFF, 512], BF16) for i in range(2)]
        tb = tail_bufs(2, need_tm=False)
        for mt in range(8):
            gb = gbl[mt % 2]
            k.dma("sp", gb[:, 0:11, :], V(gT_d.ap[0:11, :, mt * 512:(mt + 1) * 512].rearrange("c p f -> p c f"), gT_d.buf))
            k.dma("pool", gb[:, 11:NFF, :], V(gT_d.ap[11:NFF, :, mt * 512:(mt + 1) * 512].rearrange("c p f -> p c f"), gT_d.buf))
            def prod(dc, pb, gb=gb):
                for c in range(NFF):
                    k.mm(pb, wdown[:, c, dc * 128:(dc + 1) * 128], gb[:, c, :], start=(c == 0), stop=(c == NFF - 1))
            branch_tail(mt, prod, 7, x1_d, True, tb)
        k.barrier()
    k.stack = ExitStack()
    return finish(nc, k, out_d)


def finish(nc, k, out_d):
    k.barrier()
    return nc


def prep_inputs(inp, b):
    f = lambda a: np.ascontiguousarray(a, dtype=np.float32)
    colmajor = lambda v: f(np.asarray(v).reshape(-1, 128).T)
    m = {}
    m["x"] = f(inp["x"][b])
    m["ctx"] = f(inp["ctx"][b])
    m["cc"] = f(np.stack([colmajor(inp["c"][b]), colmajor(inp["c_ctx"])], axis=-1))
    m["w_ada"] = f(inp["w_ada"][0])
    m["b_ada"] = colmajor(inp["b_ada"][0])
    m["norms"] = f(np.stack([colmajor(inp[n][0]) for n in ("norm_pre_mix", "norm_post_mix", "norm_pre_ffn", "norm_post_ffn")], axis=1))
    m["w_in"] = f(inp["w_in"][0])
    cq = np.asarray(inp["conv_qkv"][0])
    m["conv_qkv"] = f(cq.T.reshape(24, 128, 3).transpose(1, 0, 2))
    gp = np.stack([np.asarray(inp["a_log"][0]).reshape(16), np.asarray(inp["dt_bias"][0]).reshape(16)], 0)
    m["gpar"] = f(np.broadcast_to(gp[None], (128, 2, 16)))
    m["dn_norm"] = f(np.asarray(inp["dn_norm"][0]).reshape(128, 1))
    m["w_fourier"] = f(inp["w_fourier"][0])
    m["w_dn"] = f(inp["w_dn"][0])
    m["w_out"] = f(inp["w_out"][0])
    m["w_up"] = f(inp["w_up"][0])
    cf = np.asarray(inp["conv_ffn"][0]).reshape(9, DFF)
    m["conv_ffn"] = f(cf.T.reshape(NFF, 128, 9).transpose(1, 0, 2))
    m["w_down"] = f(inp["w_down"][0])
    return m


_CONST = {}


def consts():
    if not _CONST:
        idx = np.arange(T, dtype=np.int64)
        ang = (2.0 * np.pi / T) * ((idx[:, None] * idx[None, :]) % T).astype(np.float64)
        s = 1.0 / np.sqrt(float(T) * 128.0)
        _CONST["dft_cos"] = (np.cos(ang) * s).astype(ml_dtypes.bfloat16)
        _CONST["dft_sin"] = (np.sin(ang) * s).astype(ml_dtypes.bfloat16)
        i8 = np.arange(128, dtype=np.int64)
        a8 = (2.0 * np.pi / 128) * ((i8[:, None] * i8[None, :]) % 128).astype(np.float64)
        j8 = np.arange(128)
        hm = []
        for m_ in range(7):
            s_ = 2 ** m_
            blk2 = (j8[:, None] // (2 * s_)) == (j8[None, :] // (2 * s_))
            half = (j8[:, None] // s_) != (j8[None, :] // s_)
            hm.append((blk2 & half).astype(np.float32))
        _CONST["hmask"] = np.stack(hm, axis=1).astype(ml_dtypes.bfloat16)
        _CONST["dft128"] = np.concatenate([np.cos(a8), -np.sin(a8)], axis=1).astype(ml_dtypes.bfloat16)
    return _CONST


def kernel(**inputs):
    inp = {k_: np.asarray(v) for k_, v in inputs.items()}
    nc = build()
    cst = consts()
    in_maps = []
    for b in range(8):
        m = prep_inputs(inp, b)
        m.update(cst)
        in_maps.append(m)
    res = run_bass_kernel_spmd(nc, in_maps, core_ids=list(range(8)))
    return np.stack([np.asarray(r["out"], dtype=np.float32) for r in res.results], axis=0)
```
